# Optimizing a Trainium2 kernel written in Bass

```python
import math
import jax, jax.numpy as jnp
from jax import lax
import numpy as np

D_MODEL = 1024
BATCH = 4
SEQ = 8192
DEPTH = 4
DEC_BATCH = 8
DEC_SEQ = 32
PAST_LEN = 1024

CHUNK = 64
N_MIXERS = 3
N_SSD_LAYERS = (DEPTH + 2) // 3
N_SB_LAYERS = (DEPTH + 1) // 3
N_SWA_LAYERS = DEPTH // 3
NORM_EPS = 1e-6
FFN_RESID = 0.5
D_FF = ((8 * D_MODEL // 3 + 127) // 128) * 128

SSD_D_INNER = 2 * D_MODEL
SSD_HEAD_DIM = 64
SSD_HEADS = SSD_D_INNER // SSD_HEAD_DIM
SSD_GROUPS = 4
SSD_STATE = 128
SSD_CONV = 4
SSD_GN = SSD_GROUPS * SSD_STATE
SSD_CONV_DIM = SSD_D_INNER + 2 * SSD_GN
SSD_IN_DIM = SSD_D_INNER + SSD_CONV_DIM + SSD_HEADS
SSD_CHUNK = CHUNK

SB_HEAD_DIM = 64
SB_HEADS = D_MODEL // SB_HEAD_DIM
SB_BLOCK = 128

SWA_HEAD_DIM = 64
SWA_Q_HEADS = D_MODEL // SWA_HEAD_DIM
SWA_KV_HEADS = 4
SWA_WINDOW = 128
SWA_BACK_CHUNKS = -(-SWA_WINDOW // CHUNK)
SWA_ROWS = SWA_BACK_CHUNKS * CHUNK

kernel_name = 'hybrid_streaming_ssd_stickbreak_swa_step'


def rmsnorm(x, g):
    x32 = x.astype(jnp.float32)
    y = x32 * lax.rsqrt(jnp.mean(x32 * x32, axis=-1, keepdims=True) + NORM_EPS)
    return (y * g.astype(jnp.float32)).astype(x.dtype)


def adaln_in(x, g, shift, scale):
    return rmsnorm(x, g) * (1 + scale) + shift


def swiglu(h, w_in, w_out):
    a, b = jnp.split(h @ w_in, 2, axis=-1)
    return (jax.nn.silu(a) * b) @ w_out


def causal_dwconv(xpad, w, b):
    y = lax.conv_general_dilated(xpad, w[:, None, :].astype(xpad.dtype), window_strides=(1,), padding='VALID',
                                 dimension_numbers=('NWC', 'WIO', 'NWC'), feature_group_count=xpad.shape[-1])
    return y + b.astype(xpad.dtype)


def ssd_scan(x, dt, A, Bm, Cm, h0, q):
    f32 = jnp.float32
    bsz, L, H, P = x.shape
    G, N = Bm.shape[2], Bm.shape[3]
    R = H // G
    nc = L // q
    x = x.astype(f32).reshape(bsz, nc, q, G, R, P)
    dt = dt.astype(f32).reshape(bsz, nc, q, G, R)
    Bm = Bm.astype(f32).reshape(bsz, nc, q, G, N)
    Cm = Cm.astype(f32).reshape(bsz, nc, q, G, N)
    acum = jnp.cumsum(dt * A.reshape(G, R), axis=2)
    causal = jnp.tril(jnp.ones((q, q), dtype=bool))[None, None, :, :, None, None]
    seg = acum[:, :, :, None] - acum[:, :, None, :]
    decay_ts = jnp.exp(jnp.where(causal, seg, -jnp.inf))
    cb = jnp.einsum('bctgn,bcsgn->bctsg', Cm, Bm)
    y_diag = jnp.einsum('bctsg,bctsgr,bcsgr,bcsgrp->bctgrp', cb, decay_ts, dt, x)
    decay_end = jnp.exp(acum[:, :, -1:] - acum)
    states = jnp.einsum('bcsgn,bcsgr,bcsgrp->bcgrpn', Bm, decay_end * dt, x)
    chunk_decay = jnp.exp(acum[:, :, -1])

    def step(h, inp):
        s_c, d_c = inp
        return h * d_c[..., None, None] + s_c, h

    h_last, h_in = lax.scan(step, h0.astype(f32).reshape(bsz, G, R, P, N),
                            (jnp.moveaxis(states, 1, 0), jnp.moveaxis(chunk_decay, 1, 0)))
    h_in = jnp.moveaxis(h_in, 0, 1)
    y_off = jnp.einsum('bctgn,bcgrpn,bctgr->bctgrp', Cm, h_in, jnp.exp(acum))
    y = (y_diag + y_off).reshape(bsz, L, H, P)
    return y, h_last.reshape(bsz, H, P, N)


def ssd_mixer(h, conv_state, ssm_state, w_in, conv_w, conv_b, dt_bias, a_log, d_skip, norm_g, w_out, chunk):
    f32 = jnp.float32
    bsz, L, _ = h.shape
    proj = h @ w_in
    z = proj[..., :SSD_D_INNER]
    xbc = proj[..., SSD_D_INNER:SSD_D_INNER + SSD_CONV_DIM]
    dt_raw = proj[..., SSD_D_INNER + SSD_CONV_DIM:]
    xpad = jnp.concatenate([conv_state.astype(xbc.dtype), xbc], axis=1)
    new_conv = xpad[:, xpad.shape[1] - (SSD_CONV - 1):]
    xbc = jax.nn.silu(causal_dwconv(xpad, conv_w, conv_b))
    xs = xbc[..., :SSD_D_INNER].reshape(bsz, L, SSD_HEADS, SSD_HEAD_DIM)
    Bm = xbc[..., SSD_D_INNER:SSD_D_INNER + SSD_GN].reshape(bsz, L, SSD_GROUPS, SSD_STATE)
    Cm = xbc[..., SSD_D_INNER + SSD_GN:].reshape(bsz, L, SSD_GROUPS, SSD_STATE)
    dt = jax.nn.softplus(dt_raw.astype(f32) + dt_bias.astype(f32))
    A = -jnp.exp(a_log.astype(f32))
    y, h_new = ssd_scan(xs, dt, A, Bm, Cm, ssm_state, chunk)
    y = y + d_skip.astype(f32)[:, None] * xs.astype(f32)
    y = y.reshape(bsz, L, SSD_D_INNER).astype(h.dtype)
    y = rmsnorm(y * jax.nn.silu(z), norm_g)
    return y @ w_out, new_conv, h_new.astype(ssm_state.dtype)


def sb_piece(q, k, v, q_pos, k_pos, acc):
    z = jnp.einsum('bqhd,bkhd->bhqk', q, k).astype(jnp.float32) * (SB_HEAD_DIM ** -0.5)
    mask = k_pos[None, :] < q_pos[:, None]
    u = jnp.where(mask, jax.nn.log_sigmoid(-z), 0.0)
    later = lax.cumsum(u, axis=3, reverse=True) - u + acc[..., None]
    w = jnp.where(mask, jnp.exp(jax.nn.log_sigmoid(z) + later), 0.0)
    out = jnp.einsum('bhqk,bkhd->bhqd', w, v.astype(jnp.float32))
    return out, acc + jnp.sum(u, axis=-1)


def sb_attention_prompt(q, k, v):
    bsz, L, H, dh = q.shape
    nb = L // SB_BLOCK

    def blocks(t):
        return jnp.swapaxes(t.reshape(bsz, nb, SB_BLOCK, H, dh), 0, 1)

    qb, kb, vb = blocks(q), blocks(k), blocks(v)
    pos = jnp.arange(L, dtype=jnp.int32).reshape(nb, SB_BLOCK)

    def one_query_block(args):
        qi, qpos = args

        def step(carry, kv):
            acc, out = carry
            kj, vj, kpos = kv
            o, acc = sb_piece(qi, kj, vj, qpos, kpos, acc)
            return (acc, out + o), None

        init = (jnp.zeros((bsz, H, SB_BLOCK), jnp.float32), jnp.zeros((bsz, H, SB_BLOCK, dh), jnp.float32))
        (_, out), _ = lax.scan(step, init, (kb, vb, pos), reverse=True)
        return out

    out = lax.map(one_query_block, (qb, pos))
    return jnp.transpose(out, (1, 0, 3, 2, 4)).reshape(bsz, L, H * dh)


def sb_mixer(h, past_k, past_v, w_qkv, w_out):
    bsz, L, _ = h.shape
    q, k, v = [t.reshape(bsz, L, SB_HEADS, SB_HEAD_DIM) for t in jnp.split(h @ w_qkv, 3, axis=-1)]
    if past_k is None:
        o = sb_attention_prompt(q, k, v)
    else:
        past = past_k.shape[1]
        kf = jnp.concatenate([past_k.astype(k.dtype), k], axis=1)
        vf = jnp.concatenate([past_v.astype(v.dtype), v], axis=1)
        q_pos = past + jnp.arange(L, dtype=jnp.int32)
        k_pos = jnp.arange(past + L, dtype=jnp.int32)
        o, _ = sb_piece(q, kf, vf, q_pos, k_pos, jnp.zeros((bsz, SB_HEADS, L), jnp.float32))
        o = jnp.transpose(o, (0, 2, 1, 3)).reshape(bsz, L, SB_HEADS * SB_HEAD_DIM)
    return o.astype(h.dtype) @ w_out, k, v


def softmax_with_sink(s, sink):
    m = jnp.maximum(jnp.max(s, axis=-1, keepdims=True), sink[..., None])
    p = jnp.exp(s - m)
    return p / (jnp.sum(p, axis=-1, keepdims=True) + jnp.exp(sink[..., None] - m))


def swa_mixer(h, past_k, past_v, w_qkv, sinks, w_out):
    bsz, L, _ = h.shape
    G, R, dh = SWA_KV_HEADS, SWA_Q_HEADS // SWA_KV_HEADS, SWA_HEAD_DIM
    proj = h @ w_qkv
    q = proj[..., :SWA_Q_HEADS * dh].reshape(bsz, L, G, R, dh)
    k = proj[..., SWA_Q_HEADS * dh:(SWA_Q_HEADS + G) * dh].reshape(bsz, L, G, dh)
    v = proj[..., (SWA_Q_HEADS + G) * dh:].reshape(bsz, L, G, dh)
    sink = sinks.astype(jnp.float32).reshape(G, R, 1)
    scale = dh ** -0.5
    if past_k is None:
        nC = L // CHUNK

        def band(t):
            tc = t.reshape(bsz, nC, CHUNK, G, dh)
            tp = jnp.pad(tc, ((0, 0), (SWA_BACK_CHUNKS, 0), (0, 0), (0, 0), (0, 0)))
            return jnp.concatenate([tp[:, j:j + nC] for j in range(SWA_BACK_CHUNKS + 1)], axis=2)

        kb, vb = band(k), band(v)
        qc = q.reshape(bsz, nC, CHUNK, G, R, dh)
        key_chunk = (jnp.arange(nC)[:, None] + (jnp.arange((SWA_BACK_CHUNKS + 1) * CHUNK) // CHUNK)[None, :]
                     - SWA_BACK_CHUNKS)
        valid = key_chunk >= 0
        s = jnp.einsum('bcqgrd,bckgd->bcgrqk', qc, kb).astype(jnp.float32) * scale
        s = jnp.where(valid[None, :, None, None, None, :], s, -jnp.inf)
        p = softmax_with_sink(s, sink)
        o = jnp.einsum('bcgrqk,bckgd->bcqgrd', p.astype(v.dtype), vb).reshape(bsz, L, SWA_Q_HEADS * dh)
        new_k, new_v = k[:, L - SWA_ROWS:], v[:, L - SWA_ROWS:]
    else:
        kf = jnp.concatenate([past_k.astype(k.dtype), k], axis=1)
        vf = jnp.concatenate([past_v.astype(v.dtype), v], axis=1)
        s = jnp.einsum('bqgrd,bkgd->bgrqk', q, kf).astype(jnp.float32) * scale
        p = softmax_with_sink(s, sink)
        o = jnp.einsum('bgrqk,bkgd->bqgrd', p.astype(v.dtype), vf).reshape(bsz, L, SWA_Q_HEADS * dh)
        n = kf.shape[1]
        new_k, new_v = kf[:, n - SWA_ROWS:], vf[:, n - SWA_ROWS:]
    return o @ w_out, new_k, new_v


def run_trunk(x, c, is_prompt, state_ssm, state_conv, cache_sb_k, cache_sb_v, cache_swa_k, cache_swa_v,
              ada_w, ada_b, norm_g, ffn_w_in, ffn_w_out,
              ssd_w_in, ssd_conv_w, ssd_conv_b, ssd_dt_bias, ssd_a_log, ssd_d, ssd_norm_g, ssd_w_out,
              sb_w_qkv, sb_w_out, swa_w_qkv, swa_sinks, swa_w_out):
    bsz, L, _ = x.shape
    c_act = jax.nn.silu(c)
    ssm_new, conv_new, sbk_new, sbv_new, swak_new, swav_new = [], [], [], [], [], []
    for i in range(DEPTH):
        mod = (c_act @ ada_w[i] + ada_b[i]).reshape(bsz, 3, 3, 1, D_MODEL)
        h = adaln_in(x, norm_g[i, 0], mod[:, 0, 0], mod[:, 0, 1])
        x = x + FFN_RESID * mod[:, 0, 2] * rmsnorm(swiglu(h, ffn_w_in[i, 0], ffn_w_out[i, 0]), norm_g[i, 1])
        h = adaln_in(x, norm_g[i, 2], mod[:, 1, 0], mod[:, 1, 1])
        kind, j = i % N_MIXERS, i // N_MIXERS
        if kind == 0:
            if is_prompt:
                conv0 = jnp.zeros((bsz, SSD_CONV - 1, SSD_CONV_DIM), x.dtype)
                ssm0 = jnp.zeros((bsz, SSD_HEADS, SSD_HEAD_DIM, SSD_STATE), x.dtype)
            else:
                conv0, ssm0 = state_conv[j], state_ssm[j]
            y, conv1, ssm1 = ssd_mixer(h, conv0, ssm0, ssd_w_in[j], ssd_conv_w[j], ssd_conv_b[j], ssd_dt_bias[j],
                                       ssd_a_log[j], ssd_d[j], ssd_norm_g[j], ssd_w_out[j],
                                       SSD_CHUNK if is_prompt else L)
            conv_new.append(conv1)
            ssm_new.append(ssm1)
        elif kind == 1:
            pk = None if is_prompt else cache_sb_k[j]
            pv = None if is_prompt else cache_sb_v[j]
            y, nk, nv = sb_mixer(h, pk, pv, sb_w_qkv[j], sb_w_out[j])
            sbk_new.append(nk)
            sbv_new.append(nv)
        else:
            pk = None if is_prompt else cache_swa_k[j]
            pv = None if is_prompt else cache_swa_v[j]
            y, nk, nv = swa_mixer(h, pk, pv, swa_w_qkv[j], swa_sinks[j], swa_w_out[j])
            swak_new.append(nk)
            swav_new.append(nv)
        x = x + mod[:, 1, 2] * rmsnorm(y, norm_g[i, 3])
        h = adaln_in(x, norm_g[i, 4], mod[:, 2, 0], mod[:, 2, 1])
        x = x + FFN_RESID * mod[:, 2, 2] * rmsnorm(swiglu(h, ffn_w_in[i, 1], ffn_w_out[i, 1]), norm_g[i, 5])
    return (x, jnp.stack(ssm_new), jnp.stack(conv_new), jnp.stack(sbk_new), jnp.stack(sbv_new),
            jnp.stack(swak_new), jnp.stack(swav_new))


def setup_inputs(seed: int = 0) -> dict:
    key = jax.random.key(seed)
    ks = jax.random.split(key, 32)
    f32 = jnp.float32

    def nrm(k, shape, scale):
        return jax.random.normal(k, shape, f32) * scale

    dt0 = jnp.exp(jax.random.uniform(ks[15], (N_SSD_LAYERS, SSD_HEADS), f32, math.log(1e-3), math.log(1e-1)))
    return {
        'x_prompt': nrm(ks[0], (BATCH, SEQ, D_MODEL), 1.0),
        'x_sample': nrm(ks[1], (DEC_BATCH, DEC_SEQ, D_MODEL), 1.0),
        'c_prompt': nrm(ks[2], (BATCH, D_MODEL), 1.0),
        'c_sample': nrm(ks[3], (DEC_BATCH, D_MODEL), 1.0),
        'state_ssm': nrm(ks[4], (N_SSD_LAYERS, DEC_BATCH, SSD_HEADS, SSD_HEAD_DIM, SSD_STATE), 0.5),
        'state_conv': nrm(ks[5], (N_SSD_LAYERS, DEC_BATCH, SSD_CONV - 1, SSD_CONV_DIM), 1.0),
        'cache_sb_k': nrm(ks[6], (N_SB_LAYERS, DEC_BATCH, PAST_LEN, SB_HEADS, SB_HEAD_DIM), 1.0),
        'cache_sb_v': nrm(ks[7], (N_SB_LAYERS, DEC_BATCH, PAST_LEN, SB_HEADS, SB_HEAD_DIM), 1.0),
        'cache_swa_k': nrm(ks[8], (N_SWA_LAYERS, DEC_BATCH, SWA_ROWS, SWA_KV_HEADS, SWA_HEAD_DIM), 1.0),
        'cache_swa_v': nrm(ks[9], (N_SWA_LAYERS, DEC_BATCH, SWA_ROWS, SWA_KV_HEADS, SWA_HEAD_DIM), 1.0),
        'ada_w': nrm(ks[10], (DEPTH, D_MODEL, 9 * D_MODEL), D_MODEL ** -0.5),
        'ada_b': nrm(ks[11], (DEPTH, 9 * D_MODEL), 0.01),
        'norm_g': 1.0 + nrm(ks[12], (DEPTH, 6, D_MODEL), 0.02),
        'ffn_w_in': nrm(ks[13], (DEPTH, 2, D_MODEL, 2 * D_FF), D_MODEL ** -0.5),
        'ffn_w_out': nrm(ks[14], (DEPTH, 2, D_FF, D_MODEL), D_FF ** -0.5),
        'ssd_w_in': nrm(ks[16], (N_SSD_LAYERS, D_MODEL, SSD_IN_DIM), D_MODEL ** -0.5),
        'ssd_conv_w': nrm(ks[17], (N_SSD_LAYERS, SSD_CONV, SSD_CONV_DIM), SSD_CONV ** -0.5),
        'ssd_conv_b': nrm(ks[18], (N_SSD_LAYERS, SSD_CONV_DIM), 0.02),
        'ssd_dt_bias': dt0 + jnp.log(-jnp.expm1(-dt0)),
        'ssd_a_log': jnp.log(jax.random.uniform(ks[19], (N_SSD_LAYERS, SSD_HEADS), f32, 1.0, 16.0)),
        'ssd_d': 1.0 + nrm(ks[20], (N_SSD_LAYERS, SSD_HEADS), 0.02),
        'ssd_norm_g': 1.0 + nrm(ks[21], (N_SSD_LAYERS, SSD_D_INNER), 0.02),
        'ssd_w_out': nrm(ks[22], (N_SSD_LAYERS, SSD_D_INNER, D_MODEL), SSD_D_INNER ** -0.5),
        'sb_w_qkv': nrm(ks[23], (N_SB_LAYERS, D_MODEL, 3 * SB_HEADS * SB_HEAD_DIM), D_MODEL ** -0.5),
        'sb_w_out': nrm(ks[24], (N_SB_LAYERS, SB_HEADS * SB_HEAD_DIM, D_MODEL), (SB_HEADS * SB_HEAD_DIM) ** -0.5),
        'swa_w_qkv': nrm(ks[25], (N_SWA_LAYERS, D_MODEL, (SWA_Q_HEADS + 2 * SWA_KV_HEADS) * SWA_HEAD_DIM),
                         D_MODEL ** -0.5),
        'swa_sinks': nrm(ks[26], (N_SWA_LAYERS, SWA_Q_HEADS), 0.5),
        'swa_w_out': nrm(ks[27], (N_SWA_LAYERS, SWA_Q_HEADS * SWA_HEAD_DIM, D_MODEL),
                         (SWA_Q_HEADS * SWA_HEAD_DIM) ** -0.5),
    }


def reference(x_prompt, x_sample, c_prompt, c_sample, state_ssm, state_conv, cache_sb_k, cache_sb_v,
              cache_swa_k, cache_swa_v, ada_w, ada_b, norm_g, ffn_w_in, ffn_w_out,
              ssd_w_in, ssd_conv_w, ssd_conv_b, ssd_dt_bias, ssd_a_log, ssd_d, ssd_norm_g, ssd_w_out,
              sb_w_qkv, sb_w_out, swa_w_qkv, swa_sinks, swa_w_out):
    y_prompt, ssm_p, conv_p, sbk_p, sbv_p, swak_p, swav_p = run_trunk(
        x_prompt, c_prompt, True, None, None, None, None, None, None,
        ada_w, ada_b, norm_g, ffn_w_in, ffn_w_out,
        ssd_w_in, ssd_conv_w, ssd_conv_b, ssd_dt_bias, ssd_a_log, ssd_d, ssd_norm_g, ssd_w_out,
        sb_w_qkv, sb_w_out, swa_w_qkv, swa_sinks, swa_w_out)
    y_sample, ssm_s, conv_s, sbk_s, sbv_s, swak_s, swav_s = run_trunk(
        x_sample, c_sample, False, state_ssm, state_conv, cache_sb_k, cache_sb_v, cache_swa_k, cache_swa_v,
        ada_w, ada_b, norm_g, ffn_w_in, ffn_w_out,
        ssd_w_in, ssd_conv_w, ssd_conv_b, ssd_dt_bias, ssd_a_log, ssd_d, ssd_norm_g, ssd_w_out,
        sb_w_qkv, sb_w_out, swa_w_qkv, swa_sinks, swa_w_out)
    return (y_prompt, y_sample, ssm_p, ssm_s, conv_p, conv_s, sbk_p, sbk_s, sbv_p, sbv_s,
            swak_p, swak_s, swav_p, swav_s)
```

```python
import numpy as np
from contextlib import ExitStack
import concourse.bass as bass
import concourse.mybir as mybir
from concourse.bass_utils import run_bass_kernel_spmd

F32 = mybir.dt.float32
BF16 = mybir.dt.bfloat16
AF = mybir.ActivationFunctionType
ALU = mybir.AluOpType

D = 1024
KC = 8
DFF = 2816
FC = 22
T = 512
SEQ = 8192
NTILE = SEQ // T
NS = 32
PAST = 1024
DEPTH = 4
EPS = 1e-6
NSLOT = 8
SLOTW = 2048


class Prog:
    def __init__(self, nc, same_sync=True):
        self.nc = nc
        self.q = {e: [] for e in ('pe', 'act', 'dve', 'pool', 'sp')}
        self.cnt = {e: 0 for e in self.q}
        self.seen = {e: {} for e in self.q}
        self.sems = {}
        self.dcnt = {}
        self.bufs = {}
        self.same_sync = same_sync

    def _waits(self, eng, r, w, deps):
        need = {}

        def add(ev):
            if ev is None:
                return
            s, v = ev
            if need.get(s, 0) < v:
                need[s] = v

        for k in r:
            b = self.bufs.get(k)
            if b:
                add(b[0])
                if isinstance(k, tuple) and k[0] == 'ps':
                    for s_, v_ in b[1].items():
                        if s_ != ('e', eng):
                            add((s_, v_))
        for k in w:
            b = self.bufs.get(k)
            if b:
                add(b[0])
                for s, v in b[1].items():
                    add((s, v))
        for d in deps:
            add(d)
        wl = []
        for s, v in need.items():
            if s == ('e', eng) and (eng == 'pe' or not self.same_sync):
                continue
            if self.seen[eng].get(s, 0) < v:
                wl.append((s, v))
                self.seen[eng][s] = v
        return wl

    def _mark(self, ev, r, w):
        for k in r:
            b = self.bufs.setdefault(k, [None, {}])
            if b[1].get(ev[0], 0) < ev[1]:
                b[1][ev[0]] = ev[1]
        for k in w:
            self.bufs[k] = [ev, {}]

    def op(self, eng, fn, r=(), w=(), deps=()):
        wl = self._waits(eng, r, w, deps)
        self.cnt[eng] += 1
        ev = (('e', eng), self.cnt[eng])
        self.q[eng].append((wl, fn, ev[0], 1))
        self._mark(ev, r, w)
        return ev

    def dma(self, eng, fn, r=(), w=(), deps=(), semkey=None, new_batch=True):
        sk = ('d', semkey)
        c = self.dcnt.get(sk, 0)
        deps = list(deps)
        if new_batch and c > 0:
            deps.append((sk, c))
        wl = self._waits(eng, r, w, deps)
        c += 16
        self.dcnt[sk] = c
        ev = (sk, c)
        self.q[eng].append((wl, fn, sk, 16))
        self._mark(ev, r, w)
        return ev

    def emit(self, es):
        nc = self.nc
        keys = [('e', e) for e in self.q] + list(self.dcnt.keys())
        for i, k in enumerate(keys):
            self.sems[k] = es.enter_context(nc.semaphore("sem%d" % i))
        block = es.enter_context(nc.Block())
        engmap = {'pe': block.tensor, 'act': block.scalar, 'dve': block.vector, 'pool': block.gpsimd,
                  'sp': block.sync}
        for e, dec in engmap.items():
            ops = self.q[e]

            def body(eng, ops=ops, e=e):
                for wl, fn, sk, inc in ops:
                    for s, v in wl:
                        eng.wait_ge(self.sems[s], v)
                    fn(eng).then_inc(self.sems[sk], inc)
                if e == 'sp':
                    for sk, c in self.dcnt.items():
                        eng.wait_ge(self.sems[sk], c)

            dec(body)


def _consts_np():
    cols = {}
    blocks = []
    off = 0

    def addc(name, arr):
        nonlocal off
        a = np.zeros((128, arr.shape[1]), np.float32)
        a[:arr.shape[0]] = arr
        cols[name] = (off, arr.shape[1])
        blocks.append(a)
        off += arr.shape[1]

    addc('ident', np.eye(128, dtype=np.float32))
    return np.concatenate(blocks, axis=1), cols


def _consts_bf_np():
    cols = {}
    blocks = []
    off = 0

    def addc(name, arr):
        nonlocal off
        a = np.zeros((128, arr.shape[1]), np.float32)
        a[:arr.shape[0]] = arr
        cols[name] = (off, arr.shape[1])
        blocks.append(a)
        off += arr.shape[1]

    s_ = np.arange(128)[:, None]
    t_ = np.arange(128)[None, :]
    k = np.arange(96)
    sel3 = np.zeros((96, 32, 128), np.float32)
    for h in range(32):
        sel3[k % 32 == h, h, :] = 1.0
    addc('sel3', sel3.reshape(96, 32 * 128))
    addc('negmask', np.where(s_ > t_, -30000.0, 0.0).astype(np.float32))
    addc('mask_le', (s_ <= t_).astype(np.float32))
    addc('i3', (k[:, None] % 32 == np.arange(32)[None, :]).astype(np.float32))
    addc('negtri', np.where(s_ > t_, -1.0, 0.0).astype(np.float32))
    addc('mask_lt', (s_ < t_).astype(np.float32))
    addc('negmask_lt', np.where(s_ >= t_, -30000.0, 0.0).astype(np.float32))
    nmswa = np.zeros((128, 256), np.float32)
    nmswa[0:64, 192:256] = -30000.0
    nmswa[64:128, 0:64] = -30000.0
    addc('nmswa', nmswa)
    addc('negones', -np.ones((128, 128), np.float32))
    addc('zeros', np.zeros((128, 128), np.float32))
    return np.concatenate(blocks, axis=1), cols


class Builder:
    def __init__(self, cfg):
        self.cfg = cfg
        self.nc = bass.Bass("TRN2", target_bir_lowering=False)
        self.es = ExitStack()
        self.P = Prog(self.nc, same_sync=cfg.get('same_sync', True))
        self.psn = 0
        self.psrot = list(range(8))
        self.cbn = 0
        self.en = 0
        self.ringn = 0
        self.stgn = 0
        self.tmpn = {}

    def din(self, name, shape, dt=F32):
        return self.nc.dram_tensor(name, list(shape), dt, kind="ExternalInput").ap()

    def dout(self, name, shape, dt=F32):
        return self.nc.dram_tensor(name, list(shape), dt, kind="ExternalOutput").ap()

    def dint(self, name, shape, dt=BF16):
        return self.nc.dram_tensor(name, list(shape), dt, kind="Internal").ap()

    def sb(self, name, shape, dt):
        return self.es.enter_context(self.nc.sbuf_tensor(name, list(shape), dt))

    def ps(self):
        rot = self.psrot
        i = rot[self.psn % len(rot)]
        self.psn += 1
        return self.psum[i], ('ps', i)

    def mm(self, out, lhsT, rhs, start, stop, r, w, sgc=False):
        return self.P.op('pe', lambda e, o=out, l=lhsT, rr=rhs, s=start, t=stop, g=sgc:
                         e.matmul(o, lhsT=l, rhs=rr, start=s, stop=t, skip_group_check=g), r=r, w=w)

    def cbc(self, name, rows=128, c0=0, c1=None):
        off, w = self.cbcols[name]
        if c1 is None:
            c1 = w
        return self.cb[0:rows, off + c0:off + c1]

    def tr(self, out, in_, ident, r, w):
        return self.P.op('pe', lambda e, o=out, i=in_, d=ident: e.transpose(o, i, d), r=r, w=w)

    def act(self, out, in_, func, r, w, bias=None, scale=None):
        kw = {}
        if bias is not None:
            kw['bias'] = bias
        if scale is not None:
            kw['scale'] = scale
        return self.P.op('act', lambda e, o=out, i=in_, f=func, kw=kw: e.activation(o, i, f, **kw), r=r, w=w)

    def tt(self, eng, out, in0, in1, op, r, w):
        return self.P.op(eng, lambda e, o=out, a=in0, b=in1, p=op: e.tensor_tensor(o, a, b, p), r=r, w=w)

    def ts(self, eng, out, in0, s1, op0, r, w, s2=None, op1=None):
        if op1 is None:
            return self.P.op(eng, lambda e, o=out, a=in0, s=s1, p=op0: e.tensor_scalar(o, a, s, None, p), r=r, w=w)
        return self.P.op(eng, lambda e, o=out, a=in0, s=s1, p=op0, s2=s2, p1=op1: e.tensor_scalar(o, a, s, s2, p, p1),
                         r=r, w=w)

    def stt(self, out, in0, scalar, in1, op0, op1, r, w, eng='dve'):
        return self.P.op(eng, lambda e, o=out, a=in0, s=scalar, b=in1, p0=op0, p1=op1:
                         e.scalar_tensor_tensor(o, a, s, b, p0, p1), r=r, w=w)

    def cp(self, eng, out, in_, r, w):
        if eng == 'act':
            return self.P.op('act', lambda e, o=out, i=in_: e.copy(o, i), r=r, w=w)
        return self.P.op(eng, lambda e, o=out, i=in_: e.tensor_copy(o, i), r=r, w=w)

    def memset(self, eng, ap, val, w):
        return self.P.op(eng, lambda e, a=ap, v=val: e.memset(a, v), r=(), w=w)

    def dma(self, eng, out, in_, r, w, semkey, new_batch=True, slow=False, deps=()):
        kw = {}
        if slow:
            kw['allow_slow_non_contiguous'] = True
        return self.P.dma(eng, lambda e, o=out, i=in_, kw=kw: e.dma_start(out=o, in_=i, **kw), r=r, w=w,
                          semkey=(semkey, eng), new_batch=new_batch, deps=deps)

    def ring_load(self, src, kc, width, r=(), eng='sp'):
        s = self.ringn % NSLOT
        self.ringn += 1
        view = self.ring[s][:, 0:kc * width].rearrange("p (k w) -> p k w", k=kc)
        key = ('ring', s)
        self.dma(eng, view, src, r=r, w=[key], semkey=key)
        return view, key

    def build(self):
        nc, cfg = self.nc, self.cfg
        xp = self.din("xp", [SEQ, D])
        xs = self.din("xs", [NS, D])
        cvec = self.din("cvec", [2, D])
        self.w_ada = self.din("ada_w", [DEPTH, D, 9 * D])
        ada_b = self.din("ada_b", [DEPTH, 9 * D])
        norm_g = self.din("norm_g", [DEPTH, 6, D])
        ffn_w_in = self.din("ffn_w_in", [DEPTH, 2, D, 2 * DFF])
        ffn_w_out = self.din("ffn_w_out", [DEPTH, 2, DFF, D])
        self.I = I = {}
        I['state_ssm'] = self.din("state_ssm", [2, 2048, 128])
        I['state_conv'] = self.din("state_conv", [2, 3, 3072])
        I['ssd_w_in'] = self.din("ssd_w_in", [2, D, 5152])
        I['ssd_conv_w'] = self.din("ssd_conv_w", [2, 4 * 3072])
        I['ssd_conv_b'] = self.din("ssd_conv_b", [2, 3072])
        I['ssd_dt_bias'] = self.din("ssd_dt_bias", [2, 32])
        I['ssd_a_log'] = self.din("ssd_a_log", [2, 32])
        I['ssd_d'] = self.din("ssd_d", [2, 32])
        I['ssd_norm_g'] = self.din("ssd_norm_g", [2, 2048])
        I['ssd_w_out'] = self.din("ssd_w_out", [2, 2048, D])
        I['cache_sb_k'] = self.din("cache_sb_k", [PAST, D])
        I['cache_sb_v'] = self.din("cache_sb_v", [PAST, D])
        I['sb_w_qkv'] = self.din("sb_w_qkv", [D, 3 * D])
        I['sb_w_out'] = self.din("sb_w_out", [D, D])
        I['cache_swa_k'] = self.din("cache_swa_k", [128, 256])
        I['cache_swa_v'] = self.din("cache_swa_v", [128, 256])
        I['swa_w_qkv'] = self.din("swa_w_qkv", [D, 1536])
        I['swa_sinks'] = self.din("swa_sinks", [16])
        I['swa_w_out'] = self.din("swa_w_out", [D, D])
        yp = self.dout("yp", [SEQ, D])
        ys = self.dout("ys", [NS, D])
        self.O = O = {}
        for nm in ('swak_p', 'swak_s', 'swav_p', 'swav_s'):
            O[nm] = self.dout(nm, [128, 256])
        O['sbk_p'] = self.dout("sbk_p", [SEQ, D])
        O['sbk_s'] = self.dout("sbk_s", [NS, D])
        O['sbv_p'] = self.dout("sbv_p", [SEQ, D])
        O['sbv_s'] = self.dout("sbv_s", [NS, D])
        O['ssm_p'] = self.dout("ssm_p", [2, 2048, 128])
        O['ssm_s'] = self.dout("ssm_s", [2, 2048, 128])
        O['conv_p'] = self.dout("conv_p", [2, 3, 3072])
        O['conv_s'] = self.dout("conv_s", [2, 3, 3072])
        cnp, ccols = _consts_np()
        cdram = nc.inline_tensor(cnp, "consts").ap()
        cbnp, cbcols = _consts_bf_np()
        cbdram = nc.inline_tensor(cbnp, "constsb").ap()
        self.wb_ffn_in = self.dint("wb_ffn_in", [DEPTH, 2, D, 2 * DFF])
        self.wb_ffn_out = self.dint("wb_ffn_out", [DEPTH, 2, DFF, D])
        self.wb_ssd_in = self.dint("wb_ssd_in", [2, D, 5152])
        self.wb_ssd_out = self.dint("wb_ssd_out", [2, 2048, D])
        self.Sd = self.dint("Sd", [2, 128, 2048], F32)
        self.wb_sb_qkv = self.dint("wb_sb_qkv", [D, 3 * D])
        self.wb_sb_out = self.dint("wb_sb_out", [D, D])
        self.wb_swa_qkv = self.dint("wb_swa_qkv", [D, 1536])
        self.wb_swa_out = self.dint("wb_swa_out", [D, D])
        self.KTs = self.dint("KTs", [8, 128, SEQ])
        self.Vs = self.dint("Vs", [8, 128, SEQ // 128, 128])
        self.KTs_s = self.dint("KTs_s", [8, 128, PAST])
        self.Vs_s = self.dint("Vs_s", [8, 128, PAST // 128, 128])

        self.psum = [self.es.enter_context(nc.psum_tensor("ps%d" % i, [128, 512], F32)) for i in range(8)]
        self.ring = [self.sb("ring%d" % i, [128, SLOTW], BF16) for i in range(NSLOT)]
        self.X = self.sb("X", [128, KC * T], F32)
        self.Hb = self.sb("Hb", [128, KC * T], BF16)
        self.G = self.sb("G", [128, FC * T], BF16)
        self.Yf = self.sb("Yf", [128, KC * T], F32)
        self.SQ = self.sb("SQ", [128, KC * T], BF16)
        self.stg = [self.sb("stg%d" % i, [128, D], F32) for i in range(2)]
        self.tmpf = [self.sb("tmpf%d" % i, [128, T], F32) for i in range(4)]
        self.rstd = self.sb("rstd", [128, T], F32)
        self.cf = self.sb("cf", [128, cnp.shape[1]], F32)
        self.cb = self.sb("cb", [128, cbnp.shape[1]], BF16)
        self.identb = self.sb("identb", [128, 128], BF16)
        self.onesb = self.sb("onesb", [128, 128], BF16)
        self.cact = self.sb("cact", [128, 16], BF16)
        self.ccol = self.sb("ccol", [128, 16], F32)
        self.adab = self.sb("adab", [128, DEPTH * 72], F32)
        self.ng = self.sb("ng", [128, DEPTH * 48], F32)
        self.mod = self.sb("mod", [128, DEPTH * 144], F32)
        self.der = self.sb("der", [128, DEPTH * 3 * 3 * 2 * 8], F32)
        self.AR8 = self.sb("AR8", [128, 2048], F32)
        self.ccols = ccols
        self.cbcols = cbcols
        self.identf = self.cf[:, ccols['ident'][0]:ccols['ident'][0] + 128]
        self.ssd_alloc()
        self.sb_alloc()
        self.KTprev = self.sb("KTprev", [128, 512], BF16)
        self.Vprev = self.sb("Vprev", [128, 256], BF16)
        self.esink = self.sb("esink", [128, 8], F32)

        self.dma('pool', self.cf[:], cdram, r=[], w=['cf'], semkey='cf')
        self.dma('pool', self.cb[:], cbdram, r=[], w=['cb'], semkey='cb')
        self.cp('dve', self.identb[:], self.identf, r=['cf'], w=['identb'])
        self.memset('dve', self.onesb[:], 1.0, w=['onesb'])
        self.load_cols(self.ccol[:, 0:8], cvec[0, :], 'ccol')
        self.load_cols(self.ccol[:, 8:16], cvec[1, :], 'ccol')
        for i in range(DEPTH):
            for h in range(2):
                self.load_cols(self.adab[:, i * 72 + h * 36:i * 72 + h * 36 + 36], ada_b[i, h * 4608:(h + 1) * 4608], 'adab')
            self.load_cols(self.ng[:, i * 48:(i + 1) * 48], norm_g[i].rearrange("a d -> (a d)"), 'ng')
        self.ssd_params()
        for hh in range(2):
            self.dma('pool', self.esink[hh * 64:(hh + 1) * 64, :], bass.AP(I['swa_sinks'].tensor, hh, [[0, 64], [2, 8]]),
                     r=[], w=['esink'], semkey='esink', slow=True, new_batch=False)
        self.act(self.esink[:], self.esink[:], AF.Exp, r=['esink'], w=['esink'])
        self.prepass_mod()
        self.prepass_weights(ffn_w_in, ffn_w_out)

        ntile = cfg.get('ntile', NTILE)
        tiles = [('p', t) for t in range(ntile)]
        if cfg.get('sample', True):
            tiles.append(('s', 0))
        stop = cfg.get('stop')
        for kind, t in tiles:
            N = T if kind == 'p' else NS
            q = 0 if kind == 'p' else 1
            src = xp[t * T:(t + 1) * T, :] if kind == 'p' else xs
            dst = yp[t * T:(t + 1) * T, :] if kind == 'p' else ys
            last = (kind == 's') or (t == ntile - 1)
            self.load_fm(src, N)
            for i in range(cfg.get('depth', DEPTH)):
                self.ffn(i, 0, q, N)
                if stop == 'x%d_0' % i:
                    break
                if i % 3 == 0:
                    self.ssd(i, q, N, t, last)
                elif i % 3 == 1:
                    if kind == 's':
                        self.sb_prep_sample()
                    self.sbmix(i, q, N, t, last)
                else:
                    self.swamix(i, q, N, t, last)
                if stop == 'x%d_1' % i:
                    break
                self.ffn(i, 1, q, N)
            self.store_fm(dst, N)
        print('sbuf bytes remaining', self.nc.sbuf_bytes_remaining() if callable(getattr(self.nc, 'sbuf_bytes_remaining', None)) else getattr(self.nc, 'sbuf_bytes_remaining', None))
        self.P.emit(self.es)
        return nc

    def load_cols(self, dst, src_vec, key):
        self.dma('pool', dst, src_vec.rearrange("(n p) -> p n", p=128), r=[], w=[key], semkey=key, slow=True,
                 new_batch=False)

    def xv(self, tile, N, nch=KC):
        return tile[:, 0:nch * N].rearrange("p (k n) -> p k n", k=nch)

    def prepass_mod(self):
        cact_v = self.cact[:].rearrange("p (k two) -> p two k", two=2)
        for q in range(2):
            self.act(cact_v[:, q, :], self.ccol[:, q * 8:(q + 1) * 8], AF.Silu, r=['ccol'], w=['cact'])
        for i in range(DEPTH):
            bank, pk = self.ps()
            for t36 in range(36):
                src = self.w_ada[i, :, t36 * 256:(t36 + 1) * 256].rearrange("(k p) w -> p k w", p=128)
                wv, wk = self.ring_load(src, KC, 256, eng='pool')
                for c in range(2):
                    oc = t36 * 2 + c
                    for k in range(KC):
                        self.mm(bank[:, oc * 2:oc * 2 + 2], wv[:, k, c * 128:(c + 1) * 128], self.cact[:, k * 2:k * 2 + 2],
                                k == 0, k == KC - 1, r=[wk, 'cact'], w=[pk])
            mv = self.mod[:, i * 144:(i + 1) * 144].rearrange("p (o two) -> p o two", two=2)
            bv = bank[:, 0:144].rearrange("p (o two) -> p o two", two=2)
            ab = self.adab[:, i * 72:(i + 1) * 72].unsqueeze(2).broadcast_to([128, 72, 2])
            self.tt('dve', mv, bv, ab, ALU.add, r=[pk, 'adab'], w=['mod'])
            for s in range(3):
                for q in range(2):
                    def m(j):
                        return self.mod[:, i * 144:(i + 1) * 144].rearrange("p (j k two) -> p j two k", j=9, two=2)[:, j, q, :]
                    gpre = self.ng[:, i * 48 + (2 * s) * 8:i * 48 + (2 * s) * 8 + 8]
                    gpost = self.ng[:, i * 48 + (2 * s + 1) * 8:i * 48 + (2 * s + 1) * 8 + 8]
                    self.stt(self.dslice(i, s, 0, q), m(3 * s + 1), 1.0, gpre, ALU.add, ALU.mult, r=['mod', 'ng'], w=['der'])
                    self.cp('dve', self.dslice(i, s, 1, q), m(3 * s + 0), r=['mod'], w=['der'])
                    self.stt(self.dslice(i, s, 2, q), m(3 * s + 2), 0.5 if s != 1 else 1.0, gpost, ALU.mult, ALU.mult,
                             r=['mod', 'ng'], w=['der'])

    def dslice(self, i, s, which, q, kc=None):
        base = (((i * 3 + s) * 3 + which) * 2 + q) * 8
        if kc is None:
            return self.der[:, base:base + 8]
        return self.der[:, base + kc:base + kc + 1]

    def prepass_weights(self, ffn_w_in, ffn_w_out):
        first = True
        I = self.I
        for j in range(2):
            self.dma('pool', self.wb_ssd_in[j].rearrange("k (a b) -> k a b", a=4),
                     I['ssd_w_in'][j].rearrange("k (a b) -> k a b", a=4), r=[], w=['wb'], semkey='wcast', new_batch=first)
            first = False
            self.dma('pool', self.wb_ssd_out[j], I['ssd_w_out'][j], r=[], w=['wb'], semkey='wcast', new_batch=False)
        self.dma('pool', self.wb_sb_qkv.rearrange("k (a b) -> k a b", a=2), I['sb_w_qkv'].rearrange("k (a b) -> k a b", a=2),
                 r=[], w=['wb'], semkey='wcast', new_batch=False)
        self.dma('pool', self.wb_sb_out, I['sb_w_out'], r=[], w=['wb'], semkey='wcast', new_batch=False)
        self.dma('pool', self.wb_swa_qkv, I['swa_w_qkv'], r=[], w=['wb'], semkey='wcast', new_batch=False)
        self.dma('pool', self.wb_swa_out, I['swa_w_out'], r=[], w=['wb'], semkey='wcast', new_batch=False)
        for i in range(DEPTH):
            for s in range(2):
                self.dma('pool', self.wb_ffn_in[i, s].rearrange("k (a b) -> k a b", a=4),
                         ffn_w_in[i, s].rearrange("k (a b) -> k a b", a=4), r=[], w=['wb'], semkey='wcast', new_batch=first)
                first = False
                self.dma('pool', self.wb_ffn_out[i, s], ffn_w_out[i, s], r=[], w=['wb'], semkey='wcast', new_batch=False)

    def load_fm(self, src, N):
        nb = (N + 127) // 128
        for tb in range(nb):
            n = min(128, N - tb * 128)
            st = self.stg[self.stgn % 2]
            sk = ('stg', self.stgn % 2)
            self.stgn += 1
            self.dma('pool', st[0:n, :], src[tb * 128:tb * 128 + n, :], r=[], w=[sk], semkey=sk)
            for half in range(2):
                bank, pk = self.ps()
                for k4 in range(4):
                    k = half * 4 + k4
                    self.tr(bank[:, k4 * 128:k4 * 128 + n], st[0:n, k * 128:(k + 1) * 128], self.identf[0:n, 0:n],
                            r=[sk, 'cf'], w=[pk])
                xo = self.xv(self.X, N)[:, half * 4:half * 4 + 4, tb * 128:tb * 128 + n]
                pv = bank[:, 0:512].rearrange("p (k n) -> p k n", k=4)[:, :, 0:n]
                self.cp('dve' if half == 0 else 'act', xo, pv, r=[pk], w=[('X', half * 4 + j) for j in range(4)])

    def store_fm(self, dst, N, src=None, skey='X', nchunk=KC, vtm=None, vkeys=None, only_last=False, kofs=0):
        if src is None:
            src = self.xv(self.X, N)
        nb = (N + 127) // 128
        W = nchunk * 128
        for tb in range(nb):
            if only_last and tb != nb - 1:
                continue
            n = min(128, N - tb * 128)
            st = self.stg[self.stgn % 2]
            sk = ('stg', self.stgn % 2)
            self.stgn += 1
            for half in range((nchunk + 3) // 4):
                bank, pk = self.ps()
                nk4 = min(4, nchunk - half * 4)
                for k4 in range(nk4):
                    k = half * 4 + k4
                    self.tr(bank[0:n, k4 * 128:(k4 + 1) * 128], src[:, k, tb * 128:tb * 128 + n],
                            self.identf, r=[(skey, kofs + k), 'cf'], w=[pk])
                self.cp('dve' if half == 0 else 'act', st[0:n, half * 512:half * 512 + nk4 * 128], bank[0:n, 0:nk4 * 128],
                        r=[pk], w=[sk])
            if dst is not None:
                d = dst[0:n, :] if only_last else dst[tb * 128:tb * 128 + n, :]
                self.dma('pool', d, st[0:n, 0:W], r=[sk], w=[], semkey=sk)
            if vtm is not None:
                self.cp('pool', vtm(tb, n), st[0:n, 0:W], r=[sk], w=vkeys(tb))

    def sumsq_rstd(self, sq_view, sq_keys, N, nch, dim):
        bank, pk = self.ps()
        for k in range(nch):
            self.mm(bank[:, 0:N], self.onesb[:], sq_view(k), k == 0, k == nch - 1, r=['onesb', sq_keys[k]], w=[pk])
        t = self.tmpf[3]
        self.act(t[:, 0:N], bank[:, 0:N], AF.Sqrt, r=[pk, 'epsb'], w=['tmp3'], bias=self.epsb[:, 0:1], scale=1.0 / dim)
        self.P.op('dve', lambda e, o=self.rstd[:, 0:N], i=t[:, 0:N]: e.reciprocal(o, i), r=['tmp3'], w=['rstd'])

    def prenorm(self, i, s, q, N):
        Xv = self.xv(self.X, N)
        SQv = self.xv(self.SQ, N)
        for k in range(KC):
            self.act(SQv[:, k, :], Xv[:, k, :], AF.Square, r=[('X', k)], w=[('SQ', k)])
        self.sumsq_rstd(lambda k: SQv[:, k, :], [('SQ', k) for k in range(KC)], N, KC, D)
        Hv = self.xv(self.Hb, N)
        for k in range(KC):
            tn = k % 3
            t = self.tmpf[tn]
            self.stt(t[:, 0:N], Xv[:, k, :], self.dslice(i, s, 0, q, k), self.rstd[:, 0:N], ALU.mult, ALU.mult,
                     r=[('X', k), 'der', 'rstd'], w=[('tmp', tn)])
            self.act(Hv[:, k, :], t[:, 0:N], AF.Identity, r=[('tmp', tn), 'der'], w=[('Hb', k)],
                     bias=self.dslice(i, s, 1, q, k))

    def postnorm_resid(self, i, s, q, N):
        Xv = self.xv(self.X, N)
        SQv = self.xv(self.SQ, N)
        Yv = self.xv(self.Yf, N)
        self.sumsq_rstd(lambda k: SQv[:, k, :], [('SQ', k) for k in range(KC)], N, KC, D)
        for k in range(KC):
            tn = k % 3
            t = self.tmpf[tn]
            self.tt('dve', t[:, 0:N], Yv[:, k, :], self.rstd[:, 0:N], ALU.mult, r=[('Yf', k), 'rstd'], w=[('tmp', tn)])
            self.tt('pool', Xv[:, k, :], Xv[:, k, :], t[:, 0:N], ALU.add, r=[('tmp', tn), ('X', k)], w=[('X', k)])

    def evac_y(self, i, s, q, N, k, bank, pk):
        Yv = self.xv(self.Yf, N)
        SQv = self.xv(self.SQ, N)
        self.act(Yv[:, k, :], bank[:, 0:N], AF.Identity, r=[pk, 'der'], w=[('Yf', k)], scale=self.dslice(i, s, 2, q, k))
        self.act(SQv[:, k, :], bank[:, 0:N], AF.Square, r=[pk], w=[('SQ', k)])

    def ffn(self, i, which, q, N):
        s = 0 if which == 0 else 2
        self.prenorm(i, s, q, N)
        Hv = self.xv(self.Hb, N)
        Gv = self.xv(self.G, N, FC)
        win = self.wb_ffn_in[i, which]
        wout = self.wb_ffn_out[i, which]
        for jp in range(FC // 2):
            wa, ka = self.ring_load(win[:, jp * 256:(jp + 1) * 256].rearrange("(k p) w -> p k w", p=128), KC, 256, r=['wb'])
            wb_, kb = self.ring_load(win[:, DFF + jp * 256:DFF + (jp + 1) * 256].rearrange("(k p) w -> p k w", p=128), KC,
                                     256, r=['wb'])
            pa = []
            for c in range(2):
                bank, pk = self.ps()
                for k in range(KC):
                    self.mm(bank[:, 0:N], wa[:, k, c * 128:(c + 1) * 128], Hv[:, k, :], k == 0, k == KC - 1,
                            r=[ka, ('Hb', k)], w=[pk])
                pa.append((bank, pk))
            for c in range(2):
                bank, pk = self.ps()
                for k in range(KC):
                    self.mm(bank[:, 0:N], wb_[:, k, c * 128:(c + 1) * 128], Hv[:, k, :], k == 0, k == KC - 1,
                            r=[kb, ('Hb', k)], w=[pk])
                tn = self.tmpn.get('ffn', 0)
                self.tmpn['ffn'] = tn + 1
                t = self.tmpf[tn % 3]
                tk = ('tmp', tn % 3)
                self.act(t[:, 0:N], pa[c][0][:, 0:N], AF.Silu, r=[pa[c][1]], w=[tk])
                j = jp * 2 + c
                self.tt('dve', Gv[:, j, :], t[:, 0:N], bank[:, 0:N], ALU.mult, r=[tk, pk], w=[('G', j)])
        HF = FC // 2
        for oc in range(KC):
            bank, pk = self.ps()
            for hf in range(2):
                wv, wk = self.ring_load(wout[hf * HF * 128:(hf + 1) * HF * 128, oc * 128:(oc + 1) * 128].rearrange(
                    "(k p) w -> p k w", p=128), HF, 128, r=['wb'])
                for k in range(HF):
                    kk = hf * HF + k
                    self.mm(bank[:, 0:N], wv[:, k, :], Gv[:, kk, :], kk == 0, kk == FC - 1, r=[wk, ('G', kk)], w=[pk])
            self.evac_y(i, s, q, N, oc, bank, pk)
        self.postnorm_resid(i, s, q, N)


    def ssd_alloc(self):
        sb = self.sb
        self.hist = sb("hist", [128, 2 * 72], F32)
        self.cwcol = sb("cwcol", [128, 2 * 96], F32)
        self.cbcol = sb("cbcol", [128, 2 * 24], F32)
        self.sngcol = sb("sngcol", [128, 2 * 16], F32)
        self.Dcol = sb("Dcol", [128, 2 * 16], F32)
        self.dtb3 = sb("dtb3", [96, 2], F32)
        self.A3 = sb("A3", [96, 2], F32)
        self.a3parts = [sb("a3p%d" % i, [96, T], BF16) for i in range(3)]
        self.a3 = sb("a3", [96, T], BF16)
        self.cols = sb("cols", [128, 4 * 96], F32)
        self.cdrep = sb("cdrep", [128, 4 * 32], F32)
        self.dg = sb("dg", [96, 32], BF16)
        self.ones32 = sb("ones32", [96, 128], F32)
        self.cbuf = [sb("cbuf%d" % i, [128, T + 3], F32) for i in range(2)]
        self.xsfm = [sb("xsfm%d" % i, [128, T], BF16) for i in range(2)]
        self.xtm = [sb("xtm%d" % i, [128, 4 * 128], BF16) for i in range(2)]
        self.xw = [sb("xw%d" % i, [128, 4 * 128], BF16) for i in range(2)]
        self.zs = [sb("zs%d" % i, [128, T], BF16) for i in range(2)]
        self.Ea = [sb("Ea%d" % i, [128, 128], F32) for i in range(2)]
        self.E = [sb("E%d" % i, [128, 128], F32) for i in range(2)]
        self.Cp = [sb("Cp%d" % i, [128, 128], BF16) for i in range(2)]
        self.Mm = [sb("Mm%d" % i, [128, 128], BF16) for i in range(2)]
        self.Sbf = sb("Sbf", [128, 2048], BF16)
        self.sqy = [sb("sqy%d" % i, [128, 128], BF16) for i in range(2)]

    def ssd_params(self):
        I = self.I
        for j in range(2):
            for h in range(2):
                self.load_cols(self.cwcol[:, j * 96 + h * 48:j * 96 + h * 48 + 48], I['ssd_conv_w'][j, h * 6144:(h + 1) * 6144], 'cwcol')
            self.load_cols(self.cbcol[:, j * 24:(j + 1) * 24], I['ssd_conv_b'][j], 'cbcol')
            self.load_cols(self.sngcol[:, j * 16:(j + 1) * 16], I['ssd_norm_g'][j], 'sngcol')
            for hh in range(2):
                src = bass.AP(I['ssd_d'].tensor, j * 32 + hh, [[0, 64], [2, 16]])
                self.dma('pool', self.Dcol[hh * 64:(hh + 1) * 64, j * 16:(j + 1) * 16], src, r=[], w=['Dcol'], semkey='Dcol',
                         slow=True, new_batch=False)
            for g in range(3):
                self.dma('pool', self.dtb3[g * 32:(g + 1) * 32, j:j + 1], bass.AP(I['ssd_dt_bias'].tensor, j * 32, [[1, 32], [1, 1]]),
                         r=[], w=['dtb3'], semkey='dtb3', slow=True, new_batch=False)
                self.dma('pool', self.A3[g * 32:(g + 1) * 32, j:j + 1], bass.AP(I['ssd_a_log'].tensor, j * 32, [[1, 32], [1, 1]]),
                         r=[], w=['A3'], semkey='A3', slow=True, new_batch=False)
        self.act(self.A3[:], self.A3[:], AF.Exp, r=['A3'], w=['A3'])
        self.ts('dve', self.A3[:], self.A3[:], -1.0, ALU.mult, r=['A3'], w=['A3'])
        self.memset('dve', self.ones32[:], 1.0, w=['ones32'])

    def conv_silu(self, j, cc, bank, pk, N, out_ap, out_keys):
        n = self.cbn % 2
        self.cbn += 1
        cb, ck = self.cbuf[n], ('cbuf', n)
        hs = self.hist[:, j * 72 + cc * 3:j * 72 + cc * 3 + 3]
        hk = ('hist', j, cc)
        self.cp('pool', cb[:, 0:3], hs, r=[hk], w=[ck])
        self.cp('act', cb[:, 3:3 + N], bank[:, 0:N], r=[pk], w=[ck])
        self.cp('pool', hs, cb[:, N:N + 3], r=[ck], w=[hk])
        tn = self.tmpn.get('conv', 0)
        self.tmpn['conv'] = tn + 1
        acc, ak = self.tmpf[tn % 2][:, 0:N], ('tmp', tn % 2)

        def wc(tap):
            c0 = j * 96 + tap * 24 + cc
            return self.cwcol[:, c0:c0 + 1]
        self.ts('dve', acc, cb[:, 0:N], wc(0), ALU.mult, r=[ck, 'cwcol', 'cbcol'], w=[ak],
                s2=self.cbcol[:, j * 24 + cc:j * 24 + cc + 1], op1=ALU.add)
        for tap in range(1, 4):
            self.stt(acc, cb[:, tap:tap + N], wc(tap), acc, ALU.mult, ALU.add, r=[ck, 'cwcol', ak], w=[ak])
        self.act(out_ap, acc, AF.Silu, r=[ak], w=out_keys)

    def ssd(self, i, q, N, t, last):
        j = i // 3
        Q = 128 if q == 0 else 32
        nch = N // Q
        I, O = self.I, self.O
        win = self.wb_ssd_in[j]
        wout = self.wb_ssd_out[j]
        self.psrot = list(range(7))
        self.prenorm(i, 1, q, N)
        Hv = self.xv(self.Hb, N)
        S = self.AR8
        Skeys = [('S', x) for x in range(16)]
        Sbkeys = [('Sbf', x) for x in range(16)]
        hkeys = [('hist', j, cc) for cc in range(24)]
        histv = self.hist[:, j * 72:(j + 1) * 72].rearrange("p (c t) -> p c t", t=3)
        if q == 0 and t == 0:
            self.memset('pool', S[:], 0.0, w=Skeys)
            self.memset('pool', self.hist[:, j * 72:(j + 1) * 72], 0.0, w=hkeys)
        elif q == 0:
            self.dma('pool', S[:], self.Sd[j], r=[('Sd', j)], w=Skeys, semkey='Sld')
        else:
            for g4 in range(4):
                st, sk = self.stg[self.stgn % 2], ('stg', self.stgn % 2)
                self.stgn += 1
                self.dma('pool', st[:, 0:512].rearrange("p (b n) -> p b n", b=4),
                         I['state_ssm'][j, g4 * 512:(g4 + 1) * 512, :].rearrange("(b p) n -> p b n", p=128), r=[], w=[sk], semkey=sk)
                bank, pk = self.ps()
                for b4 in range(4):
                    self.tr(bank[:, b4 * 128:(b4 + 1) * 128], st[:, b4 * 128:(b4 + 1) * 128], self.identf, r=[sk, 'cf'], w=[pk])
                self.cp('dve', S[:, g4 * 512:(g4 + 1) * 512], bank[:, 0:512], r=[pk], w=Skeys[g4 * 4:g4 * 4 + 4])
            for tt_ in range(3):
                self.dma('pool', histv[:, :, tt_], I['state_conv'][j, tt_].rearrange("(c p) -> p c", p=128), r=[], w=hkeys,
                         semkey='histld', slow=True, new_batch=(tt_ == 0))
        self.cp('pool', self.Sbf[:], S[:], r=Skeys, w=Sbkeys)

        wv, wk = self.ring_load(win[:, 5120:5152].rearrange("(k p) w -> p k w", p=128), KC, 32, r=['wb'])
        bank, pk = self.ps()
        for g in range(3):
            for k in range(KC):
                self.mm(bank[g * 32:(g + 1) * 32, 0:N], wv[:, k, :], Hv[:, k, :], k == 0, k == KC - 1, r=[wk, ('Hb', k)], w=[pk])
        SQf = self.SQ[:].bitcast(F32)
        dtt, dA, acum, wst = [SQf[0:96, x * 512:x * 512 + N] for x in range(4)]
        kdt, kdA, kac, kw = [[('SQ', 2 * x), ('SQ', 2 * x + 1)] for x in range(4)]
        self.act(dtt, bank[0:96, 0:N], AF.Exp, r=[pk, 'dtb3'], w=kdt, bias=self.dtb3[:, j:j + 1])
        self.act(dtt, dtt, AF.Ln, r=kdt + ['oneb'], w=kdt, bias=self.oneb[0:96, 0:1])
        self.ts('dve', dA, dtt, self.A3[:, j:j + 1], ALU.mult, r=kdt + ['A3'], w=kdA)
        for c in range(nch):
            self.P.op('dve', lambda e, o=acum[:, c * Q:(c + 1) * Q], d0=self.ones32[:, 0:Q], d1=dA[:, c * Q:(c + 1) * Q]:
                      e.tensor_tensor_scan(o, d0, d1, 0.0, ALU.mult, ALU.add), r=kdA + ['ones32'], w=kac)
        H3, M3, L3 = [x[:, 0:N] for x in self.a3parts]
        r1, r2, nac = [self.tmpf[x][0:96, 0:N] for x in range(3)]
        self.cp('dve', H3, acum, r=kac, w=['H3'])
        self.tt('dve', r1, acum, H3, ALU.subtract, r=kac + ['H3'], w=[('tmp', 0)])
        self.cp('dve', M3, r1, r=[('tmp', 0)], w=['M3'])
        self.tt('dve', r2, r1, M3, ALU.subtract, r=[('tmp', 0), 'M3'], w=[('tmp', 1)])
        self.cp('dve', L3, r2, r=[('tmp', 1)], w=['L3'])
        a3 = self.a3
        self.cp('pool', a3[0:32, 0:N], H3[0:32], r=['H3'], w=['a3'])
        self.cp('pool', a3[32:64, 0:N], M3[32:64], r=['M3'], w=['a3'])
        self.cp('pool', a3[64:96, 0:N], L3[64:96], r=['L3'], w=['a3'])
        for c in range(nch):
            self.act(wst[:, c * Q:(c + 1) * Q], acum[:, c * Q:(c + 1) * Q], AF.Exp, r=kac, w=kw, scale=-1.0,
                     bias=acum[:, (c + 1) * Q - 1:(c + 1) * Q])
        self.tt('dve', wst, wst, dtt, ALU.mult, r=kw + kdt, w=kw)
        self.ts('dve', nac, acum, -1.0, ALU.mult, r=kac, w=[('tmp', 2)])
        for c in range(nch):
            bank, pk = self.ps()
            for x, (src, sk_) in enumerate(((nac, [('tmp', 2)]), (dtt, kdt), (wst, kw))):
                self.tr(bank[0:Q, x * 32:(x + 1) * 32], src[0:32, c * Q:(c + 1) * Q], self.identf[0:32, 0:32], r=sk_ + ['cf'], w=[pk])
            self.cp('dve', self.cols[0:Q, c * 96:(c + 1) * 96], bank[0:Q, 0:96], r=[pk], w=['cols'])
        bank, pk = self.ps()
        for c in range(nch):
            self.ts('dve', self.dg[:], self.cbc('i3', 96), a3[:, (c + 1) * Q - 1:(c + 1) * Q], ALU.mult, r=['cb', 'a3'], w=['dg'])
            self.mm(bank[:, c * 32:(c + 1) * 32], self.onesb[0:96, :], self.dg[:], True, True, r=['onesb', 'dg'], w=[pk])
        self.act(self.cdrep[:, 0:nch * 32], bank[:, 0:nch * 32], AF.Exp, r=[pk], w=['cdrep'])

        Yfb = self.Yf[:].bitcast(BF16)
        Bfm = Yfb[:, 0:2048].rearrange("p (g n) -> p g n", g=4)
        Cfm = Yfb[:, 2048:4096].rearrange("p (g n) -> p g n", g=4)
        Btm = Yfb[:, 4096:6144].rearrange("p (c n) -> p c n", c=4)
        cbm = Yfb[:, 6144:8192].rearrange("p (x n) -> p x n", x=16)
        kB, kC, kBt, kcb = [[('Yf', 2 * x), ('Yf', 2 * x + 1)] for x in range(4)]
        for g8 in range(8):
            col0 = 4096 + g8 * 128
            wv, wk = self.ring_load(win[:, col0:col0 + 128].rearrange("(k p) w -> p k w", p=128), KC, 128, r=['wb'])
            bank, pk = self.ps()
            for k in range(KC):
                self.mm(bank[:, 0:N], wv[:, k, :], Hv[:, k, :], k == 0, k == KC - 1, r=[wk, ('Hb', k)], w=[pk])
            dest = (Bfm if g8 < 4 else Cfm)[:, g8 % 4, 0:N]
            self.conv_silu(j, 16 + g8, bank, pk, N, dest, kB if g8 < 4 else kC)
        for c in range(nch):
            bank, pk = self.ps()
            bb = bank[:].bitcast(BF16)
            for g in range(4):
                self.tr(bb[0:Q, g * 128:(g + 1) * 128], Bfm[:, g, c * Q:(c + 1) * Q], self.identb[:], r=kB + ['identb'], w=[pk])
            self.cp('act', Btm[0:Q, c, :], bb[0:Q, 0:512], r=[pk], w=kBt)
        mle = self.cbc('mask_le', Q, 0, Q)
        for g in range(4):
            bank, pk = self.ps()
            for c in range(nch):
                self.mm(bank[0:Q, c * 128:c * 128 + Q], Bfm[:, g, c * Q:(c + 1) * Q], Cfm[:, g, c * Q:(c + 1) * Q], True, True,
                        r=kB + kC, w=[pk])
            pv = bank[0:Q, 0:nch * 128].rearrange("p (c n) -> p c n", c=nch)[:, :, 0:Q]
            self.tt('dve', cbm[0:Q, g * 4:g * 4 + nch, 0:Q], pv, mle.unsqueeze(1).broadcast_to([Q, nch, Q]), ALU.mult,
                    r=[pk, 'cb'], w=kcb)

        Gv = self.xv(self.G, N, FC)
        ssb, ssk = self.psum[7], ('ps', 7)
        sel3 = self.cbc('sel3', 96)
        negmask = self.cbc('negmask', Q, 0, Q)
        for jj in range(16):
            g = jj // 4
            pb = jj % 2
            xs_, xk = self.xsfm[pb], ('xsfm', pb)
            wv, wk = self.ring_load(win[:, 2048 + jj * 128:2048 + (jj + 1) * 128].rearrange("(k p) w -> p k w", p=128), KC, 128, r=['wb'])
            bank, pk = self.ps()
            for k in range(KC):
                self.mm(bank[:, 0:N], wv[:, k, :], Hv[:, k, :], k == 0, k == KC - 1, r=[wk, ('Hb', k)], w=[pk])
            self.conv_silu(j, jj, bank, pk, N, xs_[:, 0:N], [xk])
            bank, pk = self.ps()
            bb = bank[:].bitcast(BF16)
            for c in range(nch):
                self.tr(bb[0:Q, c * 128:(c + 1) * 128], xs_[:, c * Q:(c + 1) * Q], self.identb[:], r=[xk, 'identb'], w=[pk])
            xt, xtk = self.xtm[pb], ('xtm', pb)
            self.cp('act', xt[0:Q, 0:nch * 128], bb[0:Q, 0:nch * 128], r=[pk], w=[xtk])
            xwt, xwk = self.xw[pb], ('xw', pb)
            wcv = self.cols[0:Q, 0:nch * 96].rearrange("p (c x) -> p c x", c=nch)[:, :, 64 + 2 * jj:64 + 2 * jj + 2]
            self.tt('dve', xwt[0:Q, 0:nch * 128].rearrange("p (c h d) -> p c h d", c=nch, h=2),
                    xt[0:Q, 0:nch * 128].rearrange("p (c h d) -> p c h d", c=nch, h=2),
                    wcv.unsqueeze(3).broadcast_to([Q, nch, 2, 64]), ALU.mult, r=[xtk, 'cols'], w=[xwk])
            wv, wk = self.ring_load(win[:, jj * 128:(jj + 1) * 128].rearrange("(k p) w -> p k w", p=128), KC, 128, r=['wb'])
            bank, pk = self.ps()
            for k in range(KC):
                self.mm(bank[:, 0:N], wv[:, k, :], Hv[:, k, :], k == 0, k == KC - 1, r=[wk, ('Hb', k)], w=[pk])
            zt, zk = self.zs[pb], ('zs', pb)
            self.act(zt[:, 0:N], bank[:, 0:N], AF.Silu, r=[pk], w=[zk])
            for c in range(nch):
                ybank, ypk = self.ps()
                for hh in range(2):
                    h = 2 * jj + hh
                    rb, rpk = self.ps()
                    sel = sel3[:, h * 128:(h + 1) * 128]
                    a3c = a3[:, c * Q:(c + 1) * Q]
                    self.mm(rb[:, 0:Q], sel, a3c, True, True, r=['cb', 'a3'], w=[rpk])
                    self.mm(rb[0:Q, 128:128 + Q], sel[:, 0:Q], a3c, True, False, r=['cb', 'a3'], w=[rpk])
                    self.mm(rb[0:Q, 128:128 + Q], self.identb[0:Q, 0:Q], negmask, False, True, r=['cb', 'identb'], w=[rpk])
                    e = self.en % 2
                    self.en += 1
                    self.act(self.Ea[e][:, 0:Q], rb[:, 0:Q], AF.Exp, r=[rpk], w=[('Ea', e)])
                    self.act(self.E[e][0:Q, 0:Q], rb[0:Q, 128:128 + Q], AF.Exp, r=[rpk, 'cols'], w=[('E', e)],
                             bias=self.cols[0:Q, c * 96 + h:c * 96 + h + 1])
                    self.tt('dve', self.Cp[e][:, 0:Q], Cfm[:, g, c * Q:(c + 1) * Q], self.Ea[e][:, 0:Q], ALU.mult,
                            r=kC + [('Ea', e)], w=[('Cp', e)])
                    self.stt(self.Mm[e][0:Q, 0:Q], self.E[e][0:Q, 0:Q], self.cols[0:Q, c * 96 + 32 + h:c * 96 + 32 + h + 1],
                             cbm[0:Q, g * 4 + c, 0:Q], ALU.mult, ALU.mult, r=[('E', e), 'cols'] + kcb, w=[('Mm', e)])
                    self.mm(ybank[hh * 64:(hh + 1) * 64, 0:Q], self.Sbf[:, h * 64:(h + 1) * 64], self.Cp[e][:, 0:Q], True, False,
                            r=[('Sbf', jj), ('Cp', e)], w=[ypk])
                    self.mm(ybank[hh * 64:(hh + 1) * 64, 0:Q], xt[0:Q, c * 128 + hh * 64:c * 128 + (hh + 1) * 64],
                            self.Mm[e][0:Q, 0:Q], False, True, r=[xtk, ('Mm', e)], w=[ypk])
                y1, y1k = self.tmpf[2][:, 0:Q], ('tmp', 2)
                y2, y2k = self.tmpf[3][:, 0:Q], 'tmp3'
                self.stt(y1, xs_[:, c * Q:(c + 1) * Q], self.Dcol[:, j * 16 + jj:j * 16 + jj + 1], ybank[:, 0:Q], ALU.mult, ALU.add,
                         r=[xk, 'Dcol', ypk], w=[y1k])
                self.tt('dve', y2, y1, zt[:, c * Q:(c + 1) * Q], ALU.mult, r=[y1k, zk], w=[y2k])
                self.act(Gv[:, jj, c * Q:(c + 1) * Q], y2, AF.Identity, r=[y2k, 'sngcol'], w=[('G', jj)],
                         scale=self.sngcol[:, j * 16 + jj:j * 16 + jj + 1])
                e = self.en % 2
                self.act(self.sqy[e][:, 0:Q], y2, AF.Square, r=[y2k], w=[('sqy', e)])
                self.mm(ssb[:, c * Q:(c + 1) * Q], self.onesb[:], self.sqy[e][:, 0:Q], jj == 0 and c == 0,
                        jj == 15 and c == nch - 1, r=['onesb', ('sqy', e)], w=[ssk], sgc=True)
                sn, snk = self.ps()
                self.mm(sn[:, 0:128], Btm[0:Q, c, g * 128:(g + 1) * 128], xwt[0:Q, c * 128:(c + 1) * 128], True, True,
                        r=kBt + [xwk], w=[snk])
                Sp = S[:, jj * 128:(jj + 1) * 128]
                Sp3 = Sp.rearrange("p (h d) -> p h d", h=2)
                cdv = self.cdrep[:, c * 32 + 2 * jj:c * 32 + 2 * jj + 2].unsqueeze(2).broadcast_to([128, 2, 64])
                self.tt('dve', Sp3, Sp3, cdv, ALU.mult, r=[('S', jj), 'cdrep'], w=[('S', jj)])
                self.tt('dve', Sp, Sp, sn[:, 0:128], ALU.add, r=[('S', jj), snk], w=[('S', jj)])
                self.cp('pool', self.Sbf[:, jj * 128:(jj + 1) * 128], Sp, r=[('S', jj)], w=[('Sbf', jj)])

        t3 = self.tmpf[3]
        self.act(t3[:, 0:N], ssb[:, 0:N], AF.Sqrt, r=[ssk, 'epsb'], w=['tmp3'], bias=self.epsb[:, 0:1], scale=1.0 / 2048)
        self.P.op('dve', lambda e, o=self.rstd[:, 0:N], i_=t3[:, 0:N]: e.reciprocal(o, i_), r=['tmp3'], w=['rstd'])
        for oc in range(KC):
            wv, wk = self.ring_load(wout[:, oc * 128:(oc + 1) * 128].rearrange("(k p) w -> p k w", p=128), 16, 128, r=['wb'])
            bank, pk = self.ps()
            for k in range(16):
                self.mm(bank[:, 0:N], wv[:, k, :], Gv[:, k, :], k == 0, k == 15, r=[wk, ('G', k)], w=[pk])
            tn = oc % 3
            tq = self.tmpf[tn]
            self.tt('dve', tq[:, 0:N], bank[:, 0:N], self.rstd[:, 0:N], ALU.mult, r=[pk, 'rstd'], w=[('tmp', tn)])
            self.evac_y(i, 1, q, N, oc, tq, ('tmp', tn))
        self.postnorm_resid(i, 1, q, N)

        if q == 0 and not last:
            self.dma('pool', self.Sd[j], S[:], r=Skeys, w=[('Sd', j)], semkey='Sst')
        if last:
            dst = O['ssm_p' if q == 0 else 'ssm_s'][j]
            for g4 in range(4):
                bank, pk = self.ps()
                for b4 in range(4):
                    blk = g4 * 4 + b4
                    self.tr(bank[:, b4 * 128:(b4 + 1) * 128], S[:, blk * 128:(blk + 1) * 128], self.identf, r=[('S', blk), 'cf'], w=[pk])
                st, sk = self.stg[self.stgn % 2], ('stg', self.stgn % 2)
                self.stgn += 1
                self.cp('dve', st[:, 0:512], bank[:, 0:512], r=[pk], w=[sk])
                self.dma('pool', dst[g4 * 512:(g4 + 1) * 512, :].rearrange("(b p) n -> p b n", p=128),
                         st[:, 0:512].rearrange("p (b n) -> p b n", b=4), r=[sk], w=[], semkey=sk)
            cdst = O['conv_p' if q == 0 else 'conv_s'][j]
            for tt_ in range(3):
                self.dma('pool', cdst[tt_].rearrange("(c p) -> p c", p=128), histv[:, :, tt_], r=hkeys, w=[], semkey='histst',
                         slow=True, new_batch=(tt_ == 0))
        self.psrot = list(range(8))


    def sb_alloc(self):
        sb = self.sb
        self.KTseg = [sb("KTseg%d" % i, [128, 1024], BF16) for i in range(2)]
        self.Vseg = [sb("Vseg%d" % i, [128, 1024], BF16) for i in range(2)]
        self.SPrun = self.cbuf
        self.SPrunb = self.xsfm
        self.segn = 0
        self.un = 0

    def tk(self, n):
        return ('tmp', n) if n < 3 else 'tmp3'

    def sb_prep_sample(self):
        I = self.I
        for c in range(8):
            self.dma('pool', self.Vs_s[c], I['cache_sb_v'][:, c * 128:(c + 1) * 128].rearrange("(b p) d -> p b d", p=128),
                     r=[], w=['Vs_s'], semkey='vss', new_batch=(c == 0))
        Gv = self.xv(self.G, 128, FC)
        for b in range(PAST // 128):
            st, sk = self.stg[self.stgn % 2], ('stg', self.stgn % 2)
            self.stgn += 1
            self.dma('pool', st[:, :], I['cache_sb_k'][b * 128:(b + 1) * 128, :], r=[], w=[sk], semkey=sk)
            for half in range(2):
                bank, pk = self.ps()
                for k4 in range(4):
                    k = half * 4 + k4
                    self.tr(bank[:, k4 * 128:(k4 + 1) * 128], st[:, k * 128:(k + 1) * 128], self.identf, r=[sk, 'cf'], w=[pk])
                self.cp('dve' if half == 0 else 'act', Gv[:, 14 + half * 4:14 + half * 4 + 4, :],
                        bank[:, 0:512].rearrange("p (k n) -> p k n", k=4), r=[pk], w=[('G', 14 + half * 4 + x) for x in range(4)])
            self.dma('pool', self.KTs_s[:, :, b * 128:(b + 1) * 128].rearrange("c p t -> p c t"), Gv[:, 14:22, :],
                     r=[('G', 14 + x) for x in range(8)], w=['KTs_s'], semkey='ktss')

    def sbmix(self, i, q, N, t, last):
        I, O = self.I, self.O
        self.psrot = list(range(7))
        self.prenorm(i, 1, q, N)
        Hv = self.xv(self.Hb, N)
        Gv = self.xv(self.G, N, FC)
        Yv = self.xv(self.Yf, N)
        win, wout = self.wb_sb_qkv, self.wb_sb_out
        nb = (N + 127) // 128
        Vtm = self.AR8[:].bitcast(BF16).rearrange("p (b f) -> p b f", b=4)
        vk = lambda tb: [('S', 4 * tb + x) for x in range(4)]
        allvk = [('S', x) for x in range(16)]
        for part in range(3):
            if part >= self.cfg.get('sbparts', 3):
                continue
            for c in range(8):
                col0 = part * D + c * 128
                wv, wk = self.ring_load(win[:, col0:col0 + 128].rearrange("(k p) w -> p k w", p=128), KC, 128, r=['wb'])
                bank, pk = self.ps()
                for k in range(KC):
                    self.mm(bank[:, 0:N], wv[:, k, :], Hv[:, k, :], k == 0, k == KC - 1, r=[wk, ('Hb', k)], w=[pk])
                if part == 0:
                    self.cp('act', Gv[:, c, :], bank[:, 0:N], r=[pk], w=[('G', c)])
                elif part == 1:
                    self.cp('act', Gv[:, 8 + c, :], bank[:, 0:N], r=[pk], w=[('G', 8 + c)])
                    self.cp('dve', Yv[:, c, :], bank[:, 0:N], r=[pk], w=[('Yf', c)])
                else:
                    self.cp('dve', Yv[:, c, :], bank[:, 0:N], r=[pk], w=[('Yf', c)])
            if part == 1:
                dst = O['sbk_p'][t * T:(t + 1) * T, :] if q == 0 else O['sbk_s']
                if self.cfg.get('sbdbg') == 'nodma':
                    dst = None
                if self.cfg.get('sbdbg') != 'nostore':
                    self.store_fm(dst, N, src=Yv, skey='Yf')
                if q == 0 and not last:
                    self.dma('pool', self.KTs[:, :, t * T:(t + 1) * T].rearrange("c p t -> p c t"), Gv[:, 8:16, :],
                             r=[('G', 8 + x) for x in range(8)], w=['KTs'], semkey='kts')
            elif part == 2:
                dst = O['sbv_p'][t * T:(t + 1) * T, :] if q == 0 else O['sbv_s']
                self.store_fm(dst, N, src=Yv, skey='Yf', vtm=lambda tb, n: Vtm[0:n, tb, :], vkeys=vk)
                if q == 0 and not last:
                    for tb in range(nb):
                        self.dma('pool', self.Vs[:, :, 4 * t + tb, :].rearrange("c p d -> p c d"),
                                 Vtm[:, tb, :].rearrange("p (c d) -> p c d", c=8), r=vk(tb), w=['Vs'], semkey='vs',
                                 new_batch=(tb == 0))
        if self.cfg.get('sbstage', 9) < 2:
            self.psrot = list(range(8))
            return
        OTv = Hv
        ob, ok = self.psum[7], ('ps', 7)
        npast = 4 * t if q == 0 else PAST // 128
        KTd, kdk = (self.KTs, 'KTs') if q == 0 else (self.KTs_s, 'KTs_s')
        Vd, vdk = (self.Vs, 'Vs') if q == 0 else (self.Vs_s, 'Vs_s')
        SEGB = 8
        zeros = self.cbc('zeros', 128, 0, 64)
        for c in range(8):
            units = []
            for r_ in reversed(range(nb)):
                nk = min(128, N - r_ * 128)
                for hh in range(2):
                    units.append(('in', r_, nk, hh))
            segs = [(b0, min(b0 + SEGB, npast)) for b0 in range(0, npast, SEGB)]
            for (b0, b1) in reversed(segs):
                for b in reversed(range(b0, b1)):
                    for hh in range(2):
                        units.append(('past', b, (b0, b1), hh))
            for hh in range(2):
                self.mm(ob[hh * 64:(hh + 1) * 64, 0:N], zeros, Gv[:, c, :], True, False, r=['cb', ('G', c)], w=[ok], sgc=True)
                self.memset('pool', self.SPrun[hh][:, 0:N], 0.0, w=[('cbuf', hh)])
                self.memset('pool', self.SPrunb[hh][:, 0:N], 0.0, w=[('xsfm', hh)])
            curseg = None
            if self.cfg.get('sbstage', 9) < 3:
                units = []
            for ui, u in enumerate(units):
                lastu = ui >= len(units) - 2
                if u[0] == 'in':
                    _, r_, nk, hh = u
                    c0 = r_ * 128
                    self.sb_unit(N, c, hh, Gv[hh * 64:(hh + 1) * 64, 8 + c, c0:c0 + nk], [('G', 8 + c)],
                                 Vtm[0:nk, r_, c * 128 + hh * 64:c * 128 + (hh + 1) * 64], vk(r_), c0, nk, True, lastu)
                else:
                    _, b, (b0, b1), hh = u
                    if curseg != (b0, b1):
                        curseg = (b0, b1)
                        si = self.segn % 2
                        self.segn += 1
                        nbl = b1 - b0
                        self.dma('sp', self.KTseg[si][:, 0:nbl * 128], KTd[c, :, b0 * 128:b1 * 128], r=[kdk], w=[('kseg', si)],
                                 semkey=('kseg', si))
                        self.dma('sp', self.Vseg[si][:, 0:nbl * 128].rearrange("p (b d) -> p b d", d=128), Vd[c, :, b0:b1, :],
                                 r=[vdk], w=[('vseg', si)], semkey=('vseg', si))
                    o_ = (b - b0) * 128
                    self.sb_unit(N, c, hh, self.KTseg[si][hh * 64:(hh + 1) * 64, o_:o_ + 128], [('kseg', si)],
                                 self.Vseg[si][:, o_ + hh * 64:o_ + (hh + 1) * 64], [('vseg', si)], 0, 128, False, lastu)
            self.cp('act', OTv[:, c, :], ob[:, 0:N], r=[ok], w=[('Hb', c)])
        for oc in range(KC):
            wv, wk = self.ring_load(wout[:, oc * 128:(oc + 1) * 128].rearrange("(k p) w -> p k w", p=128), KC, 128, r=['wb'])
            bank, pk = self.ps()
            for k in range(KC):
                self.mm(bank[:, 0:N], wv[:, k, :], OTv[:, k, :], k == 0, k == KC - 1, r=[wk, ('Hb', k)], w=[pk])
            self.evac_y(i, 1, q, N, oc, bank, pk)
        self.postnorm_resid(i, 1, q, N)
        self.psrot = list(range(8))

    def sb_unit(self, N, c, hh, KTb, kK, Vb, kV, c0, nk, diag, lastu):
        Gv = self.xv(self.G, N, FC)
        SQv = self.xv(self.SQ, N)
        ncol = N - c0
        ob, ok = self.psum[7], ('ps', 7)
        zb, zk = self.ps()
        self.mm(zb[0:nk, 0:ncol], KTb, Gv[hh * 64:(hh + 1) * 64, c, c0:N], True, True, r=kK + [('G', c)], w=[zk])
        u = self.un
        self.un += 1
        e_t, ek = self.tmpf[u % 2][0:nk, 0:ncol], self.tk(u % 2)
        sp_t, spk = self.tmpf[2 + u % 2][0:nk, 0:ncol], self.tk(2 + u % 2)
        self.act(e_t, zb[0:nk, 0:ncol], AF.Exp, r=[zk], w=[ek], scale=0.125)
        self.act(sp_t, e_t, AF.Ln, r=[ek, 'oneb'], w=[spk], bias=self.oneb[0:nk, 0:1])
        if diag:
            self.tt('pool', sp_t[:, 0:nk], sp_t[:, 0:nk], self.cbc('mask_lt', nk, 0, nk), ALU.mult, r=[spk, 'cb'], w=[spk])
        lsb, lk = SQv[0:nk, u % 2, 0:ncol], ('SQ', u % 2)
        self.stt(lsb, zb[0:nk, 0:ncol], 0.125, sp_t, ALU.mult, ALU.subtract, r=[zk, spk], w=[lk])
        spb, sbk_ = SQv[0:nk, 2 + u % 2, 0:ncol], ('SQ', 2 + u % 2)
        self.cp('pool', spb, sp_t, r=[spk], w=[sbk_])
        ab, ak = self.ps()
        self.mm(ab[0:nk, 0:ncol], self.cbc('negtri', nk, 0, nk), spb, True, False, r=['cb', sbk_], w=[ak])
        self.mm(ab[0:nk, 0:ncol], self.cbc('negones', 128, 0, nk), self.SPrunb[hh][:, c0:N], False, False,
                r=['cb', ('xsfm', hh)], w=[ak])
        self.mm(ab[0:nk, 0:ncol], self.identb[0:nk, 0:nk], lsb, False, not diag, r=['identb', lk], w=[ak])
        if diag:
            self.mm(ab[0:nk, 0:nk], self.identb[0:nk, 0:nk], self.cbc('negmask_lt', nk, 0, nk), False, True, r=['identb', 'cb'],
                    w=[ak], sgc=True)
        w_t, wk_ = SQv[0:nk, 4 + u % 2, 0:ncol], ('SQ', 4 + u % 2)
        self.act(w_t, ab[0:nk, 0:ncol], AF.Exp, r=[ak], w=[wk_])
        self.mm(ob[hh * 64:(hh + 1) * 64, c0:N], Vb, w_t, False, lastu, r=kV + [wk_], w=[ok], sgc=True)
        self.tt('pool', self.SPrun[hh][0:nk, c0:N], self.SPrun[hh][0:nk, c0:N], sp_t, ALU.add, r=[spk, ('cbuf', hh)],
                w=[('cbuf', hh)])
        self.cp('pool', self.SPrunb[hh][0:nk, c0:N], self.SPrun[hh][0:nk, c0:N], r=[('cbuf', hh)], w=[('xsfm', hh)])


    def swamix(self, i, q, N, t, last):
        I, O = self.I, self.O
        self.psrot = list(range(6))
        self.prenorm(i, 1, q, N)
        Hv = self.xv(self.Hb, N)
        Gv = self.xv(self.G, N, FC)
        Yv = self.xv(self.Yf, N)
        win, wout = self.wb_swa_qkv, self.wb_swa_out
        W = 128 + N
        base = 8 * T
        KTd = self.G[:, base:base + 4 * W].rearrange("p (g w) -> p g w", g=4)
        kK = [('G', 8 + x) for x in range(5)]
        vb0 = base + 4 * (128 + T)
        Vt = self.G[:, vb0:vb0 + 5 * 256].rearrange("p (b f) -> p b f", b=5)
        kV = [('G', 13 + x) for x in range(3)]
        nb = (N + 127) // 128
        for c in range(8):
            wv, wk = self.ring_load(win[:, c * 128:(c + 1) * 128].rearrange("(k p) w -> p k w", p=128), KC, 128, r=['wb'])
            bank, pk = self.ps()
            for k in range(KC):
                self.mm(bank[:, 0:N], wv[:, k, :], Hv[:, k, :], k == 0, k == KC - 1, r=[wk, ('Hb', k)], w=[pk])
            self.cp('act', Gv[:, c, :], bank[:, 0:N], r=[pk], w=[('G', c)])
        wv, wk = self.ring_load(win[:, 1024:1280].rearrange("(k p) w -> p k w", p=128), KC, 256, r=['wb'])
        for g in range(4):
            bank, pk = self.ps()
            for half in range(2):
                for k in range(KC):
                    self.mm(bank[half * 64:(half + 1) * 64, 0:N], wv[:, k, g * 64:(g + 1) * 64], Hv[:, k, :], k == 0, k == KC - 1,
                            r=[wk, ('Hb', k)], w=[pk])
            self.cp('act', KTd[:, g, 128:128 + N], bank[:, 0:N], r=[pk], w=kK)
        if last:
            for c2 in range(2):
                bank, pk = self.ps()
                for k in range(KC):
                    self.mm(bank[:, 0:N], wv[:, k, c2 * 128:(c2 + 1) * 128], Hv[:, k, :], k == 0, k == KC - 1,
                            r=[wk, ('Hb', k)], w=[pk])
                self.cp('dve', Yv[:, c2, :], bank[:, 0:N], r=[pk], w=[('Yf', c2)])
        wv, wk = self.ring_load(win[:, 1280:1536].rearrange("(k p) w -> p k w", p=128), KC, 256, r=['wb'])
        for c2 in range(2):
            bank, pk = self.ps()
            for k in range(KC):
                self.mm(bank[:, 0:N], wv[:, k, c2 * 128:(c2 + 1) * 128], Hv[:, k, :], k == 0, k == KC - 1, r=[wk, ('Hb', k)], w=[pk])
            self.cp('dve', Yv[:, 2 + c2, :], bank[:, 0:N], r=[pk], w=[('Yf', 2 + c2)])
        if q == 0 and t > 0:
            self.cp('pool', KTd[:, :, 0:128], self.KTprev[:].rearrange("p (g w) -> p g w", g=4), r=['KTprev'], w=kK)
            self.cp('pool', Vt[:, 0, :], self.Vprev[:], r=['Vprev'], w=kV)
        elif q == 1:
            st, sk = self.stg[self.stgn % 2], ('stg', self.stgn % 2)
            self.stgn += 1
            for dup in range(2):
                self.dma('pool', st[:, 0:512].rearrange("p (g u d) -> p g u d", g=4, u=2)[:, :, dup, :],
                         I['cache_swa_k'].rearrange("p (g d) -> p g d", g=4), r=[], w=[sk], semkey=sk, new_batch=(dup == 0))
            bank, pk = self.ps()
            for g in range(4):
                self.tr(bank[:, g * 128:(g + 1) * 128], st[:, g * 128:(g + 1) * 128], self.identf, r=[sk, 'cf'], w=[pk])
            self.cp('act', KTd[:, :, 0:128], bank[:, 0:512].rearrange("p (g w) -> p g w", g=4), r=[pk], w=kK)
            st, sk = self.stg[self.stgn % 2], ('stg', self.stgn % 2)
            self.stgn += 1
            self.dma('pool', st[:, 0:256], I['cache_swa_v'], r=[], w=[sk], semkey=sk)
            self.cp('pool', Vt[:, 0, :], st[:, 0:256], r=[sk], w=kV)
            self.dma('pool', O['swak_s'][0:96, :], I['cache_swa_k'][32:128, :], r=[], w=[], semkey='swaout', new_batch=True)
            self.dma('pool', O['swav_s'][0:96, :], I['cache_swa_v'][32:128, :], r=[], w=[], semkey='swaout', new_batch=False)
        vdst = None
        if last:
            vdst = O['swav_p'] if q == 0 else O['swav_s'][96:128, :]
        self.store_fm(vdst, N, src=Yv[:, 2:4], skey='Yf', nchunk=2, vtm=lambda tb, n: Vt[0:n, 1 + tb, :], vkeys=lambda tb: kV,
                      only_last=False, kofs=2) if not (last and q == 0) else None
        if last and q == 0:
            for tb in range(nb):
                self.store_fm(O['swav_p'] if tb == nb - 1 else None, 128, src=Yv[:, 2:4, tb * 128:(tb + 1) * 128], skey='Yf', nchunk=2,
                              vtm=lambda tb_, n, tb=tb: Vt[0:n, 1 + tb, :], vkeys=lambda tb_: kV, kofs=2)
            self.store_fm(O['swak_p'], 128, src=Yv[:, 0:2, (nb - 1) * 128:nb * 128], skey='Yf', nchunk=2)
        elif last:
            self.store_fm(O['swak_s'][96:128, :], N, src=Yv[:, 0:2], skey='Yf', nchunk=2)
        OTv = Hv
        ob, ok = self.psum[7], ('ps', 7)
        db, dk = self.psum[6], ('ps', 6)
        NM = self.cbc('nmswa')
        zeros = self.cbc('zeros', 128, 0, 64)
        SQv = self.xv(self.SQ, N)
        if q == 0:
            blocks = []
            for kb in range(-1, nb):
                if kb == -1 and t == 0:
                    continue
                q0, q1 = max(0, 128 * kb), min(N, 128 * kb + 256)
                blocks.append(((kb + 1) * 128, 128, kb + 1, q0, q1, q0 - 128 * kb))
        else:
            blocks = [(0, 128, 0, 0, N, None), (128, N, 1, 0, N, None)]
        for c in range(8):
            for hh in range(2):
                h = 2 * c + hh
                g = h // 4
                self.mm(ob[hh * 64:(hh + 1) * 64, 0:N], zeros, Gv[:, c, :], True, False, r=['cb', ('G', c)], w=[ok], sgc=True)
                self.mm(db[hh * 64:(hh + 1) * 64, 0:N], zeros, Gv[:, c, :], True, False, r=['cb', ('G', c)], w=[dk], sgc=True)
                for bi, (kc0, nk, vblk, q0, q1, p0) in enumerate(blocks):
                    nq = q1 - q0
                    lastb = bi == len(blocks) - 1
                    sb_, sk_ = self.ps()
                    self.mm(sb_[0:nk, 0:nq], KTd[hh * 64:(hh + 1) * 64, g, kc0:kc0 + nk], Gv[hh * 64:(hh + 1) * 64, c, q0:q1],
                            True, p0 is None, r=kK + [('G', c)], w=[sk_])
                    if p0 is not None:
                        self.mm(sb_[0:nk, 0:nq], self.identb[0:nk, 0:nk], NM[0:nk, p0:p0 + nq], False, True, r=['identb', 'cb'], w=[sk_])
                    u = self.un
                    self.un += 1
                    P, pk_ = SQv[0:nk, u % 4, 0:nq], ('SQ', u % 4)
                    self.act(P, sb_[0:nk, 0:nq], AF.Exp, r=[sk_], w=[pk_], scale=0.125)
                    self.mm(ob[hh * 64:(hh + 1) * 64, q0:q1], Vt[0:nk, vblk, g * 64:(g + 1) * 64], P, False, lastb, r=kV + [pk_], w=[ok],
                            sgc=True)
                    self.mm(db[hh * 64:(hh + 1) * 64, q0:q1], self.onesb[0:nk, 0:64], P, False, lastb, r=['onesb', pk_], w=[dk], sgc=True)
            den, dnk = self.tmpf[c % 2][:, 0:N], self.tk(c % 2)
            self.ts('dve', den, db[:, 0:N], self.esink[:, c:c + 1], ALU.add, r=[dk, 'esink'], w=[dnk])
            self.P.op('dve', lambda e, o=den, i_=den: e.reciprocal(o, i_), r=[dnk], w=[dnk])
            self.tt('dve', OTv[:, c, :], ob[:, 0:N], den, ALU.mult, r=[ok, dnk], w=[('Hb', c)])
        if q == 0 and not last:
            self.cp('pool', self.KTprev[:].rearrange("p (g w) -> p g w", g=4), KTd[:, :, N:N + 128], r=kK, w=['KTprev'])
            self.cp('pool', self.Vprev[:], Vt[:, nb, :], r=kV, w=['Vprev'])
        for oc in range(KC):
            wv, wk = self.ring_load(wout[:, oc * 128:(oc + 1) * 128].rearrange("(k p) w -> p k w", p=128), KC, 128, r=['wb'])
            bank, pk = self.ps()
            for k in range(KC):
                self.mm(bank[:, 0:N], wv[:, k, :], OTv[:, k, :], k == 0, k == KC - 1, r=[wk, ('Hb', k)], w=[pk])
            self.evac_y(i, 1, q, N, oc, bank, pk)
        self.postnorm_resid(i, 1, q, N)
        self.psrot = list(range(8))


def build_program(cfg):
    b = Builder(cfg)
    b.epsb = b.sb("epsb", [128, 1], F32)
    b.memset('dve', b.epsb[:], EPS, w=['epsb'])
    b.oneb = b.sb("oneb", [128, 1], F32)
    b.memset('dve', b.oneb[:], 1.0, w=['oneb'])
    nc = b.build()
    return b, nc


def make_in_maps(inputs):
    maps = []
    f = np.ascontiguousarray
    shared = {k: f(inputs[k]) for k in ('ada_w', 'ada_b', 'norm_g', 'ffn_w_in', 'ffn_w_out', 'ssd_w_in', 'ssd_conv_b',
                                        'ssd_dt_bias', 'ssd_a_log', 'ssd_d', 'ssd_norm_g', 'ssd_w_out')}
    shared['ssd_conv_w'] = f(inputs['ssd_conv_w'].reshape(2, 4 * 3072))
    shared['sb_w_qkv'] = f(inputs['sb_w_qkv'][0])
    shared['sb_w_out'] = f(inputs['sb_w_out'][0])
    shared['swa_w_qkv'] = f(inputs['swa_w_qkv'][0])
    shared['swa_w_out'] = f(inputs['swa_w_out'][0])
    shared['swa_sinks'] = f(inputs['swa_sinks'][0])
    for c in range(8):
        m = dict(shared)
        m['xp'] = f(inputs['x_prompt'][c % 4])
        m['xs'] = f(inputs['x_sample'][c])
        m['cvec'] = f(np.stack([inputs['c_prompt'][c % 4], inputs['c_sample'][c]]))
        m['state_ssm'] = f(inputs['state_ssm'][:, c].reshape(2, 2048, 128))
        m['state_conv'] = f(inputs['state_conv'][:, c])
        m['cache_sb_k'] = f(inputs['cache_sb_k'][0, c].reshape(PAST, D))
        m['cache_sb_v'] = f(inputs['cache_sb_v'][0, c].reshape(PAST, D))
        m['cache_swa_k'] = f(inputs['cache_swa_k'][0, c].reshape(128, 256))
        m['cache_swa_v'] = f(inputs['cache_swa_v'][0, c].reshape(128, 256))
        maps.append(m)
    return maps


def kernel(**inputs):
    cfg = {}
    b, nc = build_program(cfg)
    maps = make_in_maps(inputs)
    res = run_bass_kernel_spmd(nc, maps, core_ids=list(range(8)))
    r = res.results
    y_prompt = np.stack([r[c]['yp'] for c in range(4)])
    y_sample = np.stack([r[c]['ys'] for c in range(8)])
    ssm_p = np.stack([r[c]['ssm_p'].reshape(2, 32, 64, 128) for c in range(4)], axis=1)
    ssm_s = np.stack([r[c]['ssm_s'].reshape(2, 32, 64, 128) for c in range(8)], axis=1)
    conv_p = np.stack([r[c]['conv_p'] for c in range(4)], axis=1)
    conv_s = np.stack([r[c]['conv_s'] for c in range(8)], axis=1)
    sbk_p = np.stack([r[c]['sbk_p'].reshape(SEQ, 16, 64) for c in range(4)])[None]
    sbk_s = np.stack([r[c]['sbk_s'].reshape(NS, 16, 64) for c in range(8)])[None]
    sbv_p = np.stack([r[c]['sbv_p'].reshape(SEQ, 16, 64) for c in range(4)])[None]
    sbv_s = np.stack([r[c]['sbv_s'].reshape(NS, 16, 64) for c in range(8)])[None]
    sw = {}
    for nm, n in (('swak_p', 4), ('swak_s', 8), ('swav_p', 4), ('swav_s', 8)):
        sw[nm] = np.stack([r[c][nm].reshape(128, 4, 64) for c in range(n)])[None]
    return (y_prompt, y_sample, ssm_p, ssm_s, conv_p, conv_s, sbk_p, sbk_s, sbv_p, sbv_s,
            sw['swak_p'], sw['swak_s'], sw['swav_p'], sw['swav_s'])
```

```python
import numpy as np
from contextlib import ExitStack
import concourse.bass as bass
import concourse.mybir as mybir
from concourse.bass_utils import run_bass_kernel_spmd

F32 = mybir.dt.float32
BF16 = mybir.dt.bfloat16
AF = mybir.ActivationFunctionType
ALU = mybir.AluOpType

D = 1024
KC = 8
DFF = 2816
FC = 22
T = 512
SEQ = 8192
NTILE = SEQ // T
NS = 32
PAST = 1024
DEPTH = 4
EPS = 1e-6
NSLOT = 8
SLOTW = 2048


class Prog:
    def __init__(self, nc, same_sync=True):
        self.nc = nc
        self.q = {e: [] for e in ('pe', 'act', 'dve', 'pool', 'sp')}
        self.cnt = {e: 0 for e in self.q}
        self.seen = {e: {} for e in self.q}
        self.sems = {}
        self.dcnt = {}
        self.bufs = {}
        self.same_sync = same_sync

    def _waits(self, eng, r, w, deps):
        need = {}

        def add(ev):
            if ev is None:
                return
            s, v = ev
            if need.get(s, 0) < v:
                need[s] = v

        for k in r:
            b = self.bufs.get(k)
            if b:
                add(b[0])
                if isinstance(k, tuple) and k[0] == 'ps':
                    for s_, v_ in b[1].items():
                        if s_ != ('e', eng):
                            add((s_, v_))
        for k in w:
            b = self.bufs.get(k)
            if b:
                add(b[0])
                for s, v in b[1].items():
                    add((s, v))
        for d in deps:
            add(d)
        wl = []
        for s, v in need.items():
            if s == ('e', eng) and (eng == 'pe' or not self.same_sync):
                continue
            if self.seen[eng].get(s, 0) < v:
                wl.append((s, v))
                self.seen[eng][s] = v
        return wl

    def _mark(self, ev, r, w):
        for k in r:
            b = self.bufs.setdefault(k, [None, {}])
            if b[1].get(ev[0], 0) < ev[1]:
                b[1][ev[0]] = ev[1]
        for k in w:
            self.bufs[k] = [ev, {}]

    def op(self, eng, fn, r=(), w=(), deps=()):
        wl = self._waits(eng, r, w, deps)
        self.cnt[eng] += 1
        ev = (('e', eng), self.cnt[eng])
        self.q[eng].append((wl, fn, ev[0], 1))
        self._mark(ev, r, w)
        return ev

    def dma(self, eng, fn, r=(), w=(), deps=(), semkey=None, new_batch=True):
        sk = ('d', semkey)
        c = self.dcnt.get(sk, 0)
        deps = list(deps)
        if new_batch and c > 0:
            deps.append((sk, c))
        wl = self._waits(eng, r, w, deps)
        c += 16
        self.dcnt[sk] = c
        ev = (sk, c)
        self.q[eng].append((wl, fn, sk, 16))
        self._mark(ev, r, w)
        return ev

    def emit(self, es):
        nc = self.nc
        keys = [('e', e) for e in self.q] + list(self.dcnt.keys())
        for i, k in enumerate(keys):
            self.sems[k] = es.enter_context(nc.semaphore("sem%d" % i))
        block = es.enter_context(nc.Block())
        engmap = {'pe': block.tensor, 'act': block.scalar, 'dve': block.vector, 'pool': block.gpsimd,
                  'sp': block.sync}
        for e, dec in engmap.items():
            ops = self.q[e]

            def body(eng, ops=ops, e=e):
                for wl, fn, sk, inc in ops:
                    for s, v in wl:
                        eng.wait_ge(self.sems[s], v)
                    fn(eng).then_inc(self.sems[sk], inc)
                if e == 'sp':
                    for sk, c in self.dcnt.items():
                        eng.wait_ge(self.sems[sk], c)

            dec(body)


def _consts_np():
    cols = {}
    blocks = []
    off = 0

    def addc(name, arr):
        nonlocal off
        a = np.zeros((128, arr.shape[1]), np.float32)
        a[:arr.shape[0]] = arr
        cols[name] = (off, arr.shape[1])
        blocks.append(a)
        off += arr.shape[1]

    addc('ident', np.eye(128, dtype=np.float32))
    return np.concatenate(blocks, axis=1), cols


def _consts_bf_np():
    cols = {}
    blocks = []
    off = 0

    def addc(name, arr):
        nonlocal off
        a = np.zeros((128, arr.shape[1]), np.float32)
        a[:arr.shape[0]] = arr
        cols[name] = (off, arr.shape[1])
        blocks.append(a)
        off += arr.shape[1]

    s_ = np.arange(128)[:, None]
    t_ = np.arange(128)[None, :]
    k = np.arange(96)
    sel3 = np.zeros((96, 32, 128), np.float32)
    for h in range(32):
        sel3[k % 32 == h, h, :] = 1.0
    addc('sel3', sel3.reshape(96, 32 * 128))
    addc('negmask', np.where(s_ > t_, -30000.0, 0.0).astype(np.float32))
    addc('mask_le', (s_ <= t_).astype(np.float32))
    addc('i3', (k[:, None] % 32 == np.arange(32)[None, :]).astype(np.float32))
    addc('negtri', np.where(s_ > t_, -1.0, 0.0).astype(np.float32))
    addc('mask_lt', (s_ < t_).astype(np.float32))
    addc('negmask_lt', np.where(s_ >= t_, -30000.0, 0.0).astype(np.float32))
    nmswa = np.zeros((128, 256), np.float32)
    nmswa[0:64, 192:256] = -30000.0
    nmswa[64:128, 0:64] = -30000.0
    addc('nmswa', nmswa)
    addc('negones', -np.ones((128, 128), np.float32))
    addc('negtrile', np.where(s_ <= t_, -1.0, 0.0).astype(np.float32))
    addc('postri', np.where(s_ > t_, 1.0, 0.0).astype(np.float32))
    addc('negident', -np.eye(128, dtype=np.float32))
    addc('posmask_lt', np.where(s_ >= t_, 30000.0, 0.0).astype(np.float32))
    addc('zeros', np.zeros((128, 128), np.float32))
    return np.concatenate(blocks, axis=1), cols


class Builder:
    def __init__(self, cfg):
        self.cfg = cfg
        self.nc = bass.Bass("TRN2", target_bir_lowering=False)
        self.es = ExitStack()
        self.P = Prog(self.nc, same_sync=cfg.get('same_sync', True))
        self.psn = 0
        self.psrot = list(range(8))
        self.cbn = 0
        self.en = 0
        self.ringn = 0
        self.stgn = 0
        self.tmpn = {}

    def din(self, name, shape, dt=F32):
        return self.nc.dram_tensor(name, list(shape), dt, kind="ExternalInput").ap()

    def dout(self, name, shape, dt=F32):
        return self.nc.dram_tensor(name, list(shape), dt, kind="ExternalOutput").ap()

    def dint(self, name, shape, dt=BF16):
        return self.nc.dram_tensor(name, list(shape), dt, kind="Internal").ap()

    def sb(self, name, shape, dt):
        return self.es.enter_context(self.nc.sbuf_tensor(name, list(shape), dt))

    def ps(self):
        rot = self.psrot
        i = rot[self.psn % len(rot)]
        self.psn += 1
        return self.psum[i], ('ps', i)

    def mm(self, out, lhsT, rhs, start, stop, r, w, sgc=False):
        return self.P.op('pe', lambda e, o=out, l=lhsT, rr=rhs, s=start, t=stop, g=sgc:
                         e.matmul(o, lhsT=l, rhs=rr, start=s, stop=t, skip_group_check=g), r=r, w=w)

    def cbc(self, name, rows=128, c0=0, c1=None):
        off, w = self.cbcols[name]
        if c1 is None:
            c1 = w
        return self.cb[0:rows, off + c0:off + c1]

    def tr(self, out, in_, ident, r, w):
        return self.P.op('pe', lambda e, o=out, i=in_, d=ident: e.transpose(o, i, d), r=r, w=w)

    def act(self, out, in_, func, r, w, bias=None, scale=None):
        kw = {}
        if bias is not None:
            kw['bias'] = bias
        if scale is not None:
            kw['scale'] = scale
        return self.P.op('act', lambda e, o=out, i=in_, f=func, kw=kw: e.activation(o, i, f, **kw), r=r, w=w)

    def tt(self, eng, out, in0, in1, op, r, w):
        return self.P.op(eng, lambda e, o=out, a=in0, b=in1, p=op: e.tensor_tensor(o, a, b, p), r=r, w=w)

    def ts(self, eng, out, in0, s1, op0, r, w, s2=None, op1=None):
        if op1 is None:
            return self.P.op(eng, lambda e, o=out, a=in0, s=s1, p=op0: e.tensor_scalar(o, a, s, None, p), r=r, w=w)
        return self.P.op(eng, lambda e, o=out, a=in0, s=s1, p=op0, s2=s2, p1=op1: e.tensor_scalar(o, a, s, s2, p, p1),
                         r=r, w=w)

    def stt(self, out, in0, scalar, in1, op0, op1, r, w, eng='dve'):
        return self.P.op(eng, lambda e, o=out, a=in0, s=scalar, b=in1, p0=op0, p1=op1:
                         e.scalar_tensor_tensor(o, a, s, b, p0, p1), r=r, w=w)

    def cp(self, eng, out, in_, r, w):
        if eng == 'act':
            return self.P.op('act', lambda e, o=out, i=in_: e.copy(o, i), r=r, w=w)
        return self.P.op(eng, lambda e, o=out, i=in_: e.tensor_copy(o, i), r=r, w=w)

    def memset(self, eng, ap, val, w):
        return self.P.op(eng, lambda e, a=ap, v=val: e.memset(a, v), r=(), w=w)

    def dma(self, eng, out, in_, r, w, semkey, new_batch=True, slow=False, deps=()):
        kw = {}
        if slow:
            kw['allow_slow_non_contiguous'] = True
        return self.P.dma(eng, lambda e, o=out, i=in_, kw=kw: e.dma_start(out=o, in_=i, **kw), r=r, w=w,
                          semkey=(semkey, eng), new_batch=new_batch, deps=deps)

    def ring_load(self, src, kc, width, r=(), eng='sp'):
        s = self.ringn % NSLOT
        self.ringn += 1
        view = self.ring[s][:, 0:kc * width].rearrange("p (k w) -> p k w", k=kc)
        key = ('ring', s)
        self.dma(eng, view, src, r=r, w=[key], semkey=key)
        return view, key

    def build(self):
        nc, cfg = self.nc, self.cfg
        xp = self.din("xp", [SEQ, D])
        xs = self.din("xs", [NS, D])
        cvec = self.din("cvec", [2, D])
        self.w_ada = self.din("ada_w", [DEPTH, D, 9 * D])
        ada_b = self.din("ada_b", [DEPTH, 9 * D])
        norm_g = self.din("norm_g", [DEPTH, 6, D])
        ffn_w_in = self.din("ffn_w_in", [DEPTH, 2, D, 2 * DFF])
        ffn_w_out = self.din("ffn_w_out", [DEPTH, 2, DFF, D])
        self.I = I = {}
        I['state_ssm'] = self.din("state_ssm", [2, 2048, 128])
        I['state_conv'] = self.din("state_conv", [2, 3, 3072])
        I['ssd_w_in'] = self.din("ssd_w_in", [2, D, 5152])
        I['ssd_conv_w'] = self.din("ssd_conv_w", [2, 4 * 3072])
        I['ssd_conv_b'] = self.din("ssd_conv_b", [2, 3072])
        I['ssd_dt_bias'] = self.din("ssd_dt_bias", [2, 32])
        I['ssd_a_log'] = self.din("ssd_a_log", [2, 32])
        I['ssd_d'] = self.din("ssd_d", [2, 32])
        I['ssd_norm_g'] = self.din("ssd_norm_g", [2, 2048])
        I['ssd_w_out'] = self.din("ssd_w_out", [2, 2048, D])
        I['cache_sb_k'] = self.din("cache_sb_k", [PAST, D])
        I['cache_sb_v'] = self.din("cache_sb_v", [PAST, D])
        I['sb_w_qkv'] = self.din("sb_w_qkv", [D, 3 * D])
        I['sb_w_out'] = self.din("sb_w_out", [D, D])
        I['cache_swa_k'] = self.din("cache_swa_k", [128, 256])
        I['cache_swa_v'] = self.din("cache_swa_v", [128, 256])
        I['swa_w_qkv'] = self.din("swa_w_qkv", [D, 1536])
        I['swa_sinks'] = self.din("swa_sinks", [16])
        I['swa_w_out'] = self.din("swa_w_out", [D, D])
        yp = self.dout("yp", [SEQ, D])
        ys = self.dout("ys", [NS, D])
        self.O = O = {}
        for nm in ('swak_p', 'swak_s', 'swav_p', 'swav_s'):
            O[nm] = self.dout(nm, [128, 256])
        O['sbk_p'] = self.dout("sbk_p", [SEQ, D])
        O['sbk_s'] = self.dout("sbk_s", [NS, D])
        O['sbv_p'] = self.dout("sbv_p", [SEQ, D])
        O['sbv_s'] = self.dout("sbv_s", [NS, D])
        O['ssm_p'] = self.dout("ssm_p", [2, 2048, 128])
        O['ssm_s'] = self.dout("ssm_s", [2, 2048, 128])
        O['conv_p'] = self.dout("conv_p", [2, 3, 3072])
        O['conv_s'] = self.dout("conv_s", [2, 3, 3072])
        cnp, ccols = _consts_np()
        cdram = nc.inline_tensor(cnp, "consts").ap()
        cbnp, cbcols = _consts_bf_np()
        cbdram = nc.inline_tensor(cbnp, "constsb").ap()
        self.wb_ffn_in = self.dint("wb_ffn_in", [DEPTH, 2, D, 2 * DFF])
        self.wb_ffn_out = self.dint("wb_ffn_out", [DEPTH, 2, DFF, D])
        self.wb_ssd_in = self.dint("wb_ssd_in", [2, D, 5152])
        self.wb_ssd_out = self.dint("wb_ssd_out", [2, 2048, D])
        self.Sd = self.dint("Sd", [2, 128, 2048], F32)
        self.wb_sb_qkv = self.dint("wb_sb_qkv", [D, 3 * D])
        self.wb_sb_out = self.dint("wb_sb_out", [D, D])
        self.wb_swa_qkv = self.dint("wb_swa_qkv", [D, 1536])
        self.wb_swa_out = self.dint("wb_swa_out", [D, D])
        self.KTs = self.dint("KTs", [8, 128, SEQ])
        self.Vs = self.dint("Vs", [8, 128, SEQ // 128, 128])
        self.KTs_s = self.dint("KTs_s", [8, 128, PAST])
        self.Vs_s = self.dint("Vs_s", [8, 128, PAST // 128, 128])

        self.psum = [self.es.enter_context(nc.psum_tensor("ps%d" % i, [128, 512], F32)) for i in range(8)]
        self.ring = [self.sb("ring%d" % i, [128, SLOTW], BF16) for i in range(NSLOT)]
        self.X = self.sb("X", [128, KC * T], F32)
        self.Hb = self.sb("Hb", [128, KC * T], BF16)
        self.G = self.sb("G", [128, FC * T], BF16)
        self.Yf = self.sb("Yf", [128, KC * T], F32)
        self.SQ = self.sb("SQ", [128, KC * T], BF16)
        self.stg = [self.sb("stg%d" % i, [128, D], F32) for i in range(2)]
        self.tmpf = [self.sb("tmpf%d" % i, [128, T], F32) for i in range(4)]
        self.rstd = self.sb("rstd", [128, T], F32)
        self.cf = self.sb("cf", [128, cnp.shape[1]], F32)
        self.cb = self.sb("cb", [128, cbnp.shape[1]], BF16)
        self.identb = self.sb("identb", [128, 128], BF16)
        self.onesb = self.sb("onesb", [128, 128], BF16)
        self.cact = self.sb("cact", [128, 16], BF16)
        self.ccol = self.sb("ccol", [128, 16], F32)
        self.adab = self.sb("adab", [128, DEPTH * 72], F32)
        self.ng = self.sb("ng", [128, DEPTH * 48], F32)
        self.mod = self.sb("mod", [128, DEPTH * 144], F32)
        self.der = self.sb("der", [128, DEPTH * 3 * 3 * 2 * 8], F32)
        self.AR8 = self.sb("AR8", [128, 2048], F32)
        self.ccols = ccols
        self.cbcols = cbcols
        self.identf = self.cf[:, ccols['ident'][0]:ccols['ident'][0] + 128]
        self.ssd_alloc()
        self.sb_alloc()
        self.KTprev = self.sb("KTprev", [128, 512], BF16)
        self.Vprev = self.sb("Vprev", [128, 256], BF16)
        self.esink = self.sb("esink", [128, 8], F32)

        self.dma('pool', self.cf[:], cdram, r=[], w=['cf'], semkey='cf')
        self.dma('pool', self.cb[:], cbdram, r=[], w=['cb'], semkey='cb')
        self.cp('dve', self.identb[:], self.identf, r=['cf'], w=['identb'])
        self.memset('dve', self.onesb[:], 1.0, w=['onesb'])
        self.load_cols(self.ccol[:, 0:8], cvec[0, :], 'ccol')
        self.load_cols(self.ccol[:, 8:16], cvec[1, :], 'ccol')
        for i in range(DEPTH):
            for h in range(2):
                self.load_cols(self.adab[:, i * 72 + h * 36:i * 72 + h * 36 + 36], ada_b[i, h * 4608:(h + 1) * 4608], 'adab')
            self.load_cols(self.ng[:, i * 48:(i + 1) * 48], norm_g[i].rearrange("a d -> (a d)"), 'ng')
        self.ssd_params()
        for hh in range(2):
            self.dma('pool', self.esink[hh * 64:(hh + 1) * 64, :], bass.AP(I['swa_sinks'].tensor, hh, [[0, 64], [2, 8]]),
                     r=[], w=['esink'], semkey='esink', slow=True, new_batch=False)
        self.act(self.esink[:], self.esink[:], AF.Exp, r=['esink'], w=['esink'])
        self.prepass_mod()
        self.prepass_weights(ffn_w_in, ffn_w_out)

        ntile = cfg.get('ntile', NTILE)
        tiles = [('p', t) for t in range(ntile)]
        if cfg.get('sample', True):
            tiles.append(('s', 0))
        stop = cfg.get('stop')
        for kind, t in tiles:
            N = T if kind == 'p' else NS
            q = 0 if kind == 'p' else 1
            src = xp[t * T:(t + 1) * T, :] if kind == 'p' else xs
            dst = yp[t * T:(t + 1) * T, :] if kind == 'p' else ys
            last = (kind == 's') or (t == ntile - 1)
            self.load_fm(src, N)
            for i in range(cfg.get('depth', DEPTH)):
                self.ffn(i, 0, q, N)
                if stop == 'x%d_0' % i:
                    break
                if i % 3 == 0:
                    self.ssd(i, q, N, t, last)
                elif i % 3 == 1:
                    if kind == 's':
                        self.sb_prep_sample()
                    self.sbmix(i, q, N, t, last)
                else:
                    self.swamix(i, q, N, t, last)
                if stop == 'x%d_1' % i:
                    break
                self.ffn(i, 1, q, N)
            self.store_fm(dst, N)
        print('sbuf bytes remaining', self.nc.sbuf_bytes_remaining() if callable(getattr(self.nc, 'sbuf_bytes_remaining', None)) else getattr(self.nc, 'sbuf_bytes_remaining', None))
        self.P.emit(self.es)
        return nc

    def load_cols(self, dst, src_vec, key):
        self.dma('pool', dst, src_vec.rearrange("(n p) -> p n", p=128), r=[], w=[key], semkey=key, slow=True,
                 new_batch=False)

    def xv(self, tile, N, nch=KC):
        return tile[:, 0:nch * N].rearrange("p (k n) -> p k n", k=nch)

    def prepass_mod(self):
        cact_v = self.cact[:].rearrange("p (k two) -> p two k", two=2)
        for q in range(2):
            self.act(cact_v[:, q, :], self.ccol[:, q * 8:(q + 1) * 8], AF.Silu, r=['ccol'], w=['cact'])
        for i in range(DEPTH):
            bank, pk = self.ps()
            for t36 in range(36):
                src = self.w_ada[i, :, t36 * 256:(t36 + 1) * 256].rearrange("(k p) w -> p k w", p=128)
                wv, wk = self.ring_load(src, KC, 256, eng='pool')
                for c in range(2):
                    oc = t36 * 2 + c
                    for k in range(KC):
                        self.mm(bank[:, oc * 2:oc * 2 + 2], wv[:, k, c * 128:(c + 1) * 128], self.cact[:, k * 2:k * 2 + 2],
                                k == 0, k == KC - 1, r=[wk, 'cact'], w=[pk])
            mv = self.mod[:, i * 144:(i + 1) * 144].rearrange("p (o two) -> p o two", two=2)
            bv = bank[:, 0:144].rearrange("p (o two) -> p o two", two=2)
            ab = self.adab[:, i * 72:(i + 1) * 72].unsqueeze(2).broadcast_to([128, 72, 2])
            self.tt('dve', mv, bv, ab, ALU.add, r=[pk, 'adab'], w=['mod'])
            for s in range(3):
                for q in range(2):
                    def m(j):
                        return self.mod[:, i * 144:(i + 1) * 144].rearrange("p (j k two) -> p j two k", j=9, two=2)[:, j, q, :]
                    gpre = self.ng[:, i * 48 + (2 * s) * 8:i * 48 + (2 * s) * 8 + 8]
                    gpost = self.ng[:, i * 48 + (2 * s + 1) * 8:i * 48 + (2 * s + 1) * 8 + 8]
                    self.stt(self.dslice(i, s, 0, q), m(3 * s + 1), 1.0, gpre, ALU.add, ALU.mult, r=['mod', 'ng'], w=['der'])
                    self.cp('dve', self.dslice(i, s, 1, q), m(3 * s + 0), r=['mod'], w=['der'])
                    self.stt(self.dslice(i, s, 2, q), m(3 * s + 2), 0.5 if s != 1 else 1.0, gpost, ALU.mult, ALU.mult,
                             r=['mod', 'ng'], w=['der'])

    def dslice(self, i, s, which, q, kc=None):
        base = (((i * 3 + s) * 3 + which) * 2 + q) * 8
        if kc is None:
            return self.der[:, base:base + 8]
        return self.der[:, base + kc:base + kc + 1]

    def prepass_weights(self, ffn_w_in, ffn_w_out):
        first = True
        I = self.I
        for j in range(2):
            self.dma('pool', self.wb_ssd_in[j].rearrange("k (a b) -> k a b", a=4),
                     I['ssd_w_in'][j].rearrange("k (a b) -> k a b", a=4), r=[], w=['wb'], semkey='wcast', new_batch=first)
            first = False
            self.dma('pool', self.wb_ssd_out[j], I['ssd_w_out'][j], r=[], w=['wb'], semkey='wcast', new_batch=False)
        self.dma('pool', self.wb_sb_qkv.rearrange("k (a b) -> k a b", a=2), I['sb_w_qkv'].rearrange("k (a b) -> k a b", a=2),
                 r=[], w=['wb'], semkey='wcast', new_batch=False)
        self.dma('pool', self.wb_sb_out, I['sb_w_out'], r=[], w=['wb'], semkey='wcast', new_batch=False)
        self.dma('pool', self.wb_swa_qkv, I['swa_w_qkv'], r=[], w=['wb'], semkey='wcast', new_batch=False)
        self.dma('pool', self.wb_swa_out, I['swa_w_out'], r=[], w=['wb'], semkey='wcast', new_batch=False)
        for i in range(DEPTH):
            for s in range(2):
                self.dma('pool', self.wb_ffn_in[i, s].rearrange("k (a b) -> k a b", a=4),
                         ffn_w_in[i, s].rearrange("k (a b) -> k a b", a=4), r=[], w=['wb'], semkey='wcast', new_batch=first)
                first = False
                self.dma('pool', self.wb_ffn_out[i, s], ffn_w_out[i, s], r=[], w=['wb'], semkey='wcast', new_batch=False)

    def load_fm(self, src, N):
        nb = (N + 127) // 128
        for tb in range(nb):
            n = min(128, N - tb * 128)
            st = self.stg[self.stgn % 2]
            sk = ('stg', self.stgn % 2)
            self.stgn += 1
            self.dma('pool', st[0:n, :], src[tb * 128:tb * 128 + n, :], r=[], w=[sk], semkey=sk)
            for half in range(2):
                bank, pk = self.ps()
                for k4 in range(4):
                    k = half * 4 + k4
                    self.tr(bank[:, k4 * 128:k4 * 128 + n], st[0:n, k * 128:(k + 1) * 128], self.identf[0:n, 0:n],
                            r=[sk, 'cf'], w=[pk])
                xo = self.xv(self.X, N)[:, half * 4:half * 4 + 4, tb * 128:tb * 128 + n]
                pv = bank[:, 0:512].rearrange("p (k n) -> p k n", k=4)[:, :, 0:n]
                self.cp('dve' if half == 0 else 'act', xo, pv, r=[pk], w=[('X', half * 4 + j) for j in range(4)])

    def store_fm(self, dst, N, src=None, skey='X', nchunk=KC, vtm=None, vkeys=None, only_last=False, kofs=0):
        if src is None:
            src = self.xv(self.X, N)
        nb = (N + 127) // 128
        W = nchunk * 128
        for tb in range(nb):
            if only_last and tb != nb - 1:
                continue
            n = min(128, N - tb * 128)
            st = self.stg[self.stgn % 2]
            sk = ('stg', self.stgn % 2)
            self.stgn += 1
            for half in range((nchunk + 3) // 4):
                bank, pk = self.ps()
                nk4 = min(4, nchunk - half * 4)
                for k4 in range(nk4):
                    k = half * 4 + k4
                    self.tr(bank[0:n, k4 * 128:(k4 + 1) * 128], src[:, k, tb * 128:tb * 128 + n],
                            self.identf, r=[(skey, kofs + k), 'cf'], w=[pk])
                self.cp('dve' if half == 0 else 'act', st[0:n, half * 512:half * 512 + nk4 * 128], bank[0:n, 0:nk4 * 128],
                        r=[pk], w=[sk])
            if dst is not None:
                d = dst[0:n, :] if only_last else dst[tb * 128:tb * 128 + n, :]
                self.dma('pool', d, st[0:n, 0:W], r=[sk], w=[], semkey=sk)
            if vtm is not None:
                self.cp('pool', vtm(tb, n), st[0:n, 0:W], r=[sk], w=vkeys(tb))

    def sumsq_rstd(self, sq_view, sq_keys, N, nch, dim):
        bank, pk = self.ps()
        for k in range(nch):
            self.mm(bank[:, 0:N], self.onesb[:], sq_view(k), k == 0, k == nch - 1, r=['onesb', sq_keys[k]], w=[pk])
        t = self.tmpf[3]
        self.act(t[:, 0:N], bank[:, 0:N], AF.Sqrt, r=[pk, 'epsb'], w=['tmp3'], bias=self.epsb[:, 0:1], scale=1.0 / dim)
        self.P.op('dve', lambda e, o=self.rstd[:, 0:N], i=t[:, 0:N]: e.reciprocal(o, i), r=['tmp3'], w=['rstd'])

    def prenorm(self, i, s, q, N):
        Xv = self.xv(self.X, N)
        SQv = self.xv(self.SQ, N)
        for k in range(KC):
            self.act(SQv[:, k, :], Xv[:, k, :], AF.Square, r=[('X', k)], w=[('SQ', k)])
        self.sumsq_rstd(lambda k: SQv[:, k, :], [('SQ', k) for k in range(KC)], N, KC, D)
        Hv = self.xv(self.Hb, N)
        for k in range(KC):
            tn = k % 3
            t = self.tmpf[tn]
            self.stt(t[:, 0:N], Xv[:, k, :], self.dslice(i, s, 0, q, k), self.rstd[:, 0:N], ALU.mult, ALU.mult,
                     r=[('X', k), 'der', 'rstd'], w=[('tmp', tn)])
            self.act(Hv[:, k, :], t[:, 0:N], AF.Identity, r=[('tmp', tn), 'der'], w=[('Hb', k)],
                     bias=self.dslice(i, s, 1, q, k))

    def postnorm_resid(self, i, s, q, N):
        Xv = self.xv(self.X, N)
        SQv = self.xv(self.SQ, N)
        Yv = self.xv(self.Yf, N)
        self.sumsq_rstd(lambda k: SQv[:, k, :], [('SQ', k) for k in range(KC)], N, KC, D)
        for k in range(KC):
            tn = k % 3
            t = self.tmpf[tn]
            self.tt('dve', t[:, 0:N], Yv[:, k, :], self.rstd[:, 0:N], ALU.mult, r=[('Yf', k), 'rstd'], w=[('tmp', tn)])
            self.tt('pool', Xv[:, k, :], Xv[:, k, :], t[:, 0:N], ALU.add, r=[('tmp', tn), ('X', k)], w=[('X', k)])

    def evac_y(self, i, s, q, N, k, bank, pk):
        Yv = self.xv(self.Yf, N)
        SQv = self.xv(self.SQ, N)
        self.act(Yv[:, k, :], bank[:, 0:N], AF.Identity, r=[pk, 'der'], w=[('Yf', k)], scale=self.dslice(i, s, 2, q, k))
        self.act(SQv[:, k, :], bank[:, 0:N], AF.Square, r=[pk], w=[('SQ', k)])

    def ffn(self, i, which, q, N):
        s = 0 if which == 0 else 2
        self.prenorm(i, s, q, N)
        Hv = self.xv(self.Hb, N)
        Gv = self.xv(self.G, N, FC)
        win = self.wb_ffn_in[i, which]
        wout = self.wb_ffn_out[i, which]
        for jp in range(FC // 2):
            wa, ka = self.ring_load(win[:, jp * 256:(jp + 1) * 256].rearrange("(k p) w -> p k w", p=128), KC, 256, r=['wb'])
            wb_, kb = self.ring_load(win[:, DFF + jp * 256:DFF + (jp + 1) * 256].rearrange("(k p) w -> p k w", p=128), KC,
                                     256, r=['wb'])
            pa = []
            for c in range(2):
                bank, pk = self.ps()
                for k in range(KC):
                    self.mm(bank[:, 0:N], wa[:, k, c * 128:(c + 1) * 128], Hv[:, k, :], k == 0, k == KC - 1,
                            r=[ka, ('Hb', k)], w=[pk])
                pa.append((bank, pk))
            for c in range(2):
                bank, pk = self.ps()
                for k in range(KC):
                    self.mm(bank[:, 0:N], wb_[:, k, c * 128:(c + 1) * 128], Hv[:, k, :], k == 0, k == KC - 1,
                            r=[kb, ('Hb', k)], w=[pk])
                tn = self.tmpn.get('ffn', 0)
                self.tmpn['ffn'] = tn + 1
                t = self.tmpf[tn % 3]
                tk = ('tmp', tn % 3)
                self.act(t[:, 0:N], pa[c][0][:, 0:N], AF.Silu, r=[pa[c][1]], w=[tk])
                j = jp * 2 + c
                self.tt('dve', Gv[:, j, :], t[:, 0:N], bank[:, 0:N], ALU.mult, r=[tk, pk], w=[('G', j)])
        HF = FC // 2
        for oc in range(KC):
            bank, pk = self.ps()
            for hf in range(2):
                wv, wk = self.ring_load(wout[hf * HF * 128:(hf + 1) * HF * 128, oc * 128:(oc + 1) * 128].rearrange(
                    "(k p) w -> p k w", p=128), HF, 128, r=['wb'])
                for k in range(HF):
                    kk = hf * HF + k
                    self.mm(bank[:, 0:N], wv[:, k, :], Gv[:, kk, :], kk == 0, kk == FC - 1, r=[wk, ('G', kk)], w=[pk])
            self.evac_y(i, s, q, N, oc, bank, pk)
        self.postnorm_resid(i, s, q, N)


    def ssd_alloc(self):
        sb = self.sb
        self.hist = sb("hist", [128, 2 * 72], F32)
        self.cwcol = sb("cwcol", [128, 2 * 96], F32)
        self.cbcol = sb("cbcol", [128, 2 * 24], F32)
        self.sngcol = sb("sngcol", [128, 2 * 16], F32)
        self.Dcol = sb("Dcol", [128, 2 * 16], F32)
        self.dtb3 = sb("dtb3", [96, 2], F32)
        self.A3 = sb("A3", [96, 2], F32)
        self.a3parts = [sb("a3p%d" % i, [96, T], BF16) for i in range(3)]
        self.a3 = sb("a3", [96, T], BF16)
        self.cols = sb("cols", [128, 4 * 96], F32)
        self.cdrep = sb("cdrep", [128, 4 * 32], F32)
        self.dg = sb("dg", [96, 32], BF16)
        self.ones32 = sb("ones32", [96, 128], F32)
        self.cbuf = [sb("cbuf%d" % i, [128, T + 3], F32) for i in range(2)]
        self.xsfm = [sb("xsfm%d" % i, [128, T], BF16) for i in range(2)]
        self.xtm = [sb("xtm%d" % i, [128, 4 * 128], BF16) for i in range(2)]
        self.xw = [sb("xw%d" % i, [128, 4 * 128], BF16) for i in range(2)]
        self.zs = [sb("zs%d" % i, [128, T], BF16) for i in range(2)]
        self.Ea = [sb("Ea%d" % i, [128, 128], F32) for i in range(2)]
        self.E = [sb("E%d" % i, [128, 128], F32) for i in range(2)]
        self.Cp = [sb("Cp%d" % i, [128, 128], BF16) for i in range(2)]
        self.Mm = [sb("Mm%d" % i, [128, 128], BF16) for i in range(2)]
        self.Sbf = sb("Sbf", [128, 2048], BF16)
        self.sqy = [sb("sqy%d" % i, [128, 128], BF16) for i in range(2)]

    def ssd_params(self):
        I = self.I
        for j in range(2):
            for h in range(2):
                self.load_cols(self.cwcol[:, j * 96 + h * 48:j * 96 + h * 48 + 48], I['ssd_conv_w'][j, h * 6144:(h + 1) * 6144], 'cwcol')
            self.load_cols(self.cbcol[:, j * 24:(j + 1) * 24], I['ssd_conv_b'][j], 'cbcol')
            self.load_cols(self.sngcol[:, j * 16:(j + 1) * 16], I['ssd_norm_g'][j], 'sngcol')
            for hh in range(2):
                src = bass.AP(I['ssd_d'].tensor, j * 32 + hh, [[0, 64], [2, 16]])
                self.dma('pool', self.Dcol[hh * 64:(hh + 1) * 64, j * 16:(j + 1) * 16], src, r=[], w=['Dcol'], semkey='Dcol',
                         slow=True, new_batch=False)
            for g in range(3):
                self.dma('pool', self.dtb3[g * 32:(g + 1) * 32, j:j + 1], bass.AP(I['ssd_dt_bias'].tensor, j * 32, [[1, 32], [1, 1]]),
                         r=[], w=['dtb3'], semkey='dtb3', slow=True, new_batch=False)
                self.dma('pool', self.A3[g * 32:(g + 1) * 32, j:j + 1], bass.AP(I['ssd_a_log'].tensor, j * 32, [[1, 32], [1, 1]]),
                         r=[], w=['A3'], semkey='A3', slow=True, new_batch=False)
        self.act(self.A3[:], self.A3[:], AF.Exp, r=['A3'], w=['A3'])
        self.ts('dve', self.A3[:], self.A3[:], -1.0, ALU.mult, r=['A3'], w=['A3'])
        self.memset('dve', self.ones32[:], 1.0, w=['ones32'])

    def conv_silu(self, j, cc, bank, pk, N, out_ap, out_keys):
        n = self.cbn % 2
        self.cbn += 1
        cb, ck = self.cbuf[n], ('cbuf', n)
        hs = self.hist[:, j * 72 + cc * 3:j * 72 + cc * 3 + 3]
        hk = ('hist', j, cc)
        self.cp('pool', cb[:, 0:3], hs, r=[hk], w=[ck])
        self.cp('act', cb[:, 3:3 + N], bank[:, 0:N], r=[pk], w=[ck])
        self.cp('pool', hs, cb[:, N:N + 3], r=[ck], w=[hk])
        tn = self.tmpn.get('conv', 0)
        self.tmpn['conv'] = tn + 1
        acc, ak = self.tmpf[tn % 2][:, 0:N], ('tmp', tn % 2)

        def wc(tap):
            c0 = j * 96 + tap * 24 + cc
            return self.cwcol[:, c0:c0 + 1]
        self.ts('dve', acc, cb[:, 0:N], wc(0), ALU.mult, r=[ck, 'cwcol', 'cbcol'], w=[ak],
                s2=self.cbcol[:, j * 24 + cc:j * 24 + cc + 1], op1=ALU.add)
        for tap in range(1, 4):
            self.stt(acc, cb[:, tap:tap + N], wc(tap), acc, ALU.mult, ALU.add, r=[ck, 'cwcol', ak], w=[ak])
        self.act(out_ap, acc, AF.Silu, r=[ak], w=out_keys)

    def ssd(self, i, q, N, t, last):
        j = i // 3
        Q = 128 if q == 0 else 32
        nch = N // Q
        I, O = self.I, self.O
        win = self.wb_ssd_in[j]
        wout = self.wb_ssd_out[j]
        self.psrot = list(range(7))
        self.prenorm(i, 1, q, N)
        Hv = self.xv(self.Hb, N)
        S = self.AR8
        Skeys = [('S', x) for x in range(16)]
        Sbkeys = [('Sbf', x) for x in range(16)]
        hkeys = [('hist', j, cc) for cc in range(24)]
        histv = self.hist[:, j * 72:(j + 1) * 72].rearrange("p (c t) -> p c t", t=3)
        if q == 0 and t == 0:
            self.memset('pool', S[:], 0.0, w=Skeys)
            self.memset('pool', self.hist[:, j * 72:(j + 1) * 72], 0.0, w=hkeys)
        elif q == 0:
            self.dma('pool', S[:], self.Sd[j], r=[('Sd', j)], w=Skeys, semkey='Sld')
        else:
            for g4 in range(4):
                st, sk = self.stg[self.stgn % 2], ('stg', self.stgn % 2)
                self.stgn += 1
                self.dma('pool', st[:, 0:512].rearrange("p (b n) -> p b n", b=4),
                         I['state_ssm'][j, g4 * 512:(g4 + 1) * 512, :].rearrange("(b p) n -> p b n", p=128), r=[], w=[sk], semkey=sk)
                bank, pk = self.ps()
                for b4 in range(4):
                    self.tr(bank[:, b4 * 128:(b4 + 1) * 128], st[:, b4 * 128:(b4 + 1) * 128], self.identf, r=[sk, 'cf'], w=[pk])
                self.cp('dve', S[:, g4 * 512:(g4 + 1) * 512], bank[:, 0:512], r=[pk], w=Skeys[g4 * 4:g4 * 4 + 4])
            for tt_ in range(3):
                self.dma('pool', histv[:, :, tt_], I['state_conv'][j, tt_].rearrange("(c p) -> p c", p=128), r=[], w=hkeys,
                         semkey='histld', slow=True, new_batch=(tt_ == 0))
        self.cp('pool', self.Sbf[:], S[:], r=Skeys, w=Sbkeys)

        wv, wk = self.ring_load(win[:, 5120:5152].rearrange("(k p) w -> p k w", p=128), KC, 32, r=['wb'])
        bank, pk = self.ps()
        for g in range(3):
            for k in range(KC):
                self.mm(bank[g * 32:(g + 1) * 32, 0:N], wv[:, k, :], Hv[:, k, :], k == 0, k == KC - 1, r=[wk, ('Hb', k)], w=[pk])
        SQf = self.SQ[:].bitcast(F32)
        dtt, dA, acum, wst = [SQf[0:96, x * 512:x * 512 + N] for x in range(4)]
        kdt, kdA, kac, kw = [[('SQ', 2 * x), ('SQ', 2 * x + 1)] for x in range(4)]
        self.act(dtt, bank[0:96, 0:N], AF.Exp, r=[pk, 'dtb3'], w=kdt, bias=self.dtb3[:, j:j + 1])
        self.act(dtt, dtt, AF.Ln, r=kdt + ['oneb'], w=kdt, bias=self.oneb[0:96, 0:1])
        self.ts('dve', dA, dtt, self.A3[:, j:j + 1], ALU.mult, r=kdt + ['A3'], w=kdA)
        for c in range(nch):
            self.P.op('dve', lambda e, o=acum[:, c * Q:(c + 1) * Q], d0=self.ones32[:, 0:Q], d1=dA[:, c * Q:(c + 1) * Q]:
                      e.tensor_tensor_scan(o, d0, d1, 0.0, ALU.mult, ALU.add), r=kdA + ['ones32'], w=kac)
        H3, M3, L3 = [x[:, 0:N] for x in self.a3parts]
        r1, r2, nac = [self.tmpf[x][0:96, 0:N] for x in range(3)]
        self.cp('dve', H3, acum, r=kac, w=['H3'])
        self.tt('dve', r1, acum, H3, ALU.subtract, r=kac + ['H3'], w=[('tmp', 0)])
        self.cp('dve', M3, r1, r=[('tmp', 0)], w=['M3'])
        self.tt('dve', r2, r1, M3, ALU.subtract, r=[('tmp', 0), 'M3'], w=[('tmp', 1)])
        self.cp('dve', L3, r2, r=[('tmp', 1)], w=['L3'])
        a3 = self.a3
        self.cp('pool', a3[0:32, 0:N], H3[0:32], r=['H3'], w=['a3'])
        self.cp('pool', a3[32:64, 0:N], M3[32:64], r=['M3'], w=['a3'])
        self.cp('pool', a3[64:96, 0:N], L3[64:96], r=['L3'], w=['a3'])
        for c in range(nch):
            self.act(wst[:, c * Q:(c + 1) * Q], acum[:, c * Q:(c + 1) * Q], AF.Exp, r=kac, w=kw, scale=-1.0,
                     bias=acum[:, (c + 1) * Q - 1:(c + 1) * Q])
        self.tt('dve', wst, wst, dtt, ALU.mult, r=kw + kdt, w=kw)
        self.ts('dve', nac, acum, -1.0, ALU.mult, r=kac, w=[('tmp', 2)])
        for c in range(nch):
            bank, pk = self.ps()
            for x, (src, sk_) in enumerate(((nac, [('tmp', 2)]), (dtt, kdt), (wst, kw))):
                self.tr(bank[0:Q, x * 32:(x + 1) * 32], src[0:32, c * Q:(c + 1) * Q], self.identf[0:32, 0:32], r=sk_ + ['cf'], w=[pk])
            self.cp('dve', self.cols[0:Q, c * 96:(c + 1) * 96], bank[0:Q, 0:96], r=[pk], w=['cols'])
        bank, pk = self.ps()
        for c in range(nch):
            self.ts('dve', self.dg[:], self.cbc('i3', 96), a3[:, (c + 1) * Q - 1:(c + 1) * Q], ALU.mult, r=['cb', 'a3'], w=['dg'])
            self.mm(bank[:, c * 32:(c + 1) * 32], self.onesb[0:96, :], self.dg[:], True, True, r=['onesb', 'dg'], w=[pk])
        self.act(self.cdrep[:, 0:nch * 32], bank[:, 0:nch * 32], AF.Exp, r=[pk], w=['cdrep'])

        Yfb = self.Yf[:].bitcast(BF16)
        Bfm = Yfb[:, 0:2048].rearrange("p (g n) -> p g n", g=4)
        Cfm = Yfb[:, 2048:4096].rearrange("p (g n) -> p g n", g=4)
        Btm = Yfb[:, 4096:6144].rearrange("p (c n) -> p c n", c=4)
        cbm = Yfb[:, 6144:8192].rearrange("p (x n) -> p x n", x=16)
        kB, kC, kBt, kcb = [[('Yf', 2 * x), ('Yf', 2 * x + 1)] for x in range(4)]
        for g8 in range(8):
            col0 = 4096 + g8 * 128
            wv, wk = self.ring_load(win[:, col0:col0 + 128].rearrange("(k p) w -> p k w", p=128), KC, 128, r=['wb'])
            bank, pk = self.ps()
            for k in range(KC):
                self.mm(bank[:, 0:N], wv[:, k, :], Hv[:, k, :], k == 0, k == KC - 1, r=[wk, ('Hb', k)], w=[pk])
            dest = (Bfm if g8 < 4 else Cfm)[:, g8 % 4, 0:N]
            self.conv_silu(j, 16 + g8, bank, pk, N, dest, kB if g8 < 4 else kC)
        for c in range(nch):
            bank, pk = self.ps()
            bb = bank[:].bitcast(BF16)
            for g in range(4):
                self.tr(bb[0:Q, g * 128:(g + 1) * 128], Bfm[:, g, c * Q:(c + 1) * Q], self.identb[:], r=kB + ['identb'], w=[pk])
            self.cp('act', Btm[0:Q, c, :], bb[0:Q, 0:512], r=[pk], w=kBt)
        mle = self.cbc('mask_le', Q, 0, Q)
        for g in range(4):
            bank, pk = self.ps()
            for c in range(nch):
                self.mm(bank[0:Q, c * 128:c * 128 + Q], Bfm[:, g, c * Q:(c + 1) * Q], Cfm[:, g, c * Q:(c + 1) * Q], True, True,
                        r=kB + kC, w=[pk])
            pv = bank[0:Q, 0:nch * 128].rearrange("p (c n) -> p c n", c=nch)[:, :, 0:Q]
            self.tt('dve', cbm[0:Q, g * 4:g * 4 + nch, 0:Q], pv, mle.unsqueeze(1).broadcast_to([Q, nch, Q]), ALU.mult,
                    r=[pk, 'cb'], w=kcb)

        Gv = self.xv(self.G, N, FC)
        ssb, ssk = self.psum[7], ('ps', 7)
        sel3 = self.cbc('sel3', 96)
        negmask = self.cbc('negmask', Q, 0, Q)
        for jj in range(16):
            g = jj // 4
            pb = jj % 2
            xs_, xk = self.xsfm[pb], ('xsfm', pb)
            wv, wk = self.ring_load(win[:, 2048 + jj * 128:2048 + (jj + 1) * 128].rearrange("(k p) w -> p k w", p=128), KC, 128, r=['wb'])
            bank, pk = self.ps()
            for k in range(KC):
                self.mm(bank[:, 0:N], wv[:, k, :], Hv[:, k, :], k == 0, k == KC - 1, r=[wk, ('Hb', k)], w=[pk])
            self.conv_silu(j, jj, bank, pk, N, xs_[:, 0:N], [xk])
            bank, pk = self.ps()
            bb = bank[:].bitcast(BF16)
            for c in range(nch):
                self.tr(bb[0:Q, c * 128:(c + 1) * 128], xs_[:, c * Q:(c + 1) * Q], self.identb[:], r=[xk, 'identb'], w=[pk])
            xt, xtk = self.xtm[pb], ('xtm', pb)
            self.cp('act', xt[0:Q, 0:nch * 128], bb[0:Q, 0:nch * 128], r=[pk], w=[xtk])
            xwt, xwk = self.xw[pb], ('xw', pb)
            wcv = self.cols[0:Q, 0:nch * 96].rearrange("p (c x) -> p c x", c=nch)[:, :, 64 + 2 * jj:64 + 2 * jj + 2]
            self.tt('dve', xwt[0:Q, 0:nch * 128].rearrange("p (c h d) -> p c h d", c=nch, h=2),
                    xt[0:Q, 0:nch * 128].rearrange("p (c h d) -> p c h d", c=nch, h=2),
                    wcv.unsqueeze(3).broadcast_to([Q, nch, 2, 64]), ALU.mult, r=[xtk, 'cols'], w=[xwk])
            wv, wk = self.ring_load(win[:, jj * 128:(jj + 1) * 128].rearrange("(k p) w -> p k w", p=128), KC, 128, r=['wb'])
            bank, pk = self.ps()
            for k in range(KC):
                self.mm(bank[:, 0:N], wv[:, k, :], Hv[:, k, :], k == 0, k == KC - 1, r=[wk, ('Hb', k)], w=[pk])
            zt, zk = self.zs[pb], ('zs', pb)
            self.act(zt[:, 0:N], bank[:, 0:N], AF.Silu, r=[pk], w=[zk])
            for c in range(nch):
                ybank, ypk = self.ps()
                for hh in range(2):
                    h = 2 * jj + hh
                    rb, rpk = self.ps()
                    sel = sel3[:, h * 128:(h + 1) * 128]
                    a3c = a3[:, c * Q:(c + 1) * Q]
                    self.mm(rb[:, 0:Q], sel, a3c, True, True, r=['cb', 'a3'], w=[rpk])
                    self.mm(rb[0:Q, 128:128 + Q], sel[:, 0:Q], a3c, True, False, r=['cb', 'a3'], w=[rpk])
                    self.mm(rb[0:Q, 128:128 + Q], self.identb[0:Q, 0:Q], negmask, False, True, r=['cb', 'identb'], w=[rpk])
                    e = self.en % 2
                    self.en += 1
                    self.act(self.Ea[e][:, 0:Q], rb[:, 0:Q], AF.Exp, r=[rpk], w=[('Ea', e)])
                    self.act(self.E[e][0:Q, 0:Q], rb[0:Q, 128:128 + Q], AF.Exp, r=[rpk, 'cols'], w=[('E', e)],
                             bias=self.cols[0:Q, c * 96 + h:c * 96 + h + 1])
                    self.tt('dve', self.Cp[e][:, 0:Q], Cfm[:, g, c * Q:(c + 1) * Q], self.Ea[e][:, 0:Q], ALU.mult,
                            r=kC + [('Ea', e)], w=[('Cp', e)])
                    self.stt(self.Mm[e][0:Q, 0:Q], self.E[e][0:Q, 0:Q], self.cols[0:Q, c * 96 + 32 + h:c * 96 + 32 + h + 1],
                             cbm[0:Q, g * 4 + c, 0:Q], ALU.mult, ALU.mult, r=[('E', e), 'cols'] + kcb, w=[('Mm', e)])
                    self.mm(ybank[hh * 64:(hh + 1) * 64, 0:Q], self.Sbf[:, h * 64:(h + 1) * 64], self.Cp[e][:, 0:Q], True, False,
                            r=[('Sbf', jj), ('Cp', e)], w=[ypk])
                    self.mm(ybank[hh * 64:(hh + 1) * 64, 0:Q], xt[0:Q, c * 128 + hh * 64:c * 128 + (hh + 1) * 64],
                            self.Mm[e][0:Q, 0:Q], False, True, r=[xtk, ('Mm', e)], w=[ypk])
                y1, y1k = self.tmpf[2][:, 0:Q], ('tmp', 2)
                y2, y2k = self.tmpf[3][:, 0:Q], 'tmp3'
                self.stt(y1, xs_[:, c * Q:(c + 1) * Q], self.Dcol[:, j * 16 + jj:j * 16 + jj + 1], ybank[:, 0:Q], ALU.mult, ALU.add,
                         r=[xk, 'Dcol', ypk], w=[y1k])
                self.tt('dve', y2, y1, zt[:, c * Q:(c + 1) * Q], ALU.mult, r=[y1k, zk], w=[y2k])
                self.act(Gv[:, jj, c * Q:(c + 1) * Q], y2, AF.Identity, r=[y2k, 'sngcol'], w=[('G', jj)],
                         scale=self.sngcol[:, j * 16 + jj:j * 16 + jj + 1])
                e = self.en % 2
                self.act(self.sqy[e][:, 0:Q], y2, AF.Square, r=[y2k], w=[('sqy', e)])
                self.mm(ssb[:, c * Q:(c + 1) * Q], self.onesb[:], self.sqy[e][:, 0:Q], jj == 0 and c == 0,
                        jj == 15 and c == nch - 1, r=['onesb', ('sqy', e)], w=[ssk], sgc=True)
                sn, snk = self.ps()
                self.mm(sn[:, 0:128], Btm[0:Q, c, g * 128:(g + 1) * 128], xwt[0:Q, c * 128:(c + 1) * 128], True, True,
                        r=kBt + [xwk], w=[snk])
                Sp = S[:, jj * 128:(jj + 1) * 128]
                Sp3 = Sp.rearrange("p (h d) -> p h d", h=2)
                cdv = self.cdrep[:, c * 32 + 2 * jj:c * 32 + 2 * jj + 2].unsqueeze(2).broadcast_to([128, 2, 64])
                self.tt('dve', Sp3, Sp3, cdv, ALU.mult, r=[('S', jj), 'cdrep'], w=[('S', jj)])
                self.tt('dve', Sp, Sp, sn[:, 0:128], ALU.add, r=[('S', jj), snk], w=[('S', jj)])
                self.cp('pool', self.Sbf[:, jj * 128:(jj + 1) * 128], Sp, r=[('S', jj)], w=[('Sbf', jj)])

        t3 = self.tmpf[3]
        self.act(t3[:, 0:N], ssb[:, 0:N], AF.Sqrt, r=[ssk, 'epsb'], w=['tmp3'], bias=self.epsb[:, 0:1], scale=1.0 / 2048)
        self.P.op('dve', lambda e, o=self.rstd[:, 0:N], i_=t3[:, 0:N]: e.reciprocal(o, i_), r=['tmp3'], w=['rstd'])
        for oc in range(KC):
            wv, wk = self.ring_load(wout[:, oc * 128:(oc + 1) * 128].rearrange("(k p) w -> p k w", p=128), 16, 128, r=['wb'])
            bank, pk = self.ps()
            for k in range(16):
                self.mm(bank[:, 0:N], wv[:, k, :], Gv[:, k, :], k == 0, k == 15, r=[wk, ('G', k)], w=[pk])
            tn = oc % 3
            tq = self.tmpf[tn]
            self.tt('dve', tq[:, 0:N], bank[:, 0:N], self.rstd[:, 0:N], ALU.mult, r=[pk, 'rstd'], w=[('tmp', tn)])
            self.evac_y(i, 1, q, N, oc, tq, ('tmp', tn))
        self.postnorm_resid(i, 1, q, N)

        if q == 0 and not last:
            self.dma('pool', self.Sd[j], S[:], r=Skeys, w=[('Sd', j)], semkey='Sst')
        if last:
            dst = O['ssm_p' if q == 0 else 'ssm_s'][j]
            for g4 in range(4):
                bank, pk = self.ps()
                for b4 in range(4):
                    blk = g4 * 4 + b4
                    self.tr(bank[:, b4 * 128:(b4 + 1) * 128], S[:, blk * 128:(blk + 1) * 128], self.identf, r=[('S', blk), 'cf'], w=[pk])
                st, sk = self.stg[self.stgn % 2], ('stg', self.stgn % 2)
                self.stgn += 1
                self.cp('dve', st[:, 0:512], bank[:, 0:512], r=[pk], w=[sk])
                self.dma('pool', dst[g4 * 512:(g4 + 1) * 512, :].rearrange("(b p) n -> p b n", p=128),
                         st[:, 0:512].rearrange("p (b n) -> p b n", b=4), r=[sk], w=[], semkey=sk)
            cdst = O['conv_p' if q == 0 else 'conv_s'][j]
            for tt_ in range(3):
                self.dma('pool', cdst[tt_].rearrange("(c p) -> p c", p=128), histv[:, :, tt_], r=hkeys, w=[], semkey='histst',
                         slow=True, new_batch=(tt_ == 0))
        self.psrot = list(range(8))


    def sb_alloc(self):
        sb = self.sb
        self.KTseg = [sb("KTseg%d" % i, [128, 1024], BF16) for i in range(2)]
        self.Vseg = [sb("Vseg%d" % i, [128, 1024], BF16) for i in range(2)]
        self.SPrun = self.cbuf
        self.SPrunb = self.xsfm
        self.segn = 0
        self.un = 0

    def tk(self, n):
        return ('tmp', n) if n < 3 else 'tmp3'

    def sb_prep_sample(self):
        I = self.I
        for c in range(8):
            self.dma('pool', self.Vs_s[c], I['cache_sb_v'][:, c * 128:(c + 1) * 128].rearrange("(b p) d -> p b d", p=128),
                     r=[], w=['Vs_s'], semkey='vss', new_batch=(c == 0))
        Gv = self.xv(self.G, 128, FC)
        for b in range(PAST // 128):
            st, sk = self.stg[self.stgn % 2], ('stg', self.stgn % 2)
            self.stgn += 1
            self.dma('pool', st[:, :], I['cache_sb_k'][b * 128:(b + 1) * 128, :], r=[], w=[sk], semkey=sk)
            for half in range(2):
                bank, pk = self.ps()
                for k4 in range(4):
                    k = half * 4 + k4
                    self.tr(bank[:, k4 * 128:(k4 + 1) * 128], st[:, k * 128:(k + 1) * 128], self.identf, r=[sk, 'cf'], w=[pk])
                self.cp('dve' if half == 0 else 'act', Gv[:, 14 + half * 4:14 + half * 4 + 4, :],
                        bank[:, 0:512].rearrange("p (k n) -> p k n", k=4), r=[pk], w=[('G', 14 + half * 4 + x) for x in range(4)])
            self.dma('pool', self.KTs_s[:, :, b * 128:(b + 1) * 128].rearrange("c p t -> p c t"), Gv[:, 14:22, :],
                     r=[('G', 14 + x) for x in range(8)], w=['KTs_s'], semkey='ktss')

    def sbmix(self, i, q, N, t, last):
        I, O = self.I, self.O
        self.psrot = list(range(5))
        self.prenorm(i, 1, q, N)
        Hv = self.xv(self.Hb, N)
        Gv = self.xv(self.G, N, FC)
        Yv = self.xv(self.Yf, N)
        win, wout = self.wb_sb_qkv, self.wb_sb_out
        nb = (N + 127) // 128
        Vtm = self.AR8[:].bitcast(BF16).rearrange("p (b f) -> p b f", b=4)
        vk = lambda tb: [('S', 4 * tb + x) for x in range(4)]
        allvk = [('S', x) for x in range(16)]
        for part in range(3):
            if part >= self.cfg.get('sbparts', 3):
                continue
            for c in range(8):
                col0 = part * D + c * 128
                wv, wk = self.ring_load(win[:, col0:col0 + 128].rearrange("(k p) w -> p k w", p=128), KC, 128, r=['wb'])
                bank, pk = self.ps()
                for k in range(KC):
                    self.mm(bank[:, 0:N], wv[:, k, :], Hv[:, k, :], k == 0, k == KC - 1, r=[wk, ('Hb', k)], w=[pk])
                if part == 0:
                    self.cp('act', Gv[:, c, :], bank[:, 0:N], r=[pk], w=[('G', c)])
                elif part == 1:
                    self.cp('act', Gv[:, 8 + c, :], bank[:, 0:N], r=[pk], w=[('G', 8 + c)])
                    self.cp('dve', Yv[:, c, :], bank[:, 0:N], r=[pk], w=[('Yf', c)])
                else:
                    self.cp('dve', Yv[:, c, :], bank[:, 0:N], r=[pk], w=[('Yf', c)])
            if part == 1:
                dst = O['sbk_p'][t * T:(t + 1) * T, :] if q == 0 else O['sbk_s']
                if self.cfg.get('sbdbg') == 'nodma':
                    dst = None
                if self.cfg.get('sbdbg') != 'nostore':
                    self.store_fm(dst, N, src=Yv, skey='Yf')
                if q == 0 and not last:
                    self.dma('pool', self.KTs[:, :, t * T:(t + 1) * T].rearrange("c p t -> p c t"), Gv[:, 8:16, :],
                             r=[('G', 8 + x) for x in range(8)], w=['KTs'], semkey='kts')
            elif part == 2:
                dst = O['sbv_p'][t * T:(t + 1) * T, :] if q == 0 else O['sbv_s']
                self.store_fm(dst, N, src=Yv, skey='Yf', vtm=lambda tb, n: Vtm[0:n, tb, :], vkeys=vk)
                if q == 0 and not last:
                    for tb in range(nb):
                        self.dma('pool', self.Vs[:, :, 4 * t + tb, :].rearrange("c p d -> p c d"),
                                 Vtm[:, tb, :].rearrange("p (c d) -> p c d", c=8), r=vk(tb), w=['Vs'], semkey='vs',
                                 new_batch=(tb == 0))
        if self.cfg.get('sbstage', 9) < 2:
            self.psrot = list(range(8))
            return
        OTv = Hv
        ob, ok = self.psum[7], ('ps', 7)
        npast = 4 * t if q == 0 else PAST // 128
        KTd, kdk = (self.KTs, 'KTs') if q == 0 else (self.KTs_s, 'KTs_s')
        Vd, vdk = (self.Vs, 'Vs') if q == 0 else (self.Vs_s, 'Vs_s')
        SEGB = 8
        zeros = self.cbc('zeros', 128, 0, 64)
        for c in range(8):
            units = []
            for r_ in reversed(range(nb)):
                nk = min(128, N - r_ * 128)
                for hh in range(2):
                    units.append(('in', r_, nk, hh))
            segs = [(b0, min(b0 + SEGB, npast)) for b0 in range(0, npast, SEGB)]
            for (b0, b1) in reversed(segs):
                for b in reversed(range(b0, b1)):
                    for hh in range(2):
                        units.append(('past', b, (b0, b1), hh))
            for hh in range(2):
                self.mm(ob[hh * 64:(hh + 1) * 64, 0:N], zeros, Gv[:, c, :], True, False, r=['cb', ('G', c)], w=[ok], sgc=True)
                self.mm(self.psum[6 - hh][:, 0:N], self.cbc('zeros'), Gv[:, c, :], True, False, r=['cb', ('G', c)],
                        w=[('ps', 6 - hh)], sgc=True)
            if self.cfg.get('sbstage', 9) < 3:
                units = []
            curseg = None
            prev = None
            prev2 = None
            for ui, u in enumerate(units):
                lastu = ui >= len(units) - 2
                if u[0] == 'in':
                    _, r_, nk, hh = u
                    c0 = r_ * 128
                    cur = self.sb_stage1(N, c, hh, Gv[hh * 64:(hh + 1) * 64, 8 + c, c0:c0 + nk], [('G', 8 + c)],
                                         Vtm[0:nk, r_, c * 128 + hh * 64:c * 128 + (hh + 1) * 64], vk(r_), c0, nk, True, lastu)
                else:
                    _, b, (b0, b1), hh = u
                    if curseg != (b0, b1):
                        curseg = (b0, b1)
                        si = self.segn % 2
                        self.segn += 1
                        nbl = b1 - b0
                        self.dma('sp', self.KTseg[si][:, 0:nbl * 128], KTd[c, :, b0 * 128:b1 * 128], r=[kdk], w=[('kseg', si)],
                                 semkey=('kseg', si))
                        self.dma('sp', self.Vseg[si][:, 0:nbl * 128].rearrange("p (b d) -> p b d", d=128), Vd[c, :, b0:b1, :],
                                 r=[vdk], w=[('vseg', si)], semkey=('vseg', si))
                    o_ = (b - b0) * 128
                    cur = self.sb_stage1(N, c, hh, self.KTseg[si][hh * 64:(hh + 1) * 64, o_:o_ + 128], [('kseg', si)],
                                         self.Vseg[si][:, o_ + hh * 64:o_ + (hh + 1) * 64], [('vseg', si)], 0, 128, False, lastu)
                if prev is not None:
                    self.sb_stage2a(*prev)
                if prev2 is not None:
                    self.sb_stage2b(*prev2)
                prev2 = prev
                prev = cur
            if prev is not None:
                self.sb_stage2a(*prev)
            if prev2 is not None:
                self.sb_stage2b(*prev2)
            if prev is not None:
                self.sb_stage2b(*prev)
            self.cp('act', OTv[:, c, :], ob[:, 0:N], r=[ok], w=[('Hb', c)])
        for oc in range(KC):
            wv, wk = self.ring_load(wout[:, oc * 128:(oc + 1) * 128].rearrange("(k p) w -> p k w", p=128), KC, 128, r=['wb'])
            bank, pk = self.ps()
            for k in range(KC):
                self.mm(bank[:, 0:N], wv[:, k, :], OTv[:, k, :], k == 0, k == KC - 1, r=[wk, ('Hb', k)], w=[pk])
            self.evac_y(i, 1, q, N, oc, bank, pk)
        self.postnorm_resid(i, 1, q, N)
        self.psrot = list(range(8))

    def sb_stage1(self, N, c, hh, KTb, kK, Vb, kV, c0, nk, diag, lastu):
        Gv = self.xv(self.G, N, FC)
        SQv = self.xv(self.SQ, N)
        ncol = N - c0
        zb, zk = self.ps()
        self.mm(zb[0:nk, 0:ncol], KTb, Gv[hh * 64:(hh + 1) * 64, c, c0:N], True, True, r=kK + [('G', c)], w=[zk])
        u = self.un
        self.un += 1
        e_t, ek = self.tmpf[u % 2][0:nk, 0:ncol], self.tk(u % 2)
        self.act(e_t, zb[0:nk, 0:ncol], AF.Exp, r=[zk], w=[ek], scale=0.125)
        spb, sbk_ = SQv[0:nk, 3 + u % 3, 0:ncol], ('SQ', 3 + u % 3)
        self.act(spb, e_t, AF.Ln, r=[ek, 'oneb'], w=[sbk_], bias=self.oneb[0:nk, 0:1])
        if diag:
            self.tt('pool', spb[:, 0:nk], spb[:, 0:nk], self.cbc('mask_lt', nk, 0, nk), ALU.mult, r=[sbk_, 'cb'], w=[sbk_])
        lsb, lk = SQv[0:nk, u % 3, 0:ncol], ('SQ', u % 3)
        self.stt(lsb, zb[0:nk, 0:ncol], 0.125, spb, ALU.mult, ALU.subtract, r=[zk, sbk_], w=[lk])
        return (N, hh, Vb, kV, c0, nk, diag, lastu, u, spb, sbk_, lsb, lk)

    def sb_stage2a(self, N, hh, Vb, kV, c0, nk, diag, lastu, u, spb, sbk_, lsb, lk):
        SQv = self.xv(self.SQ, N)
        ncol = N - c0
        acc, ack = self.psum[6 - hh], ('ps', 6 - hh)
        a_ = acc[0:nk, c0:N]
        self.mm(a_, self.cbc('negtri', nk, 0, nk), spb, False, False, r=['cb', sbk_], w=[ack], sgc=True)
        self.mm(a_, self.identb[0:nk, 0:nk], lsb, False, False, r=['identb', lk], w=[ack], sgc=True)
        if diag:
            self.mm(acc[0:nk, c0:c0 + nk], self.identb[0:nk, 0:nk], self.cbc('negmask_lt', nk, 0, nk), False, False,
                    r=['identb', 'cb'], w=[ack], sgc=True)
        w_t, wk_ = SQv[0:nk, 6 + u % 2, 0:ncol], ('SQ', 6 + u % 2)
        self.act(w_t, a_, AF.Exp, r=[ack], w=[wk_])

    def sb_stage2b(self, N, hh, Vb, kV, c0, nk, diag, lastu, u, spb, sbk_, lsb, lk):
        SQv = self.xv(self.SQ, N)
        ncol = N - c0
        ob, ok = self.psum[7], ('ps', 7)
        acc, ack = self.psum[6 - hh], ('ps', 6 - hh)
        a_ = acc[0:nk, c0:N]
        w_t, wk_ = SQv[0:nk, 6 + u % 2, 0:ncol], ('SQ', 6 + u % 2)
        self.mm(ob[hh * 64:(hh + 1) * 64, c0:N], Vb, w_t, False, lastu, r=kV + [wk_], w=[ok], sgc=True)
        if nk == 128:
            self.mm(a_, self.cbc('negtrile', nk, 0, nk), spb, False, False, r=['cb', sbk_], w=[ack], sgc=True)
        else:
            self.mm(acc[:, c0:N], self.cbc('negones', nk, 0, 128), spb, False, False, r=['cb', sbk_], w=[ack], sgc=True)
            self.mm(a_, self.cbc('postri', nk, 0, nk), spb, False, False, r=['cb', sbk_], w=[ack], sgc=True)
        self.mm(a_, self.cbc('negident', nk, 0, nk), lsb, False, False, r=['cb', lk], w=[ack], sgc=True)
        if diag:
            self.mm(acc[0:nk, c0:c0 + nk], self.identb[0:nk, 0:nk], self.cbc('posmask_lt', nk, 0, nk), False, False,
                    r=['identb', 'cb'], w=[ack], sgc=True)

    def swamix(self, i, q, N, t, last):
        I, O = self.I, self.O
        self.psrot = list(range(6))
        self.prenorm(i, 1, q, N)
        Hv = self.xv(self.Hb, N)
        Gv = self.xv(self.G, N, FC)
        Yv = self.xv(self.Yf, N)
        win, wout = self.wb_swa_qkv, self.wb_swa_out
        W = 128 + N
        base = 8 * T
        KTd = self.G[:, base:base + 4 * W].rearrange("p (g w) -> p g w", g=4)
        kK = [('G', 8 + x) for x in range(5)]
        vb0 = base + 4 * (128 + T)
        Vt = self.G[:, vb0:vb0 + 5 * 256].rearrange("p (b f) -> p b f", b=5)
        kV = [('G', 13 + x) for x in range(3)]
        nb = (N + 127) // 128
        for c in range(8):
            wv, wk = self.ring_load(win[:, c * 128:(c + 1) * 128].rearrange("(k p) w -> p k w", p=128), KC, 128, r=['wb'])
            bank, pk = self.ps()
            for k in range(KC):
                self.mm(bank[:, 0:N], wv[:, k, :], Hv[:, k, :], k == 0, k == KC - 1, r=[wk, ('Hb', k)], w=[pk])
            self.cp('act', Gv[:, c, :], bank[:, 0:N], r=[pk], w=[('G', c)])
        wv, wk = self.ring_load(win[:, 1024:1280].rearrange("(k p) w -> p k w", p=128), KC, 256, r=['wb'])
        for g in range(4):
            bank, pk = self.ps()
            for half in range(2):
                for k in range(KC):
                    self.mm(bank[half * 64:(half + 1) * 64, 0:N], wv[:, k, g * 64:(g + 1) * 64], Hv[:, k, :], k == 0, k == KC - 1,
                            r=[wk, ('Hb', k)], w=[pk])
            self.cp('act', KTd[:, g, 128:128 + N], bank[:, 0:N], r=[pk], w=kK)
        if last:
            for c2 in range(2):
                bank, pk = self.ps()
                for k in range(KC):
                    self.mm(bank[:, 0:N], wv[:, k, c2 * 128:(c2 + 1) * 128], Hv[:, k, :], k == 0, k == KC - 1,
                            r=[wk, ('Hb', k)], w=[pk])
                self.cp('dve', Yv[:, c2, :], bank[:, 0:N], r=[pk], w=[('Yf', c2)])
        wv, wk = self.ring_load(win[:, 1280:1536].rearrange("(k p) w -> p k w", p=128), KC, 256, r=['wb'])
        for c2 in range(2):
            bank, pk = self.ps()
            for k in range(KC):
                self.mm(bank[:, 0:N], wv[:, k, c2 * 128:(c2 + 1) * 128], Hv[:, k, :], k == 0, k == KC - 1, r=[wk, ('Hb', k)], w=[pk])
            self.cp('dve', Yv[:, 2 + c2, :], bank[:, 0:N], r=[pk], w=[('Yf', 2 + c2)])
        if q == 0 and t > 0:
            self.cp('pool', KTd[:, :, 0:128], self.KTprev[:].rearrange("p (g w) -> p g w", g=4), r=['KTprev'], w=kK)
            self.cp('pool', Vt[:, 0, :], self.Vprev[:], r=['Vprev'], w=kV)
        elif q == 1:
            st, sk = self.stg[self.stgn % 2], ('stg', self.stgn % 2)
            self.stgn += 1
            for dup in range(2):
                self.dma('pool', st[:, 0:512].rearrange("p (g u d) -> p g u d", g=4, u=2)[:, :, dup, :],
                         I['cache_swa_k'].rearrange("p (g d) -> p g d", g=4), r=[], w=[sk], semkey=sk, new_batch=(dup == 0))
            bank, pk = self.ps()
            for g in range(4):
                self.tr(bank[:, g * 128:(g + 1) * 128], st[:, g * 128:(g + 1) * 128], self.identf, r=[sk, 'cf'], w=[pk])
            self.cp('act', KTd[:, :, 0:128], bank[:, 0:512].rearrange("p (g w) -> p g w", g=4), r=[pk], w=kK)
            st, sk = self.stg[self.stgn % 2], ('stg', self.stgn % 2)
            self.stgn += 1
            self.dma('pool', st[:, 0:256], I['cache_swa_v'], r=[], w=[sk], semkey=sk)
            self.cp('pool', Vt[:, 0, :], st[:, 0:256], r=[sk], w=kV)
            self.dma('pool', O['swak_s'][0:96, :], I['cache_swa_k'][32:128, :], r=[], w=[], semkey='swaout', new_batch=True)
            self.dma('pool', O['swav_s'][0:96, :], I['cache_swa_v'][32:128, :], r=[], w=[], semkey='swaout', new_batch=False)
        vdst = None
        if last:
            vdst = O['swav_p'] if q == 0 else O['swav_s'][96:128, :]
        self.store_fm(vdst, N, src=Yv[:, 2:4], skey='Yf', nchunk=2, vtm=lambda tb, n: Vt[0:n, 1 + tb, :], vkeys=lambda tb: kV,
                      only_last=False, kofs=2) if not (last and q == 0) else None
        if last and q == 0:
            for tb in range(nb):
                self.store_fm(O['swav_p'] if tb == nb - 1 else None, 128, src=Yv[:, 2:4, tb * 128:(tb + 1) * 128], skey='Yf', nchunk=2,
                              vtm=lambda tb_, n, tb=tb: Vt[0:n, 1 + tb, :], vkeys=lambda tb_: kV, kofs=2)
            self.store_fm(O['swak_p'], 128, src=Yv[:, 0:2, (nb - 1) * 128:nb * 128], skey='Yf', nchunk=2)
        elif last:
            self.store_fm(O['swak_s'][96:128, :], N, src=Yv[:, 0:2], skey='Yf', nchunk=2)
        OTv = Hv
        ob, ok = self.psum[7], ('ps', 7)
        db, dk = self.psum[6], ('ps', 6)
        NM = self.cbc('nmswa')
        zeros = self.cbc('zeros', 128, 0, 64)
        SQv = self.xv(self.SQ, N)
        if q == 0:
            blocks = []
            for kb in range(-1, nb):
                if kb == -1 and t == 0:
                    continue
                q0, q1 = max(0, 128 * kb), min(N, 128 * kb + 256)
                blocks.append(((kb + 1) * 128, 128, kb + 1, q0, q1, q0 - 128 * kb))
        else:
            blocks = [(0, 128, 0, 0, N, None), (128, N, 1, 0, N, None)]
        for c in range(8):
            for hh in range(2):
                h = 2 * c + hh
                g = h // 4
                self.mm(ob[hh * 64:(hh + 1) * 64, 0:N], zeros, Gv[:, c, :], True, False, r=['cb', ('G', c)], w=[ok], sgc=True)
                self.mm(db[hh * 64:(hh + 1) * 64, 0:N], zeros, Gv[:, c, :], True, False, r=['cb', ('G', c)], w=[dk], sgc=True)
                for bi, (kc0, nk, vblk, q0, q1, p0) in enumerate(blocks):
                    nq = q1 - q0
                    lastb = bi == len(blocks) - 1
                    sb_, sk_ = self.ps()
                    self.mm(sb_[0:nk, 0:nq], KTd[hh * 64:(hh + 1) * 64, g, kc0:kc0 + nk], Gv[hh * 64:(hh + 1) * 64, c, q0:q1],
                            True, p0 is None, r=kK + [('G', c)], w=[sk_])
                    if p0 is not None:
                        self.mm(sb_[0:nk, 0:nq], self.identb[0:nk, 0:nk], NM[0:nk, p0:p0 + nq], False, True, r=['identb', 'cb'], w=[sk_])
                    u = self.un
                    self.un += 1
                    P, pk_ = SQv[0:nk, u % 4, 0:nq], ('SQ', u % 4)
                    self.act(P, sb_[0:nk, 0:nq], AF.Exp, r=[sk_], w=[pk_], scale=0.125)
                    self.mm(ob[hh * 64:(hh + 1) * 64, q0:q1], Vt[0:nk, vblk, g * 64:(g + 1) * 64], P, False, lastb, r=kV + [pk_], w=[ok],
                            sgc=True)
                    self.mm(db[hh * 64:(hh + 1) * 64, q0:q1], self.onesb[0:nk, 0:64], P, False, lastb, r=['onesb', pk_], w=[dk], sgc=True)
            den, dnk = self.tmpf[c % 2][:, 0:N], self.tk(c % 2)
            self.ts('dve', den, db[:, 0:N], self.esink[:, c:c + 1], ALU.add, r=[dk, 'esink'], w=[dnk])
            self.P.op('dve', lambda e, o=den, i_=den: e.reciprocal(o, i_), r=[dnk], w=[dnk])
            self.tt('dve', OTv[:, c, :], ob[:, 0:N], den, ALU.mult, r=[ok, dnk], w=[('Hb', c)])
        if q == 0 and not last:
            self.cp('pool', self.KTprev[:].rearrange("p (g w) -> p g w", g=4), KTd[:, :, N:N + 128], r=kK, w=['KTprev'])
            self.cp('pool', self.Vprev[:], Vt[:, nb, :], r=kV, w=['Vprev'])
        for oc in range(KC):
            wv, wk = self.ring_load(wout[:, oc * 128:(oc + 1) * 128].rearrange("(k p) w -> p k w", p=128), KC, 128, r=['wb'])
            bank, pk = self.ps()
            for k in range(KC):
                self.mm(bank[:, 0:N], wv[:, k, :], OTv[:, k, :], k == 0, k == KC - 1, r=[wk, ('Hb', k)], w=[pk])
            self.evac_y(i, 1, q, N, oc, bank, pk)
        self.postnorm_resid(i, 1, q, N)
        self.psrot = list(range(8))


def build_program(cfg):
    b = Builder(cfg)
    b.epsb = b.sb("epsb", [128, 1], F32)
    b.memset('dve', b.epsb[:], EPS, w=['epsb'])
    b.oneb = b.sb("oneb", [128, 1], F32)
    b.memset('dve', b.oneb[:], 1.0, w=['oneb'])
    nc = b.build()
    return b, nc


def make_in_maps(inputs):
    maps = []
    f = np.ascontiguousarray
    shared = {k: f(inputs[k]) for k in ('ada_w', 'ada_b', 'norm_g', 'ffn_w_in', 'ffn_w_out', 'ssd_w_in', 'ssd_conv_b',
                                        'ssd_dt_bias', 'ssd_a_log', 'ssd_d', 'ssd_norm_g', 'ssd_w_out')}
    shared['ssd_conv_w'] = f(inputs['ssd_conv_w'].reshape(2, 4 * 3072))
    shared['sb_w_qkv'] = f(inputs['sb_w_qkv'][0])
    shared['sb_w_out'] = f(inputs['sb_w_out'][0])
    shared['swa_w_qkv'] = f(inputs['swa_w_qkv'][0])
    shared['swa_w_out'] = f(inputs['swa_w_out'][0])
    shared['swa_sinks'] = f(inputs['swa_sinks'][0])
    for c in range(8):
        m = dict(shared)
        m['xp'] = f(inputs['x_prompt'][c % 4])
        m['xs'] = f(inputs['x_sample'][c])
        m['cvec'] = f(np.stack([inputs['c_prompt'][c % 4], inputs['c_sample'][c]]))
        m['state_ssm'] = f(inputs['state_ssm'][:, c].reshape(2, 2048, 128))
        m['state_conv'] = f(inputs['state_conv'][:, c])
        m['cache_sb_k'] = f(inputs['cache_sb_k'][0, c].reshape(PAST, D))
        m['cache_sb_v'] = f(inputs['cache_sb_v'][0, c].reshape(PAST, D))
        m['cache_swa_k'] = f(inputs['cache_swa_k'][0, c].reshape(128, 256))
        m['cache_swa_v'] = f(inputs['cache_swa_v'][0, c].reshape(128, 256))
        maps.append(m)
    return maps


def kernel(**inputs):
    cfg = {}
    b, nc = build_program(cfg)
    maps = make_in_maps(inputs)
    res = run_bass_kernel_spmd(nc, maps, core_ids=list(range(8)))
    r = res.results
    y_prompt = np.stack([r[c]['yp'] for c in range(4)])
    y_sample = np.stack([r[c]['ys'] for c in range(8)])
    ssm_p = np.stack([r[c]['ssm_p'].reshape(2, 32, 64, 128) for c in range(4)], axis=1)
    ssm_s = np.stack([r[c]['ssm_s'].reshape(2, 32, 64, 128) for c in range(8)], axis=1)
    conv_p = np.stack([r[c]['conv_p'] for c in range(4)], axis=1)
    conv_s = np.stack([r[c]['conv_s'] for c in range(8)], axis=1)
    sbk_p = np.stack([r[c]['sbk_p'].reshape(SEQ, 16, 64) for c in range(4)])[None]
    sbk_s = np.stack([r[c]['sbk_s'].reshape(NS, 16, 64) for c in range(8)])[None]
    sbv_p = np.stack([r[c]['sbv_p'].reshape(SEQ, 16, 64) for c in range(4)])[None]
    sbv_s = np.stack([r[c]['sbv_s'].reshape(NS, 16, 64) for c in range(8)])[None]
    sw = {}
    for nm, n in (('swak_p', 4), ('swak_s', 8), ('swav_p', 4), ('swav_s', 8)):
        sw[nm] = np.stack([r[c][nm].reshape(128, 4, 64) for c in range(n)])[None]
    return (y_prompt, y_sample, ssm_p, ssm_s, conv_p, conv_s, sbk_p, sbk_s, sbv_p, sbv_s,
            sw['swak_p'], sw['swak_s'], sw['swav_p'], sw['swav_s'])
```

```python
import numpy as np
from contextlib import ExitStack
import concourse.bass as bass
import concourse.mybir as mybir
from concourse.bass_utils import run_bass_kernel_spmd

F32 = mybir.dt.float32
BF16 = mybir.dt.bfloat16
AF = mybir.ActivationFunctionType
ALU = mybir.AluOpType

D = 1024
KC = 8
DFF = 2816
FC = 22
T = 512
SEQ = 8192
NTILE = SEQ // T
NS = 32
PAST = 1024
DEPTH = 4
EPS = 1e-6
NSLOT = 8
SLOTW = 2048


class Prog:
    def __init__(self, nc, same_sync=True):
        self.nc = nc
        self.q = {e: [] for e in ('pe', 'act', 'dve', 'pool', 'sp')}
        self.cnt = {e: 0 for e in self.q}
        self.seen = {e: {} for e in self.q}
        self.sems = {}
        self.dcnt = {}
        self.bufs = {}
        self.same_sync = same_sync

    def _waits(self, eng, r, w, deps):
        need = {}

        def add(ev):
            if ev is None:
                return
            s, v = ev
            if need.get(s, 0) < v:
                need[s] = v

        for k in r:
            b = self.bufs.get(k)
            if b:
                add(b[0])
                if isinstance(k, tuple) and k[0] == 'ps':
                    for s_, v_ in b[1].items():
                        if s_ != ('e', eng):
                            add((s_, v_))
        for k in w:
            b = self.bufs.get(k)
            if b:
                add(b[0])
                for s, v in b[1].items():
                    add((s, v))
        for d in deps:
            add(d)
        wl = []
        for s, v in need.items():
            if s == ('e', eng) and (eng == 'pe' or not self.same_sync):
                continue
            if self.seen[eng].get(s, 0) < v:
                wl.append((s, v))
                self.seen[eng][s] = v
        return wl

    def _mark(self, ev, r, w):
        for k in r:
            b = self.bufs.setdefault(k, [None, {}])
            if b[1].get(ev[0], 0) < ev[1]:
                b[1][ev[0]] = ev[1]
        for k in w:
            self.bufs[k] = [ev, {}]

    def op(self, eng, fn, r=(), w=(), deps=()):
        wl = self._waits(eng, r, w, deps)
        self.cnt[eng] += 1
        ev = (('e', eng), self.cnt[eng])
        self.q[eng].append((wl, fn, ev[0], 1))
        self._mark(ev, r, w)
        return ev

    def dma(self, eng, fn, r=(), w=(), deps=(), semkey=None, new_batch=True):
        sk = ('d', semkey)
        c = self.dcnt.get(sk, 0)
        deps = list(deps)
        if new_batch and c > 0:
            deps.append((sk, c))
        wl = self._waits(eng, r, w, deps)
        c += 16
        self.dcnt[sk] = c
        ev = (sk, c)
        self.q[eng].append((wl, fn, sk, 16))
        self._mark(ev, r, w)
        return ev

    def emit(self, es):
        nc = self.nc
        keys = [('e', e) for e in self.q] + list(self.dcnt.keys())
        for i, k in enumerate(keys):
            self.sems[k] = es.enter_context(nc.semaphore("sem%d" % i))
        block = es.enter_context(nc.Block())
        engmap = {'pe': block.tensor, 'act': block.scalar, 'dve': block.vector, 'pool': block.gpsimd,
                  'sp': block.sync}
        for e, dec in engmap.items():
            ops = self.q[e]

            def body(eng, ops=ops, e=e):
                for wl, fn, sk, inc in ops:
                    for s, v in wl:
                        eng.wait_ge(self.sems[s], v)
                    fn(eng).then_inc(self.sems[sk], inc)
                if e == 'sp':
                    for sk, c in self.dcnt.items():
                        eng.wait_ge(self.sems[sk], c)

            dec(body)


def _consts_np():
    cols = {}
    blocks = []
    off = 0

    def addc(name, arr):
        nonlocal off
        a = np.zeros((128, arr.shape[1]), np.float32)
        a[:arr.shape[0]] = arr
        cols[name] = (off, arr.shape[1])
        blocks.append(a)
        off += arr.shape[1]

    addc('ident', np.eye(128, dtype=np.float32))
    return np.concatenate(blocks, axis=1), cols


def _consts_bf_np():
    cols = {}
    blocks = []
    off = 0

    def addc(name, arr):
        nonlocal off
        a = np.zeros((128, arr.shape[1]), np.float32)
        a[:arr.shape[0]] = arr
        cols[name] = (off, arr.shape[1])
        blocks.append(a)
        off += arr.shape[1]

    s_ = np.arange(128)[:, None]
    t_ = np.arange(128)[None, :]
    k = np.arange(96)
    sel3 = np.zeros((96, 32, 128), np.float32)
    for h in range(32):
        sel3[k % 32 == h, h, :] = 1.0
    addc('sel3', sel3.reshape(96, 32 * 128))
    addc('negmask', np.where(s_ > t_, -30000.0, 0.0).astype(np.float32))
    addc('mask_le', (s_ <= t_).astype(np.float32))
    addc('i3', (k[:, None] % 32 == np.arange(32)[None, :]).astype(np.float32))
    addc('negtri', np.where(s_ > t_, -1.0, 0.0).astype(np.float32))
    addc('mask_lt', (s_ < t_).astype(np.float32))
    addc('negmask_lt', np.where(s_ >= t_, -30000.0, 0.0).astype(np.float32))
    nmswa = np.zeros((128, 256), np.float32)
    nmswa[0:64, 192:256] = -30000.0
    nmswa[64:128, 0:64] = -30000.0
    addc('nmswa', nmswa)
    addc('negones', -np.ones((128, 128), np.float32))
    addc('negtrile', np.where(s_ <= t_, -1.0, 0.0).astype(np.float32))
    addc('postri', np.where(s_ > t_, 1.0, 0.0).astype(np.float32))
    addc('negident', -np.eye(128, dtype=np.float32))
    addc('posmask_lt', np.where(s_ >= t_, 30000.0, 0.0).astype(np.float32))
    addc('zeros', np.zeros((128, 128), np.float32))
    return np.concatenate(blocks, axis=1), cols


class Builder:
    def __init__(self, cfg):
        self.cfg = cfg
        self.nc = bass.Bass("TRN2", target_bir_lowering=False)
        self.es = ExitStack()
        self.P = Prog(self.nc, same_sync=cfg.get('same_sync', True))
        self.psn = 0
        self.psrot = list(range(8))
        self.cbn = 0
        self.en = 0
        self.ringn = 0
        self.stgn = 0
        self.tmpn = {}
        self.wblock = {}

    def din(self, name, shape, dt=F32):
        return self.nc.dram_tensor(name, list(shape), dt, kind="ExternalInput").ap()

    def dout(self, name, shape, dt=F32):
        return self.nc.dram_tensor(name, list(shape), dt, kind="ExternalOutput").ap()

    def dint(self, name, shape, dt=BF16):
        return self.nc.dram_tensor(name, list(shape), dt, kind="Internal").ap()

    def sb(self, name, shape, dt):
        return self.es.enter_context(self.nc.sbuf_tensor(name, list(shape), dt))

    def ps(self):
        rot = self.psrot
        i = rot[self.psn % len(rot)]
        self.psn += 1
        return self.psum[i], ('ps', i)

    def mm(self, out, lhsT, rhs, start, stop, r, w, sgc=False):
        return self.P.op('pe', lambda e, o=out, l=lhsT, rr=rhs, s=start, t=stop, g=sgc:
                         e.matmul(o, lhsT=l, rhs=rr, start=s, stop=t, skip_group_check=g), r=r, w=w)

    def cbc(self, name, rows=128, c0=0, c1=None):
        off, w = self.cbcols[name]
        if c1 is None:
            c1 = w
        return self.cb[0:rows, off + c0:off + c1]

    def tr(self, out, in_, ident, r, w):
        return self.P.op('pe', lambda e, o=out, i=in_, d=ident: e.transpose(o, i, d), r=r, w=w)

    def act(self, out, in_, func, r, w, bias=None, scale=None):
        kw = {}
        if bias is not None:
            kw['bias'] = bias
        if scale is not None:
            kw['scale'] = scale
        return self.P.op('act', lambda e, o=out, i=in_, f=func, kw=kw: e.activation(o, i, f, **kw), r=r, w=w)

    def tt(self, eng, out, in0, in1, op, r, w):
        return self.P.op(eng, lambda e, o=out, a=in0, b=in1, p=op: e.tensor_tensor(o, a, b, p), r=r, w=w)

    def ts(self, eng, out, in0, s1, op0, r, w, s2=None, op1=None):
        if op1 is None:
            return self.P.op(eng, lambda e, o=out, a=in0, s=s1, p=op0: e.tensor_scalar(o, a, s, None, p), r=r, w=w)
        return self.P.op(eng, lambda e, o=out, a=in0, s=s1, p=op0, s2=s2, p1=op1: e.tensor_scalar(o, a, s, s2, p, p1),
                         r=r, w=w)

    def stt(self, out, in0, scalar, in1, op0, op1, r, w, eng='dve'):
        return self.P.op(eng, lambda e, o=out, a=in0, s=scalar, b=in1, p0=op0, p1=op1:
                         e.scalar_tensor_tensor(o, a, s, b, p0, p1), r=r, w=w)

    def cp(self, eng, out, in_, r, w):
        if eng == 'act':
            return self.P.op('act', lambda e, o=out, i=in_: e.copy(o, i), r=r, w=w)
        return self.P.op(eng, lambda e, o=out, i=in_: e.tensor_copy(o, i), r=r, w=w)

    def memset(self, eng, ap, val, w):
        return self.P.op(eng, lambda e, a=ap, v=val: e.memset(a, v), r=(), w=w)

    def dma(self, eng, out, in_, r, w, semkey, new_batch=True, slow=False, deps=()):
        kw = {}
        if slow:
            kw['allow_slow_non_contiguous'] = True
        return self.P.dma(eng, lambda e, o=out, i=in_, kw=kw: e.dma_start(out=o, in_=i, **kw), r=r, w=w,
                          semkey=(semkey, eng), new_batch=new_batch, deps=deps)

    def ring_load(self, src, kc, width, r=(), eng='sp'):
        s = self.ringn % NSLOT
        self.ringn += 1
        view = self.ring[s][:, 0:kc * width].rearrange("p (k w) -> p k w", k=kc)
        key = ('ring', s)
        nm = src.tensor.name
        if nm in self.wblock:
            r = [('wb', nm, src.offset // self.wblock[nm])]
        self.dma(eng, view, src, r=r, w=[key], semkey=key)
        return view, key

    def build(self):
        nc, cfg = self.nc, self.cfg
        xp = self.din("xp", [SEQ, D])
        xs = self.din("xs", [NS, D])
        cvec = self.din("cvec", [2, D])
        self.w_ada = self.din("ada_w", [DEPTH, D, 9 * D])
        ada_b = self.din("ada_b", [DEPTH, 9 * D])
        norm_g = self.din("norm_g", [DEPTH, 6, D])
        ffn_w_in = self.din("ffn_w_in", [DEPTH, 2, D, 2 * DFF])
        ffn_w_out = self.din("ffn_w_out", [DEPTH, 2, DFF, D])
        self.I = I = {}
        I['state_ssm'] = self.din("state_ssm", [2, 2048, 128])
        I['state_conv'] = self.din("state_conv", [2, 3, 3072])
        I['ssd_w_in'] = self.din("ssd_w_in", [2, D, 5152])
        I['ssd_conv_w'] = self.din("ssd_conv_w", [2, 4 * 3072])
        I['ssd_conv_b'] = self.din("ssd_conv_b", [2, 3072])
        I['ssd_dt_bias'] = self.din("ssd_dt_bias", [2, 32])
        I['ssd_a_log'] = self.din("ssd_a_log", [2, 32])
        I['ssd_d'] = self.din("ssd_d", [2, 32])
        I['ssd_norm_g'] = self.din("ssd_norm_g", [2, 2048])
        I['ssd_w_out'] = self.din("ssd_w_out", [2, 2048, D])
        I['cache_sb_k'] = self.din("cache_sb_k", [PAST, D])
        I['cache_sb_v'] = self.din("cache_sb_v", [PAST, D])
        I['sb_w_qkv'] = self.din("sb_w_qkv", [D, 3 * D])
        I['sb_w_out'] = self.din("sb_w_out", [D, D])
        I['cache_swa_k'] = self.din("cache_swa_k", [128, 256])
        I['cache_swa_v'] = self.din("cache_swa_v", [128, 256])
        I['swa_w_qkv'] = self.din("swa_w_qkv", [D, 1536])
        I['swa_sinks'] = self.din("swa_sinks", [16])
        I['swa_w_out'] = self.din("swa_w_out", [D, D])
        yp = self.dout("yp", [SEQ, D])
        ys = self.dout("ys", [NS, D])
        self.O = O = {}
        for nm in ('swak_p', 'swak_s', 'swav_p', 'swav_s'):
            O[nm] = self.dout(nm, [128, 256])
        O['sbk_p'] = self.dout("sbk_p", [SEQ, D])
        O['sbk_s'] = self.dout("sbk_s", [NS, D])
        O['sbv_p'] = self.dout("sbv_p", [SEQ, D])
        O['sbv_s'] = self.dout("sbv_s", [NS, D])
        O['ssm_p'] = self.dout("ssm_p", [2, 2048, 128])
        O['ssm_s'] = self.dout("ssm_s", [2, 2048, 128])
        O['conv_p'] = self.dout("conv_p", [2, 3, 3072])
        O['conv_s'] = self.dout("conv_s", [2, 3, 3072])
        cnp, ccols = _consts_np()
        cdram = nc.inline_tensor(cnp, "consts").ap()
        cbnp, cbcols = _consts_bf_np()
        cbdram = nc.inline_tensor(cbnp, "constsb").ap()
        self.wb_ffn_in = self.dint("wb_ffn_in", [DEPTH, 2, D, 2 * DFF])
        self.wb_ffn_out = self.dint("wb_ffn_out", [DEPTH, 2, DFF, D])
        self.wb_ssd_in = self.dint("wb_ssd_in", [2, D, 5152])
        self.wb_ssd_out = self.dint("wb_ssd_out", [2, 2048, D])
        self.Sd = self.dint("Sd", [2, 128, 2048], F32)
        self.wb_sb_qkv = self.dint("wb_sb_qkv", [D, 3 * D])
        self.wb_sb_out = self.dint("wb_sb_out", [D, D])
        self.wb_swa_qkv = self.dint("wb_swa_qkv", [D, 1536])
        self.wb_swa_out = self.dint("wb_swa_out", [D, D])
        self.KTs = self.dint("KTs", [8, 128, SEQ])
        self.Vs = self.dint("Vs", [8, 128, SEQ // 128, 128])
        self.KTs_s = self.dint("KTs_s", [8, 128, PAST])
        self.Vs_s = self.dint("Vs_s", [8, 128, PAST // 128, 128])

        self.psum = [self.es.enter_context(nc.psum_tensor("ps%d" % i, [128, 512], F32)) for i in range(8)]
        self.ring = [self.sb("ring%d" % i, [128, SLOTW], BF16) for i in range(NSLOT)]
        self.X = self.sb("X", [128, KC * T], F32)
        self.Hb = self.sb("Hb", [128, KC * T], BF16)
        self.G = self.sb("G", [128, FC * T], BF16)
        self.Yf = self.sb("Yf", [128, KC * T], F32)
        self.SQ = self.sb("SQ", [128, KC * T], BF16)
        self.stg = [self.sb("stg%d" % i, [128, D], F32) for i in range(2)]
        self.tmpf = [self.sb("tmpf%d" % i, [128, T], F32) for i in range(4)]
        self.rstd = self.sb("rstd", [128, T], F32)
        self.cf = self.sb("cf", [128, cnp.shape[1]], F32)
        self.cb = self.sb("cb", [128, cbnp.shape[1]], BF16)
        self.identb = self.sb("identb", [128, 128], BF16)
        self.onesb = self.sb("onesb", [128, 128], BF16)
        self.cact = self.sb("cact", [128, 16], BF16)
        self.ccol = self.sb("ccol", [128, 16], F32)
        self.adab = self.sb("adab", [128, DEPTH * 72], F32)
        self.ng = self.sb("ng", [128, DEPTH * 48], F32)
        self.mod = self.sb("mod", [128, DEPTH * 144], F32)
        self.der = self.sb("der", [128, DEPTH * 3 * 3 * 2 * 8], F32)
        self.AR8 = self.sb("AR8", [128, 2048], F32)
        self.ccols = ccols
        self.cbcols = cbcols
        self.identf = self.cf[:, ccols['ident'][0]:ccols['ident'][0] + 128]
        self.ssd_alloc()
        self.sb_alloc()
        self.KTprev = self.sb("KTprev", [128, 512], BF16)
        self.Vprev = self.sb("Vprev", [128, 256], BF16)
        self.esink = self.sb("esink", [128, 8], F32)

        self.dma('pool', self.cf[:], cdram, r=[], w=['cf'], semkey='cf')
        self.dma('pool', self.cb[:], cbdram, r=[], w=['cb'], semkey='cb')
        self.cp('dve', self.identb[:], self.identf, r=['cf'], w=['identb'])
        self.memset('dve', self.onesb[:], 1.0, w=['onesb'])
        self.load_cols(self.ccol[:, 0:8], cvec[0, :], 'ccol')
        self.load_cols(self.ccol[:, 8:16], cvec[1, :], 'ccol')
        for i in range(DEPTH):
            for h in range(2):
                self.load_cols(self.adab[:, i * 72 + h * 36:i * 72 + h * 36 + 36], ada_b[i, h * 4608:(h + 1) * 4608], 'adab')
            self.load_cols(self.ng[:, i * 48:(i + 1) * 48], norm_g[i].rearrange("a d -> (a d)"), 'ng')
        self.ssd_params()
        for hh in range(2):
            self.dma('pool', self.esink[hh * 64:(hh + 1) * 64, :], bass.AP(I['swa_sinks'].tensor, hh, [[0, 64], [2, 8]]),
                     r=[], w=['esink'], semkey='esink', slow=True, new_batch=False)
        self.act(self.esink[:], self.esink[:], AF.Exp, r=['esink'], w=['esink'])
        self.prepass_mod()
        self.prepass_weights(ffn_w_in, ffn_w_out)

        ntile = cfg.get('ntile', NTILE)
        tiles = [('p', t) for t in range(ntile)]
        if cfg.get('sample', True):
            tiles.append(('s', 0))
        stop = cfg.get('stop')
        for kind, t in tiles:
            N = T if kind == 'p' else NS
            q = 0 if kind == 'p' else 1
            src = xp[t * T:(t + 1) * T, :] if kind == 'p' else xs
            dst = yp[t * T:(t + 1) * T, :] if kind == 'p' else ys
            last = (kind == 's') or (t == ntile - 1)
            self.load_fm(src, N)
            for i in range(cfg.get('depth', DEPTH)):
                self.ffn(i, 0, q, N)
                if stop == 'x%d_0' % i:
                    break
                if i % 3 == 0:
                    self.ssd(i, q, N, t, last)
                elif i % 3 == 1:
                    if kind == 's':
                        self.sb_prep_sample()
                    self.sbmix(i, q, N, t, last)
                else:
                    self.swamix(i, q, N, t, last)
                if stop == 'x%d_1' % i:
                    break
                self.ffn(i, 1, q, N)
            self.store_fm(dst, N)
        print('sbuf bytes remaining', self.nc.sbuf_bytes_remaining() if callable(getattr(self.nc, 'sbuf_bytes_remaining', None)) else getattr(self.nc, 'sbuf_bytes_remaining', None))
        self.P.emit(self.es)
        return nc

    def load_cols(self, dst, src_vec, key):
        self.dma('pool', dst, src_vec.rearrange("(n p) -> p n", p=128), r=[], w=[key], semkey=key, slow=True,
                 new_batch=False)

    def xv(self, tile, N, nch=KC):
        return tile[:, 0:nch * N].rearrange("p (k n) -> p k n", k=nch)

    def prepass_mod(self):
        cact_v = self.cact[:].rearrange("p (k two) -> p two k", two=2)
        for q in range(2):
            self.act(cact_v[:, q, :], self.ccol[:, q * 8:(q + 1) * 8], AF.Silu, r=['ccol'], w=['cact'])
        for i in range(DEPTH):
            bank, pk = self.ps()
            for t36 in range(36):
                src = self.w_ada[i, :, t36 * 256:(t36 + 1) * 256].rearrange("(k p) w -> p k w", p=128)
                wv, wk = self.ring_load(src, KC, 256, eng='pool')
                for c in range(2):
                    oc = t36 * 2 + c
                    for k in range(KC):
                        self.mm(bank[:, oc * 2:oc * 2 + 2], wv[:, k, c * 128:(c + 1) * 128], self.cact[:, k * 2:k * 2 + 2],
                                k == 0, k == KC - 1, r=[wk, 'cact'], w=[pk])
            mv = self.mod[:, i * 144:(i + 1) * 144].rearrange("p (o two) -> p o two", two=2)
            bv = bank[:, 0:144].rearrange("p (o two) -> p o two", two=2)
            ab = self.adab[:, i * 72:(i + 1) * 72].unsqueeze(2).broadcast_to([128, 72, 2])
            self.tt('dve', mv, bv, ab, ALU.add, r=[pk, 'adab'], w=['mod'])
            for s in range(3):
                for q in range(2):
                    def m(j):
                        return self.mod[:, i * 144:(i + 1) * 144].rearrange("p (j k two) -> p j two k", j=9, two=2)[:, j, q, :]
                    gpre = self.ng[:, i * 48 + (2 * s) * 8:i * 48 + (2 * s) * 8 + 8]
                    gpost = self.ng[:, i * 48 + (2 * s + 1) * 8:i * 48 + (2 * s + 1) * 8 + 8]
                    self.stt(self.dslice(i, s, 0, q), m(3 * s + 1), 1.0, gpre, ALU.add, ALU.mult, r=['mod', 'ng'], w=['der'])
                    self.cp('dve', self.dslice(i, s, 1, q), m(3 * s + 0), r=['mod'], w=['der'])
                    self.stt(self.dslice(i, s, 2, q), m(3 * s + 2), 0.5 if s != 1 else 1.0, gpost, ALU.mult, ALU.mult,
                             r=['mod', 'ng'], w=['der'])

    def dslice(self, i, s, which, q, kc=None):
        base = (((i * 3 + s) * 3 + which) * 2 + q) * 8
        if kc is None:
            return self.der[:, base:base + 8]
        return self.der[:, base + kc:base + kc + 1]

    def prepass_weights(self, ffn_w_in, ffn_w_out):
        I = self.I
        self.wblock = {'wb_ffn_in': D * 2 * DFF, 'wb_ffn_out': DFF * D, 'wb_ssd_in': D * 5152, 'wb_ssd_out': 2048 * D,
                       'wb_sb_qkv': D * 3 * D, 'wb_sb_out': D * D, 'wb_swa_qkv': D * 1536, 'wb_swa_out': D * D}

        def cast(dst, src, nm, idx, split=None):
            if split:
                dst = dst.rearrange("k (a b) -> k a b", a=split)
                src = src.rearrange("k (a b) -> k a b", a=split)
            self.dma('pool', dst, src, r=[], w=[('wb', nm, idx)], semkey=('wcast', nm, idx))

        def ffn(i, s_):
            cast(self.wb_ffn_in[i, s_], ffn_w_in[i, s_], 'wb_ffn_in', i * 2 + s_, 4)
            cast(self.wb_ffn_out[i, s_], ffn_w_out[i, s_], 'wb_ffn_out', i * 2 + s_)

        for i in range(DEPTH):
            ffn(i, 0)
            if i % 3 == 0:
                j = i // 3
                cast(self.wb_ssd_in[j], I['ssd_w_in'][j], 'wb_ssd_in', j, 4)
                cast(self.wb_ssd_out[j], I['ssd_w_out'][j], 'wb_ssd_out', j)
            elif i % 3 == 1:
                cast(self.wb_sb_qkv, I['sb_w_qkv'], 'wb_sb_qkv', 0, 2)
                cast(self.wb_sb_out, I['sb_w_out'], 'wb_sb_out', 0)
            else:
                cast(self.wb_swa_qkv, I['swa_w_qkv'], 'wb_swa_qkv', 0)
                cast(self.wb_swa_out, I['swa_w_out'], 'wb_swa_out', 0)
            ffn(i, 1)

    def load_fm(self, src, N):
        nb = (N + 127) // 128
        for tb in range(nb):
            n = min(128, N - tb * 128)
            st = self.stg[self.stgn % 2]
            sk = ('stg', self.stgn % 2)
            self.stgn += 1
            self.dma('pool', st[0:n, :], src[tb * 128:tb * 128 + n, :], r=[], w=[sk], semkey=sk)
            for half in range(2):
                bank, pk = self.ps()
                for k4 in range(4):
                    k = half * 4 + k4
                    self.tr(bank[:, k4 * 128:k4 * 128 + n], st[0:n, k * 128:(k + 1) * 128], self.identf[0:n, 0:n],
                            r=[sk, 'cf'], w=[pk])
                xo = self.xv(self.X, N)[:, half * 4:half * 4 + 4, tb * 128:tb * 128 + n]
                pv = bank[:, 0:512].rearrange("p (k n) -> p k n", k=4)[:, :, 0:n]
                self.cp('dve' if half == 0 else 'act', xo, pv, r=[pk], w=[('X', half * 4 + j) for j in range(4)])

    def store_fm(self, dst, N, src=None, skey='X', nchunk=KC, vtm=None, vkeys=None, only_last=False, kofs=0):
        if src is None:
            src = self.xv(self.X, N)
        nb = (N + 127) // 128
        W = nchunk * 128
        for tb in range(nb):
            if only_last and tb != nb - 1:
                continue
            n = min(128, N - tb * 128)
            st = self.stg[self.stgn % 2]
            sk = ('stg', self.stgn % 2)
            self.stgn += 1
            for half in range((nchunk + 3) // 4):
                bank, pk = self.ps()
                nk4 = min(4, nchunk - half * 4)
                for k4 in range(nk4):
                    k = half * 4 + k4
                    self.tr(bank[0:n, k4 * 128:(k4 + 1) * 128], src[:, k, tb * 128:tb * 128 + n],
                            self.identf, r=[(skey, kofs + k), 'cf'], w=[pk])
                self.cp('dve' if half == 0 else 'act', st[0:n, half * 512:half * 512 + nk4 * 128], bank[0:n, 0:nk4 * 128],
                        r=[pk], w=[sk])
            if dst is not None:
                d = dst[0:n, :] if only_last else dst[tb * 128:tb * 128 + n, :]
                self.dma('pool', d, st[0:n, 0:W], r=[sk], w=[], semkey=sk)
            if vtm is not None:
                self.cp('pool', vtm(tb, n), st[0:n, 0:W], r=[sk], w=vkeys(tb))

    def sumsq_rstd(self, sq_view, sq_keys, N, nch, dim):
        bank, pk = self.ps()
        for k in range(nch):
            self.mm(bank[:, 0:N], self.onesb[:], sq_view(k), k == 0, k == nch - 1, r=['onesb', sq_keys[k]], w=[pk])
        t = self.tmpf[3]
        self.act(t[:, 0:N], bank[:, 0:N], AF.Sqrt, r=[pk, 'epsb'], w=['tmp3'], bias=self.epsb[:, 0:1], scale=1.0 / dim)
        self.P.op('dve', lambda e, o=self.rstd[:, 0:N], i=t[:, 0:N]: e.reciprocal(o, i), r=['tmp3'], w=['rstd'])

    def prenorm(self, i, s, q, N):
        Xv = self.xv(self.X, N)
        SQv = self.xv(self.SQ, N)
        for k in range(KC):
            self.act(SQv[:, k, :], Xv[:, k, :], AF.Square, r=[('X', k)], w=[('SQ', k)])
        self.sumsq_rstd(lambda k: SQv[:, k, :], [('SQ', k) for k in range(KC)], N, KC, D)
        Hv = self.xv(self.Hb, N)
        for k in range(KC):
            tn = k % 3
            t = self.tmpf[tn]
            self.stt(t[:, 0:N], Xv[:, k, :], self.dslice(i, s, 0, q, k), self.rstd[:, 0:N], ALU.mult, ALU.mult,
                     r=[('X', k), 'der', 'rstd'], w=[('tmp', tn)])
            self.act(Hv[:, k, :], t[:, 0:N], AF.Identity, r=[('tmp', tn), 'der'], w=[('Hb', k)],
                     bias=self.dslice(i, s, 1, q, k))

    def postnorm_resid(self, i, s, q, N):
        Xv = self.xv(self.X, N)
        SQv = self.xv(self.SQ, N)
        Yv = self.xv(self.Yf, N)
        self.sumsq_rstd(lambda k: SQv[:, k, :], [('SQ', k) for k in range(KC)], N, KC, D)
        for k in range(KC):
            tn = k % 3
            t = self.tmpf[tn]
            self.tt('dve', t[:, 0:N], Yv[:, k, :], self.rstd[:, 0:N], ALU.mult, r=[('Yf', k), 'rstd'], w=[('tmp', tn)])
            self.tt('pool', Xv[:, k, :], Xv[:, k, :], t[:, 0:N], ALU.add, r=[('tmp', tn), ('X', k)], w=[('X', k)])

    def evac_y(self, i, s, q, N, k, bank, pk):
        Yv = self.xv(self.Yf, N)
        SQv = self.xv(self.SQ, N)
        self.act(Yv[:, k, :], bank[:, 0:N], AF.Identity, r=[pk, 'der'], w=[('Yf', k)], scale=self.dslice(i, s, 2, q, k))
        self.act(SQv[:, k, :], bank[:, 0:N], AF.Square, r=[pk], w=[('SQ', k)])

    def ffn(self, i, which, q, N):
        s = 0 if which == 0 else 2
        self.prenorm(i, s, q, N)
        Hv = self.xv(self.Hb, N)
        Gv = self.xv(self.G, N, FC)
        win = self.wb_ffn_in[i, which]
        wout = self.wb_ffn_out[i, which]
        for jp in range(FC // 2):
            wa, ka = self.ring_load(win[:, jp * 256:(jp + 1) * 256].rearrange("(k p) w -> p k w", p=128), KC, 256, r=['wb'])
            wb_, kb = self.ring_load(win[:, DFF + jp * 256:DFF + (jp + 1) * 256].rearrange("(k p) w -> p k w", p=128), KC,
                                     256, r=['wb'])
            pa = []
            for c in range(2):
                bank, pk = self.ps()
                for k in range(KC):
                    self.mm(bank[:, 0:N], wa[:, k, c * 128:(c + 1) * 128], Hv[:, k, :], k == 0, k == KC - 1,
                            r=[ka, ('Hb', k)], w=[pk])
                pa.append((bank, pk))
            for c in range(2):
                bank, pk = self.ps()
                for k in range(KC):
                    self.mm(bank[:, 0:N], wb_[:, k, c * 128:(c + 1) * 128], Hv[:, k, :], k == 0, k == KC - 1,
                            r=[kb, ('Hb', k)], w=[pk])
                tn = self.tmpn.get('ffn', 0)
                self.tmpn['ffn'] = tn + 1
                t = self.tmpf[tn % 3]
                tk = ('tmp', tn % 3)
                self.act(t[:, 0:N], pa[c][0][:, 0:N], AF.Silu, r=[pa[c][1]], w=[tk])
                j = jp * 2 + c
                self.tt('dve', Gv[:, j, :], t[:, 0:N], bank[:, 0:N], ALU.mult, r=[tk, pk], w=[('G', j)])
        HF = FC // 2
        for oc in range(KC):
            bank, pk = self.ps()
            for hf in range(2):
                wv, wk = self.ring_load(wout[hf * HF * 128:(hf + 1) * HF * 128, oc * 128:(oc + 1) * 128].rearrange(
                    "(k p) w -> p k w", p=128), HF, 128, r=['wb'])
                for k in range(HF):
                    kk = hf * HF + k
                    self.mm(bank[:, 0:N], wv[:, k, :], Gv[:, kk, :], kk == 0, kk == FC - 1, r=[wk, ('G', kk)], w=[pk])
            self.evac_y(i, s, q, N, oc, bank, pk)
        self.postnorm_resid(i, s, q, N)


    def ssd_alloc(self):
        sb = self.sb
        self.hist = sb("hist", [128, 2 * 72], F32)
        self.cwcol = sb("cwcol", [128, 2 * 96], F32)
        self.cbcol = sb("cbcol", [128, 2 * 24], F32)
        self.sngcol = sb("sngcol", [128, 2 * 16], F32)
        self.Dcol = sb("Dcol", [128, 2 * 16], F32)
        self.dtb3 = sb("dtb3", [96, 2], F32)
        self.A3 = sb("A3", [96, 2], F32)
        self.ARX = sb("ARX", [128, 8 * 512], BF16)
        self.a3parts = [self.ARX[0:96, (4 + i) * 512:(5 + i) * 512] for i in range(3)]
        self.a3 = sb("a3", [96, T], BF16)
        self.cols = sb("cols", [128, 4 * 96], F32)
        self.cdrep = sb("cdrep", [128, 4 * 32], F32)
        self.dg = sb("dg", [96, 32], BF16)
        self.ones32 = sb("ones32", [96, 128], F32)
        self.cbuf = [sb("cbuf%d" % i, [128, T + 3], F32) for i in range(2)]
        self.xsfm = [sb("xsfm%d" % i, [128, T], BF16) for i in range(4)]
        self.xtm = [self.ARX[:, i * 512:(i + 1) * 512] for i in range(4)]
        self.xw = [self.ARX[:, (4 + i) * 512:(5 + i) * 512] for i in range(4)]
        self.zs = [sb("zs%d" % i, [128, T], BF16) for i in range(4)]
        self.Ea = [sb("Ea%d" % i, [128, 128], F32) for i in range(2)]
        self.E = [sb("E%d" % i, [128, 128], F32) for i in range(2)]
        self.Cp = [sb("Cp%d" % i, [128, 128], BF16) for i in range(2)]
        self.Mm = [sb("Mm%d" % i, [128, 128], BF16) for i in range(2)]
        self.Sbf = sb("Sbf", [128, 2048], BF16)
        self.sqy = [sb("sqy%d" % i, [128, 128], BF16) for i in range(2)]

    def ssd_params(self):
        I = self.I
        for j in range(2):
            for h in range(2):
                self.load_cols(self.cwcol[:, j * 96 + h * 48:j * 96 + h * 48 + 48], I['ssd_conv_w'][j, h * 6144:(h + 1) * 6144], 'cwcol')
            self.load_cols(self.cbcol[:, j * 24:(j + 1) * 24], I['ssd_conv_b'][j], 'cbcol')
            self.load_cols(self.sngcol[:, j * 16:(j + 1) * 16], I['ssd_norm_g'][j], 'sngcol')
            for hh in range(2):
                src = bass.AP(I['ssd_d'].tensor, j * 32 + hh, [[0, 64], [2, 16]])
                self.dma('pool', self.Dcol[hh * 64:(hh + 1) * 64, j * 16:(j + 1) * 16], src, r=[], w=['Dcol'], semkey='Dcol',
                         slow=True, new_batch=False)
            for g in range(3):
                self.dma('pool', self.dtb3[g * 32:(g + 1) * 32, j:j + 1], bass.AP(I['ssd_dt_bias'].tensor, j * 32, [[1, 32], [1, 1]]),
                         r=[], w=['dtb3'], semkey='dtb3', slow=True, new_batch=False)
                self.dma('pool', self.A3[g * 32:(g + 1) * 32, j:j + 1], bass.AP(I['ssd_a_log'].tensor, j * 32, [[1, 32], [1, 1]]),
                         r=[], w=['A3'], semkey='A3', slow=True, new_batch=False)
        self.act(self.A3[:], self.A3[:], AF.Exp, r=['A3'], w=['A3'])
        self.ts('dve', self.A3[:], self.A3[:], -1.0, ALU.mult, r=['A3'], w=['A3'])
        self.memset('dve', self.ones32[:], 1.0, w=['ones32'])

    def conv_silu(self, j, cc, bank, pk, N, out_ap, out_keys):
        n = self.cbn % 2
        self.cbn += 1
        cb, ck = self.cbuf[n], ('cbuf', n)
        hs = self.hist[:, j * 72 + cc * 3:j * 72 + cc * 3 + 3]
        hk = ('hist', j, cc)
        self.cp('pool', cb[:, 0:3], hs, r=[hk], w=[ck])
        self.cp('act', cb[:, 3:3 + N], bank[:, 0:N], r=[pk], w=[ck])
        self.cp('pool', hs, cb[:, N:N + 3], r=[ck], w=[hk])
        tn = self.tmpn.get('conv', 0)
        self.tmpn['conv'] = tn + 1
        acc, ak = self.tmpf[tn % 2][:, 0:N], ('tmp', tn % 2)

        def wc(tap):
            c0 = j * 96 + tap * 24 + cc
            return self.cwcol[:, c0:c0 + 1]
        self.ts('dve', acc, cb[:, 0:N], wc(0), ALU.mult, r=[ck, 'cwcol', 'cbcol'], w=[ak],
                s2=self.cbcol[:, j * 24 + cc:j * 24 + cc + 1], op1=ALU.add)
        for tap in range(1, 4):
            self.stt(acc, cb[:, tap:tap + N], wc(tap), acc, ALU.mult, ALU.add, r=[ck, 'cwcol', ak], w=[ak])
        self.act(out_ap, acc, AF.Silu, r=[ak], w=out_keys)

    def ssd(self, i, q, N, t, last):
        j = i // 3
        Q = 128 if q == 0 else 32
        nch = N // Q
        I, O = self.I, self.O
        win = self.wb_ssd_in[j]
        wout = self.wb_ssd_out[j]
        self.psrot = list(range(7))
        self.prenorm(i, 1, q, N)
        Hv = self.xv(self.Hb, N)
        S = self.AR8
        Skeys = [('S', x) for x in range(16)]
        Sbkeys = [('Sbf', x) for x in range(16)]
        hkeys = [('hist', j, cc) for cc in range(24)]
        histv = self.hist[:, j * 72:(j + 1) * 72].rearrange("p (c t) -> p c t", t=3)
        if q == 0 and t == 0:
            self.memset('pool', S[:], 0.0, w=Skeys)
            self.memset('pool', self.hist[:, j * 72:(j + 1) * 72], 0.0, w=hkeys)
        elif q == 0:
            self.dma('pool', S[:], self.Sd[j], r=[('Sd', j)], w=Skeys, semkey='Sld')
        else:
            for g4 in range(4):
                st, sk = self.stg[self.stgn % 2], ('stg', self.stgn % 2)
                self.stgn += 1
                self.dma('pool', st[:, 0:512].rearrange("p (b n) -> p b n", b=4),
                         I['state_ssm'][j, g4 * 512:(g4 + 1) * 512, :].rearrange("(b p) n -> p b n", p=128), r=[], w=[sk], semkey=sk)
                bank, pk = self.ps()
                for b4 in range(4):
                    self.tr(bank[:, b4 * 128:(b4 + 1) * 128], st[:, b4 * 128:(b4 + 1) * 128], self.identf, r=[sk, 'cf'], w=[pk])
                self.cp('dve', S[:, g4 * 512:(g4 + 1) * 512], bank[:, 0:512], r=[pk], w=Skeys[g4 * 4:g4 * 4 + 4])
            for tt_ in range(3):
                self.dma('pool', histv[:, :, tt_], I['state_conv'][j, tt_].rearrange("(c p) -> p c", p=128), r=[], w=hkeys,
                         semkey='histld', slow=True, new_batch=(tt_ == 0))
        self.cp('pool', self.Sbf[:], S[:], r=Skeys, w=Sbkeys)

        wv, wk = self.ring_load(win[:, 5120:5152].rearrange("(k p) w -> p k w", p=128), KC, 32, r=['wb'])
        bank, pk = self.ps()
        for g in range(3):
            for k in range(KC):
                self.mm(bank[g * 32:(g + 1) * 32, 0:N], wv[:, k, :], Hv[:, k, :], k == 0, k == KC - 1, r=[wk, ('Hb', k)], w=[pk])
        SQf = self.SQ[:].bitcast(F32)
        dtt, dA, acum, wst = [SQf[0:96, x * 512:x * 512 + N] for x in range(4)]
        kdt, kdA, kac, kw = [[('SQ', 2 * x), ('SQ', 2 * x + 1)] for x in range(4)]
        self.act(dtt, bank[0:96, 0:N], AF.Exp, r=[pk, 'dtb3'], w=kdt, bias=self.dtb3[:, j:j + 1])
        self.act(dtt, dtt, AF.Ln, r=kdt + ['oneb'], w=kdt, bias=self.oneb[0:96, 0:1])
        self.ts('dve', dA, dtt, self.A3[:, j:j + 1], ALU.mult, r=kdt + ['A3'], w=kdA)
        for c in range(nch):
            self.P.op('dve', lambda e, o=acum[:, c * Q:(c + 1) * Q], d0=self.ones32[:, 0:Q], d1=dA[:, c * Q:(c + 1) * Q]:
                      e.tensor_tensor_scan(o, d0, d1, 0.0, ALU.mult, ALU.add), r=kdA + ['ones32'], w=kac)
        H3, M3, L3 = [x[:, 0:N] for x in self.a3parts]
        kH3, kM3, kL3 = ('ARX', 4), ('ARX', 5), ('ARX', 6)
        r1, r2, nac = [self.tmpf[x][0:96, 0:N] for x in range(3)]
        self.cp('dve', H3, acum, r=kac, w=[kH3])
        self.tt('dve', r1, acum, H3, ALU.subtract, r=kac + [kH3], w=[('tmp', 0)])
        self.cp('dve', M3, r1, r=[('tmp', 0)], w=[kM3])
        self.tt('dve', r2, r1, M3, ALU.subtract, r=[('tmp', 0), kM3], w=[('tmp', 1)])
        self.cp('dve', L3, r2, r=[('tmp', 1)], w=[kL3])
        a3 = self.a3
        self.cp('pool', a3[0:32, 0:N], H3[0:32], r=[kH3], w=['a3'])
        self.cp('pool', a3[32:64, 0:N], M3[32:64], r=[kM3], w=['a3'])
        self.cp('pool', a3[64:96, 0:N], L3[64:96], r=[kL3], w=['a3'])
        for c in range(nch):
            self.act(wst[:, c * Q:(c + 1) * Q], acum[:, c * Q:(c + 1) * Q], AF.Exp, r=kac, w=kw, scale=-1.0,
                     bias=acum[:, (c + 1) * Q - 1:(c + 1) * Q])
        self.tt('dve', wst, wst, dtt, ALU.mult, r=kw + kdt, w=kw)
        self.ts('dve', nac, acum, -1.0, ALU.mult, r=kac, w=[('tmp', 2)])
        for c in range(nch):
            bank, pk = self.ps()
            for x, (src, sk_) in enumerate(((nac, [('tmp', 2)]), (dtt, kdt), (wst, kw))):
                self.tr(bank[0:Q, x * 32:(x + 1) * 32], src[0:32, c * Q:(c + 1) * Q], self.identf[0:32, 0:32], r=sk_ + ['cf'], w=[pk])
            self.cp('dve', self.cols[0:Q, c * 96:(c + 1) * 96], bank[0:Q, 0:96], r=[pk], w=['cols'])
        bank, pk = self.ps()
        for c in range(nch):
            self.ts('dve', self.dg[:], self.cbc('i3', 96), a3[:, (c + 1) * Q - 1:(c + 1) * Q], ALU.mult, r=['cb', 'a3'], w=['dg'])
            self.mm(bank[:, c * 32:(c + 1) * 32], self.onesb[0:96, :], self.dg[:], True, True, r=['onesb', 'dg'], w=[pk])
        self.act(self.cdrep[:, 0:nch * 32], bank[:, 0:nch * 32], AF.Exp, r=[pk], w=['cdrep'])

        Yfb = self.Yf[:].bitcast(BF16)
        Bfm = Yfb[:, 0:2048].rearrange("p (g n) -> p g n", g=4)
        Cfm = Yfb[:, 2048:4096].rearrange("p (g n) -> p g n", g=4)
        Btm = Yfb[:, 4096:6144].rearrange("p (c n) -> p c n", c=4)
        cbm = Yfb[:, 6144:8192].rearrange("p (x n) -> p x n", x=16)
        kB, kC, kBt, kcb = [[('Yf', 2 * x), ('Yf', 2 * x + 1)] for x in range(4)]
        for g8 in range(8):
            col0 = 4096 + g8 * 128
            wv, wk = self.ring_load(win[:, col0:col0 + 128].rearrange("(k p) w -> p k w", p=128), KC, 128, r=['wb'])
            bank, pk = self.ps()
            for k in range(KC):
                self.mm(bank[:, 0:N], wv[:, k, :], Hv[:, k, :], k == 0, k == KC - 1, r=[wk, ('Hb', k)], w=[pk])
            dest = (Bfm if g8 < 4 else Cfm)[:, g8 % 4, 0:N]
            self.conv_silu(j, 16 + g8, bank, pk, N, dest, kB if g8 < 4 else kC)
        for c in range(nch):
            bank, pk = self.ps()
            bb = bank[:].bitcast(BF16)
            for g in range(4):
                self.tr(bb[0:Q, g * 128:(g + 1) * 128], Bfm[:, g, c * Q:(c + 1) * Q], self.identb[:], r=kB + ['identb'], w=[pk])
            self.cp('act', Btm[0:Q, c, :], bb[0:Q, 0:512], r=[pk], w=kBt)
        mle = self.cbc('mask_le', Q, 0, Q)
        for g in range(4):
            bank, pk = self.ps()
            for c in range(nch):
                self.mm(bank[0:Q, c * 128:c * 128 + Q], Bfm[:, g, c * Q:(c + 1) * Q], Cfm[:, g, c * Q:(c + 1) * Q], True, True,
                        r=kB + kC, w=[pk])
            pv = bank[0:Q, 0:nch * 128].rearrange("p (c n) -> p c n", c=nch)[:, :, 0:Q]
            self.tt('dve', cbm[0:Q, g * 4:g * 4 + nch, 0:Q], pv, mle.unsqueeze(1).broadcast_to([Q, nch, Q]), ALU.mult,
                    r=[pk, 'cb'], w=kcb)

        Gv = self.xv(self.G, N, FC)
        ssb, ssk = self.psum[7], ('ps', 7)
        RB = [(self.psum[x], ('ps', x)) for x in (0, 1)]
        YB = [(self.psum[x], ('ps', x)) for x in (2, 3)]
        SNB = [(self.psum[x], ('ps', x)) for x in (4, 5)]
        sel3 = self.cbc('sel3', 96)
        negmask = self.cbc('negmask', Q, 0, Q)
        for g in range(4):
            for pi in range(4):
                jj = 4 * g + pi
                xs_, xk = self.xsfm[pi], ('xsfm', pi)
                wv, wk = self.ring_load(win[:, 2048 + jj * 128:2048 + (jj + 1) * 128].rearrange("(k p) w -> p k w", p=128), KC, 128, r=['wb'])
                bank, pk = self.ps()
                for k in range(KC):
                    self.mm(bank[:, 0:N], wv[:, k, :], Hv[:, k, :], k == 0, k == KC - 1, r=[wk, ('Hb', k)], w=[pk])
                self.conv_silu(j, jj, bank, pk, N, xs_[:, 0:N], [xk])
                bank, pk = self.ps()
                bb = bank[:].bitcast(BF16)
                for c in range(nch):
                    self.tr(bb[0:Q, c * 128:(c + 1) * 128], xs_[:, c * Q:(c + 1) * Q], self.identb[:], r=[xk, 'identb'], w=[pk])
                xt, xtk = self.xtm[pi], ('ARX', pi)
                self.cp('act', xt[0:Q, 0:nch * 128], bb[0:Q, 0:nch * 128], r=[pk], w=[xtk])
                xwt, xwk = self.xw[pi], ('ARX', 4 + pi)
                wcv = self.cols[0:Q, 0:nch * 96].rearrange("p (c x) -> p c x", c=nch)[:, :, 64 + 2 * jj:64 + 2 * jj + 2]
                self.tt('dve', xwt[0:Q, 0:nch * 128].rearrange("p (c h d) -> p c h d", c=nch, h=2),
                        xt[0:Q, 0:nch * 128].rearrange("p (c h d) -> p c h d", c=nch, h=2),
                        wcv.unsqueeze(3).broadcast_to([Q, nch, 2, 64]), ALU.mult, r=[xtk, 'cols'], w=[xwk])
                wv, wk = self.ring_load(win[:, jj * 128:(jj + 1) * 128].rearrange("(k p) w -> p k w", p=128), KC, 128, r=['wb'])
                bank, pk = self.ps()
                for k in range(KC):
                    self.mm(bank[:, 0:N], wv[:, k, :], Hv[:, k, :], k == 0, k == KC - 1, r=[wk, ('Hb', k)], w=[pk])
                self.act(self.zs[pi][:, 0:N], bank[:, 0:N], AF.Silu, r=[pk], w=[('zs', pi)])

            items = [(c, pi, hh) for c in range(nch) for pi in range(4) for hh in range(2)]

            def sR(x):
                c, pi, hh = items[x]
                h = 2 * (4 * g + pi) + hh
                rb, rpk = RB[x % 2]
                sel = sel3[:, h * 128:(h + 1) * 128]
                a3c = a3[:, c * Q:(c + 1) * Q]
                self.mm(rb[:, 0:Q], sel, a3c, True, True, r=['cb', 'a3'], w=[rpk])
                self.mm(rb[0:Q, 128:128 + Q], sel[:, 0:Q], a3c, True, False, r=['cb', 'a3'], w=[rpk])
                self.mm(rb[0:Q, 128:128 + Q], self.identb[0:Q, 0:Q], negmask, False, True, r=['cb', 'identb'], w=[rpk])

            def sA(x):
                c, pi, hh = items[x]
                h = 2 * (4 * g + pi) + hh
                rb, rpk = RB[x % 2]
                e = x % 2
                self.act(self.Ea[e][:, 0:Q], rb[:, 0:Q], AF.Exp, r=[rpk], w=[('Ea', e)])
                self.act(self.E[e][0:Q, 0:Q], rb[0:Q, 128:128 + Q], AF.Exp, r=[rpk, 'cols'], w=[('E', e)],
                         bias=self.cols[0:Q, c * 96 + h:c * 96 + h + 1])

            def sD(x):
                c, pi, hh = items[x]
                h = 2 * (4 * g + pi) + hh
                e = x % 2
                self.tt('dve', self.Cp[e][:, 0:Q], Cfm[:, g, c * Q:(c + 1) * Q], self.Ea[e][:, 0:Q], ALU.mult,
                        r=kC + [('Ea', e)], w=[('Cp', e)])
                self.stt(self.Mm[e][0:Q, 0:Q], self.E[e][0:Q, 0:Q], self.cols[0:Q, c * 96 + 32 + h:c * 96 + 32 + h + 1],
                         cbm[0:Q, g * 4 + c, 0:Q], ALU.mult, ALU.mult, r=[('E', e), 'cols'] + kcb, w=[('Mm', e)])

            def sY(x):
                c, pi, hh = items[x]
                jj = 4 * g + pi
                h = 2 * jj + hh
                e = x % 2
                ybank, ypk = YB[(x // 2) % 2]
                self.mm(ybank[hh * 64:(hh + 1) * 64, 0:Q], self.Sbf[:, h * 64:(h + 1) * 64], self.Cp[e][:, 0:Q], True, False,
                        r=[('Sbf', jj), ('Cp', e)], w=[ypk])
                self.mm(ybank[hh * 64:(hh + 1) * 64, 0:Q], self.xtm[pi][0:Q, c * 128 + hh * 64:c * 128 + (hh + 1) * 64],
                        self.Mm[e][0:Q, 0:Q], False, True, r=[('ARX', pi), ('Mm', e)], w=[ypk])

            def pP1(x):
                c, pi, hh = items[x]
                jj = 4 * g + pi
                pp = (x // 2) % 2
                ybank, ypk = YB[pp]
                xs_, xk = self.xsfm[pi], ('xsfm', pi)
                y1, y1k = self.tmpf[pp][:, 0:Q], self.tk(pp)
                y2, y2k = self.tmpf[2 + pp][:, 0:Q], self.tk(2 + pp)
                self.stt(y1, xs_[:, c * Q:(c + 1) * Q], self.Dcol[:, j * 16 + jj:j * 16 + jj + 1], ybank[:, 0:Q], ALU.mult, ALU.add,
                         r=[xk, 'Dcol', ypk], w=[y1k])
                self.tt('dve', y2, y1, self.zs[pi][:, c * Q:(c + 1) * Q], ALU.mult, r=[y1k, ('zs', pi)], w=[y2k])
                self.act(Gv[:, jj, c * Q:(c + 1) * Q], y2, AF.Identity, r=[y2k, 'sngcol'], w=[('G', jj)],
                         scale=self.sngcol[:, j * 16 + jj:j * 16 + jj + 1])
                self.act(self.sqy[pp][:, 0:Q], y2, AF.Square, r=[y2k], w=[('sqy', pp)])
                first = (g == 0 and x == 1)
                lastm = (g == 3 and x == len(items) - 1)
                self.mm(ssb[:, c * Q:(c + 1) * Q], self.onesb[:], self.sqy[pp][:, 0:Q], first, lastm, r=['onesb', ('sqy', pp)],
                        w=[ssk], sgc=True)
                sn, snk = SNB[pp]
                self.mm(sn[:, 0:128], Btm[0:Q, c, g * 128:(g + 1) * 128], self.xw[pi][0:Q, c * 128:(c + 1) * 128], True, True,
                        r=kBt + [('ARX', 4 + pi)], w=[snk])

            def pP2(x):
                c, pi, hh = items[x]
                jj = 4 * g + pi
                pp = (x // 2) % 2
                sn, snk = SNB[pp]
                Sp = S[:, jj * 128:(jj + 1) * 128]
                Sp3 = Sp.rearrange("p (h d) -> p h d", h=2)
                cdv = self.cdrep[:, c * 32 + 2 * jj:c * 32 + 2 * jj + 2].unsqueeze(2).broadcast_to([128, 2, 64])
                self.tt('dve', Sp3, Sp3, cdv, ALU.mult, r=[('S', jj), 'cdrep'], w=[('S', jj)])
                self.tt('dve', Sp, Sp, sn[:, 0:128], ALU.add, r=[('S', jj), snk], w=[('S', jj)])
                self.cp('pool', self.Sbf[:, jj * 128:(jj + 1) * 128], Sp, r=[('S', jj)], w=[('Sbf', jj)])

            n_it = len(items)
            for it in range(n_it + 6):
                if it < n_it:
                    sR(it)
                if 0 <= it - 1 < n_it:
                    sA(it - 1)
                if 0 <= it - 2 < n_it:
                    sD(it - 2)
                if 0 <= it - 3 < n_it:
                    sY(it - 3)
                if 0 <= it - 4 < n_it and (it - 4) % 2 == 1:
                    pP1(it - 4)
                if 0 <= it - 5 < n_it and (it - 5) % 2 == 1:
                    pP2(it - 5)

        t3 = self.tmpf[3]
        self.act(t3[:, 0:N], ssb[:, 0:N], AF.Sqrt, r=[ssk, 'epsb'], w=['tmp3'], bias=self.epsb[:, 0:1], scale=1.0 / 2048)
        self.P.op('dve', lambda e, o=self.rstd[:, 0:N], i_=t3[:, 0:N]: e.reciprocal(o, i_), r=['tmp3'], w=['rstd'])
        for oc in range(KC):
            wv, wk = self.ring_load(wout[:, oc * 128:(oc + 1) * 128].rearrange("(k p) w -> p k w", p=128), 16, 128, r=['wb'])
            bank, pk = self.ps()
            for k in range(16):
                self.mm(bank[:, 0:N], wv[:, k, :], Gv[:, k, :], k == 0, k == 15, r=[wk, ('G', k)], w=[pk])
            tn = oc % 3
            tq = self.tmpf[tn]
            self.tt('dve', tq[:, 0:N], bank[:, 0:N], self.rstd[:, 0:N], ALU.mult, r=[pk, 'rstd'], w=[('tmp', tn)])
            self.evac_y(i, 1, q, N, oc, tq, ('tmp', tn))
        self.postnorm_resid(i, 1, q, N)

        if q == 0 and not last:
            self.dma('pool', self.Sd[j], S[:], r=Skeys, w=[('Sd', j)], semkey='Sst')
        if last:
            dst = O['ssm_p' if q == 0 else 'ssm_s'][j]
            for g4 in range(4):
                bank, pk = self.ps()
                for b4 in range(4):
                    blk = g4 * 4 + b4
                    self.tr(bank[:, b4 * 128:(b4 + 1) * 128], S[:, blk * 128:(blk + 1) * 128], self.identf, r=[('S', blk), 'cf'], w=[pk])
                st, sk = self.stg[self.stgn % 2], ('stg', self.stgn % 2)
                self.stgn += 1
                self.cp('dve', st[:, 0:512], bank[:, 0:512], r=[pk], w=[sk])
                self.dma('pool', dst[g4 * 512:(g4 + 1) * 512, :].rearrange("(b p) n -> p b n", p=128),
                         st[:, 0:512].rearrange("p (b n) -> p b n", b=4), r=[sk], w=[], semkey=sk)
            cdst = O['conv_p' if q == 0 else 'conv_s'][j]
            for tt_ in range(3):
                self.dma('pool', cdst[tt_].rearrange("(c p) -> p c", p=128), histv[:, :, tt_], r=hkeys, w=[], semkey='histst',
                         slow=True, new_batch=(tt_ == 0))
        self.psrot = list(range(8))


    def sb_alloc(self):
        sb = self.sb
        self.KTseg = [self.ARX[:, i * 1024:(i + 1) * 1024] for i in range(2)]
        self.Vseg = [self.ARX[:, 2048 + i * 1024:2048 + (i + 1) * 1024] for i in range(2)]
        self.SPrun = self.cbuf
        self.SPrunb = self.xsfm
        self.segn = 0
        self.un = 0

    def tk(self, n):
        return ('tmp', n) if n < 3 else 'tmp3'

    def sb_prep_sample(self):
        I = self.I
        for c in range(8):
            self.dma('pool', self.Vs_s[c], I['cache_sb_v'][:, c * 128:(c + 1) * 128].rearrange("(b p) d -> p b d", p=128),
                     r=[], w=['Vs_s'], semkey='vss', new_batch=(c == 0))
        Gv = self.xv(self.G, 128, FC)
        for b in range(PAST // 128):
            st, sk = self.stg[self.stgn % 2], ('stg', self.stgn % 2)
            self.stgn += 1
            self.dma('pool', st[:, :], I['cache_sb_k'][b * 128:(b + 1) * 128, :], r=[], w=[sk], semkey=sk)
            for half in range(2):
                bank, pk = self.ps()
                for k4 in range(4):
                    k = half * 4 + k4
                    self.tr(bank[:, k4 * 128:(k4 + 1) * 128], st[:, k * 128:(k + 1) * 128], self.identf, r=[sk, 'cf'], w=[pk])
                self.cp('dve' if half == 0 else 'act', Gv[:, 14 + half * 4:14 + half * 4 + 4, :],
                        bank[:, 0:512].rearrange("p (k n) -> p k n", k=4), r=[pk], w=[('G', 14 + half * 4 + x) for x in range(4)])
            self.dma('pool', self.KTs_s[:, :, b * 128:(b + 1) * 128].rearrange("c p t -> p c t"), Gv[:, 14:22, :],
                     r=[('G', 14 + x) for x in range(8)], w=['KTs_s'], semkey='ktss')

    def sbmix(self, i, q, N, t, last):
        I, O = self.I, self.O
        self.psrot = list(range(5))
        self.prenorm(i, 1, q, N)
        Hv = self.xv(self.Hb, N)
        Gv = self.xv(self.G, N, FC)
        Yv = self.xv(self.Yf, N)
        win, wout = self.wb_sb_qkv, self.wb_sb_out
        nb = (N + 127) // 128
        Vtm = self.AR8[:].bitcast(BF16).rearrange("p (b f) -> p b f", b=4)
        vk = lambda tb: [('S', 4 * tb + x) for x in range(4)]
        allvk = [('S', x) for x in range(16)]
        for part in range(3):
            if part >= self.cfg.get('sbparts', 3):
                continue
            for c in range(8):
                col0 = part * D + c * 128
                wv, wk = self.ring_load(win[:, col0:col0 + 128].rearrange("(k p) w -> p k w", p=128), KC, 128, r=['wb'])
                bank, pk = self.ps()
                for k in range(KC):
                    self.mm(bank[:, 0:N], wv[:, k, :], Hv[:, k, :], k == 0, k == KC - 1, r=[wk, ('Hb', k)], w=[pk])
                if part == 0:
                    self.cp('act', Gv[:, c, :], bank[:, 0:N], r=[pk], w=[('G', c)])
                elif part == 1:
                    self.cp('act', Gv[:, 8 + c, :], bank[:, 0:N], r=[pk], w=[('G', 8 + c)])
                    self.cp('dve', Yv[:, c, :], bank[:, 0:N], r=[pk], w=[('Yf', c)])
                else:
                    self.cp('dve', Yv[:, c, :], bank[:, 0:N], r=[pk], w=[('Yf', c)])
            if part == 1:
                dst = O['sbk_p'][t * T:(t + 1) * T, :] if q == 0 else O['sbk_s']
                if self.cfg.get('sbdbg') == 'nodma':
                    dst = None
                if self.cfg.get('sbdbg') != 'nostore':
                    self.store_fm(dst, N, src=Yv, skey='Yf')
                if q == 0 and not last:
                    self.dma('pool', self.KTs[:, :, t * T:(t + 1) * T].rearrange("c p t -> p c t"), Gv[:, 8:16, :],
                             r=[('G', 8 + x) for x in range(8)], w=['KTs'], semkey='kts')
            elif part == 2:
                dst = O['sbv_p'][t * T:(t + 1) * T, :] if q == 0 else O['sbv_s']
                self.store_fm(dst, N, src=Yv, skey='Yf', vtm=lambda tb, n: Vtm[0:n, tb, :], vkeys=vk)
                if q == 0 and not last:
                    for tb in range(nb):
                        self.dma('pool', self.Vs[:, :, 4 * t + tb, :].rearrange("c p d -> p c d"),
                                 Vtm[:, tb, :].rearrange("p (c d) -> p c d", c=8), r=vk(tb), w=['Vs'], semkey='vs',
                                 new_batch=(tb == 0))
        if self.cfg.get('sbstage', 9) < 2:
            self.psrot = list(range(8))
            return
        OTv = Hv
        ob, ok = self.psum[7], ('ps', 7)
        npast = 4 * t if q == 0 else PAST // 128
        KTd, kdk = (self.KTs, 'KTs') if q == 0 else (self.KTs_s, 'KTs_s')
        Vd, vdk = (self.Vs, 'Vs') if q == 0 else (self.Vs_s, 'Vs_s')
        SEGB = 8
        zeros = self.cbc('zeros', 128, 0, 64)
        for c in range(8):
            units = []
            for r_ in reversed(range(nb)):
                nk = min(128, N - r_ * 128)
                for hh in range(2):
                    units.append(('in', r_, nk, hh))
            segs = [(b0, min(b0 + SEGB, npast)) for b0 in range(0, npast, SEGB)]
            for (b0, b1) in reversed(segs):
                for b in reversed(range(b0, b1)):
                    for hh in range(2):
                        units.append(('past', b, (b0, b1), hh))
            for hh in range(2):
                self.mm(ob[hh * 64:(hh + 1) * 64, 0:N], zeros, Gv[:, c, :], True, False, r=['cb', ('G', c)], w=[ok], sgc=True)
                self.mm(self.psum[6 - hh][:, 0:N], self.cbc('zeros'), Gv[:, c, :], True, False, r=['cb', ('G', c)],
                        w=[('ps', 6 - hh)], sgc=True)
            if self.cfg.get('sbstage', 9) < 3:
                units = []
            curseg = None
            prev = None
            prev2 = None
            for ui, u in enumerate(units):
                lastu = ui >= len(units) - 2
                if u[0] == 'in':
                    _, r_, nk, hh = u
                    c0 = r_ * 128
                    cur = self.sb_stage1(N, c, hh, Gv[hh * 64:(hh + 1) * 64, 8 + c, c0:c0 + nk], [('G', 8 + c)],
                                         Vtm[0:nk, r_, c * 128 + hh * 64:c * 128 + (hh + 1) * 64], vk(r_), c0, nk, True, lastu)
                else:
                    _, b, (b0, b1), hh = u
                    if curseg != (b0, b1):
                        curseg = (b0, b1)
                        si = self.segn % 2
                        self.segn += 1
                        nbl = b1 - b0
                        self.dma('sp', self.KTseg[si][:, 0:nbl * 128], KTd[c, :, b0 * 128:b1 * 128], r=[kdk], w=[('ARX', 2 * si), ('ARX', 2 * si + 1)],
                                 semkey=('kseg', si))
                        self.dma('sp', self.Vseg[si][:, 0:nbl * 128].rearrange("p (b d) -> p b d", d=128), Vd[c, :, b0:b1, :],
                                 r=[vdk], w=[('ARX', 4 + 2 * si), ('ARX', 5 + 2 * si)], semkey=('vseg', si))
                    o_ = (b - b0) * 128
                    cur = self.sb_stage1(N, c, hh, self.KTseg[si][hh * 64:(hh + 1) * 64, o_:o_ + 128], [('ARX', 2 * si), ('ARX', 2 * si + 1)],
                                         self.Vseg[si][:, o_ + hh * 64:o_ + (hh + 1) * 64], [('ARX', 4 + 2 * si), ('ARX', 5 + 2 * si)], 0, 128, False, lastu)
                if prev is not None:
                    self.sb_stage2a(*prev)
                if prev2 is not None:
                    self.sb_stage2b(*prev2)
                prev2 = prev
                prev = cur
            if prev is not None:
                self.sb_stage2a(*prev)
            if prev2 is not None:
                self.sb_stage2b(*prev2)
            if prev is not None:
                self.sb_stage2b(*prev)
            self.cp('act', OTv[:, c, :], ob[:, 0:N], r=[ok], w=[('Hb', c)])
        for oc in range(KC):
            wv, wk = self.ring_load(wout[:, oc * 128:(oc + 1) * 128].rearrange("(k p) w -> p k w", p=128), KC, 128, r=['wb'])
            bank, pk = self.ps()
            for k in range(KC):
                self.mm(bank[:, 0:N], wv[:, k, :], OTv[:, k, :], k == 0, k == KC - 1, r=[wk, ('Hb', k)], w=[pk])
            self.evac_y(i, 1, q, N, oc, bank, pk)
        self.postnorm_resid(i, 1, q, N)
        self.psrot = list(range(8))

    def sb_stage1(self, N, c, hh, KTb, kK, Vb, kV, c0, nk, diag, lastu):
        Gv = self.xv(self.G, N, FC)
        SQv = self.xv(self.SQ, N)
        ncol = N - c0
        zb, zk = self.ps()
        self.mm(zb[0:nk, 0:ncol], KTb, Gv[hh * 64:(hh + 1) * 64, c, c0:N], True, True, r=kK + [('G', c)], w=[zk])
        u = self.un
        self.un += 1
        e_t, ek = self.tmpf[u % 2][0:nk, 0:ncol], self.tk(u % 2)
        self.act(e_t, zb[0:nk, 0:ncol], AF.Exp, r=[zk], w=[ek], scale=0.125)
        spb, sbk_ = SQv[0:nk, 3 + u % 3, 0:ncol], ('SQ', 3 + u % 3)
        self.act(spb, e_t, AF.Ln, r=[ek, 'oneb'], w=[sbk_], bias=self.oneb[0:nk, 0:1])
        if diag:
            self.tt('pool', spb[:, 0:nk], spb[:, 0:nk], self.cbc('mask_lt', nk, 0, nk), ALU.mult, r=[sbk_, 'cb'], w=[sbk_])
        lsb, lk = SQv[0:nk, u % 3, 0:ncol], ('SQ', u % 3)
        self.stt(lsb, zb[0:nk, 0:ncol], 0.125, spb, ALU.mult, ALU.subtract, r=[zk, sbk_], w=[lk])
        return (N, hh, Vb, kV, c0, nk, diag, lastu, u, spb, sbk_, lsb, lk)

    def sb_stage2a(self, N, hh, Vb, kV, c0, nk, diag, lastu, u, spb, sbk_, lsb, lk):
        SQv = self.xv(self.SQ, N)
        ncol = N - c0
        acc, ack = self.psum[6 - hh], ('ps', 6 - hh)
        a_ = acc[0:nk, c0:N]
        self.mm(a_, self.cbc('negtri', nk, 0, nk), spb, False, False, r=['cb', sbk_], w=[ack], sgc=True)
        self.mm(a_, self.identb[0:nk, 0:nk], lsb, False, False, r=['identb', lk], w=[ack], sgc=True)
        if diag:
            self.mm(acc[0:nk, c0:c0 + nk], self.identb[0:nk, 0:nk], self.cbc('negmask_lt', nk, 0, nk), False, False,
                    r=['identb', 'cb'], w=[ack], sgc=True)
        w_t, wk_ = SQv[0:nk, 6 + u % 2, 0:ncol], ('SQ', 6 + u % 2)
        self.act(w_t, a_, AF.Exp, r=[ack], w=[wk_])

    def sb_stage2b(self, N, hh, Vb, kV, c0, nk, diag, lastu, u, spb, sbk_, lsb, lk):
        SQv = self.xv(self.SQ, N)
        ncol = N - c0
        ob, ok = self.psum[7], ('ps', 7)
        acc, ack = self.psum[6 - hh], ('ps', 6 - hh)
        a_ = acc[0:nk, c0:N]
        w_t, wk_ = SQv[0:nk, 6 + u % 2, 0:ncol], ('SQ', 6 + u % 2)
        self.mm(ob[hh * 64:(hh + 1) * 64, c0:N], Vb, w_t, False, lastu, r=kV + [wk_], w=[ok], sgc=True)
        if nk == 128:
            self.mm(a_, self.cbc('negtrile', nk, 0, nk), spb, False, False, r=['cb', sbk_], w=[ack], sgc=True)
        else:
            self.mm(acc[:, c0:N], self.cbc('negones', nk, 0, 128), spb, False, False, r=['cb', sbk_], w=[ack], sgc=True)
            self.mm(a_, self.cbc('postri', nk, 0, nk), spb, False, False, r=['cb', sbk_], w=[ack], sgc=True)
        self.mm(a_, self.cbc('negident', nk, 0, nk), lsb, False, False, r=['cb', lk], w=[ack], sgc=True)
        if diag:
            self.mm(acc[0:nk, c0:c0 + nk], self.identb[0:nk, 0:nk], self.cbc('posmask_lt', nk, 0, nk), False, False,
                    r=['identb', 'cb'], w=[ack], sgc=True)

    def swamix(self, i, q, N, t, last):
        I, O = self.I, self.O
        self.psrot = list(range(6))
        self.prenorm(i, 1, q, N)
        Hv = self.xv(self.Hb, N)
        Gv = self.xv(self.G, N, FC)
        Yv = self.xv(self.Yf, N)
        win, wout = self.wb_swa_qkv, self.wb_swa_out
        W = 128 + N
        base = 8 * T
        KTd = self.G[:, base:base + 4 * W].rearrange("p (g w) -> p g w", g=4)
        kK = [('G', 8 + x) for x in range(5)]
        vb0 = base + 4 * (128 + T)
        Vt = self.G[:, vb0:vb0 + 5 * 256].rearrange("p (b f) -> p b f", b=5)
        kV = [('G', 13 + x) for x in range(3)]
        nb = (N + 127) // 128
        for c in range(8):
            wv, wk = self.ring_load(win[:, c * 128:(c + 1) * 128].rearrange("(k p) w -> p k w", p=128), KC, 128, r=['wb'])
            bank, pk = self.ps()
            for k in range(KC):
                self.mm(bank[:, 0:N], wv[:, k, :], Hv[:, k, :], k == 0, k == KC - 1, r=[wk, ('Hb', k)], w=[pk])
            self.cp('act', Gv[:, c, :], bank[:, 0:N], r=[pk], w=[('G', c)])
        wv, wk = self.ring_load(win[:, 1024:1280].rearrange("(k p) w -> p k w", p=128), KC, 256, r=['wb'])
        for g in range(4):
            bank, pk = self.ps()
            for half in range(2):
                for k in range(KC):
                    self.mm(bank[half * 64:(half + 1) * 64, 0:N], wv[:, k, g * 64:(g + 1) * 64], Hv[:, k, :], k == 0, k == KC - 1,
                            r=[wk, ('Hb', k)], w=[pk])
            self.cp('act', KTd[:, g, 128:128 + N], bank[:, 0:N], r=[pk], w=kK)
        if last:
            for c2 in range(2):
                bank, pk = self.ps()
                for k in range(KC):
                    self.mm(bank[:, 0:N], wv[:, k, c2 * 128:(c2 + 1) * 128], Hv[:, k, :], k == 0, k == KC - 1,
                            r=[wk, ('Hb', k)], w=[pk])
                self.cp('dve', Yv[:, c2, :], bank[:, 0:N], r=[pk], w=[('Yf', c2)])
        wv, wk = self.ring_load(win[:, 1280:1536].rearrange("(k p) w -> p k w", p=128), KC, 256, r=['wb'])
        for c2 in range(2):
            bank, pk = self.ps()
            for k in range(KC):
                self.mm(bank[:, 0:N], wv[:, k, c2 * 128:(c2 + 1) * 128], Hv[:, k, :], k == 0, k == KC - 1, r=[wk, ('Hb', k)], w=[pk])
            self.cp('dve', Yv[:, 2 + c2, :], bank[:, 0:N], r=[pk], w=[('Yf', 2 + c2)])
        if q == 0 and t > 0:
            self.cp('pool', KTd[:, :, 0:128], self.KTprev[:].rearrange("p (g w) -> p g w", g=4), r=['KTprev'], w=kK)
            self.cp('pool', Vt[:, 0, :], self.Vprev[:], r=['Vprev'], w=kV)
        elif q == 1:
            st, sk = self.stg[self.stgn % 2], ('stg', self.stgn % 2)
            self.stgn += 1
            for dup in range(2):
                self.dma('pool', st[:, 0:512].rearrange("p (g u d) -> p g u d", g=4, u=2)[:, :, dup, :],
                         I['cache_swa_k'].rearrange("p (g d) -> p g d", g=4), r=[], w=[sk], semkey=sk, new_batch=(dup == 0))
            bank, pk = self.ps()
            for g in range(4):
                self.tr(bank[:, g * 128:(g + 1) * 128], st[:, g * 128:(g + 1) * 128], self.identf, r=[sk, 'cf'], w=[pk])
            self.cp('act', KTd[:, :, 0:128], bank[:, 0:512].rearrange("p (g w) -> p g w", g=4), r=[pk], w=kK)
            st, sk = self.stg[self.stgn % 2], ('stg', self.stgn % 2)
            self.stgn += 1
            self.dma('pool', st[:, 0:256], I['cache_swa_v'], r=[], w=[sk], semkey=sk)
            self.cp('pool', Vt[:, 0, :], st[:, 0:256], r=[sk], w=kV)
            self.dma('pool', O['swak_s'][0:96, :], I['cache_swa_k'][32:128, :], r=[], w=[], semkey='swaout', new_batch=True)
            self.dma('pool', O['swav_s'][0:96, :], I['cache_swa_v'][32:128, :], r=[], w=[], semkey='swaout', new_batch=False)
        vdst = None
        if last:
            vdst = O['swav_p'] if q == 0 else O['swav_s'][96:128, :]
        self.store_fm(vdst, N, src=Yv[:, 2:4], skey='Yf', nchunk=2, vtm=lambda tb, n: Vt[0:n, 1 + tb, :], vkeys=lambda tb: kV,
                      only_last=False, kofs=2) if not (last and q == 0) else None
        if last and q == 0:
            for tb in range(nb):
                self.store_fm(O['swav_p'] if tb == nb - 1 else None, 128, src=Yv[:, 2:4, tb * 128:(tb + 1) * 128], skey='Yf', nchunk=2,
                              vtm=lambda tb_, n, tb=tb: Vt[0:n, 1 + tb, :], vkeys=lambda tb_: kV, kofs=2)
            self.store_fm(O['swak_p'], 128, src=Yv[:, 0:2, (nb - 1) * 128:nb * 128], skey='Yf', nchunk=2)
        elif last:
            self.store_fm(O['swak_s'][96:128, :], N, src=Yv[:, 0:2], skey='Yf', nchunk=2)
        OTv = Hv
        ob, ok = self.psum[7], ('ps', 7)
        db, dk = self.psum[6], ('ps', 6)
        NM = self.cbc('nmswa')
        zeros = self.cbc('zeros', 128, 0, 64)
        SQv = self.xv(self.SQ, N)
        if q == 0:
            blocks = []
            for kb in range(-1, nb):
                if kb == -1 and t == 0:
                    continue
                q0, q1 = max(0, 128 * kb), min(N, 128 * kb + 256)
                blocks.append(((kb + 1) * 128, 128, kb + 1, q0, q1, q0 - 128 * kb))
        else:
            blocks = [(0, 128, 0, 0, N, None), (128, N, 1, 0, N, None)]
        for c in range(8):
            for hh in range(2):
                h = 2 * c + hh
                g = h // 4
                self.mm(ob[hh * 64:(hh + 1) * 64, 0:N], zeros, Gv[:, c, :], True, False, r=['cb', ('G', c)], w=[ok], sgc=True)
                self.mm(db[hh * 64:(hh + 1) * 64, 0:N], zeros, Gv[:, c, :], True, False, r=['cb', ('G', c)], w=[dk], sgc=True)
                for bi, (kc0, nk, vblk, q0, q1, p0) in enumerate(blocks):
                    nq = q1 - q0
                    lastb = bi == len(blocks) - 1
                    sb_, sk_ = self.ps()
                    self.mm(sb_[0:nk, 0:nq], KTd[hh * 64:(hh + 1) * 64, g, kc0:kc0 + nk], Gv[hh * 64:(hh + 1) * 64, c, q0:q1],
                            True, p0 is None, r=kK + [('G', c)], w=[sk_])
                    if p0 is not None:
                        self.mm(sb_[0:nk, 0:nq], self.identb[0:nk, 0:nk], NM[0:nk, p0:p0 + nq], False, True, r=['identb', 'cb'], w=[sk_])
                    u = self.un
                    self.un += 1
                    P, pk_ = SQv[0:nk, u % 4, 0:nq], ('SQ', u % 4)
                    self.act(P, sb_[0:nk, 0:nq], AF.Exp, r=[sk_], w=[pk_], scale=0.125)
                    self.mm(ob[hh * 64:(hh + 1) * 64, q0:q1], Vt[0:nk, vblk, g * 64:(g + 1) * 64], P, False, lastb, r=kV + [pk_], w=[ok],
                            sgc=True)
                    self.mm(db[hh * 64:(hh + 1) * 64, q0:q1], self.onesb[0:nk, 0:64], P, False, lastb, r=['onesb', pk_], w=[dk], sgc=True)
            den, dnk = self.tmpf[c % 2][:, 0:N], self.tk(c % 2)
            self.ts('dve', den, db[:, 0:N], self.esink[:, c:c + 1], ALU.add, r=[dk, 'esink'], w=[dnk])
            self.P.op('dve', lambda e, o=den, i_=den: e.reciprocal(o, i_), r=[dnk], w=[dnk])
            self.tt('dve', OTv[:, c, :], ob[:, 0:N], den, ALU.mult, r=[ok, dnk], w=[('Hb', c)])
        if q == 0 and not last:
            self.cp('pool', self.KTprev[:].rearrange("p (g w) -> p g w", g=4), KTd[:, :, N:N + 128], r=kK, w=['KTprev'])
            self.cp('pool', self.Vprev[:], Vt[:, nb, :], r=kV, w=['Vprev'])
        for oc in range(KC):
            wv, wk = self.ring_load(wout[:, oc * 128:(oc + 1) * 128].rearrange("(k p) w -> p k w", p=128), KC, 128, r=['wb'])
            bank, pk = self.ps()
            for k in range(KC):
                self.mm(bank[:, 0:N], wv[:, k, :], OTv[:, k, :], k == 0, k == KC - 1, r=[wk, ('Hb', k)], w=[pk])
            self.evac_y(i, 1, q, N, oc, bank, pk)
        self.postnorm_resid(i, 1, q, N)
        self.psrot = list(range(8))


def build_program(cfg):
    b = Builder(cfg)
    b.epsb = b.sb("epsb", [128, 1], F32)
    b.memset('dve', b.epsb[:], EPS, w=['epsb'])
    b.oneb = b.sb("oneb", [128, 1], F32)
    b.memset('dve', b.oneb[:], 1.0, w=['oneb'])
    nc = b.build()
    return b, nc


def make_in_maps(inputs):
    maps = []
    f = np.ascontiguousarray
    shared = {k: f(inputs[k]) for k in ('ada_w', 'ada_b', 'norm_g', 'ffn_w_in', 'ffn_w_out', 'ssd_w_in', 'ssd_conv_b',
                                        'ssd_dt_bias', 'ssd_a_log', 'ssd_d', 'ssd_norm_g', 'ssd_w_out')}
    shared['ssd_conv_w'] = f(inputs['ssd_conv_w'].reshape(2, 4 * 3072))
    shared['sb_w_qkv'] = f(inputs['sb_w_qkv'][0])
    shared['sb_w_out'] = f(inputs['sb_w_out'][0])
    shared['swa_w_qkv'] = f(inputs['swa_w_qkv'][0])
    shared['swa_w_out'] = f(inputs['swa_w_out'][0])
    shared['swa_sinks'] = f(inputs['swa_sinks'][0])
    for c in range(8):
        m = dict(shared)
        m['xp'] = f(inputs['x_prompt'][c % 4])
        m['xs'] = f(inputs['x_sample'][c])
        m['cvec'] = f(np.stack([inputs['c_prompt'][c % 4], inputs['c_sample'][c]]))
        m['state_ssm'] = f(inputs['state_ssm'][:, c].reshape(2, 2048, 128))
        m['state_conv'] = f(inputs['state_conv'][:, c])
        m['cache_sb_k'] = f(inputs['cache_sb_k'][0, c].reshape(PAST, D))
        m['cache_sb_v'] = f(inputs['cache_sb_v'][0, c].reshape(PAST, D))
        m['cache_swa_k'] = f(inputs['cache_swa_k'][0, c].reshape(128, 256))
        m['cache_swa_v'] = f(inputs['cache_swa_v'][0, c].reshape(128, 256))
        maps.append(m)
    return maps


def kernel(**inputs):
    cfg = {}
    b, nc = build_program(cfg)
    maps = make_in_maps(inputs)
    res = run_bass_kernel_spmd(nc, maps, core_ids=list(range(8)))
    r = res.results
    y_prompt = np.stack([r[c]['yp'] for c in range(4)])
    y_sample = np.stack([r[c]['ys'] for c in range(8)])
    ssm_p = np.stack([r[c]['ssm_p'].reshape(2, 32, 64, 128) for c in range(4)], axis=1)
    ssm_s = np.stack([r[c]['ssm_s'].reshape(2, 32, 64, 128) for c in range(8)], axis=1)
    conv_p = np.stack([r[c]['conv_p'] for c in range(4)], axis=1)
    conv_s = np.stack([r[c]['conv_s'] for c in range(8)], axis=1)
    sbk_p = np.stack([r[c]['sbk_p'].reshape(SEQ, 16, 64) for c in range(4)])[None]
    sbk_s = np.stack([r[c]['sbk_s'].reshape(NS, 16, 64) for c in range(8)])[None]
    sbv_p = np.stack([r[c]['sbv_p'].reshape(SEQ, 16, 64) for c in range(4)])[None]
    sbv_s = np.stack([r[c]['sbv_s'].reshape(NS, 16, 64) for c in range(8)])[None]
    sw = {}
    for nm, n in (('swak_p', 4), ('swak_s', 8), ('swav_p', 4), ('swav_s', 8)):
        sw[nm] = np.stack([r[c][nm].reshape(128, 4, 64) for c in range(n)])[None]
    return (y_prompt, y_sample, ssm_p, ssm_s, conv_p, conv_s, sbk_p, sbk_s, sbv_p, sbv_s,
            sw['swak_p'], sw['swak_s'], sw['swav_p'], sw['swav_s'])
```

```python
import numpy as np
from contextlib import ExitStack
import concourse.bass as bass
import concourse.mybir as mybir
from concourse.bass_utils import run_bass_kernel_spmd

F32 = mybir.dt.float32
BF16 = mybir.dt.bfloat16
AF = mybir.ActivationFunctionType
ALU = mybir.AluOpType

D = 1024
KC = 8
DFF = 2816
FC = 22
T = 512
SEQ = 8192
NTILE = SEQ // T
NS = 32
PAST = 1024
DEPTH = 4
EPS = 1e-6
NSLOT = 8
SLOTW = 2048


class Prog:
    def __init__(self, nc, same_sync=True):
        self.nc = nc
        self.q = {e: [] for e in ('pe', 'act', 'dve', 'pool', 'sp')}
        self.cnt = {e: 0 for e in self.q}
        self.seen = {e: {} for e in self.q}
        self.sems = {}
        self.dcnt = {}
        self.bufs = {}
        self.same_sync = same_sync

    def _waits(self, eng, r, w, deps):
        need = {}

        def add(ev):
            if ev is None:
                return
            s, v = ev
            if need.get(s, 0) < v:
                need[s] = v

        for k in r:
            b = self.bufs.get(k)
            if b:
                add(b[0])
                if isinstance(k, tuple) and k[0] == 'ps':
                    for s_, v_ in b[1].items():
                        if s_ != ('e', eng):
                            add((s_, v_))
        for k in w:
            b = self.bufs.get(k)
            if b:
                add(b[0])
                for s, v in b[1].items():
                    add((s, v))
        for d in deps:
            add(d)
        wl = []
        for s, v in need.items():
            if s == ('e', eng) and (eng == 'pe' or not self.same_sync):
                continue
            if self.seen[eng].get(s, 0) < v:
                wl.append((s, v))
                self.seen[eng][s] = v
        return wl

    def _mark(self, ev, r, w):
        for k in r:
            b = self.bufs.setdefault(k, [None, {}])
            if b[1].get(ev[0], 0) < ev[1]:
                b[1][ev[0]] = ev[1]
        for k in w:
            self.bufs[k] = [ev, {}]

    def op(self, eng, fn, r=(), w=(), deps=()):
        wl = self._waits(eng, r, w, deps)
        self.cnt[eng] += 1
        ev = (('e', eng), self.cnt[eng])
        self.q[eng].append((wl, fn, ev[0], 1))
        self._mark(ev, r, w)
        return ev

    def dma(self, eng, fn, r=(), w=(), deps=(), semkey=None, new_batch=True):
        sk = ('d', semkey)
        c = self.dcnt.get(sk, 0)
        deps = list(deps)
        if new_batch and c > 0:
            deps.append((sk, c))
        wl = self._waits(eng, r, w, deps)
        c += 16
        self.dcnt[sk] = c
        ev = (sk, c)
        self.q[eng].append((wl, fn, sk, 16))
        self._mark(ev, r, w)
        return ev

    def emit(self, es):
        nc = self.nc
        keys = [('e', e) for e in self.q] + list(self.dcnt.keys())
        for i, k in enumerate(keys):
            self.sems[k] = es.enter_context(nc.semaphore("sem%d" % i))
        block = es.enter_context(nc.Block())
        engmap = {'pe': block.tensor, 'act': block.scalar, 'dve': block.vector, 'pool': block.gpsimd,
                  'sp': block.sync}
        for e, dec in engmap.items():
            ops = self.q[e]

            def body(eng, ops=ops, e=e):
                for wl, fn, sk, inc in ops:
                    for s, v in wl:
                        eng.wait_ge(self.sems[s], v)
                    fn(eng).then_inc(self.sems[sk], inc)
                if e == 'sp':
                    for sk, c in self.dcnt.items():
                        eng.wait_ge(self.sems[sk], c)

            dec(body)


def _consts_np():
    cols = {}
    blocks = []
    off = 0

    def addc(name, arr):
        nonlocal off
        a = np.zeros((128, arr.shape[1]), np.float32)
        a[:arr.shape[0]] = arr
        cols[name] = (off, arr.shape[1])
        blocks.append(a)
        off += arr.shape[1]

    addc('ident', np.eye(128, dtype=np.float32))
    return np.concatenate(blocks, axis=1), cols


def _consts_bf_np():
    cols = {}
    blocks = []
    off = 0

    def addc(name, arr):
        nonlocal off
        a = np.zeros((128, arr.shape[1]), np.float32)
        a[:arr.shape[0]] = arr
        cols[name] = (off, arr.shape[1])
        blocks.append(a)
        off += arr.shape[1]

    s_ = np.arange(128)[:, None]
    t_ = np.arange(128)[None, :]
    k = np.arange(96)
    sel3 = np.zeros((96, 32, 128), np.float32)
    for h in range(32):
        sel3[k % 32 == h, h, :] = 1.0
    addc('sel3', sel3.reshape(96, 32 * 128))
    addc('negmask', np.where(s_ > t_, -30000.0, 0.0).astype(np.float32))
    addc('mask_le', (s_ <= t_).astype(np.float32))
    addc('i3', (k[:, None] % 32 == np.arange(32)[None, :]).astype(np.float32))
    addc('negtri', np.where(s_ > t_, -1.0, 0.0).astype(np.float32))
    addc('mask_lt', (s_ < t_).astype(np.float32))
    addc('negmask_lt', np.where(s_ >= t_, -30000.0, 0.0).astype(np.float32))
    nmswa = np.zeros((128, 256), np.float32)
    nmswa[0:64, 192:256] = -30000.0
    nmswa[64:128, 0:64] = -30000.0
    addc('nmswa', nmswa)
    addc('negones', -np.ones((128, 128), np.float32))
    addc('negtrile', np.where(s_ <= t_, -1.0, 0.0).astype(np.float32))
    addc('postri', np.where(s_ > t_, 1.0, 0.0).astype(np.float32))
    addc('negident', -np.eye(128, dtype=np.float32))
    addc('posmask_lt', np.where(s_ >= t_, 30000.0, 0.0).astype(np.float32))
    addc('zeros', np.zeros((128, 128), np.float32))
    return np.concatenate(blocks, axis=1), cols


class Builder:
    def __init__(self, cfg):
        self.cfg = cfg
        self.nc = bass.Bass("TRN2", target_bir_lowering=False)
        self.es = ExitStack()
        self.P = Prog(self.nc, same_sync=cfg.get('same_sync', True))
        self.psn = 0
        self.psrot = list(range(8))
        self.cbn = 0
        self.en = 0
        self.ringn = 0
        self.stgn = 0
        self.tmpn = {}
        self.wblock = {}

    def din(self, name, shape, dt=F32):
        return self.nc.dram_tensor(name, list(shape), dt, kind="ExternalInput").ap()

    def dout(self, name, shape, dt=F32):
        return self.nc.dram_tensor(name, list(shape), dt, kind="ExternalOutput").ap()

    def dint(self, name, shape, dt=BF16):
        return self.nc.dram_tensor(name, list(shape), dt, kind="Internal").ap()

    def sb(self, name, shape, dt):
        return self.es.enter_context(self.nc.sbuf_tensor(name, list(shape), dt))

    def ps(self):
        rot = self.psrot
        i = rot[self.psn % len(rot)]
        self.psn += 1
        return self.psum[i], ('ps', i)

    def mm(self, out, lhsT, rhs, start, stop, r, w, sgc=False):
        return self.P.op('pe', lambda e, o=out, l=lhsT, rr=rhs, s=start, t=stop, g=sgc:
                         e.matmul(o, lhsT=l, rhs=rr, start=s, stop=t, skip_group_check=g), r=r, w=w)

    def cbc(self, name, rows=128, c0=0, c1=None):
        off, w = self.cbcols[name]
        if c1 is None:
            c1 = w
        return self.cb[0:rows, off + c0:off + c1]

    def tr(self, out, in_, ident, r, w):
        return self.P.op('pe', lambda e, o=out, i=in_, d=ident: e.transpose(o, i, d), r=r, w=w)

    def act(self, out, in_, func, r, w, bias=None, scale=None):
        kw = {}
        if bias is not None:
            kw['bias'] = bias
        if scale is not None:
            kw['scale'] = scale
        return self.P.op('act', lambda e, o=out, i=in_, f=func, kw=kw: e.activation(o, i, f, **kw), r=r, w=w)

    def tt(self, eng, out, in0, in1, op, r, w):
        return self.P.op(eng, lambda e, o=out, a=in0, b=in1, p=op: e.tensor_tensor(o, a, b, p), r=r, w=w)

    def ts(self, eng, out, in0, s1, op0, r, w, s2=None, op1=None):
        if op1 is None:
            return self.P.op(eng, lambda e, o=out, a=in0, s=s1, p=op0: e.tensor_scalar(o, a, s, None, p), r=r, w=w)
        return self.P.op(eng, lambda e, o=out, a=in0, s=s1, p=op0, s2=s2, p1=op1: e.tensor_scalar(o, a, s, s2, p, p1),
                         r=r, w=w)

    def stt(self, out, in0, scalar, in1, op0, op1, r, w, eng='dve'):
        return self.P.op(eng, lambda e, o=out, a=in0, s=scalar, b=in1, p0=op0, p1=op1:
                         e.scalar_tensor_tensor(o, a, s, b, p0, p1), r=r, w=w)

    def cp(self, eng, out, in_, r, w):
        if eng == 'act':
            return self.P.op('act', lambda e, o=out, i=in_: e.copy(o, i), r=r, w=w)
        return self.P.op(eng, lambda e, o=out, i=in_: e.tensor_copy(o, i), r=r, w=w)

    def memset(self, eng, ap, val, w):
        return self.P.op(eng, lambda e, a=ap, v=val: e.memset(a, v), r=(), w=w)

    def dma(self, eng, out, in_, r, w, semkey, new_batch=True, slow=False, deps=()):
        kw = {}
        if slow:
            kw['allow_slow_non_contiguous'] = True
        return self.P.dma(eng, lambda e, o=out, i=in_, kw=kw: e.dma_start(out=o, in_=i, **kw), r=r, w=w,
                          semkey=(semkey, eng), new_batch=new_batch, deps=deps)

    def ring_load(self, src, kc, width, r=(), eng='sp'):
        s = self.ringn % NSLOT
        self.ringn += 1
        view = self.ring[s][:, 0:kc * width].rearrange("p (k w) -> p k w", k=kc)
        key = ('ring', s)
        nm = src.tensor.name
        if nm in self.wblock:
            r = [('wb', nm, src.offset // self.wblock[nm])]
        self.dma(eng, view, src, r=r, w=[key], semkey=key)
        return view, key

    def build(self):
        nc, cfg = self.nc, self.cfg
        xp = self.din("xp", [SEQ, D])
        xs = self.din("xs", [NS, D])
        cvec = self.din("cvec", [2, D])
        self.w_ada = self.din("ada_w", [DEPTH, D, 9 * D])
        ada_b = self.din("ada_b", [DEPTH, 9 * D])
        norm_g = self.din("norm_g", [DEPTH, 6, D])
        ffn_w_in = self.din("ffn_w_in", [DEPTH, 2, D, 2 * DFF])
        ffn_w_out = self.din("ffn_w_out", [DEPTH, 2, DFF, D])
        self.I = I = {}
        I['state_ssm'] = self.din("state_ssm", [2, 2048, 128])
        I['state_conv'] = self.din("state_conv", [2, 3, 3072])
        I['ssd_w_in'] = self.din("ssd_w_in", [2, D, 5152])
        I['ssd_conv_w'] = self.din("ssd_conv_w", [2, 4 * 3072])
        I['ssd_conv_b'] = self.din("ssd_conv_b", [2, 3072])
        I['ssd_dt_bias'] = self.din("ssd_dt_bias", [2, 32])
        I['ssd_a_log'] = self.din("ssd_a_log", [2, 32])
        I['ssd_d'] = self.din("ssd_d", [2, 32])
        I['ssd_norm_g'] = self.din("ssd_norm_g", [2, 2048])
        I['ssd_w_out'] = self.din("ssd_w_out", [2, 2048, D])
        I['cache_sb_k'] = self.din("cache_sb_k", [PAST, D])
        I['cache_sb_v'] = self.din("cache_sb_v", [PAST, D])
        I['sb_w_qkv'] = self.din("sb_w_qkv", [D, 3 * D])
        I['sb_w_out'] = self.din("sb_w_out", [D, D])
        I['cache_swa_k'] = self.din("cache_swa_k", [128, 256])
        I['cache_swa_v'] = self.din("cache_swa_v", [128, 256])
        I['swa_w_qkv'] = self.din("swa_w_qkv", [D, 1536])
        I['swa_sinks'] = self.din("swa_sinks", [16])
        I['swa_w_out'] = self.din("swa_w_out", [D, D])
        yp = self.dout("yp", [SEQ, D])
        ys = self.dout("ys", [NS, D])
        self.O = O = {}
        for nm in ('swak_p', 'swak_s', 'swav_p', 'swav_s'):
            O[nm] = self.dout(nm, [128, 256])
        O['sbk_p'] = self.dout("sbk_p", [SEQ, D])
        O['sbk_s'] = self.dout("sbk_s", [NS, D])
        O['sbv_p'] = self.dout("sbv_p", [SEQ, D])
        O['sbv_s'] = self.dout("sbv_s", [NS, D])
        O['ssm_p'] = self.dout("ssm_p", [2, 2048, 128])
        O['ssm_s'] = self.dout("ssm_s", [2, 2048, 128])
        O['conv_p'] = self.dout("conv_p", [2, 3, 3072])
        O['conv_s'] = self.dout("conv_s", [2, 3, 3072])
        cnp, ccols = _consts_np()
        cdram = nc.inline_tensor(cnp, "consts").ap()
        cbnp, cbcols = _consts_bf_np()
        cbdram = nc.inline_tensor(cbnp, "constsb").ap()
        self.wb_ffn_in = self.dint("wb_ffn_in", [DEPTH, 2, D, 2 * DFF])
        self.wb_ffn_out = self.dint("wb_ffn_out", [DEPTH, 2, DFF, D])
        self.wb_ssd_in = self.dint("wb_ssd_in", [2, D, 5152])
        self.wb_ssd_out = self.dint("wb_ssd_out", [2, 2048, D])
        self.Sd = self.dint("Sd", [2, 128, 2048], F32)
        self.wb_sb_qkv = self.dint("wb_sb_qkv", [D, 3 * D])
        self.wb_sb_out = self.dint("wb_sb_out", [D, D])
        self.wb_swa_qkv = self.dint("wb_swa_qkv", [D, 1536])
        self.wb_swa_out = self.dint("wb_swa_out", [D, D])
        self.KTs = self.dint("KTs", [8, 128, SEQ])
        self.Vs = self.dint("Vs", [8, 128, SEQ // 128, 128])
        self.KTs_s = self.dint("KTs_s", [8, 128, PAST])
        self.Vs_s = self.dint("Vs_s", [8, 128, PAST // 128, 128])

        self.psum = [self.es.enter_context(nc.psum_tensor("ps%d" % i, [128, 512], F32)) for i in range(8)]
        self.ring = [self.sb("ring%d" % i, [128, SLOTW], BF16) for i in range(NSLOT)]
        self.X = self.sb("X", [128, KC * T], F32)
        self.Hb = self.sb("Hb", [128, KC * T], BF16)
        self.G = self.sb("G", [128, FC * T], BF16)
        self.Yf = self.sb("Yf", [128, KC * T], F32)
        self.SQ = self.sb("SQ", [128, KC * T], BF16)
        self.stg = [self.sb("stg%d" % i, [128, D], F32) for i in range(2)]
        self.tmpf = [self.sb("tmpf%d" % i, [128, T], F32) for i in range(4)]
        self.rstd = self.sb("rstd", [128, T], F32)
        self.cf = self.sb("cf", [128, cnp.shape[1]], F32)
        self.cb = self.sb("cb", [128, cbnp.shape[1]], BF16)
        self.identb = self.sb("identb", [128, 128], BF16)
        self.onesb = self.sb("onesb", [128, 128], BF16)
        self.cact = self.sb("cact", [128, 16], BF16)
        self.ccol = self.sb("ccol", [128, 16], F32)
        self.adab = self.sb("adab", [128, DEPTH * 72], F32)
        self.ng = self.sb("ng", [128, DEPTH * 48], F32)
        self.mod = self.sb("mod", [128, DEPTH * 144], F32)
        self.der = self.sb("der", [128, DEPTH * 3 * 3 * 2 * 8], F32)
        self.AR8 = self.sb("AR8", [128, 2048], F32)
        self.ccols = ccols
        self.cbcols = cbcols
        self.identf = self.cf[:, ccols['ident'][0]:ccols['ident'][0] + 128]
        self.ssd_alloc()
        self.sb_alloc()
        self.KTprev = self.sb("KTprev", [128, 512], BF16)
        self.Vprev = self.sb("Vprev", [128, 256], BF16)
        self.esink = self.sb("esink", [128, 8], F32)

        self.dma('pool', self.cf[:], cdram, r=[], w=['cf'], semkey='cf')
        self.dma('pool', self.cb[:], cbdram, r=[], w=['cb'], semkey='cb')
        self.cp('dve', self.identb[:], self.identf, r=['cf'], w=['identb'])
        self.memset('dve', self.onesb[:], 1.0, w=['onesb'])
        self.load_cols(self.ccol[:, 0:8], cvec[0, :], 'ccol')
        self.load_cols(self.ccol[:, 8:16], cvec[1, :], 'ccol')
        for i in range(DEPTH):
            for h in range(2):
                self.load_cols(self.adab[:, i * 72 + h * 36:i * 72 + h * 36 + 36], ada_b[i, h * 4608:(h + 1) * 4608], 'adab')
            self.load_cols(self.ng[:, i * 48:(i + 1) * 48], norm_g[i].rearrange("a d -> (a d)"), 'ng')
        self.ssd_params()
        for hh in range(2):
            self.dma('pool', self.esink[hh * 64:(hh + 1) * 64, :], bass.AP(I['swa_sinks'].tensor, hh, [[0, 64], [2, 8]]),
                     r=[], w=['esink'], semkey='esink', slow=True, new_batch=False)
        self.act(self.esink[:], self.esink[:], AF.Exp, r=['esink'], w=['esink'])
        self.prepass_weights(ffn_w_in, ffn_w_out)
        self.prepass_mod()

        ntile = cfg.get('ntile', NTILE)
        tiles = [('p', t) for t in range(ntile)]
        if cfg.get('sample', True):
            tiles.append(('s', 0))
        stop = cfg.get('stop')
        for kind, t in tiles:
            N = T if kind == 'p' else NS
            q = 0 if kind == 'p' else 1
            src = xp[t * T:(t + 1) * T, :] if kind == 'p' else xs
            dst = yp[t * T:(t + 1) * T, :] if kind == 'p' else ys
            last = (kind == 's') or (t == ntile - 1)
            self.load_fm(src, N)
            for i in range(cfg.get('depth', DEPTH)):
                self.ffn(i, 0, q, N)
                if stop == 'x%d_0' % i:
                    break
                if i % 3 == 0:
                    self.ssd(i, q, N, t, last)
                elif i % 3 == 1:
                    if kind == 's':
                        self.sb_prep_sample()
                    self.sbmix(i, q, N, t, last)
                else:
                    self.swamix(i, q, N, t, last)
                if stop == 'x%d_1' % i:
                    break
                self.ffn(i, 1, q, N)
            self.store_fm(dst, N)
        print('sbuf bytes remaining', self.nc.sbuf_bytes_remaining() if callable(getattr(self.nc, 'sbuf_bytes_remaining', None)) else getattr(self.nc, 'sbuf_bytes_remaining', None))
        self.P.emit(self.es)
        return nc

    def load_cols(self, dst, src_vec, key):
        self.dma('pool', dst, src_vec.rearrange("(n p) -> p n", p=128), r=[], w=[key], semkey=key, slow=True,
                 new_batch=False)

    def xv(self, tile, N, nch=KC):
        return tile[:, 0:nch * N].rearrange("p (k n) -> p k n", k=nch)

    def prepass_mod(self):
        cact_v = self.cact[:].rearrange("p (k two) -> p two k", two=2)
        for q in range(2):
            self.act(cact_v[:, q, :], self.ccol[:, q * 8:(q + 1) * 8], AF.Silu, r=['ccol'], w=['cact'])
        for i in range(DEPTH):
            bank, pk = self.ps()
            for t36 in range(36):
                src = self.w_ada[i, :, t36 * 256:(t36 + 1) * 256].rearrange("(k p) w -> p k w", p=128)
                wv, wk = self.ring_load(src, KC, 256, eng='pool')
                for c in range(2):
                    oc = t36 * 2 + c
                    for k in range(KC):
                        self.mm(bank[:, oc * 2:oc * 2 + 2], wv[:, k, c * 128:(c + 1) * 128], self.cact[:, k * 2:k * 2 + 2],
                                k == 0, k == KC - 1, r=[wk, 'cact'], w=[pk])
            mv = self.mod[:, i * 144:(i + 1) * 144].rearrange("p (o two) -> p o two", two=2)
            bv = bank[:, 0:144].rearrange("p (o two) -> p o two", two=2)
            ab = self.adab[:, i * 72:(i + 1) * 72].unsqueeze(2).broadcast_to([128, 72, 2])
            self.tt('dve', mv, bv, ab, ALU.add, r=[pk, 'adab'], w=['mod'])
            for s in range(3):
                for q in range(2):
                    def m(j):
                        return self.mod[:, i * 144:(i + 1) * 144].rearrange("p (j k two) -> p j two k", j=9, two=2)[:, j, q, :]
                    gpre = self.ng[:, i * 48 + (2 * s) * 8:i * 48 + (2 * s) * 8 + 8]
                    gpost = self.ng[:, i * 48 + (2 * s + 1) * 8:i * 48 + (2 * s + 1) * 8 + 8]
                    self.stt(self.dslice(i, s, 0, q), m(3 * s + 1), 1.0, gpre, ALU.add, ALU.mult, r=['mod', 'ng'], w=['der'])
                    self.cp('dve', self.dslice(i, s, 1, q), m(3 * s + 0), r=['mod'], w=['der'])
                    self.stt(self.dslice(i, s, 2, q), m(3 * s + 2), 0.5 if s != 1 else 1.0, gpost, ALU.mult, ALU.mult,
                             r=['mod', 'ng'], w=['der'])

    def dslice(self, i, s, which, q, kc=None):
        base = (((i * 3 + s) * 3 + which) * 2 + q) * 8
        if kc is None:
            return self.der[:, base:base + 8]
        return self.der[:, base + kc:base + kc + 1]

    def prepass_weights(self, ffn_w_in, ffn_w_out):
        I = self.I
        self.wblock = {'wb_ffn_in': D * 2 * DFF, 'wb_ffn_out': DFF * D, 'wb_ssd_in': D * 5152, 'wb_ssd_out': 2048 * D,
                       'wb_sb_qkv': D * 3 * D, 'wb_sb_out': D * D, 'wb_swa_qkv': D * 1536, 'wb_swa_out': D * D}

        def cast(dst, src, nm, idx, split=None):
            if split:
                dst = dst.rearrange("k (a b) -> k a b", a=split)
                src = src.rearrange("k (a b) -> k a b", a=split)
            self.dma('pool', dst, src, r=[], w=[('wb', nm, idx)], semkey=('wcast', nm, idx))

        def ffn(i, s_):
            cast(self.wb_ffn_in[i, s_], ffn_w_in[i, s_], 'wb_ffn_in', i * 2 + s_, 4)
            cast(self.wb_ffn_out[i, s_], ffn_w_out[i, s_], 'wb_ffn_out', i * 2 + s_)

        for i in range(DEPTH):
            ffn(i, 0)
            if i % 3 == 0:
                j = i // 3
                cast(self.wb_ssd_in[j], I['ssd_w_in'][j], 'wb_ssd_in', j, 4)
                cast(self.wb_ssd_out[j], I['ssd_w_out'][j], 'wb_ssd_out', j)
            elif i % 3 == 1:
                cast(self.wb_sb_qkv, I['sb_w_qkv'], 'wb_sb_qkv', 0, 2)
                cast(self.wb_sb_out, I['sb_w_out'], 'wb_sb_out', 0)
            else:
                cast(self.wb_swa_qkv, I['swa_w_qkv'], 'wb_swa_qkv', 0)
                cast(self.wb_swa_out, I['swa_w_out'], 'wb_swa_out', 0)
            ffn(i, 1)

    def load_fm(self, src, N):
        nb = (N + 127) // 128
        for tb in range(nb):
            n = min(128, N - tb * 128)
            st = self.stg[self.stgn % 2]
            sk = ('stg', self.stgn % 2)
            self.stgn += 1
            self.dma('pool', st[0:n, :], src[tb * 128:tb * 128 + n, :], r=[], w=[sk], semkey=sk)
            for half in range(2):
                bank, pk = self.ps()
                for k4 in range(4):
                    k = half * 4 + k4
                    self.tr(bank[:, k4 * 128:k4 * 128 + n], st[0:n, k * 128:(k + 1) * 128], self.identf[0:n, 0:n],
                            r=[sk, 'cf'], w=[pk])
                xo = self.xv(self.X, N)[:, half * 4:half * 4 + 4, tb * 128:tb * 128 + n]
                pv = bank[:, 0:512].rearrange("p (k n) -> p k n", k=4)[:, :, 0:n]
                self.cp('dve' if half == 0 else 'act', xo, pv, r=[pk], w=[('X', half * 4 + j) for j in range(4)])

    def store_fm(self, dst, N, src=None, skey='X', nchunk=KC, vtm=None, vkeys=None, only_last=False, kofs=0):
        if src is None:
            src = self.xv(self.X, N)
        nb = (N + 127) // 128
        W = nchunk * 128
        for tb in range(nb):
            if only_last and tb != nb - 1:
                continue
            n = min(128, N - tb * 128)
            st = self.stg[self.stgn % 2]
            sk = ('stg', self.stgn % 2)
            self.stgn += 1
            for half in range((nchunk + 3) // 4):
                bank, pk = self.ps()
                nk4 = min(4, nchunk - half * 4)
                for k4 in range(nk4):
                    k = half * 4 + k4
                    self.tr(bank[0:n, k4 * 128:(k4 + 1) * 128], src[:, k, tb * 128:tb * 128 + n],
                            self.identf, r=[(skey, kofs + k), 'cf'], w=[pk])
                self.cp('dve' if half == 0 else 'act', st[0:n, half * 512:half * 512 + nk4 * 128], bank[0:n, 0:nk4 * 128],
                        r=[pk], w=[sk])
            if dst is not None:
                d = dst[0:n, :] if only_last else dst[tb * 128:tb * 128 + n, :]
                self.dma('pool', d, st[0:n, 0:W], r=[sk], w=[], semkey=sk)
            if vtm is not None:
                self.cp('pool', vtm(tb, n), st[0:n, 0:W], r=[sk], w=vkeys(tb))

    def sumsq_rstd(self, sq_view, sq_keys, N, nch, dim):
        bank, pk = self.ps()
        for k in range(nch):
            self.mm(bank[:, 0:N], self.onesb[:], sq_view(k), k == 0, k == nch - 1, r=['onesb', sq_keys[k]], w=[pk])
        t = self.tmpf[3]
        self.act(t[:, 0:N], bank[:, 0:N], AF.Sqrt, r=[pk, 'epsb'], w=['tmp3'], bias=self.epsb[:, 0:1], scale=1.0 / dim)
        self.P.op('dve', lambda e, o=self.rstd[:, 0:N], i=t[:, 0:N]: e.reciprocal(o, i), r=['tmp3'], w=['rstd'])

    def prenorm(self, i, s, q, N):
        Xv = self.xv(self.X, N)
        SQv = self.xv(self.SQ, N)
        for k in range(KC):
            self.act(SQv[:, k, :], Xv[:, k, :], AF.Square, r=[('X', k)], w=[('SQ', k)])
        self.sumsq_rstd(lambda k: SQv[:, k, :], [('SQ', k) for k in range(KC)], N, KC, D)
        Hv = self.xv(self.Hb, N)
        for k in range(KC):
            tn = k % 3
            t = self.tmpf[tn]
            self.stt(t[:, 0:N], Xv[:, k, :], self.dslice(i, s, 0, q, k), self.rstd[:, 0:N], ALU.mult, ALU.mult,
                     r=[('X', k), 'der', 'rstd'], w=[('tmp', tn)])
            self.act(Hv[:, k, :], t[:, 0:N], AF.Identity, r=[('tmp', tn), 'der'], w=[('Hb', k)],
                     bias=self.dslice(i, s, 1, q, k))

    def postnorm_resid(self, i, s, q, N):
        Xv = self.xv(self.X, N)
        SQv = self.xv(self.SQ, N)
        Yv = self.xv(self.Yf, N)
        self.sumsq_rstd(lambda k: SQv[:, k, :], [('SQ', k) for k in range(KC)], N, KC, D)
        for k in range(KC):
            tn = k % 3
            t = self.tmpf[tn]
            self.tt('dve', t[:, 0:N], Yv[:, k, :], self.rstd[:, 0:N], ALU.mult, r=[('Yf', k), 'rstd'], w=[('tmp', tn)])
            self.tt('pool', Xv[:, k, :], Xv[:, k, :], t[:, 0:N], ALU.add, r=[('tmp', tn), ('X', k)], w=[('X', k)])

    def evac_y(self, i, s, q, N, k, bank, pk):
        Yv = self.xv(self.Yf, N)
        SQv = self.xv(self.SQ, N)
        self.act(Yv[:, k, :], bank[:, 0:N], AF.Identity, r=[pk, 'der'], w=[('Yf', k)], scale=self.dslice(i, s, 2, q, k))
        self.act(SQv[:, k, :], bank[:, 0:N], AF.Square, r=[pk], w=[('SQ', k)])

    def ffn(self, i, which, q, N):
        s = 0 if which == 0 else 2
        self.prenorm(i, s, q, N)
        Hv = self.xv(self.Hb, N)
        Gv = self.xv(self.G, N, FC)
        win = self.wb_ffn_in[i, which]
        wout = self.wb_ffn_out[i, which]
        for jp in range(FC // 2):
            wa, ka = self.ring_load(win[:, jp * 256:(jp + 1) * 256].rearrange("(k p) w -> p k w", p=128), KC, 256, r=['wb'])
            wb_, kb = self.ring_load(win[:, DFF + jp * 256:DFF + (jp + 1) * 256].rearrange("(k p) w -> p k w", p=128), KC,
                                     256, r=['wb'])
            pa = []
            for c in range(2):
                bank, pk = self.ps()
                for k in range(KC):
                    self.mm(bank[:, 0:N], wa[:, k, c * 128:(c + 1) * 128], Hv[:, k, :], k == 0, k == KC - 1,
                            r=[ka, ('Hb', k)], w=[pk])
                pa.append((bank, pk))
            for c in range(2):
                bank, pk = self.ps()
                for k in range(KC):
                    self.mm(bank[:, 0:N], wb_[:, k, c * 128:(c + 1) * 128], Hv[:, k, :], k == 0, k == KC - 1,
                            r=[kb, ('Hb', k)], w=[pk])
                tn = self.tmpn.get('ffn', 0)
                self.tmpn['ffn'] = tn + 1
                t = self.tmpf[tn % 3]
                tk = ('tmp', tn % 3)
                self.act(t[:, 0:N], pa[c][0][:, 0:N], AF.Silu, r=[pa[c][1]], w=[tk])
                j = jp * 2 + c
                self.tt('dve', Gv[:, j, :], t[:, 0:N], bank[:, 0:N], ALU.mult, r=[tk, pk], w=[('G', j)])
        HF = FC // 2
        for oc in range(KC):
            bank, pk = self.ps()
            for hf in range(2):
                wv, wk = self.ring_load(wout[hf * HF * 128:(hf + 1) * HF * 128, oc * 128:(oc + 1) * 128].rearrange(
                    "(k p) w -> p k w", p=128), HF, 128, r=['wb'])
                for k in range(HF):
                    kk = hf * HF + k
                    self.mm(bank[:, 0:N], wv[:, k, :], Gv[:, kk, :], kk == 0, kk == FC - 1, r=[wk, ('G', kk)], w=[pk])
            self.evac_y(i, s, q, N, oc, bank, pk)
        self.postnorm_resid(i, s, q, N)


    def ssd_alloc(self):
        sb = self.sb
        self.hist = sb("hist", [128, 2 * 72], F32)
        self.cwcol = sb("cwcol", [128, 2 * 96], F32)
        self.cbcol = sb("cbcol", [128, 2 * 24], F32)
        self.sngcol = sb("sngcol", [128, 2 * 16], F32)
        self.Dcol = sb("Dcol", [128, 2 * 16], F32)
        self.dtb3 = sb("dtb3", [96, 2], F32)
        self.A3 = sb("A3", [96, 2], F32)
        self.ARX = sb("ARX", [128, 8 * 512], BF16)
        self.a3parts = [self.ARX[0:96, (4 + i) * 512:(5 + i) * 512] for i in range(3)]
        self.a3 = sb("a3", [96, T], BF16)
        self.cols = sb("cols", [128, 4 * 96], F32)
        self.cdrep = sb("cdrep", [128, 4 * 32], F32)
        self.dg = sb("dg", [96, 32], BF16)
        self.ones32 = sb("ones32", [96, 128], F32)
        self.cbuf = [sb("cbuf%d" % i, [128, T + 3], F32) for i in range(2)]
        self.xsfm = [sb("xsfm%d" % i, [128, T], BF16) for i in range(4)]
        self.xtm = [self.ARX[:, i * 512:(i + 1) * 512] for i in range(4)]
        self.xw = [self.ARX[:, (4 + i) * 512:(5 + i) * 512] for i in range(4)]
        self.zs = [sb("zs%d" % i, [128, T], BF16) for i in range(4)]
        self.Ea = [sb("Ea%d" % i, [128, 128], F32) for i in range(2)]
        self.E = [sb("E%d" % i, [128, 128], F32) for i in range(2)]
        self.Cp = [sb("Cp%d" % i, [128, 128], BF16) for i in range(2)]
        self.Mm = [sb("Mm%d" % i, [128, 128], BF16) for i in range(2)]
        self.Sbf = sb("Sbf", [128, 2048], BF16)
        self.sqy = [sb("sqy%d" % i, [128, 128], BF16) for i in range(2)]

    def ssd_params(self):
        I = self.I
        for j in range(2):
            for h in range(2):
                self.load_cols(self.cwcol[:, j * 96 + h * 48:j * 96 + h * 48 + 48], I['ssd_conv_w'][j, h * 6144:(h + 1) * 6144], 'cwcol')
            self.load_cols(self.cbcol[:, j * 24:(j + 1) * 24], I['ssd_conv_b'][j], 'cbcol')
            self.load_cols(self.sngcol[:, j * 16:(j + 1) * 16], I['ssd_norm_g'][j], 'sngcol')
            for hh in range(2):
                src = bass.AP(I['ssd_d'].tensor, j * 32 + hh, [[0, 64], [2, 16]])
                self.dma('pool', self.Dcol[hh * 64:(hh + 1) * 64, j * 16:(j + 1) * 16], src, r=[], w=['Dcol'], semkey='Dcol',
                         slow=True, new_batch=False)
            for g in range(3):
                self.dma('pool', self.dtb3[g * 32:(g + 1) * 32, j:j + 1], bass.AP(I['ssd_dt_bias'].tensor, j * 32, [[1, 32], [1, 1]]),
                         r=[], w=['dtb3'], semkey='dtb3', slow=True, new_batch=False)
                self.dma('pool', self.A3[g * 32:(g + 1) * 32, j:j + 1], bass.AP(I['ssd_a_log'].tensor, j * 32, [[1, 32], [1, 1]]),
                         r=[], w=['A3'], semkey='A3', slow=True, new_batch=False)
        self.act(self.A3[:], self.A3[:], AF.Exp, r=['A3'], w=['A3'])
        self.ts('dve', self.A3[:], self.A3[:], -1.0, ALU.mult, r=['A3'], w=['A3'])
        self.memset('dve', self.ones32[:], 1.0, w=['ones32'])

    def conv_silu(self, j, cc, bank, pk, N, out_ap, out_keys):
        n = self.cbn % 2
        self.cbn += 1
        cb, ck = self.cbuf[n], ('cbuf', n)
        hs = self.hist[:, j * 72 + cc * 3:j * 72 + cc * 3 + 3]
        hk = ('hist', j, cc)
        self.cp('pool', cb[:, 0:3], hs, r=[hk], w=[ck])
        self.cp('act', cb[:, 3:3 + N], bank[:, 0:N], r=[pk], w=[ck])
        self.cp('pool', hs, cb[:, N:N + 3], r=[ck], w=[hk])
        tn = self.tmpn.get('conv', 0)
        self.tmpn['conv'] = tn + 1
        acc, ak = self.tmpf[tn % 2][:, 0:N], ('tmp', tn % 2)

        def wc(tap):
            c0 = j * 96 + tap * 24 + cc
            return self.cwcol[:, c0:c0 + 1]
        self.ts('dve', acc, cb[:, 0:N], wc(0), ALU.mult, r=[ck, 'cwcol', 'cbcol'], w=[ak],
                s2=self.cbcol[:, j * 24 + cc:j * 24 + cc + 1], op1=ALU.add)
        for tap in range(1, 4):
            self.stt(acc, cb[:, tap:tap + N], wc(tap), acc, ALU.mult, ALU.add, r=[ck, 'cwcol', ak], w=[ak])
        self.act(out_ap, acc, AF.Silu, r=[ak], w=out_keys)

    def ssd(self, i, q, N, t, last):
        j = i // 3
        Q = 128 if q == 0 else 32
        nch = N // Q
        I, O = self.I, self.O
        win = self.wb_ssd_in[j]
        wout = self.wb_ssd_out[j]
        self.psrot = list(range(7))
        self.prenorm(i, 1, q, N)
        Hv = self.xv(self.Hb, N)
        S = self.AR8
        Skeys = [('S', x) for x in range(16)]
        Sbkeys = [('Sbf', x) for x in range(16)]
        hkeys = [('hist', j, cc) for cc in range(24)]
        histv = self.hist[:, j * 72:(j + 1) * 72].rearrange("p (c t) -> p c t", t=3)
        if q == 0 and t == 0:
            self.memset('pool', S[:], 0.0, w=Skeys)
            self.memset('pool', self.hist[:, j * 72:(j + 1) * 72], 0.0, w=hkeys)
        elif q == 0:
            self.dma('pool', S[:], self.Sd[j], r=[('Sd', j)], w=Skeys, semkey='Sld')
        else:
            for g4 in range(4):
                st, sk = self.stg[self.stgn % 2], ('stg', self.stgn % 2)
                self.stgn += 1
                self.dma('pool', st[:, 0:512].rearrange("p (b n) -> p b n", b=4),
                         I['state_ssm'][j, g4 * 512:(g4 + 1) * 512, :].rearrange("(b p) n -> p b n", p=128), r=[], w=[sk], semkey=sk)
                bank, pk = self.ps()
                for b4 in range(4):
                    self.tr(bank[:, b4 * 128:(b4 + 1) * 128], st[:, b4 * 128:(b4 + 1) * 128], self.identf, r=[sk, 'cf'], w=[pk])
                self.cp('dve', S[:, g4 * 512:(g4 + 1) * 512], bank[:, 0:512], r=[pk], w=Skeys[g4 * 4:g4 * 4 + 4])
            for tt_ in range(3):
                self.dma('pool', histv[:, :, tt_], I['state_conv'][j, tt_].rearrange("(c p) -> p c", p=128), r=[], w=hkeys,
                         semkey='histld', slow=True, new_batch=(tt_ == 0))
        self.cp('pool', self.Sbf[:], S[:], r=Skeys, w=Sbkeys)

        wv, wk = self.ring_load(win[:, 5120:5152].rearrange("(k p) w -> p k w", p=128), KC, 32, r=['wb'])
        bank, pk = self.ps()
        for g in range(3):
            for k in range(KC):
                self.mm(bank[g * 32:(g + 1) * 32, 0:N], wv[:, k, :], Hv[:, k, :], k == 0, k == KC - 1, r=[wk, ('Hb', k)], w=[pk])
        SQf = self.SQ[:].bitcast(F32)
        dtt, dA, acum, wst = [SQf[0:96, x * 512:x * 512 + N] for x in range(4)]
        kdt, kdA, kac, kw = [[('SQ', 2 * x), ('SQ', 2 * x + 1)] for x in range(4)]
        self.act(dtt, bank[0:96, 0:N], AF.Exp, r=[pk, 'dtb3'], w=kdt, bias=self.dtb3[:, j:j + 1])
        self.act(dtt, dtt, AF.Ln, r=kdt + ['oneb'], w=kdt, bias=self.oneb[0:96, 0:1])
        self.ts('dve', dA, dtt, self.A3[:, j:j + 1], ALU.mult, r=kdt + ['A3'], w=kdA)
        for c in range(nch):
            self.P.op('dve', lambda e, o=acum[:, c * Q:(c + 1) * Q], d0=self.ones32[:, 0:Q], d1=dA[:, c * Q:(c + 1) * Q]:
                      e.tensor_tensor_scan(o, d0, d1, 0.0, ALU.mult, ALU.add), r=kdA + ['ones32'], w=kac)
        H3, M3, L3 = [x[:, 0:N] for x in self.a3parts]
        kH3, kM3, kL3 = ('ARX', 4), ('ARX', 5), ('ARX', 6)
        r1, r2, nac = [self.tmpf[x][0:96, 0:N] for x in range(3)]
        self.cp('dve', H3, acum, r=kac, w=[kH3])
        self.tt('dve', r1, acum, H3, ALU.subtract, r=kac + [kH3], w=[('tmp', 0)])
        self.cp('dve', M3, r1, r=[('tmp', 0)], w=[kM3])
        self.tt('dve', r2, r1, M3, ALU.subtract, r=[('tmp', 0), kM3], w=[('tmp', 1)])
        self.cp('dve', L3, r2, r=[('tmp', 1)], w=[kL3])
        a3 = self.a3
        self.cp('pool', a3[0:32, 0:N], H3[0:32], r=[kH3], w=['a3'])
        self.cp('pool', a3[32:64, 0:N], M3[32:64], r=[kM3], w=['a3'])
        self.cp('pool', a3[64:96, 0:N], L3[64:96], r=[kL3], w=['a3'])
        for c in range(nch):
            self.act(wst[:, c * Q:(c + 1) * Q], acum[:, c * Q:(c + 1) * Q], AF.Exp, r=kac, w=kw, scale=-1.0,
                     bias=acum[:, (c + 1) * Q - 1:(c + 1) * Q])
        self.tt('dve', wst, wst, dtt, ALU.mult, r=kw + kdt, w=kw)
        self.ts('dve', nac, acum, -1.0, ALU.mult, r=kac, w=[('tmp', 2)])
        for c in range(nch):
            bank, pk = self.ps()
            for x, (src, sk_) in enumerate(((nac, [('tmp', 2)]), (dtt, kdt), (wst, kw))):
                self.tr(bank[0:Q, x * 32:(x + 1) * 32], src[0:32, c * Q:(c + 1) * Q], self.identf[0:32, 0:32], r=sk_ + ['cf'], w=[pk])
            self.cp('dve', self.cols[0:Q, c * 96:(c + 1) * 96], bank[0:Q, 0:96], r=[pk], w=['cols'])
        bank, pk = self.ps()
        for c in range(nch):
            self.ts('dve', self.dg[:], self.cbc('i3', 96), a3[:, (c + 1) * Q - 1:(c + 1) * Q], ALU.mult, r=['cb', 'a3'], w=['dg'])
            self.mm(bank[:, c * 32:(c + 1) * 32], self.onesb[0:96, :], self.dg[:], True, True, r=['onesb', 'dg'], w=[pk])
        self.act(self.cdrep[:, 0:nch * 32], bank[:, 0:nch * 32], AF.Exp, r=[pk], w=['cdrep'])

        Yfb = self.Yf[:].bitcast(BF16)
        Bfm = Yfb[:, 0:2048].rearrange("p (g n) -> p g n", g=4)
        Cfm = Yfb[:, 2048:4096].rearrange("p (g n) -> p g n", g=4)
        Btm = Yfb[:, 4096:6144].rearrange("p (c n) -> p c n", c=4)
        cbm = Yfb[:, 6144:8192].rearrange("p (x n) -> p x n", x=16)
        kB, kC, kBt, kcb = [[('Yf', 2 * x), ('Yf', 2 * x + 1)] for x in range(4)]
        for g8 in range(8):
            col0 = 4096 + g8 * 128
            wv, wk = self.ring_load(win[:, col0:col0 + 128].rearrange("(k p) w -> p k w", p=128), KC, 128, r=['wb'])
            bank, pk = self.ps()
            for k in range(KC):
                self.mm(bank[:, 0:N], wv[:, k, :], Hv[:, k, :], k == 0, k == KC - 1, r=[wk, ('Hb', k)], w=[pk])
            dest = (Bfm if g8 < 4 else Cfm)[:, g8 % 4, 0:N]
            self.conv_silu(j, 16 + g8, bank, pk, N, dest, kB if g8 < 4 else kC)
        for c in range(nch):
            bank, pk = self.ps()
            bb = bank[:].bitcast(BF16)
            for g in range(4):
                self.tr(bb[0:Q, g * 128:(g + 1) * 128], Bfm[:, g, c * Q:(c + 1) * Q], self.identb[:], r=kB + ['identb'], w=[pk])
            self.cp('act', Btm[0:Q, c, :], bb[0:Q, 0:512], r=[pk], w=kBt)
        mle = self.cbc('mask_le', Q, 0, Q)
        for g in range(4):
            bank, pk = self.ps()
            for c in range(nch):
                self.mm(bank[0:Q, c * 128:c * 128 + Q], Bfm[:, g, c * Q:(c + 1) * Q], Cfm[:, g, c * Q:(c + 1) * Q], True, True,
                        r=kB + kC, w=[pk])
            pv = bank[0:Q, 0:nch * 128].rearrange("p (c n) -> p c n", c=nch)[:, :, 0:Q]
            self.tt('dve', cbm[0:Q, g * 4:g * 4 + nch, 0:Q], pv, mle.unsqueeze(1).broadcast_to([Q, nch, Q]), ALU.mult,
                    r=[pk, 'cb'], w=kcb)

        Gv = self.xv(self.G, N, FC)
        ssb, ssk = self.psum[7], ('ps', 7)
        RB = [(self.psum[x], ('ps', x)) for x in (0, 1)]
        YB = [(self.psum[x], ('ps', x)) for x in (2, 3)]
        SNB = [(self.psum[x], ('ps', x)) for x in (4, 5)]
        sel3 = self.cbc('sel3', 96)
        negmask = self.cbc('negmask', Q, 0, Q)
        for g in range(4):
            for pi in range(4):
                jj = 4 * g + pi
                xs_, xk = self.xsfm[pi], ('xsfm', pi)
                wv, wk = self.ring_load(win[:, 2048 + jj * 128:2048 + (jj + 1) * 128].rearrange("(k p) w -> p k w", p=128), KC, 128, r=['wb'])
                bank, pk = self.ps()
                for k in range(KC):
                    self.mm(bank[:, 0:N], wv[:, k, :], Hv[:, k, :], k == 0, k == KC - 1, r=[wk, ('Hb', k)], w=[pk])
                self.conv_silu(j, jj, bank, pk, N, xs_[:, 0:N], [xk])
                wv, wk = self.ring_load(win[:, jj * 128:(jj + 1) * 128].rearrange("(k p) w -> p k w", p=128), KC, 128, r=['wb'])
                bank, pk = self.ps()
                for k in range(KC):
                    self.mm(bank[:, 0:N], wv[:, k, :], Hv[:, k, :], k == 0, k == KC - 1, r=[wk, ('Hb', k)], w=[pk])
                self.act(self.zs[pi][:, 0:N], bank[:, 0:N], AF.Silu, r=[pk], w=[('zs', pi)])
            for pi in range(4):
                jj = 4 * g + pi
                xs_, xk = self.xsfm[pi], ('xsfm', pi)
                bank, pk = self.ps()
                bb = bank[:].bitcast(BF16)
                for c in range(nch):
                    self.tr(bb[0:Q, c * 128:(c + 1) * 128], xs_[:, c * Q:(c + 1) * Q], self.identb[:], r=[xk, 'identb'], w=[pk])
                xt, xtk = self.xtm[pi], ('ARX', pi)
                self.cp('act', xt[0:Q, 0:nch * 128], bb[0:Q, 0:nch * 128], r=[pk], w=[xtk])
                xwt, xwk = self.xw[pi], ('ARX', 4 + pi)
                wcv = self.cols[0:Q, 0:nch * 96].rearrange("p (c x) -> p c x", c=nch)[:, :, 64 + 2 * jj:64 + 2 * jj + 2]
                self.tt('dve', xwt[0:Q, 0:nch * 128].rearrange("p (c h d) -> p c h d", c=nch, h=2),
                        xt[0:Q, 0:nch * 128].rearrange("p (c h d) -> p c h d", c=nch, h=2),
                        wcv.unsqueeze(3).broadcast_to([Q, nch, 2, 64]), ALU.mult, r=[xtk, 'cols'], w=[xwk])

            items = [(c, pi, hh) for c in range(nch) for pi in range(4) for hh in range(2)]

            def sR(x):
                c, pi, hh = items[x]
                h = 2 * (4 * g + pi) + hh
                rb, rpk = RB[x % 2]
                sel = sel3[:, h * 128:(h + 1) * 128]
                a3c = a3[:, c * Q:(c + 1) * Q]
                self.mm(rb[:, 0:Q], sel, a3c, True, True, r=['cb', 'a3'], w=[rpk])
                self.mm(rb[0:Q, 128:128 + Q], sel[:, 0:Q], a3c, True, False, r=['cb', 'a3'], w=[rpk])
                self.mm(rb[0:Q, 128:128 + Q], self.identb[0:Q, 0:Q], negmask, False, True, r=['cb', 'identb'], w=[rpk])

            def sA(x):
                c, pi, hh = items[x]
                h = 2 * (4 * g + pi) + hh
                rb, rpk = RB[x % 2]
                e = x % 2
                self.act(self.Ea[e][:, 0:Q], rb[:, 0:Q], AF.Exp, r=[rpk], w=[('Ea', e)])
                self.act(self.E[e][0:Q, 0:Q], rb[0:Q, 128:128 + Q], AF.Exp, r=[rpk, 'cols'], w=[('E', e)],
                         bias=self.cols[0:Q, c * 96 + h:c * 96 + h + 1])

            def sD(x):
                c, pi, hh = items[x]
                h = 2 * (4 * g + pi) + hh
                e = x % 2
                self.tt('dve', self.Cp[e][:, 0:Q], Cfm[:, g, c * Q:(c + 1) * Q], self.Ea[e][:, 0:Q], ALU.mult,
                        r=kC + [('Ea', e)], w=[('Cp', e)])
                self.stt(self.Mm[e][0:Q, 0:Q], self.E[e][0:Q, 0:Q], self.cols[0:Q, c * 96 + 32 + h:c * 96 + 32 + h + 1],
                         cbm[0:Q, g * 4 + c, 0:Q], ALU.mult, ALU.mult, r=[('E', e), 'cols'] + kcb, w=[('Mm', e)])

            def sY(x):
                c, pi, hh = items[x]
                jj = 4 * g + pi
                h = 2 * jj + hh
                e = x % 2
                ybank, ypk = YB[(x // 2) % 2]
                self.mm(ybank[hh * 64:(hh + 1) * 64, 0:Q], self.Sbf[:, h * 64:(h + 1) * 64], self.Cp[e][:, 0:Q], True, False,
                        r=[('Sbf', jj), ('Cp', e)], w=[ypk])
                self.mm(ybank[hh * 64:(hh + 1) * 64, 0:Q], self.xtm[pi][0:Q, c * 128 + hh * 64:c * 128 + (hh + 1) * 64],
                        self.Mm[e][0:Q, 0:Q], False, True, r=[('ARX', pi), ('Mm', e)], w=[ypk])

            def pP1(x):
                c, pi, hh = items[x]
                jj = 4 * g + pi
                pp = (x // 2) % 2
                ybank, ypk = YB[pp]
                xs_, xk = self.xsfm[pi], ('xsfm', pi)
                y1, y1k = self.tmpf[pp][:, 0:Q], self.tk(pp)
                y2, y2k = self.tmpf[2 + pp][:, 0:Q], self.tk(2 + pp)
                self.stt(y1, xs_[:, c * Q:(c + 1) * Q], self.Dcol[:, j * 16 + jj:j * 16 + jj + 1], ybank[:, 0:Q], ALU.mult, ALU.add,
                         r=[xk, 'Dcol', ypk], w=[y1k])
                self.tt('dve', y2, y1, self.zs[pi][:, c * Q:(c + 1) * Q], ALU.mult, r=[y1k, ('zs', pi)], w=[y2k])
                self.act(Gv[:, jj, c * Q:(c + 1) * Q], y2, AF.Identity, r=[y2k, 'sngcol'], w=[('G', jj)],
                         scale=self.sngcol[:, j * 16 + jj:j * 16 + jj + 1])
                self.act(self.sqy[pp][:, 0:Q], y2, AF.Square, r=[y2k], w=[('sqy', pp)])
                first = (g == 0 and x == 1)
                lastm = (g == 3 and x == len(items) - 1)
                self.mm(ssb[:, c * Q:(c + 1) * Q], self.onesb[:], self.sqy[pp][:, 0:Q], first, lastm, r=['onesb', ('sqy', pp)],
                        w=[ssk], sgc=True)
                sn, snk = SNB[pp]
                self.mm(sn[:, 0:128], Btm[0:Q, c, g * 128:(g + 1) * 128], self.xw[pi][0:Q, c * 128:(c + 1) * 128], True, True,
                        r=kBt + [('ARX', 4 + pi)], w=[snk])

            def pP2(x):
                c, pi, hh = items[x]
                jj = 4 * g + pi
                pp = (x // 2) % 2
                sn, snk = SNB[pp]
                Sp = S[:, jj * 128:(jj + 1) * 128]
                Sp3 = Sp.rearrange("p (h d) -> p h d", h=2)
                cdv = self.cdrep[:, c * 32 + 2 * jj:c * 32 + 2 * jj + 2].unsqueeze(2).broadcast_to([128, 2, 64])
                self.tt('dve', Sp3, Sp3, cdv, ALU.mult, r=[('S', jj), 'cdrep'], w=[('S', jj)])
                self.tt('dve', Sp, Sp, sn[:, 0:128], ALU.add, r=[('S', jj), snk], w=[('S', jj)])
                self.cp('pool', self.Sbf[:, jj * 128:(jj + 1) * 128], Sp, r=[('S', jj)], w=[('Sbf', jj)])

            n_it = len(items)
            for it in range(n_it + 6):
                if it < n_it:
                    sR(it)
                if 0 <= it - 1 < n_it:
                    sA(it - 1)
                if 0 <= it - 2 < n_it:
                    sD(it - 2)
                if 0 <= it - 3 < n_it:
                    sY(it - 3)
                if 0 <= it - 4 < n_it and (it - 4) % 2 == 1:
                    pP1(it - 4)
                if 0 <= it - 5 < n_it and (it - 5) % 2 == 1:
                    pP2(it - 5)

        t3 = self.tmpf[3]
        self.act(t3[:, 0:N], ssb[:, 0:N], AF.Sqrt, r=[ssk, 'epsb'], w=['tmp3'], bias=self.epsb[:, 0:1], scale=1.0 / 2048)
        self.P.op('dve', lambda e, o=self.rstd[:, 0:N], i_=t3[:, 0:N]: e.reciprocal(o, i_), r=['tmp3'], w=['rstd'])
        for oc in range(KC):
            wv, wk = self.ring_load(wout[:, oc * 128:(oc + 1) * 128].rearrange("(k p) w -> p k w", p=128), 16, 128, r=['wb'])
            bank, pk = self.ps()
            for k in range(16):
                self.mm(bank[:, 0:N], wv[:, k, :], Gv[:, k, :], k == 0, k == 15, r=[wk, ('G', k)], w=[pk])
            tn = oc % 3
            tq = self.tmpf[tn]
            self.tt('dve', tq[:, 0:N], bank[:, 0:N], self.rstd[:, 0:N], ALU.mult, r=[pk, 'rstd'], w=[('tmp', tn)])
            self.evac_y(i, 1, q, N, oc, tq, ('tmp', tn))
        self.postnorm_resid(i, 1, q, N)

        if q == 0 and not last:
            self.dma('pool', self.Sd[j], S[:], r=Skeys, w=[('Sd', j)], semkey='Sst')
        if last:
            dst = O['ssm_p' if q == 0 else 'ssm_s'][j]
            for g4 in range(4):
                bank, pk = self.ps()
                for b4 in range(4):
                    blk = g4 * 4 + b4
                    self.tr(bank[:, b4 * 128:(b4 + 1) * 128], S[:, blk * 128:(blk + 1) * 128], self.identf, r=[('S', blk), 'cf'], w=[pk])
                st, sk = self.stg[self.stgn % 2], ('stg', self.stgn % 2)
                self.stgn += 1
                self.cp('dve', st[:, 0:512], bank[:, 0:512], r=[pk], w=[sk])
                self.dma('pool', dst[g4 * 512:(g4 + 1) * 512, :].rearrange("(b p) n -> p b n", p=128),
                         st[:, 0:512].rearrange("p (b n) -> p b n", b=4), r=[sk], w=[], semkey=sk)
            cdst = O['conv_p' if q == 0 else 'conv_s'][j]
            for tt_ in range(3):
                self.dma('pool', cdst[tt_].rearrange("(c p) -> p c", p=128), histv[:, :, tt_], r=hkeys, w=[], semkey='histst',
                         slow=True, new_batch=(tt_ == 0))
        self.psrot = list(range(8))


    def sb_alloc(self):
        sb = self.sb
        self.KTseg = [self.ARX[:, i * 1024:(i + 1) * 1024] for i in range(2)]
        self.Vseg = [self.ARX[:, 2048 + i * 1024:2048 + (i + 1) * 1024] for i in range(2)]
        self.SPrun = self.cbuf
        self.SPrunb = self.xsfm
        self.segn = 0
        self.un = 0

    def tk(self, n):
        return ('tmp', n) if n < 3 else 'tmp3'

    def sb_prep_sample(self):
        I = self.I
        for c in range(8):
            self.dma('pool', self.Vs_s[c], I['cache_sb_v'][:, c * 128:(c + 1) * 128].rearrange("(b p) d -> p b d", p=128),
                     r=[], w=['Vs_s'], semkey='vss', new_batch=(c == 0))
        Gv = self.xv(self.G, 128, FC)
        for b in range(PAST // 128):
            st, sk = self.stg[self.stgn % 2], ('stg', self.stgn % 2)
            self.stgn += 1
            self.dma('pool', st[:, :], I['cache_sb_k'][b * 128:(b + 1) * 128, :], r=[], w=[sk], semkey=sk)
            for half in range(2):
                bank, pk = self.ps()
                for k4 in range(4):
                    k = half * 4 + k4
                    self.tr(bank[:, k4 * 128:(k4 + 1) * 128], st[:, k * 128:(k + 1) * 128], self.identf, r=[sk, 'cf'], w=[pk])
                self.cp('dve' if half == 0 else 'act', Gv[:, 14 + half * 4:14 + half * 4 + 4, :],
                        bank[:, 0:512].rearrange("p (k n) -> p k n", k=4), r=[pk], w=[('G', 14 + half * 4 + x) for x in range(4)])
            self.dma('pool', self.KTs_s[:, :, b * 128:(b + 1) * 128].rearrange("c p t -> p c t"), Gv[:, 14:22, :],
                     r=[('G', 14 + x) for x in range(8)], w=['KTs_s'], semkey='ktss')

    def sbmix(self, i, q, N, t, last):
        I, O = self.I, self.O
        self.psrot = list(range(5))
        self.prenorm(i, 1, q, N)
        Hv = self.xv(self.Hb, N)
        Gv = self.xv(self.G, N, FC)
        Yv = self.xv(self.Yf, N)
        win, wout = self.wb_sb_qkv, self.wb_sb_out
        nb = (N + 127) // 128
        Vtm = self.AR8[:].bitcast(BF16).rearrange("p (b f) -> p b f", b=4)
        vk = lambda tb: [('S', 4 * tb + x) for x in range(4)]
        allvk = [('S', x) for x in range(16)]
        for part in range(3):
            if part >= self.cfg.get('sbparts', 3):
                continue
            for c in range(8):
                col0 = part * D + c * 128
                wv, wk = self.ring_load(win[:, col0:col0 + 128].rearrange("(k p) w -> p k w", p=128), KC, 128, r=['wb'])
                bank, pk = self.ps()
                for k in range(KC):
                    self.mm(bank[:, 0:N], wv[:, k, :], Hv[:, k, :], k == 0, k == KC - 1, r=[wk, ('Hb', k)], w=[pk])
                if part == 0:
                    self.cp('act', Gv[:, c, :], bank[:, 0:N], r=[pk], w=[('G', c)])
                elif part == 1:
                    self.cp('act', Gv[:, 8 + c, :], bank[:, 0:N], r=[pk], w=[('G', 8 + c)])
                    self.cp('dve', Yv[:, c, :], bank[:, 0:N], r=[pk], w=[('Yf', c)])
                else:
                    self.cp('dve', Yv[:, c, :], bank[:, 0:N], r=[pk], w=[('Yf', c)])
            if part == 1:
                dst = O['sbk_p'][t * T:(t + 1) * T, :] if q == 0 else O['sbk_s']
                if self.cfg.get('sbdbg') == 'nodma':
                    dst = None
                if self.cfg.get('sbdbg') != 'nostore':
                    self.store_fm(dst, N, src=Yv, skey='Yf')
                if q == 0 and not last:
                    self.dma('pool', self.KTs[:, :, t * T:(t + 1) * T].rearrange("c p t -> p c t"), Gv[:, 8:16, :],
                             r=[('G', 8 + x) for x in range(8)], w=['KTs'], semkey='kts')
            elif part == 2:
                dst = O['sbv_p'][t * T:(t + 1) * T, :] if q == 0 else O['sbv_s']
                self.store_fm(dst, N, src=Yv, skey='Yf', vtm=lambda tb, n: Vtm[0:n, tb, :], vkeys=vk)
                if q == 0 and not last:
                    for tb in range(nb):
                        self.dma('pool', self.Vs[:, :, 4 * t + tb, :].rearrange("c p d -> p c d"),
                                 Vtm[:, tb, :].rearrange("p (c d) -> p c d", c=8), r=vk(tb), w=['Vs'], semkey='vs',
                                 new_batch=(tb == 0))
        if self.cfg.get('sbstage', 9) < 2:
            self.psrot = list(range(8))
            return
        OTv = Hv
        ob, ok = self.psum[7], ('ps', 7)
        npast = 4 * t if q == 0 else PAST // 128
        KTd, kdk = (self.KTs, 'KTs') if q == 0 else (self.KTs_s, 'KTs_s')
        Vd, vdk = (self.Vs, 'Vs') if q == 0 else (self.Vs_s, 'Vs_s')
        SEGB = 8
        zeros = self.cbc('zeros', 128, 0, 64)
        for c in range(8):
            units = []
            for r_ in reversed(range(nb)):
                nk = min(128, N - r_ * 128)
                for hh in range(2):
                    units.append(('in', r_, nk, hh))
            segs = [(b0, min(b0 + SEGB, npast)) for b0 in range(0, npast, SEGB)]
            for (b0, b1) in reversed(segs):
                for b in reversed(range(b0, b1)):
                    for hh in range(2):
                        units.append(('past', b, (b0, b1), hh))
            for hh in range(2):
                self.mm(ob[hh * 64:(hh + 1) * 64, 0:N], zeros, Gv[:, c, :], True, False, r=['cb', ('G', c)], w=[ok], sgc=True)
                self.mm(self.psum[6 - hh][:, 0:N], self.cbc('zeros'), Gv[:, c, :], True, False, r=['cb', ('G', c)],
                        w=[('ps', 6 - hh)], sgc=True)
            if self.cfg.get('sbstage', 9) < 3:
                units = []
            curseg = None
            prev = None
            prev2 = None
            for ui, u in enumerate(units):
                lastu = ui >= len(units) - 2
                if u[0] == 'in':
                    _, r_, nk, hh = u
                    c0 = r_ * 128
                    cur = self.sb_stage1(N, c, hh, Gv[hh * 64:(hh + 1) * 64, 8 + c, c0:c0 + nk], [('G', 8 + c)],
                                         Vtm[0:nk, r_, c * 128 + hh * 64:c * 128 + (hh + 1) * 64], vk(r_), c0, nk, True, lastu)
                else:
                    _, b, (b0, b1), hh = u
                    if curseg != (b0, b1):
                        curseg = (b0, b1)
                        si = self.segn % 2
                        self.segn += 1
                        nbl = b1 - b0
                        self.dma('sp', self.KTseg[si][:, 0:nbl * 128], KTd[c, :, b0 * 128:b1 * 128], r=[kdk], w=[('ARX', 2 * si), ('ARX', 2 * si + 1)],
                                 semkey=('kseg', si))
                        self.dma('sp', self.Vseg[si][:, 0:nbl * 128].rearrange("p (b d) -> p b d", d=128), Vd[c, :, b0:b1, :],
                                 r=[vdk], w=[('ARX', 4 + 2 * si), ('ARX', 5 + 2 * si)], semkey=('vseg', si))
                    o_ = (b - b0) * 128
                    cur = self.sb_stage1(N, c, hh, self.KTseg[si][hh * 64:(hh + 1) * 64, o_:o_ + 128], [('ARX', 2 * si), ('ARX', 2 * si + 1)],
                                         self.Vseg[si][:, o_ + hh * 64:o_ + (hh + 1) * 64], [('ARX', 4 + 2 * si), ('ARX', 5 + 2 * si)], 0, 128, False, lastu)
                if prev is not None:
                    self.sb_stage2a(*prev)
                if prev2 is not None:
                    self.sb_stage2b(*prev2)
                prev2 = prev
                prev = cur
            if prev is not None:
                self.sb_stage2a(*prev)
            if prev2 is not None:
                self.sb_stage2b(*prev2)
            if prev is not None:
                self.sb_stage2b(*prev)
            self.cp('act', OTv[:, c, :], ob[:, 0:N], r=[ok], w=[('Hb', c)])
        for oc in range(KC):
            wv, wk = self.ring_load(wout[:, oc * 128:(oc + 1) * 128].rearrange("(k p) w -> p k w", p=128), KC, 128, r=['wb'])
            bank, pk = self.ps()
            for k in range(KC):
                self.mm(bank[:, 0:N], wv[:, k, :], OTv[:, k, :], k == 0, k == KC - 1, r=[wk, ('Hb', k)], w=[pk])
            self.evac_y(i, 1, q, N, oc, bank, pk)
        self.postnorm_resid(i, 1, q, N)
        self.psrot = list(range(8))

    def sb_stage1(self, N, c, hh, KTb, kK, Vb, kV, c0, nk, diag, lastu):
        Gv = self.xv(self.G, N, FC)
        SQv = self.xv(self.SQ, N)
        ncol = N - c0
        zb, zk = self.ps()
        self.mm(zb[0:nk, 0:ncol], KTb, Gv[hh * 64:(hh + 1) * 64, c, c0:N], True, True, r=kK + [('G', c)], w=[zk])
        u = self.un
        self.un += 1
        e_t, ek = self.tmpf[u % 2][0:nk, 0:ncol], self.tk(u % 2)
        self.act(e_t, zb[0:nk, 0:ncol], AF.Exp, r=[zk], w=[ek], scale=0.125)
        spb, sbk_ = SQv[0:nk, 3 + u % 3, 0:ncol], ('SQ', 3 + u % 3)
        self.act(spb, e_t, AF.Ln, r=[ek, 'oneb'], w=[sbk_], bias=self.oneb[0:nk, 0:1])
        if diag:
            self.tt('pool', spb[:, 0:nk], spb[:, 0:nk], self.cbc('mask_lt', nk, 0, nk), ALU.mult, r=[sbk_, 'cb'], w=[sbk_])
        lsb, lk = SQv[0:nk, u % 3, 0:ncol], ('SQ', u % 3)
        self.stt(lsb, zb[0:nk, 0:ncol], 0.125, spb, ALU.mult, ALU.subtract, r=[zk, sbk_], w=[lk])
        return (N, hh, Vb, kV, c0, nk, diag, lastu, u, spb, sbk_, lsb, lk)

    def sb_stage2a(self, N, hh, Vb, kV, c0, nk, diag, lastu, u, spb, sbk_, lsb, lk):
        SQv = self.xv(self.SQ, N)
        ncol = N - c0
        acc, ack = self.psum[6 - hh], ('ps', 6 - hh)
        a_ = acc[0:nk, c0:N]
        self.mm(a_, self.cbc('negtri', nk, 0, nk), spb, False, False, r=['cb', sbk_], w=[ack], sgc=True)
        self.mm(a_, self.identb[0:nk, 0:nk], lsb, False, False, r=['identb', lk], w=[ack], sgc=True)
        if diag:
            self.mm(acc[0:nk, c0:c0 + nk], self.identb[0:nk, 0:nk], self.cbc('negmask_lt', nk, 0, nk), False, False,
                    r=['identb', 'cb'], w=[ack], sgc=True)
        w_t, wk_ = SQv[0:nk, 6 + u % 2, 0:ncol], ('SQ', 6 + u % 2)
        self.act(w_t, a_, AF.Exp, r=[ack], w=[wk_])

    def sb_stage2b(self, N, hh, Vb, kV, c0, nk, diag, lastu, u, spb, sbk_, lsb, lk):
        SQv = self.xv(self.SQ, N)
        ncol = N - c0
        ob, ok = self.psum[7], ('ps', 7)
        acc, ack = self.psum[6 - hh], ('ps', 6 - hh)
        a_ = acc[0:nk, c0:N]
        w_t, wk_ = SQv[0:nk, 6 + u % 2, 0:ncol], ('SQ', 6 + u % 2)
        self.mm(ob[hh * 64:(hh + 1) * 64, c0:N], Vb, w_t, False, lastu, r=kV + [wk_], w=[ok], sgc=True)
        if nk == 128:
            self.mm(a_, self.cbc('negtrile', nk, 0, nk), spb, False, False, r=['cb', sbk_], w=[ack], sgc=True)
        else:
            self.mm(acc[:, c0:N], self.cbc('negones', nk, 0, 128), spb, False, False, r=['cb', sbk_], w=[ack], sgc=True)
            self.mm(a_, self.cbc('postri', nk, 0, nk), spb, False, False, r=['cb', sbk_], w=[ack], sgc=True)
        self.mm(a_, self.cbc('negident', nk, 0, nk), lsb, False, False, r=['cb', lk], w=[ack], sgc=True)
        if diag:
            self.mm(acc[0:nk, c0:c0 + nk], self.identb[0:nk, 0:nk], self.cbc('posmask_lt', nk, 0, nk), False, False,
                    r=['identb', 'cb'], w=[ack], sgc=True)

    def swamix(self, i, q, N, t, last):
        I, O = self.I, self.O
        self.psrot = list(range(6))
        self.prenorm(i, 1, q, N)
        Hv = self.xv(self.Hb, N)
        Gv = self.xv(self.G, N, FC)
        Yv = self.xv(self.Yf, N)
        win, wout = self.wb_swa_qkv, self.wb_swa_out
        W = 128 + N
        base = 8 * T
        KTd = self.G[:, base:base + 4 * W].rearrange("p (g w) -> p g w", g=4)
        kK = [('G', 8 + x) for x in range(5)]
        vb0 = base + 4 * (128 + T)
        Vt = self.G[:, vb0:vb0 + 5 * 256].rearrange("p (b f) -> p b f", b=5)
        kV = [('G', 13 + x) for x in range(3)]
        nb = (N + 127) // 128
        for c in range(8):
            wv, wk = self.ring_load(win[:, c * 128:(c + 1) * 128].rearrange("(k p) w -> p k w", p=128), KC, 128, r=['wb'])
            bank, pk = self.ps()
            for k in range(KC):
                self.mm(bank[:, 0:N], wv[:, k, :], Hv[:, k, :], k == 0, k == KC - 1, r=[wk, ('Hb', k)], w=[pk])
            self.cp('act', Gv[:, c, :], bank[:, 0:N], r=[pk], w=[('G', c)])
        wv, wk = self.ring_load(win[:, 1024:1280].rearrange("(k p) w -> p k w", p=128), KC, 256, r=['wb'])
        for g in range(4):
            bank, pk = self.ps()
            for half in range(2):
                for k in range(KC):
                    self.mm(bank[half * 64:(half + 1) * 64, 0:N], wv[:, k, g * 64:(g + 1) * 64], Hv[:, k, :], k == 0, k == KC - 1,
                            r=[wk, ('Hb', k)], w=[pk])
            self.cp('act', KTd[:, g, 128:128 + N], bank[:, 0:N], r=[pk], w=kK)
        if last:
            for c2 in range(2):
                bank, pk = self.ps()
                for k in range(KC):
                    self.mm(bank[:, 0:N], wv[:, k, c2 * 128:(c2 + 1) * 128], Hv[:, k, :], k == 0, k == KC - 1,
                            r=[wk, ('Hb', k)], w=[pk])
                self.cp('dve', Yv[:, c2, :], bank[:, 0:N], r=[pk], w=[('Yf', c2)])
        wv, wk = self.ring_load(win[:, 1280:1536].rearrange("(k p) w -> p k w", p=128), KC, 256, r=['wb'])
        for c2 in range(2):
            bank, pk = self.ps()
            for k in range(KC):
                self.mm(bank[:, 0:N], wv[:, k, c2 * 128:(c2 + 1) * 128], Hv[:, k, :], k == 0, k == KC - 1, r=[wk, ('Hb', k)], w=[pk])
            self.cp('dve', Yv[:, 2 + c2, :], bank[:, 0:N], r=[pk], w=[('Yf', 2 + c2)])
        if q == 0 and t > 0:
            self.cp('pool', KTd[:, :, 0:128], self.KTprev[:].rearrange("p (g w) -> p g w", g=4), r=['KTprev'], w=kK)
            self.cp('pool', Vt[:, 0, :], self.Vprev[:], r=['Vprev'], w=kV)
        elif q == 1:
            st, sk = self.stg[self.stgn % 2], ('stg', self.stgn % 2)
            self.stgn += 1
            for dup in range(2):
                self.dma('pool', st[:, 0:512].rearrange("p (g u d) -> p g u d", g=4, u=2)[:, :, dup, :],
                         I['cache_swa_k'].rearrange("p (g d) -> p g d", g=4), r=[], w=[sk], semkey=sk, new_batch=(dup == 0))
            bank, pk = self.ps()
            for g in range(4):
                self.tr(bank[:, g * 128:(g + 1) * 128], st[:, g * 128:(g + 1) * 128], self.identf, r=[sk, 'cf'], w=[pk])
            self.cp('act', KTd[:, :, 0:128], bank[:, 0:512].rearrange("p (g w) -> p g w", g=4), r=[pk], w=kK)
            st, sk = self.stg[self.stgn % 2], ('stg', self.stgn % 2)
            self.stgn += 1
            self.dma('pool', st[:, 0:256], I['cache_swa_v'], r=[], w=[sk], semkey=sk)
            self.cp('pool', Vt[:, 0, :], st[:, 0:256], r=[sk], w=kV)
            self.dma('pool', O['swak_s'][0:96, :], I['cache_swa_k'][32:128, :], r=[], w=[], semkey='swaout', new_batch=True)
            self.dma('pool', O['swav_s'][0:96, :], I['cache_swa_v'][32:128, :], r=[], w=[], semkey='swaout', new_batch=False)
        vdst = None
        if last:
            vdst = O['swav_p'] if q == 0 else O['swav_s'][96:128, :]
        self.store_fm(vdst, N, src=Yv[:, 2:4], skey='Yf', nchunk=2, vtm=lambda tb, n: Vt[0:n, 1 + tb, :], vkeys=lambda tb: kV,
                      only_last=False, kofs=2) if not (last and q == 0) else None
        if last and q == 0:
            for tb in range(nb):
                self.store_fm(O['swav_p'] if tb == nb - 1 else None, 128, src=Yv[:, 2:4, tb * 128:(tb + 1) * 128], skey='Yf', nchunk=2,
                              vtm=lambda tb_, n, tb=tb: Vt[0:n, 1 + tb, :], vkeys=lambda tb_: kV, kofs=2)
            self.store_fm(O['swak_p'], 128, src=Yv[:, 0:2, (nb - 1) * 128:nb * 128], skey='Yf', nchunk=2)
        elif last:
            self.store_fm(O['swak_s'][96:128, :], N, src=Yv[:, 0:2], skey='Yf', nchunk=2)
        OTv = Hv
        ob, ok = self.psum[7], ('ps', 7)
        db, dk = self.psum[6], ('ps', 6)
        NM = self.cbc('nmswa')
        zeros = self.cbc('zeros', 128, 0, 64)
        SQv = self.xv(self.SQ, N)
        if q == 0:
            blocks = []
            for kb in range(-1, nb):
                if kb == -1 and t == 0:
                    continue
                q0, q1 = max(0, 128 * kb), min(N, 128 * kb + 256)
                blocks.append(((kb + 1) * 128, 128, kb + 1, q0, q1, q0 - 128 * kb))
        else:
            blocks = [(0, 128, 0, 0, N, None), (128, N, 1, 0, N, None)]
        for c in range(8):
            for hh in range(2):
                h = 2 * c + hh
                g = h // 4
                self.mm(ob[hh * 64:(hh + 1) * 64, 0:N], zeros, Gv[:, c, :], True, False, r=['cb', ('G', c)], w=[ok], sgc=True)
                self.mm(db[hh * 64:(hh + 1) * 64, 0:N], zeros, Gv[:, c, :], True, False, r=['cb', ('G', c)], w=[dk], sgc=True)
                for bi, (kc0, nk, vblk, q0, q1, p0) in enumerate(blocks):
                    nq = q1 - q0
                    lastb = bi == len(blocks) - 1
                    sb_, sk_ = self.ps()
                    self.mm(sb_[0:nk, 0:nq], KTd[hh * 64:(hh + 1) * 64, g, kc0:kc0 + nk], Gv[hh * 64:(hh + 1) * 64, c, q0:q1],
                            True, p0 is None, r=kK + [('G', c)], w=[sk_])
                    if p0 is not None:
                        self.mm(sb_[0:nk, 0:nq], self.identb[0:nk, 0:nk], NM[0:nk, p0:p0 + nq], False, True, r=['identb', 'cb'], w=[sk_])
                    u = self.un
                    self.un += 1
                    P, pk_ = SQv[0:nk, u % 4, 0:nq], ('SQ', u % 4)
                    self.act(P, sb_[0:nk, 0:nq], AF.Exp, r=[sk_], w=[pk_], scale=0.125)
                    self.mm(ob[hh * 64:(hh + 1) * 64, q0:q1], Vt[0:nk, vblk, g * 64:(g + 1) * 64], P, False, lastb, r=kV + [pk_], w=[ok],
                            sgc=True)
                    self.mm(db[hh * 64:(hh + 1) * 64, q0:q1], self.onesb[0:nk, 0:64], P, False, lastb, r=['onesb', pk_], w=[dk], sgc=True)
            den, dnk = self.tmpf[c % 2][:, 0:N], self.tk(c % 2)
            self.ts('dve', den, db[:, 0:N], self.esink[:, c:c + 1], ALU.add, r=[dk, 'esink'], w=[dnk])
            self.P.op('dve', lambda e, o=den, i_=den: e.reciprocal(o, i_), r=[dnk], w=[dnk])
            self.tt('dve', OTv[:, c, :], ob[:, 0:N], den, ALU.mult, r=[ok, dnk], w=[('Hb', c)])
        if q == 0 and not last:
            self.cp('pool', self.KTprev[:].rearrange("p (g w) -> p g w", g=4), KTd[:, :, N:N + 128], r=kK, w=['KTprev'])
            self.cp('pool', self.Vprev[:], Vt[:, nb, :], r=kV, w=['Vprev'])
        for oc in range(KC):
            wv, wk = self.ring_load(wout[:, oc * 128:(oc + 1) * 128].rearrange("(k p) w -> p k w", p=128), KC, 128, r=['wb'])
            bank, pk = self.ps()
            for k in range(KC):
                self.mm(bank[:, 0:N], wv[:, k, :], OTv[:, k, :], k == 0, k == KC - 1, r=[wk, ('Hb', k)], w=[pk])
            self.evac_y(i, 1, q, N, oc, bank, pk)
        self.postnorm_resid(i, 1, q, N)
        self.psrot = list(range(8))


def build_program(cfg):
    b = Builder(cfg)
    b.epsb = b.sb("epsb", [128, 1], F32)
    b.memset('dve', b.epsb[:], EPS, w=['epsb'])
    b.oneb = b.sb("oneb", [128, 1], F32)
    b.memset('dve', b.oneb[:], 1.0, w=['oneb'])
    nc = b.build()
    return b, nc


def make_in_maps(inputs):
    maps = []
    f = np.ascontiguousarray
    shared = {k: f(inputs[k]) for k in ('ada_w', 'ada_b', 'norm_g', 'ffn_w_in', 'ffn_w_out', 'ssd_w_in', 'ssd_conv_b',
                                        'ssd_dt_bias', 'ssd_a_log', 'ssd_d', 'ssd_norm_g', 'ssd_w_out')}
    shared['ssd_conv_w'] = f(inputs['ssd_conv_w'].reshape(2, 4 * 3072))
    shared['sb_w_qkv'] = f(inputs['sb_w_qkv'][0])
    shared['sb_w_out'] = f(inputs['sb_w_out'][0])
    shared['swa_w_qkv'] = f(inputs['swa_w_qkv'][0])
    shared['swa_w_out'] = f(inputs['swa_w_out'][0])
    shared['swa_sinks'] = f(inputs['swa_sinks'][0])
    for c in range(8):
        m = dict(shared)
        m['xp'] = f(inputs['x_prompt'][c % 4])
        m['xs'] = f(inputs['x_sample'][c])
        m['cvec'] = f(np.stack([inputs['c_prompt'][c % 4], inputs['c_sample'][c]]))
        m['state_ssm'] = f(inputs['state_ssm'][:, c].reshape(2, 2048, 128))
        m['state_conv'] = f(inputs['state_conv'][:, c])
        m['cache_sb_k'] = f(inputs['cache_sb_k'][0, c].reshape(PAST, D))
        m['cache_sb_v'] = f(inputs['cache_sb_v'][0, c].reshape(PAST, D))
        m['cache_swa_k'] = f(inputs['cache_swa_k'][0, c].reshape(128, 256))
        m['cache_swa_v'] = f(inputs['cache_swa_v'][0, c].reshape(128, 256))
        maps.append(m)
    return maps


def kernel(**inputs):
    cfg = {}
    b, nc = build_program(cfg)
    maps = make_in_maps(inputs)
    res = run_bass_kernel_spmd(nc, maps, core_ids=list(range(8)))
    r = res.results
    y_prompt = np.stack([r[c]['yp'] for c in range(4)])
    y_sample = np.stack([r[c]['ys'] for c in range(8)])
    ssm_p = np.stack([r[c]['ssm_p'].reshape(2, 32, 64, 128) for c in range(4)], axis=1)
    ssm_s = np.stack([r[c]['ssm_s'].reshape(2, 32, 64, 128) for c in range(8)], axis=1)
    conv_p = np.stack([r[c]['conv_p'] for c in range(4)], axis=1)
    conv_s = np.stack([r[c]['conv_s'] for c in range(8)], axis=1)
    sbk_p = np.stack([r[c]['sbk_p'].reshape(SEQ, 16, 64) for c in range(4)])[None]
    sbk_s = np.stack([r[c]['sbk_s'].reshape(NS, 16, 64) for c in range(8)])[None]
    sbv_p = np.stack([r[c]['sbv_p'].reshape(SEQ, 16, 64) for c in range(4)])[None]
    sbv_s = np.stack([r[c]['sbv_s'].reshape(NS, 16, 64) for c in range(8)])[None]
    sw = {}
    for nm, n in (('swak_p', 4), ('swak_s', 8), ('swav_p', 4), ('swav_s', 8)):
        sw[nm] = np.stack([r[c][nm].reshape(128, 4, 64) for c in range(n)])[None]
    return (y_prompt, y_sample, ssm_p, ssm_s, conv_p, conv_s, sbk_p, sbk_s, sbv_p, sbv_s,
            sw['swak_p'], sw['swak_s'], sw['swav_p'], sw['swav_s'])
```

```python
import numpy as np
from contextlib import ExitStack
import concourse.bass as bass
import concourse.mybir as mybir
from concourse.bass_utils import run_bass_kernel_spmd

F32 = mybir.dt.float32
BF16 = mybir.dt.bfloat16
AF = mybir.ActivationFunctionType
ALU = mybir.AluOpType

D = 1024
KC = 8
DFF = 2816
FC = 22
T = 512
SEQ = 8192
NTILE = SEQ // T
NS = 32
PAST = 1024
DEPTH = 4
EPS = 1e-6
NSLOT = 8
SLOTW = 2048


class Prog:
    def __init__(self, nc, same_sync=True):
        self.nc = nc
        self.q = {e: [] for e in ('pe', 'act', 'dve', 'pool', 'sp')}
        self.cnt = {e: 0 for e in self.q}
        self.seen = {e: {} for e in self.q}
        self.sems = {}
        self.dcnt = {}
        self.bufs = {}
        self.same_sync = same_sync

    def _waits(self, eng, r, w, deps):
        need = {}

        def add(ev):
            if ev is None:
                return
            s, v = ev
            if need.get(s, 0) < v:
                need[s] = v

        for k in r:
            b = self.bufs.get(k)
            if b:
                add(b[0])
                if isinstance(k, tuple) and k[0] == 'ps':
                    for s_, v_ in b[1].items():
                        if s_ != ('e', eng):
                            add((s_, v_))
        for k in w:
            b = self.bufs.get(k)
            if b:
                add(b[0])
                for s, v in b[1].items():
                    add((s, v))
        for d in deps:
            add(d)
        wl = []
        for s, v in need.items():
            if s == ('e', eng) and (eng == 'pe' or not self.same_sync):
                continue
            if self.seen[eng].get(s, 0) < v:
                wl.append((s, v))
                self.seen[eng][s] = v
        return wl

    def _mark(self, ev, r, w):
        for k in r:
            b = self.bufs.setdefault(k, [None, {}])
            if b[1].get(ev[0], 0) < ev[1]:
                b[1][ev[0]] = ev[1]
        for k in w:
            self.bufs[k] = [ev, {}]

    def op(self, eng, fn, r=(), w=(), deps=()):
        wl = self._waits(eng, r, w, deps)
        self.cnt[eng] += 1
        ev = (('e', eng), self.cnt[eng])
        self.q[eng].append((wl, fn, ev[0], 1))
        self._mark(ev, r, w)
        return ev

    def dma(self, eng, fn, r=(), w=(), deps=(), semkey=None, new_batch=True):
        sk = ('d', semkey)
        c = self.dcnt.get(sk, 0)
        deps = list(deps)
        if new_batch and c > 0:
            deps.append((sk, c))
        wl = self._waits(eng, r, w, deps)
        c += 16
        self.dcnt[sk] = c
        ev = (sk, c)
        self.q[eng].append((wl, fn, sk, 16))
        self._mark(ev, r, w)
        return ev

    def emit(self, es):
        nc = self.nc
        keys = [('e', e) for e in self.q] + list(self.dcnt.keys())
        for i, k in enumerate(keys):
            self.sems[k] = es.enter_context(nc.semaphore("sem%d" % i))
        block = es.enter_context(nc.Block())
        engmap = {'pe': block.tensor, 'act': block.scalar, 'dve': block.vector, 'pool': block.gpsimd,
                  'sp': block.sync}
        for e, dec in engmap.items():
            ops = self.q[e]

            def body(eng, ops=ops, e=e):
                for wl, fn, sk, inc in ops:
                    for s, v in wl:
                        eng.wait_ge(self.sems[s], v)
                    fn(eng).then_inc(self.sems[sk], inc)
                if e == 'sp':
                    for sk, c in self.dcnt.items():
                        eng.wait_ge(self.sems[sk], c)

            dec(body)


def _consts_np():
    cols = {}
    blocks = []
    off = 0

    def addc(name, arr):
        nonlocal off
        a = np.zeros((128, arr.shape[1]), np.float32)
        a[:arr.shape[0]] = arr
        cols[name] = (off, arr.shape[1])
        blocks.append(a)
        off += arr.shape[1]

    addc('ident', np.eye(128, dtype=np.float32))
    return np.concatenate(blocks, axis=1), cols


def _consts_bf_np():
    cols = {}
    blocks = []
    off = 0

    def addc(name, arr):
        nonlocal off
        a = np.zeros((128, arr.shape[1]), np.float32)
        a[:arr.shape[0]] = arr
        cols[name] = (off, arr.shape[1])
        blocks.append(a)
        off += arr.shape[1]

    s_ = np.arange(128)[:, None]
    t_ = np.arange(128)[None, :]
    k = np.arange(96)
    sel3 = np.zeros((96, 32, 128), np.float32)
    for h in range(32):
        sel3[k % 32 == h, h, :] = 1.0
    addc('sel3', sel3.reshape(96, 32 * 128))
    addc('negmask', np.where(s_ > t_, -30000.0, 0.0).astype(np.float32))
    addc('mask_le', (s_ <= t_).astype(np.float32))
    addc('i3', (k[:, None] % 32 == np.arange(32)[None, :]).astype(np.float32))
    addc('negtri', np.where(s_ > t_, -1.0, 0.0).astype(np.float32))
    addc('mask_lt', (s_ < t_).astype(np.float32))
    addc('negmask_lt', np.where(s_ >= t_, -30000.0, 0.0).astype(np.float32))
    nmswa = np.zeros((128, 256), np.float32)
    nmswa[0:64, 192:256] = -30000.0
    nmswa[64:128, 0:64] = -30000.0
    addc('nmswa', nmswa)
    addc('negones', -np.ones((128, 128), np.float32))
    addc('negtrile', np.where(s_ <= t_, -1.0, 0.0).astype(np.float32))
    addc('postri', np.where(s_ > t_, 1.0, 0.0).astype(np.float32))
    addc('negident', -np.eye(128, dtype=np.float32))
    addc('posmask_lt', np.where(s_ >= t_, 30000.0, 0.0).astype(np.float32))
    addc('zeros', np.zeros((128, 128), np.float32))
    return np.concatenate(blocks, axis=1), cols


class Builder:
    def __init__(self, cfg):
        self.cfg = cfg
        self.nc = bass.Bass("TRN2", target_bir_lowering=False)
        self.es = ExitStack()
        self.P = Prog(self.nc, same_sync=cfg.get('same_sync', True))
        self.psn = 0
        self.psrot = list(range(8))
        self.cbn = 0
        self.en = 0
        self.ringn = 0
        self.stgn = 0
        self.tmpn = {}
        self.wblock = {}

    def din(self, name, shape, dt=F32):
        return self.nc.dram_tensor(name, list(shape), dt, kind="ExternalInput").ap()

    def dout(self, name, shape, dt=F32):
        return self.nc.dram_tensor(name, list(shape), dt, kind="ExternalOutput").ap()

    def dint(self, name, shape, dt=BF16):
        return self.nc.dram_tensor(name, list(shape), dt, kind="Internal").ap()

    def sb(self, name, shape, dt):
        return self.es.enter_context(self.nc.sbuf_tensor(name, list(shape), dt))

    def ps(self):
        rot = self.psrot
        i = rot[self.psn % len(rot)]
        self.psn += 1
        return self.psum[i], ('ps', i)

    def mm(self, out, lhsT, rhs, start, stop, r, w, sgc=False):
        return self.P.op('pe', lambda e, o=out, l=lhsT, rr=rhs, s=start, t=stop, g=sgc:
                         e.matmul(o, lhsT=l, rhs=rr, start=s, stop=t, skip_group_check=g), r=r, w=w)

    def cbc(self, name, rows=128, c0=0, c1=None):
        off, w = self.cbcols[name]
        if c1 is None:
            c1 = w
        return self.cb[0:rows, off + c0:off + c1]

    def tr(self, out, in_, ident, r, w):
        return self.P.op('pe', lambda e, o=out, i=in_, d=ident: e.transpose(o, i, d), r=r, w=w)

    def act(self, out, in_, func, r, w, bias=None, scale=None):
        kw = {}
        if bias is not None:
            kw['bias'] = bias
        if scale is not None:
            kw['scale'] = scale
        return self.P.op('act', lambda e, o=out, i=in_, f=func, kw=kw: e.activation(o, i, f, **kw), r=r, w=w)

    def tt(self, eng, out, in0, in1, op, r, w):
        return self.P.op(eng, lambda e, o=out, a=in0, b=in1, p=op: e.tensor_tensor(o, a, b, p), r=r, w=w)

    def ts(self, eng, out, in0, s1, op0, r, w, s2=None, op1=None):
        if op1 is None:
            return self.P.op(eng, lambda e, o=out, a=in0, s=s1, p=op0: e.tensor_scalar(o, a, s, None, p), r=r, w=w)
        return self.P.op(eng, lambda e, o=out, a=in0, s=s1, p=op0, s2=s2, p1=op1: e.tensor_scalar(o, a, s, s2, p, p1),
                         r=r, w=w)

    def stt(self, out, in0, scalar, in1, op0, op1, r, w, eng='dve'):
        return self.P.op(eng, lambda e, o=out, a=in0, s=scalar, b=in1, p0=op0, p1=op1:
                         e.scalar_tensor_tensor(o, a, s, b, p0, p1), r=r, w=w)

    def cp(self, eng, out, in_, r, w):
        if eng == 'act':
            return self.P.op('act', lambda e, o=out, i=in_: e.copy(o, i), r=r, w=w)
        return self.P.op(eng, lambda e, o=out, i=in_: e.tensor_copy(o, i), r=r, w=w)

    def memset(self, eng, ap, val, w):
        return self.P.op(eng, lambda e, a=ap, v=val: e.memset(a, v), r=(), w=w)

    def dma(self, eng, out, in_, r, w, semkey, new_batch=True, slow=False, deps=()):
        kw = {}
        if slow:
            kw['allow_slow_non_contiguous'] = True
        return self.P.dma(eng, lambda e, o=out, i=in_, kw=kw: e.dma_start(out=o, in_=i, **kw), r=r, w=w,
                          semkey=(semkey, eng), new_batch=new_batch, deps=deps)

    def ring_load(self, src, kc, width, r=(), eng='sp'):
        s = self.ringn % NSLOT
        self.ringn += 1
        view = self.ring[s][:, 0:kc * width].rearrange("p (k w) -> p k w", k=kc)
        key = ('ring', s)
        nm = src.tensor.name
        if nm in self.wblock:
            r = [('wb', nm, src.offset // self.wblock[nm])]
        self.dma(eng, view, src, r=r, w=[key], semkey=key)
        return view, key

    def build(self):
        nc, cfg = self.nc, self.cfg
        xp = self.din("xp", [SEQ, D])
        xs = self.din("xs", [NS, D])
        cvec = self.din("cvec", [2, D])
        self.w_ada = self.din("ada_w", [DEPTH, D, 9 * D])
        ada_b = self.din("ada_b", [DEPTH, 9 * D])
        norm_g = self.din("norm_g", [DEPTH, 6, D])
        ffn_w_in = self.din("ffn_w_in", [DEPTH, 2, D, 2 * DFF])
        ffn_w_out = self.din("ffn_w_out", [DEPTH, 2, DFF, D])
        self.I = I = {}
        I['state_ssm'] = self.din("state_ssm", [2, 2048, 128])
        I['state_conv'] = self.din("state_conv", [2, 3, 3072])
        I['ssd_w_in'] = self.din("ssd_w_in", [2, D, 5152])
        I['ssd_conv_w'] = self.din("ssd_conv_w", [2, 4 * 3072])
        I['ssd_conv_b'] = self.din("ssd_conv_b", [2, 3072])
        I['ssd_dt_bias'] = self.din("ssd_dt_bias", [2, 32])
        I['ssd_a_log'] = self.din("ssd_a_log", [2, 32])
        I['ssd_d'] = self.din("ssd_d", [2, 32])
        I['ssd_norm_g'] = self.din("ssd_norm_g", [2, 2048])
        I['ssd_w_out'] = self.din("ssd_w_out", [2, 2048, D])
        I['cache_sb_k'] = self.din("cache_sb_k", [PAST, D])
        I['cache_sb_v'] = self.din("cache_sb_v", [PAST, D])
        I['sb_w_qkv'] = self.din("sb_w_qkv", [D, 3 * D])
        I['sb_w_out'] = self.din("sb_w_out", [D, D])
        I['cache_swa_k'] = self.din("cache_swa_k", [128, 256])
        I['cache_swa_v'] = self.din("cache_swa_v", [128, 256])
        I['swa_w_qkv'] = self.din("swa_w_qkv", [D, 1536])
        I['swa_sinks'] = self.din("swa_sinks", [16])
        I['swa_w_out'] = self.din("swa_w_out", [D, D])
        yp = self.dout("yp", [SEQ, D])
        ys = self.dout("ys", [NS, D])
        self.O = O = {}
        for nm in ('swak_p', 'swak_s', 'swav_p', 'swav_s'):
            O[nm] = self.dout(nm, [128, 256])
        O['sbk_p'] = self.dout("sbk_p", [SEQ, D])
        O['sbk_s'] = self.dout("sbk_s", [NS, D])
        O['sbv_p'] = self.dout("sbv_p", [SEQ, D])
        O['sbv_s'] = self.dout("sbv_s", [NS, D])
        O['ssm_p'] = self.dout("ssm_p", [2, 2048, 128])
        O['ssm_s'] = self.dout("ssm_s", [2, 2048, 128])
        O['conv_p'] = self.dout("conv_p", [2, 3, 3072])
        O['conv_s'] = self.dout("conv_s", [2, 3, 3072])
        cnp, ccols = _consts_np()
        cdram = nc.inline_tensor(cnp, "consts").ap()
        cbnp, cbcols = _consts_bf_np()
        cbdram = nc.inline_tensor(cbnp, "constsb").ap()
        self.wb_ffn_in = self.dint("wb_ffn_in", [DEPTH, 2, D, 2 * DFF])
        self.wb_ffn_out = self.dint("wb_ffn_out", [DEPTH, 2, DFF, D])
        self.wb_ssd_in = self.dint("wb_ssd_in", [2, D, 5152])
        self.wb_ssd_out = self.dint("wb_ssd_out", [2, 2048, D])
        self.Sd = self.dint("Sd", [2, 128, 2048], F32)
        self.wb_sb_qkv = self.dint("wb_sb_qkv", [D, 3 * D])
        self.wb_sb_out = self.dint("wb_sb_out", [D, D])
        self.wb_swa_qkv = self.dint("wb_swa_qkv", [D, 1536])
        self.wb_swa_out = self.dint("wb_swa_out", [D, D])
        self.KTs = self.dint("KTs", [8, 128, SEQ])
        self.Vs = self.dint("Vs", [8, 128, SEQ // 128, 128])
        self.KTs_s = self.dint("KTs_s", [8, 128, PAST])
        self.Vs_s = self.dint("Vs_s", [8, 128, PAST // 128, 128])

        self.psum = [self.es.enter_context(nc.psum_tensor("ps%d" % i, [128, 512], F32)) for i in range(8)]
        self.ring = [self.sb("ring%d" % i, [128, SLOTW], BF16) for i in range(NSLOT)]
        self.X = self.sb("X", [128, KC * T], F32)
        self.Hb = self.sb("Hb", [128, KC * T], BF16)
        self.G = self.sb("G", [128, FC * T], BF16)
        self.Yf = self.sb("Yf", [128, KC * T], F32)
        self.SQ = self.sb("SQ", [128, KC * T], BF16)
        self.stg = [self.sb("stg%d" % i, [128, D], F32) for i in range(2)]
        self.tmpf = [self.sb("tmpf%d" % i, [128, T], F32) for i in range(4)]
        self.rstd = self.sb("rstd", [128, T], F32)
        self.cf = self.sb("cf", [128, cnp.shape[1]], F32)
        self.cb = self.sb("cb", [128, cbnp.shape[1]], BF16)
        self.identb = self.sb("identb", [128, 128], BF16)
        self.onesb = self.sb("onesb", [128, 128], BF16)
        self.cact = self.sb("cact", [128, 16], BF16)
        self.ccol = self.sb("ccol", [128, 16], F32)
        self.adab = self.sb("adab", [128, DEPTH * 72], F32)
        self.ng = self.sb("ng", [128, DEPTH * 48], F32)
        self.mod = self.sb("mod", [128, DEPTH * 144], F32)
        self.der = self.sb("der", [128, DEPTH * 3 * 3 * 2 * 8], F32)
        self.AR8 = self.sb("AR8", [128, 2048], F32)
        self.ccols = ccols
        self.cbcols = cbcols
        self.identf = self.cf[:, ccols['ident'][0]:ccols['ident'][0] + 128]
        self.ssd_alloc()
        self.sb_alloc()
        self.KTprev = self.sb("KTprev", [128, 512], BF16)
        self.Vprev = self.sb("Vprev", [128, 256], BF16)
        self.esink = self.sb("esink", [128, 8], F32)

        self.dma('pool', self.cf[:], cdram, r=[], w=['cf'], semkey='cf')
        self.dma('pool', self.cb[:], cbdram, r=[], w=['cb'], semkey='cb')
        self.cp('dve', self.identb[:], self.identf, r=['cf'], w=['identb'])
        self.memset('dve', self.onesb[:], 1.0, w=['onesb'])
        self.load_cols(self.ccol[:, 0:8], cvec[0, :], 'ccol')
        self.load_cols(self.ccol[:, 8:16], cvec[1, :], 'ccol')
        for i in range(DEPTH):
            for h in range(2):
                self.load_cols(self.adab[:, i * 72 + h * 36:i * 72 + h * 36 + 36], ada_b[i, h * 4608:(h + 1) * 4608], 'adab')
            self.load_cols(self.ng[:, i * 48:(i + 1) * 48], norm_g[i].rearrange("a d -> (a d)"), 'ng')
        for x in range(4):
            self.memset('pool', self.Qm[x][:], 0.0, w=[('Qm', x)])
        self.ssd_params()
        for hh in range(2):
            self.dma('pool', self.esink[hh * 64:(hh + 1) * 64, :], bass.AP(I['swa_sinks'].tensor, hh, [[0, 64], [2, 8]]),
                     r=[], w=['esink'], semkey='esink', slow=True, new_batch=False)
        self.act(self.esink[:], self.esink[:], AF.Exp, r=['esink'], w=['esink'])
        self.prepass_weights(ffn_w_in, ffn_w_out)
        self.prepass_mod()

        ntile = cfg.get('ntile', NTILE)
        tiles = [('p', t) for t in range(ntile)]
        if cfg.get('sample', True):
            tiles.append(('s', 0))
        stop = cfg.get('stop')
        for kind, t in tiles:
            N = T if kind == 'p' else NS
            q = 0 if kind == 'p' else 1
            src = xp[t * T:(t + 1) * T, :] if kind == 'p' else xs
            dst = yp[t * T:(t + 1) * T, :] if kind == 'p' else ys
            last = (kind == 's') or (t == ntile - 1)
            self.load_fm(src, N)
            for i in range(cfg.get('depth', DEPTH)):
                self.ffn(i, 0, q, N)
                if stop == 'x%d_0' % i:
                    break
                if i % 3 == 0:
                    self.ssd(i, q, N, t, last)
                elif i % 3 == 1:
                    if kind == 's':
                        self.sb_prep_sample()
                    self.sbmix(i, q, N, t, last)
                else:
                    self.swamix(i, q, N, t, last)
                if stop == 'x%d_1' % i:
                    break
                self.ffn(i, 1, q, N)
            self.store_fm(dst, N)
        print('sbuf bytes remaining', self.nc.sbuf_bytes_remaining() if callable(getattr(self.nc, 'sbuf_bytes_remaining', None)) else getattr(self.nc, 'sbuf_bytes_remaining', None))
        self.P.emit(self.es)
        return nc

    def load_cols(self, dst, src_vec, key):
        self.dma('pool', dst, src_vec.rearrange("(n p) -> p n", p=128), r=[], w=[key], semkey=key, slow=True,
                 new_batch=False)

    def xv(self, tile, N, nch=KC):
        return tile[:, 0:nch * N].rearrange("p (k n) -> p k n", k=nch)

    def prepass_mod(self):
        cact_v = self.cact[:].rearrange("p (k two) -> p two k", two=2)
        for q in range(2):
            self.act(cact_v[:, q, :], self.ccol[:, q * 8:(q + 1) * 8], AF.Silu, r=['ccol'], w=['cact'])
        for i in range(DEPTH):
            bank, pk = self.ps()
            for t36 in range(36):
                src = self.w_ada[i, :, t36 * 256:(t36 + 1) * 256].rearrange("(k p) w -> p k w", p=128)
                wv, wk = self.ring_load(src, KC, 256, eng='pool')
                for c in range(2):
                    oc = t36 * 2 + c
                    for k in range(KC):
                        self.mm(bank[:, oc * 2:oc * 2 + 2], wv[:, k, c * 128:(c + 1) * 128], self.cact[:, k * 2:k * 2 + 2],
                                k == 0, k == KC - 1, r=[wk, 'cact'], w=[pk])
            mv = self.mod[:, i * 144:(i + 1) * 144].rearrange("p (o two) -> p o two", two=2)
            bv = bank[:, 0:144].rearrange("p (o two) -> p o two", two=2)
            ab = self.adab[:, i * 72:(i + 1) * 72].unsqueeze(2).broadcast_to([128, 72, 2])
            self.tt('dve', mv, bv, ab, ALU.add, r=[pk, 'adab'], w=['mod'])
            for s in range(3):
                for q in range(2):
                    def m(j):
                        return self.mod[:, i * 144:(i + 1) * 144].rearrange("p (j k two) -> p j two k", j=9, two=2)[:, j, q, :]
                    gpre = self.ng[:, i * 48 + (2 * s) * 8:i * 48 + (2 * s) * 8 + 8]
                    gpost = self.ng[:, i * 48 + (2 * s + 1) * 8:i * 48 + (2 * s + 1) * 8 + 8]
                    self.stt(self.dslice(i, s, 0, q), m(3 * s + 1), 1.0, gpre, ALU.add, ALU.mult, r=['mod', 'ng'], w=['der'])
                    self.cp('dve', self.dslice(i, s, 1, q), m(3 * s + 0), r=['mod'], w=['der'])
                    self.stt(self.dslice(i, s, 2, q), m(3 * s + 2), 0.5 if s != 1 else 1.0, gpost, ALU.mult, ALU.mult,
                             r=['mod', 'ng'], w=['der'])

    def dslice(self, i, s, which, q, kc=None):
        base = (((i * 3 + s) * 3 + which) * 2 + q) * 8
        if kc is None:
            return self.der[:, base:base + 8]
        return self.der[:, base + kc:base + kc + 1]

    def prepass_weights(self, ffn_w_in, ffn_w_out):
        I = self.I
        self.wblock = {'wb_ffn_in': D * 2 * DFF, 'wb_ffn_out': DFF * D, 'wb_ssd_in': D * 5152, 'wb_ssd_out': 2048 * D,
                       'wb_sb_qkv': D * 3 * D, 'wb_sb_out': D * D, 'wb_swa_qkv': D * 1536, 'wb_swa_out': D * D}

        def cast(dst, src, nm, idx, split=None):
            if split:
                dst = dst.rearrange("k (a b) -> k a b", a=split)
                src = src.rearrange("k (a b) -> k a b", a=split)
            self.dma('pool', dst, src, r=[], w=[('wb', nm, idx)], semkey=('wcast', nm, idx))

        def ffn(i, s_):
            cast(self.wb_ffn_in[i, s_], ffn_w_in[i, s_], 'wb_ffn_in', i * 2 + s_, 4)
            cast(self.wb_ffn_out[i, s_], ffn_w_out[i, s_], 'wb_ffn_out', i * 2 + s_)

        for i in range(DEPTH):
            ffn(i, 0)
            if i % 3 == 0:
                j = i // 3
                cast(self.wb_ssd_in[j], I['ssd_w_in'][j], 'wb_ssd_in', j, 4)
                cast(self.wb_ssd_out[j], I['ssd_w_out'][j], 'wb_ssd_out', j)
            elif i % 3 == 1:
                cast(self.wb_sb_qkv, I['sb_w_qkv'], 'wb_sb_qkv', 0, 2)
                cast(self.wb_sb_out, I['sb_w_out'], 'wb_sb_out', 0)
            else:
                cast(self.wb_swa_qkv, I['swa_w_qkv'], 'wb_swa_qkv', 0)
                cast(self.wb_swa_out, I['swa_w_out'], 'wb_swa_out', 0)
            ffn(i, 1)

    def load_fm(self, src, N):
        nb = (N + 127) // 128
        for tb in range(nb):
            n = min(128, N - tb * 128)
            st = self.stg[self.stgn % 2]
            sk = ('stg', self.stgn % 2)
            self.stgn += 1
            self.dma('pool', st[0:n, :], src[tb * 128:tb * 128 + n, :], r=[], w=[sk], semkey=sk)
            for half in range(2):
                bank, pk = self.ps()
                for k4 in range(4):
                    k = half * 4 + k4
                    self.tr(bank[:, k4 * 128:k4 * 128 + n], st[0:n, k * 128:(k + 1) * 128], self.identf[0:n, 0:n],
                            r=[sk, 'cf'], w=[pk])
                xo = self.xv(self.X, N)[:, half * 4:half * 4 + 4, tb * 128:tb * 128 + n]
                pv = bank[:, 0:512].rearrange("p (k n) -> p k n", k=4)[:, :, 0:n]
                self.cp('dve' if half == 0 else 'act', xo, pv, r=[pk], w=[('X', half * 4 + j) for j in range(4)])

    def store_fm(self, dst, N, src=None, skey='X', nchunk=KC, vtm=None, vkeys=None, only_last=False, kofs=0):
        if src is None:
            src = self.xv(self.X, N)
        nb = (N + 127) // 128
        W = nchunk * 128
        for tb in range(nb):
            if only_last and tb != nb - 1:
                continue
            n = min(128, N - tb * 128)
            st = self.stg[self.stgn % 2]
            sk = ('stg', self.stgn % 2)
            self.stgn += 1
            for half in range((nchunk + 3) // 4):
                bank, pk = self.ps()
                nk4 = min(4, nchunk - half * 4)
                for k4 in range(nk4):
                    k = half * 4 + k4
                    self.tr(bank[0:n, k4 * 128:(k4 + 1) * 128], src[:, k, tb * 128:tb * 128 + n],
                            self.identf, r=[(skey, kofs + k), 'cf'], w=[pk])
                self.cp('dve' if half == 0 else 'act', st[0:n, half * 512:half * 512 + nk4 * 128], bank[0:n, 0:nk4 * 128],
                        r=[pk], w=[sk])
            if dst is not None:
                d = dst[0:n, :] if only_last else dst[tb * 128:tb * 128 + n, :]
                self.dma('pool', d, st[0:n, 0:W], r=[sk], w=[], semkey=sk)
            if vtm is not None:
                self.cp('dve', vtm(tb, n), st[0:n, 0:W], r=[sk], w=vkeys(tb))

    def sumsq_rstd(self, sq_view, sq_keys, N, nch, dim):
        bank, pk = self.ps()
        for k in range(nch):
            self.mm(bank[:, 0:N], self.onesb[:], sq_view(k), k == 0, k == nch - 1, r=['onesb', sq_keys[k]], w=[pk])
        t = self.tmpf[3]
        self.act(t[:, 0:N], bank[:, 0:N], AF.Sqrt, r=[pk, 'epsb'], w=['tmp3'], bias=self.epsb[:, 0:1], scale=1.0 / dim)
        self.P.op('dve', lambda e, o=self.rstd[:, 0:N], i=t[:, 0:N]: e.reciprocal(o, i), r=['tmp3'], w=['rstd'])

    def prenorm(self, i, s, q, N):
        Xv = self.xv(self.X, N)
        SQv = self.xv(self.SQ, N)
        for k in range(KC):
            self.act(SQv[:, k, :], Xv[:, k, :], AF.Square, r=[('X', k)], w=[('SQ', k)])
        self.sumsq_rstd(lambda k: SQv[:, k, :], [('SQ', k) for k in range(KC)], N, KC, D)
        Hv = self.xv(self.Hb, N)
        for k in range(KC):
            tn = k % 3
            t = self.tmpf[tn]
            self.stt(t[:, 0:N], Xv[:, k, :], self.dslice(i, s, 0, q, k), self.rstd[:, 0:N], ALU.mult, ALU.mult,
                     r=[('X', k), 'der', 'rstd'], w=[('tmp', tn)])
            self.act(Hv[:, k, :], t[:, 0:N], AF.Identity, r=[('tmp', tn), 'der'], w=[('Hb', k)],
                     bias=self.dslice(i, s, 1, q, k))

    def postnorm_resid(self, i, s, q, N):
        Xv = self.xv(self.X, N)
        SQv = self.xv(self.SQ, N)
        Yv = self.xv(self.Yf, N)
        self.sumsq_rstd(lambda k: SQv[:, k, :], [('SQ', k) for k in range(KC)], N, KC, D)
        for k in range(KC):
            tn = k % 3
            t = self.tmpf[tn]
            self.tt('dve', t[:, 0:N], Yv[:, k, :], self.rstd[:, 0:N], ALU.mult, r=[('Yf', k), 'rstd'], w=[('tmp', tn)])
            self.tt('pool', Xv[:, k, :], Xv[:, k, :], t[:, 0:N], ALU.add, r=[('tmp', tn), ('X', k)], w=[('X', k)])

    def evac_y(self, i, s, q, N, k, bank, pk):
        Yv = self.xv(self.Yf, N)
        SQv = self.xv(self.SQ, N)
        self.act(Yv[:, k, :], bank[:, 0:N], AF.Identity, r=[pk, 'der'], w=[('Yf', k)], scale=self.dslice(i, s, 2, q, k))
        self.act(SQv[:, k, :], bank[:, 0:N], AF.Square, r=[pk], w=[('SQ', k)])

    def ffn(self, i, which, q, N):
        s = 0 if which == 0 else 2
        self.prenorm(i, s, q, N)
        Hv = self.xv(self.Hb, N)
        Gv = self.xv(self.G, N, FC)
        win = self.wb_ffn_in[i, which]
        wout = self.wb_ffn_out[i, which]
        for jp in range(FC // 2):
            wa, ka = self.ring_load(win[:, jp * 256:(jp + 1) * 256].rearrange("(k p) w -> p k w", p=128), KC, 256, r=['wb'])
            wb_, kb = self.ring_load(win[:, DFF + jp * 256:DFF + (jp + 1) * 256].rearrange("(k p) w -> p k w", p=128), KC,
                                     256, r=['wb'])
            pa = []
            for c in range(2):
                bank, pk = self.ps()
                for k in range(KC):
                    self.mm(bank[:, 0:N], wa[:, k, c * 128:(c + 1) * 128], Hv[:, k, :], k == 0, k == KC - 1,
                            r=[ka, ('Hb', k)], w=[pk])
                pa.append((bank, pk))
            for c in range(2):
                bank, pk = self.ps()
                for k in range(KC):
                    self.mm(bank[:, 0:N], wb_[:, k, c * 128:(c + 1) * 128], Hv[:, k, :], k == 0, k == KC - 1,
                            r=[kb, ('Hb', k)], w=[pk])
                tn = self.tmpn.get('ffn', 0)
                self.tmpn['ffn'] = tn + 1
                t = self.tmpf[tn % 3]
                tk = ('tmp', tn % 3)
                self.act(t[:, 0:N], pa[c][0][:, 0:N], AF.Silu, r=[pa[c][1]], w=[tk])
                j = jp * 2 + c
                self.tt('dve', Gv[:, j, :], t[:, 0:N], bank[:, 0:N], ALU.mult, r=[tk, pk], w=[('G', j)])
        HF = FC // 2
        for oc in range(KC):
            bank, pk = self.ps()
            for hf in range(2):
                wv, wk = self.ring_load(wout[hf * HF * 128:(hf + 1) * HF * 128, oc * 128:(oc + 1) * 128].rearrange(
                    "(k p) w -> p k w", p=128), HF, 128, r=['wb'])
                for k in range(HF):
                    kk = hf * HF + k
                    self.mm(bank[:, 0:N], wv[:, k, :], Gv[:, kk, :], kk == 0, kk == FC - 1, r=[wk, ('G', kk)], w=[pk])
            self.evac_y(i, s, q, N, oc, bank, pk)
        self.postnorm_resid(i, s, q, N)


    def ssd_alloc(self):
        sb = self.sb
        self.hist = sb("hist", [128, 2 * 72], F32)
        self.cwcol = sb("cwcol", [128, 2 * 96], F32)
        self.cbcol = sb("cbcol", [128, 2 * 24], F32)
        self.sngcol = sb("sngcol", [128, 2 * 16], F32)
        self.Dcol = sb("Dcol", [128, 2 * 16], F32)
        self.dtb3 = sb("dtb3", [96, 2], F32)
        self.A3 = sb("A3", [96, 2], F32)
        self.ARX = sb("ARX", [128, 8 * 512], BF16)
        self.a3parts = [self.ARX[0:96, (4 + i) * 512:(5 + i) * 512] for i in range(3)]
        self.a3 = sb("a3", [96, T], BF16)
        self.cols = sb("cols", [128, 4 * 96], F32)
        self.cdrep = sb("cdrep", [128, 4 * 32], F32)
        self.dg = sb("dg", [96, 32], BF16)
        self.ones32 = sb("ones32", [96, 128], F32)
        self.cbuf = [sb("cbuf%d" % i, [128, T + 3], F32) for i in range(2)]
        self.xsfm = [sb("xsfm%d" % i, [128, T], BF16) for i in range(4)]
        self.xtm = [self.ARX[:, i * 512:(i + 1) * 512] for i in range(4)]
        self.xw = [self.ARX[:, (4 + i) * 512:(5 + i) * 512] for i in range(4)]
        self.zs = [sb("zs%d" % i, [128, T], BF16) for i in range(4)]
        self.Ea = [sb("Ea%d" % i, [128, 128], F32) for i in range(2)]
        self.E = [sb("E%d" % i, [128, 128], F32) for i in range(2)]
        self.Cp = [sb("Cp%d" % i, [128, 128], BF16) for i in range(2)]
        self.Mm = [sb("Mm%d" % i, [128, 128], BF16) for i in range(2)]
        self.Sbf = sb("Sbf", [128, 2048], BF16)
        self.sqy = [sb("sqy%d" % i, [128, 128], BF16) for i in range(2)]

    def ssd_params(self):
        I = self.I
        for j in range(2):
            for h in range(2):
                self.load_cols(self.cwcol[:, j * 96 + h * 48:j * 96 + h * 48 + 48], I['ssd_conv_w'][j, h * 6144:(h + 1) * 6144], 'cwcol')
            self.load_cols(self.cbcol[:, j * 24:(j + 1) * 24], I['ssd_conv_b'][j], 'cbcol')
            self.load_cols(self.sngcol[:, j * 16:(j + 1) * 16], I['ssd_norm_g'][j], 'sngcol')
            for hh in range(2):
                src = bass.AP(I['ssd_d'].tensor, j * 32 + hh, [[0, 64], [2, 16]])
                self.dma('pool', self.Dcol[hh * 64:(hh + 1) * 64, j * 16:(j + 1) * 16], src, r=[], w=['Dcol'], semkey='Dcol',
                         slow=True, new_batch=False)
            for g in range(3):
                self.dma('pool', self.dtb3[g * 32:(g + 1) * 32, j:j + 1], bass.AP(I['ssd_dt_bias'].tensor, j * 32, [[1, 32], [1, 1]]),
                         r=[], w=['dtb3'], semkey='dtb3', slow=True, new_batch=False)
                self.dma('pool', self.A3[g * 32:(g + 1) * 32, j:j + 1], bass.AP(I['ssd_a_log'].tensor, j * 32, [[1, 32], [1, 1]]),
                         r=[], w=['A3'], semkey='A3', slow=True, new_batch=False)
        self.act(self.A3[:], self.A3[:], AF.Exp, r=['A3'], w=['A3'])
        self.ts('dve', self.A3[:], self.A3[:], -1.0, ALU.mult, r=['A3'], w=['A3'])
        self.memset('dve', self.ones32[:], 1.0, w=['ones32'])

    def conv_silu(self, j, cc, bank, pk, N, out_ap, out_keys):
        n = self.cbn % 2
        self.cbn += 1
        cb, ck = self.cbuf[n], ('cbuf', n)
        hs = self.hist[:, j * 72 + cc * 3:j * 72 + cc * 3 + 3]
        hk = ('hist', j, cc)
        self.cp('pool', cb[:, 0:3], hs, r=[hk], w=[ck])
        self.cp('act', cb[:, 3:3 + N], bank[:, 0:N], r=[pk], w=[ck])
        self.cp('pool', hs, cb[:, N:N + 3], r=[ck], w=[hk])
        tn = self.tmpn.get('conv', 0)
        self.tmpn['conv'] = tn + 1
        acc, ak = self.tmpf[tn % 2][:, 0:N], ('tmp', tn % 2)

        def wc(tap):
            c0 = j * 96 + tap * 24 + cc
            return self.cwcol[:, c0:c0 + 1]
        self.ts('dve', acc, cb[:, 0:N], wc(0), ALU.mult, r=[ck, 'cwcol', 'cbcol'], w=[ak],
                s2=self.cbcol[:, j * 24 + cc:j * 24 + cc + 1], op1=ALU.add)
        for tap in range(1, 4):
            self.stt(acc, cb[:, tap:tap + N], wc(tap), acc, ALU.mult, ALU.add, r=[ck, 'cwcol', ak], w=[ak])
        self.act(out_ap, acc, AF.Silu, r=[ak], w=out_keys)

    def ssd(self, i, q, N, t, last):
        j = i // 3
        Q = 128 if q == 0 else 32
        nch = N // Q
        I, O = self.I, self.O
        win = self.wb_ssd_in[j]
        wout = self.wb_ssd_out[j]
        self.psrot = list(range(7))
        self.prenorm(i, 1, q, N)
        Hv = self.xv(self.Hb, N)
        S = self.AR8
        Skeys = [('S', x) for x in range(16)]
        Sbkeys = [('Sbf', x) for x in range(16)]
        hkeys = [('hist', j, cc) for cc in range(24)]
        histv = self.hist[:, j * 72:(j + 1) * 72].rearrange("p (c t) -> p c t", t=3)
        if q == 0 and t == 0:
            self.memset('pool', S[:], 0.0, w=Skeys)
            self.memset('pool', self.hist[:, j * 72:(j + 1) * 72], 0.0, w=hkeys)
        elif q == 0:
            self.dma('pool', S[:], self.Sd[j], r=[('Sd', j)], w=Skeys, semkey='Sld')
        else:
            for g4 in range(4):
                st, sk = self.stg[self.stgn % 2], ('stg', self.stgn % 2)
                self.stgn += 1
                self.dma('pool', st[:, 0:512].rearrange("p (b n) -> p b n", b=4),
                         I['state_ssm'][j, g4 * 512:(g4 + 1) * 512, :].rearrange("(b p) n -> p b n", p=128), r=[], w=[sk], semkey=sk)
                bank, pk = self.ps()
                for b4 in range(4):
                    self.tr(bank[:, b4 * 128:(b4 + 1) * 128], st[:, b4 * 128:(b4 + 1) * 128], self.identf, r=[sk, 'cf'], w=[pk])
                self.cp('dve', S[:, g4 * 512:(g4 + 1) * 512], bank[:, 0:512], r=[pk], w=Skeys[g4 * 4:g4 * 4 + 4])
            for tt_ in range(3):
                self.dma('pool', histv[:, :, tt_], I['state_conv'][j, tt_].rearrange("(c p) -> p c", p=128), r=[], w=hkeys,
                         semkey='histld', slow=True, new_batch=(tt_ == 0))
        self.cp('pool', self.Sbf[:], S[:], r=Skeys, w=Sbkeys)

        wv, wk = self.ring_load(win[:, 5120:5152].rearrange("(k p) w -> p k w", p=128), KC, 32, r=['wb'])
        bank, pk = self.ps()
        for g in range(3):
            for k in range(KC):
                self.mm(bank[g * 32:(g + 1) * 32, 0:N], wv[:, k, :], Hv[:, k, :], k == 0, k == KC - 1, r=[wk, ('Hb', k)], w=[pk])
        SQf = self.SQ[:].bitcast(F32)
        dtt, dA, acum, wst = [SQf[0:96, x * 512:x * 512 + N] for x in range(4)]
        kdt, kdA, kac, kw = [[('SQ', 2 * x), ('SQ', 2 * x + 1)] for x in range(4)]
        self.act(dtt, bank[0:96, 0:N], AF.Exp, r=[pk, 'dtb3'], w=kdt, bias=self.dtb3[:, j:j + 1])
        self.act(dtt, dtt, AF.Ln, r=kdt + ['oneb'], w=kdt, bias=self.oneb[0:96, 0:1])
        self.ts('dve', dA, dtt, self.A3[:, j:j + 1], ALU.mult, r=kdt + ['A3'], w=kdA)
        for c in range(nch):
            self.P.op('dve', lambda e, o=acum[:, c * Q:(c + 1) * Q], d0=self.ones32[:, 0:Q], d1=dA[:, c * Q:(c + 1) * Q]:
                      e.tensor_tensor_scan(o, d0, d1, 0.0, ALU.mult, ALU.add), r=kdA + ['ones32'], w=kac)
        H3, M3, L3 = [x[:, 0:N] for x in self.a3parts]
        kH3, kM3, kL3 = ('ARX', 4), ('ARX', 5), ('ARX', 6)
        r1, r2, nac = [self.tmpf[x][0:96, 0:N] for x in range(3)]
        self.cp('dve', H3, acum, r=kac, w=[kH3])
        self.tt('dve', r1, acum, H3, ALU.subtract, r=kac + [kH3], w=[('tmp', 0)])
        self.cp('dve', M3, r1, r=[('tmp', 0)], w=[kM3])
        self.tt('dve', r2, r1, M3, ALU.subtract, r=[('tmp', 0), kM3], w=[('tmp', 1)])
        self.cp('dve', L3, r2, r=[('tmp', 1)], w=[kL3])
        a3 = self.a3
        self.cp('pool', a3[0:32, 0:N], H3[0:32], r=[kH3], w=['a3'])
        self.cp('pool', a3[32:64, 0:N], M3[32:64], r=[kM3], w=['a3'])
        self.cp('pool', a3[64:96, 0:N], L3[64:96], r=[kL3], w=['a3'])
        for c in range(nch):
            self.act(wst[:, c * Q:(c + 1) * Q], acum[:, c * Q:(c + 1) * Q], AF.Exp, r=kac, w=kw, scale=-1.0,
                     bias=acum[:, (c + 1) * Q - 1:(c + 1) * Q])
        self.tt('dve', wst, wst, dtt, ALU.mult, r=kw + kdt, w=kw)
        self.ts('dve', nac, acum, -1.0, ALU.mult, r=kac, w=[('tmp', 2)])
        for c in range(nch):
            bank, pk = self.ps()
            for x, (src, sk_) in enumerate(((nac, [('tmp', 2)]), (dtt, kdt), (wst, kw))):
                self.tr(bank[0:Q, x * 32:(x + 1) * 32], src[0:32, c * Q:(c + 1) * Q], self.identf[0:32, 0:32], r=sk_ + ['cf'], w=[pk])
            self.cp('dve', self.cols[0:Q, c * 96:(c + 1) * 96], bank[0:Q, 0:96], r=[pk], w=['cols'])
        bank, pk = self.ps()
        for c in range(nch):
            self.ts('dve', self.dg[:], self.cbc('i3', 96), a3[:, (c + 1) * Q - 1:(c + 1) * Q], ALU.mult, r=['cb', 'a3'], w=['dg'])
            self.mm(bank[:, c * 32:(c + 1) * 32], self.onesb[0:96, :], self.dg[:], True, True, r=['onesb', 'dg'], w=[pk])
        self.act(self.cdrep[:, 0:nch * 32], bank[:, 0:nch * 32], AF.Exp, r=[pk], w=['cdrep'])

        Yfb = self.Yf[:].bitcast(BF16)
        Bfm = Yfb[:, 0:2048].rearrange("p (g n) -> p g n", g=4)
        Cfm = Yfb[:, 2048:4096].rearrange("p (g n) -> p g n", g=4)
        Btm = Yfb[:, 4096:6144].rearrange("p (c n) -> p c n", c=4)
        cbm = Yfb[:, 6144:8192].rearrange("p (x n) -> p x n", x=16)
        kB, kC, kBt, kcb = [[('Yf', 2 * x), ('Yf', 2 * x + 1)] for x in range(4)]
        for g8 in range(8):
            col0 = 4096 + g8 * 128
            wv, wk = self.ring_load(win[:, col0:col0 + 128].rearrange("(k p) w -> p k w", p=128), KC, 128, r=['wb'])
            bank, pk = self.ps()
            for k in range(KC):
                self.mm(bank[:, 0:N], wv[:, k, :], Hv[:, k, :], k == 0, k == KC - 1, r=[wk, ('Hb', k)], w=[pk])
            dest = (Bfm if g8 < 4 else Cfm)[:, g8 % 4, 0:N]
            self.conv_silu(j, 16 + g8, bank, pk, N, dest, kB if g8 < 4 else kC)
        for c in range(nch):
            bank, pk = self.ps()
            bb = bank[:].bitcast(BF16)
            for g in range(4):
                self.tr(bb[0:Q, g * 128:(g + 1) * 128], Bfm[:, g, c * Q:(c + 1) * Q], self.identb[:], r=kB + ['identb'], w=[pk])
            self.cp('act', Btm[0:Q, c, :], bb[0:Q, 0:512], r=[pk], w=kBt)
        mle = self.cbc('mask_le', Q, 0, Q)
        for g in range(4):
            bank, pk = self.ps()
            for c in range(nch):
                self.mm(bank[0:Q, c * 128:c * 128 + Q], Bfm[:, g, c * Q:(c + 1) * Q], Cfm[:, g, c * Q:(c + 1) * Q], True, True,
                        r=kB + kC, w=[pk])
            pv = bank[0:Q, 0:nch * 128].rearrange("p (c n) -> p c n", c=nch)[:, :, 0:Q]
            self.tt('dve', cbm[0:Q, g * 4:g * 4 + nch, 0:Q], pv, mle.unsqueeze(1).broadcast_to([Q, nch, Q]), ALU.mult,
                    r=[pk, 'cb'], w=kcb)

        Gv = self.xv(self.G, N, FC)
        ssb, ssk = self.psum[7], ('ps', 7)
        RB = [(self.psum[x], ('ps', x)) for x in (0, 1)]
        YB = [(self.psum[x], ('ps', x)) for x in (2, 3)]
        SNB = [(self.psum[x], ('ps', x)) for x in (4, 5)]
        sel3 = self.cbc('sel3', 96)
        negmask = self.cbc('negmask', Q, 0, Q)
        for g in range(4):
            for pi in range(4):
                jj = 4 * g + pi
                xs_, xk = self.xsfm[pi], ('xsfm', pi)
                wv, wk = self.ring_load(win[:, 2048 + jj * 128:2048 + (jj + 1) * 128].rearrange("(k p) w -> p k w", p=128), KC, 128, r=['wb'])
                bank, pk = self.ps()
                for k in range(KC):
                    self.mm(bank[:, 0:N], wv[:, k, :], Hv[:, k, :], k == 0, k == KC - 1, r=[wk, ('Hb', k)], w=[pk])
                self.conv_silu(j, jj, bank, pk, N, xs_[:, 0:N], [xk])
                wv, wk = self.ring_load(win[:, jj * 128:(jj + 1) * 128].rearrange("(k p) w -> p k w", p=128), KC, 128, r=['wb'])
                bank, pk = self.ps()
                for k in range(KC):
                    self.mm(bank[:, 0:N], wv[:, k, :], Hv[:, k, :], k == 0, k == KC - 1, r=[wk, ('Hb', k)], w=[pk])
                self.act(self.zs[pi][:, 0:N], bank[:, 0:N], AF.Silu, r=[pk], w=[('zs', pi)])
            for pi in range(4):
                jj = 4 * g + pi
                xs_, xk = self.xsfm[pi], ('xsfm', pi)
                bank, pk = self.ps()
                bb = bank[:].bitcast(BF16)
                for c in range(nch):
                    self.tr(bb[0:Q, c * 128:(c + 1) * 128], xs_[:, c * Q:(c + 1) * Q], self.identb[:], r=[xk, 'identb'], w=[pk])
                xt, xtk = self.xtm[pi], ('ARX', pi)
                self.cp('act', xt[0:Q, 0:nch * 128], bb[0:Q, 0:nch * 128], r=[pk], w=[xtk])
                xwt, xwk = self.xw[pi], ('ARX', 4 + pi)
                wcv = self.cols[0:Q, 0:nch * 96].rearrange("p (c x) -> p c x", c=nch)[:, :, 64 + 2 * jj:64 + 2 * jj + 2]
                self.tt('dve', xwt[0:Q, 0:nch * 128].rearrange("p (c h d) -> p c h d", c=nch, h=2),
                        xt[0:Q, 0:nch * 128].rearrange("p (c h d) -> p c h d", c=nch, h=2),
                        wcv.unsqueeze(3).broadcast_to([Q, nch, 2, 64]), ALU.mult, r=[xtk, 'cols'], w=[xwk])

            items = [(c, pi, hh) for c in range(nch) for pi in range(4) for hh in range(2)]

            def sR(x):
                c, pi, hh = items[x]
                h = 2 * (4 * g + pi) + hh
                rb, rpk = RB[x % 2]
                sel = sel3[:, h * 128:(h + 1) * 128]
                a3c = a3[:, c * Q:(c + 1) * Q]
                self.mm(rb[:, 0:Q], sel, a3c, True, True, r=['cb', 'a3'], w=[rpk])
                self.mm(rb[0:Q, 128:128 + Q], sel[:, 0:Q], a3c, True, False, r=['cb', 'a3'], w=[rpk])
                self.mm(rb[0:Q, 128:128 + Q], self.identb[0:Q, 0:Q], negmask, False, True, r=['cb', 'identb'], w=[rpk])

            def sA(x):
                c, pi, hh = items[x]
                h = 2 * (4 * g + pi) + hh
                rb, rpk = RB[x % 2]
                e = x % 2
                self.act(self.Ea[e][:, 0:Q], rb[:, 0:Q], AF.Exp, r=[rpk], w=[('Ea', e)])
                self.act(self.E[e][0:Q, 0:Q], rb[0:Q, 128:128 + Q], AF.Exp, r=[rpk, 'cols'], w=[('E', e)],
                         bias=self.cols[0:Q, c * 96 + h:c * 96 + h + 1])

            def sD(x):
                c, pi, hh = items[x]
                h = 2 * (4 * g + pi) + hh
                e = x % 2
                self.tt('dve', self.Cp[e][:, 0:Q], Cfm[:, g, c * Q:(c + 1) * Q], self.Ea[e][:, 0:Q], ALU.mult,
                        r=kC + [('Ea', e)], w=[('Cp', e)])
                self.stt(self.Mm[e][0:Q, 0:Q], self.E[e][0:Q, 0:Q], self.cols[0:Q, c * 96 + 32 + h:c * 96 + 32 + h + 1],
                         cbm[0:Q, g * 4 + c, 0:Q], ALU.mult, ALU.mult, r=[('E', e), 'cols'] + kcb, w=[('Mm', e)])

            def sY(x):
                c, pi, hh = items[x]
                jj = 4 * g + pi
                h = 2 * jj + hh
                e = x % 2
                ybank, ypk = YB[(x // 2) % 2]
                self.mm(ybank[hh * 64:(hh + 1) * 64, 0:Q], self.Sbf[:, h * 64:(h + 1) * 64], self.Cp[e][:, 0:Q], True, False,
                        r=[('Sbf', jj), ('Cp', e)], w=[ypk])
                self.mm(ybank[hh * 64:(hh + 1) * 64, 0:Q], self.xtm[pi][0:Q, c * 128 + hh * 64:c * 128 + (hh + 1) * 64],
                        self.Mm[e][0:Q, 0:Q], False, True, r=[('ARX', pi), ('Mm', e)], w=[ypk])

            def pP1(x):
                c, pi, hh = items[x]
                jj = 4 * g + pi
                pp = (x // 2) % 2
                ybank, ypk = YB[pp]
                xs_, xk = self.xsfm[pi], ('xsfm', pi)
                y1, y1k = self.tmpf[pp][:, 0:Q], self.tk(pp)
                y2, y2k = self.tmpf[2 + pp][:, 0:Q], self.tk(2 + pp)
                self.stt(y1, xs_[:, c * Q:(c + 1) * Q], self.Dcol[:, j * 16 + jj:j * 16 + jj + 1], ybank[:, 0:Q], ALU.mult, ALU.add,
                         r=[xk, 'Dcol', ypk], w=[y1k])
                self.tt('dve', y2, y1, self.zs[pi][:, c * Q:(c + 1) * Q], ALU.mult, r=[y1k, ('zs', pi)], w=[y2k])
                self.act(Gv[:, jj, c * Q:(c + 1) * Q], y2, AF.Identity, r=[y2k, 'sngcol'], w=[('G', jj)],
                         scale=self.sngcol[:, j * 16 + jj:j * 16 + jj + 1])
                self.act(self.sqy[pp][:, 0:Q], y2, AF.Square, r=[y2k], w=[('sqy', pp)])
                first = (g == 0 and x == 1)
                lastm = (g == 3 and x == len(items) - 1)
                self.mm(ssb[:, c * Q:(c + 1) * Q], self.onesb[:], self.sqy[pp][:, 0:Q], first, lastm, r=['onesb', ('sqy', pp)],
                        w=[ssk], sgc=True)
                sn, snk = SNB[pp]
                self.mm(sn[:, 0:128], Btm[0:Q, c, g * 128:(g + 1) * 128], self.xw[pi][0:Q, c * 128:(c + 1) * 128], True, True,
                        r=kBt + [('ARX', 4 + pi)], w=[snk])

            def pP2(x):
                c, pi, hh = items[x]
                jj = 4 * g + pi
                pp = (x // 2) % 2
                sn, snk = SNB[pp]
                Sp = S[:, jj * 128:(jj + 1) * 128]
                Sp3 = Sp.rearrange("p (h d) -> p h d", h=2)
                cdv = self.cdrep[:, c * 32 + 2 * jj:c * 32 + 2 * jj + 2].unsqueeze(2).broadcast_to([128, 2, 64])
                self.tt('dve', Sp3, Sp3, cdv, ALU.mult, r=[('S', jj), 'cdrep'], w=[('S', jj)])
                self.tt('dve', Sp, Sp, sn[:, 0:128], ALU.add, r=[('S', jj), snk], w=[('S', jj)])
                self.cp('pool', self.Sbf[:, jj * 128:(jj + 1) * 128], Sp, r=[('S', jj)], w=[('Sbf', jj)])

            n_it = len(items)
            for it in range(n_it + 6):
                if it < n_it:
                    sR(it)
                if 0 <= it - 1 < n_it:
                    sA(it - 1)
                if 0 <= it - 2 < n_it:
                    sD(it - 2)
                if 0 <= it - 3 < n_it:
                    sY(it - 3)
                if 0 <= it - 4 < n_it and (it - 4) % 2 == 1:
                    pP1(it - 4)
                if 0 <= it - 5 < n_it and (it - 5) % 2 == 1:
                    pP2(it - 5)

        t3 = self.tmpf[3]
        self.act(t3[:, 0:N], ssb[:, 0:N], AF.Sqrt, r=[ssk, 'epsb'], w=['tmp3'], bias=self.epsb[:, 0:1], scale=1.0 / 2048)
        self.P.op('dve', lambda e, o=self.rstd[:, 0:N], i_=t3[:, 0:N]: e.reciprocal(o, i_), r=['tmp3'], w=['rstd'])
        for oc in range(KC):
            wv, wk = self.ring_load(wout[:, oc * 128:(oc + 1) * 128].rearrange("(k p) w -> p k w", p=128), 16, 128, r=['wb'])
            bank, pk = self.ps()
            for k in range(16):
                self.mm(bank[:, 0:N], wv[:, k, :], Gv[:, k, :], k == 0, k == 15, r=[wk, ('G', k)], w=[pk])
            tn = oc % 3
            tq = self.tmpf[tn]
            self.tt('dve', tq[:, 0:N], bank[:, 0:N], self.rstd[:, 0:N], ALU.mult, r=[pk, 'rstd'], w=[('tmp', tn)])
            self.evac_y(i, 1, q, N, oc, tq, ('tmp', tn))
        self.postnorm_resid(i, 1, q, N)

        if q == 0 and not last:
            self.dma('pool', self.Sd[j], S[:], r=Skeys, w=[('Sd', j)], semkey='Sst')
        if last:
            dst = O['ssm_p' if q == 0 else 'ssm_s'][j]
            for g4 in range(4):
                bank, pk = self.ps()
                for b4 in range(4):
                    blk = g4 * 4 + b4
                    self.tr(bank[:, b4 * 128:(b4 + 1) * 128], S[:, blk * 128:(blk + 1) * 128], self.identf, r=[('S', blk), 'cf'], w=[pk])
                st, sk = self.stg[self.stgn % 2], ('stg', self.stgn % 2)
                self.stgn += 1
                self.cp('dve', st[:, 0:512], bank[:, 0:512], r=[pk], w=[sk])
                self.dma('pool', dst[g4 * 512:(g4 + 1) * 512, :].rearrange("(b p) n -> p b n", p=128),
                         st[:, 0:512].rearrange("p (b n) -> p b n", b=4), r=[sk], w=[], semkey=sk)
            cdst = O['conv_p' if q == 0 else 'conv_s'][j]
            for tt_ in range(3):
                self.dma('pool', cdst[tt_].rearrange("(c p) -> p c", p=128), histv[:, :, tt_], r=hkeys, w=[], semkey='histst',
                         slow=True, new_batch=(tt_ == 0))
        self.psrot = list(range(8))


    def sb_alloc(self):
        sb = self.sb
        self.KTseg = [self.ARX[:, i * 1024:(i + 1) * 1024] for i in range(2)]
        self.Vseg = [self.ARX[:, 2048 + i * 1024:2048 + (i + 1) * 1024] for i in range(2)]
        self.SPrun = self.cbuf
        self.SPrunb = self.xsfm
        self.segn = 0
        self.un = 0
        self.Qm = [sb("Qm%d" % i, [128, T], BF16) for i in range(4)]

    def tk(self, n):
        return ('tmp', n) if n < 3 else 'tmp3'

    def sb_prep_sample(self):
        I = self.I
        for c in range(8):
            self.dma('pool', self.Vs_s[c], I['cache_sb_v'][:, c * 128:(c + 1) * 128].rearrange("(b p) d -> p b d", p=128),
                     r=[], w=['Vs_s'], semkey='vss', new_batch=(c == 0))
        Gv = self.xv(self.G, 128, FC)
        for b in range(PAST // 128):
            st, sk = self.stg[self.stgn % 2], ('stg', self.stgn % 2)
            self.stgn += 1
            self.dma('pool', st[:, :], I['cache_sb_k'][b * 128:(b + 1) * 128, :], r=[], w=[sk], semkey=sk)
            for half in range(2):
                bank, pk = self.ps()
                for k4 in range(4):
                    k = half * 4 + k4
                    self.tr(bank[:, k4 * 128:(k4 + 1) * 128], st[:, k * 128:(k + 1) * 128], self.identf, r=[sk, 'cf'], w=[pk])
                self.cp('dve' if half == 0 else 'act', Gv[:, 14 + half * 4:14 + half * 4 + 4, :],
                        bank[:, 0:512].rearrange("p (k n) -> p k n", k=4), r=[pk], w=[('G', 14 + half * 4 + x) for x in range(4)])
            self.dma('pool', self.KTs_s[:, :, b * 128:(b + 1) * 128].rearrange("c p t -> p c t"), Gv[:, 14:22, :],
                     r=[('G', 14 + x) for x in range(8)], w=['KTs_s'], semkey='ktss')

    def sbmix(self, i, q, N, t, last):
        I, O = self.I, self.O
        self.psrot = list(range(5))
        self.prenorm(i, 1, q, N)
        Hv = self.xv(self.Hb, N)
        Gv = self.xv(self.G, N, FC)
        Yv = self.xv(self.Yf, N)
        win, wout = self.wb_sb_qkv, self.wb_sb_out
        nb = (N + 127) // 128
        Vtm = self.AR8[:].bitcast(BF16).rearrange("p (b f) -> p b f", b=4)
        vk = lambda tb: [('S', 4 * tb + x) for x in range(4)]
        allvk = [('S', x) for x in range(16)]
        for part in range(3):
            if part >= self.cfg.get('sbparts', 3):
                continue
            for c in range(8):
                col0 = part * D + c * 128
                wv, wk = self.ring_load(win[:, col0:col0 + 128].rearrange("(k p) w -> p k w", p=128), KC, 128, r=['wb'])
                bank, pk = self.ps()
                for k in range(KC):
                    self.mm(bank[:, 0:N], wv[:, k, :], Hv[:, k, :], k == 0, k == KC - 1, r=[wk, ('Hb', k)], w=[pk])
                if part == 0:
                    self.cp('act', Gv[:, c, :], bank[:, 0:N], r=[pk], w=[('G', c)])
                elif part == 1:
                    self.cp('act', Gv[:, 8 + c, :], bank[:, 0:N], r=[pk], w=[('G', 8 + c)])
                    self.cp('dve', Yv[:, c, :], bank[:, 0:N], r=[pk], w=[('Yf', c)])
                else:
                    self.cp('dve', Yv[:, c, :], bank[:, 0:N], r=[pk], w=[('Yf', c)])
            if part == 1:
                dst = O['sbk_p'][t * T:(t + 1) * T, :] if q == 0 else O['sbk_s']
                if self.cfg.get('sbdbg') == 'nodma':
                    dst = None
                if self.cfg.get('sbdbg') != 'nostore':
                    self.store_fm(dst, N, src=Yv, skey='Yf')
                if q == 0 and not last:
                    self.dma('pool', self.KTs[:, :, t * T:(t + 1) * T].rearrange("c p t -> p c t"), Gv[:, 8:16, :],
                             r=[('G', 8 + x) for x in range(8)], w=['KTs'], semkey='kts')
            elif part == 2:
                dst = O['sbv_p'][t * T:(t + 1) * T, :] if q == 0 else O['sbv_s']
                self.store_fm(dst, N, src=Yv, skey='Yf', vtm=lambda tb, n: Vtm[0:n, tb, :], vkeys=vk)
                if q == 0 and not last:
                    for tb in range(nb):
                        self.dma('pool', self.Vs[:, :, 4 * t + tb, :].rearrange("c p d -> p c d"),
                                 Vtm[:, tb, :].rearrange("p (c d) -> p c d", c=8), r=vk(tb), w=['Vs'], semkey='vs',
                                 new_batch=(tb == 0))
        if self.cfg.get('sbstage', 9) < 2:
            self.psrot = list(range(8))
            return
        OTv = Hv
        ob, ok = self.psum[7], ('ps', 7)
        npast = 4 * t if q == 0 else PAST // 128
        KTd, kdk = (self.KTs, 'KTs') if q == 0 else (self.KTs_s, 'KTs_s')
        Vd, vdk = (self.Vs, 'Vs') if q == 0 else (self.Vs_s, 'Vs_s')
        SEGB = 8
        zeros = self.cbc('zeros', 128, 0, 64)
        for c in range(8):
            units = []
            for r_ in reversed(range(nb)):
                nk = min(128, N - r_ * 128)
                for hh in range(2):
                    units.append(('in', r_, nk, hh))
            segs = [(b0, min(b0 + SEGB, npast)) for b0 in range(0, npast, SEGB)]
            for (b0, b1) in reversed(segs):
                for b in reversed(range(b0, b1)):
                    for hh in range(2):
                        units.append(('past', b, (b0, b1), hh))
            for hh in range(2):
                self.mm(ob[hh * 64:(hh + 1) * 64, 0:N], zeros, Gv[:, c, :], True, False, r=['cb', ('G', c)], w=[ok], sgc=True)
                self.mm(self.psum[6 - hh][:, 0:N], self.cbc('zeros'), Gv[:, c, :], True, False, r=['cb', ('G', c)],
                        w=[('ps', 6 - hh)], sgc=True)
                qx = (c % 2) * 2 + hh
                self.cp('pool', self.Qm[qx][hh * 64:(hh + 1) * 64, 0:N], Gv[hh * 64:(hh + 1) * 64, c, :], r=[('G', c)], w=[('Qm', qx)])
            if self.cfg.get('sbstage', 9) < 3:
                units = []
            curseg = None
            prev = None
            prev2 = None
            for ui, u in enumerate(units):
                lastu = ui >= len(units) - 2
                if u[0] == 'in':
                    _, r_, nk, hh = u
                    c0 = r_ * 128
                    cur = self.sb_stage1(N, c, hh, Gv[:, 8 + c, c0:c0 + nk], [('G', 8 + c)],
                                         Vtm[0:nk, r_, c * 128 + hh * 64:c * 128 + (hh + 1) * 64], vk(r_), c0, nk, True, lastu)
                else:
                    _, b, (b0, b1), hh = u
                    if curseg != (b0, b1):
                        curseg = (b0, b1)
                        si = self.segn % 2
                        self.segn += 1
                        nbl = b1 - b0
                        self.dma('sp', self.KTseg[si][:, 0:nbl * 128], KTd[c, :, b0 * 128:b1 * 128], r=[kdk], w=[('ARX', 2 * si), ('ARX', 2 * si + 1)],
                                 semkey=('kseg', si))
                        self.dma('sp', self.Vseg[si][:, 0:nbl * 128].rearrange("p (b d) -> p b d", d=128), Vd[c, :, b0:b1, :],
                                 r=[vdk], w=[('ARX', 4 + 2 * si), ('ARX', 5 + 2 * si)], semkey=('vseg', si))
                    o_ = (b - b0) * 128
                    cur = self.sb_stage1(N, c, hh, self.KTseg[si][:, o_:o_ + 128], [('ARX', 2 * si), ('ARX', 2 * si + 1)],
                                         self.Vseg[si][:, o_ + hh * 64:o_ + (hh + 1) * 64], [('ARX', 4 + 2 * si), ('ARX', 5 + 2 * si)], 0, 128, False, lastu)
                if prev is not None:
                    self.sb_stage2a(*prev)
                if prev2 is not None:
                    self.sb_stage2b(*prev2)
                prev2 = prev
                prev = cur
            if prev is not None:
                self.sb_stage2a(*prev)
            if prev2 is not None:
                self.sb_stage2b(*prev2)
            if prev is not None:
                self.sb_stage2b(*prev)
            self.cp('act', OTv[:, c, :], ob[:, 0:N], r=[ok], w=[('Hb', c)])
        for oc in range(KC):
            wv, wk = self.ring_load(wout[:, oc * 128:(oc + 1) * 128].rearrange("(k p) w -> p k w", p=128), KC, 128, r=['wb'])
            bank, pk = self.ps()
            for k in range(KC):
                self.mm(bank[:, 0:N], wv[:, k, :], OTv[:, k, :], k == 0, k == KC - 1, r=[wk, ('Hb', k)], w=[pk])
            self.evac_y(i, 1, q, N, oc, bank, pk)
        self.postnorm_resid(i, 1, q, N)
        self.psrot = list(range(8))

    def sb_stage1(self, N, c, hh, KTb, kK, Vb, kV, c0, nk, diag, lastu):
        Gv = self.xv(self.G, N, FC)
        SQv = self.xv(self.SQ, N)
        ncol = N - c0
        zb, zk = self.ps()
        qx = (c % 2) * 2 + hh
        self.mm(zb[0:nk, 0:ncol], KTb, self.Qm[qx][:, c0:N], True, True, r=kK + [('Qm', qx)], w=[zk])
        u = self.un
        self.un += 1
        e_t, ek = self.tmpf[u % 2][0:nk, 0:ncol], self.tk(u % 2)
        self.act(e_t, zb[0:nk, 0:ncol], AF.Exp, r=[zk], w=[ek], scale=0.125)
        spb, sbk_ = SQv[0:nk, 3 + u % 3, 0:ncol], ('SQ', 3 + u % 3)
        self.act(spb, e_t, AF.Ln, r=[ek, 'oneb'], w=[sbk_], bias=self.oneb[0:nk, 0:1])
        if diag:
            self.tt('pool', spb[:, 0:nk], spb[:, 0:nk], self.cbc('mask_lt', nk, 0, nk), ALU.mult, r=[sbk_, 'cb'], w=[sbk_])
        lsb, lk = SQv[0:nk, u % 3, 0:ncol], ('SQ', u % 3)
        self.stt(lsb, zb[0:nk, 0:ncol], 0.125, spb, ALU.mult, ALU.subtract, r=[zk, sbk_], w=[lk])
        return (N, hh, Vb, kV, c0, nk, diag, lastu, u, spb, sbk_, lsb, lk)

    def sb_stage2a(self, N, hh, Vb, kV, c0, nk, diag, lastu, u, spb, sbk_, lsb, lk):
        SQv = self.xv(self.SQ, N)
        ncol = N - c0
        acc, ack = self.psum[6 - hh], ('ps', 6 - hh)
        a_ = acc[0:nk, c0:N]
        self.mm(a_, self.cbc('negtri', nk, 0, nk), spb, False, False, r=['cb', sbk_], w=[ack], sgc=True)
        self.mm(a_, self.identb[0:nk, 0:nk], lsb, False, False, r=['identb', lk], w=[ack], sgc=True)
        if diag:
            self.mm(acc[0:nk, c0:c0 + nk], self.identb[0:nk, 0:nk], self.cbc('negmask_lt', nk, 0, nk), False, False,
                    r=['identb', 'cb'], w=[ack], sgc=True)
        w_t, wk_ = SQv[0:nk, 6 + u % 2, 0:ncol], ('SQ', 6 + u % 2)
        self.act(w_t, a_, AF.Exp, r=[ack], w=[wk_])

    def sb_stage2b(self, N, hh, Vb, kV, c0, nk, diag, lastu, u, spb, sbk_, lsb, lk):
        SQv = self.xv(self.SQ, N)
        ncol = N - c0
        ob, ok = self.psum[7], ('ps', 7)
        acc, ack = self.psum[6 - hh], ('ps', 6 - hh)
        a_ = acc[0:nk, c0:N]
        w_t, wk_ = SQv[0:nk, 6 + u % 2, 0:ncol], ('SQ', 6 + u % 2)
        self.mm(ob[hh * 64:(hh + 1) * 64, c0:N], Vb, w_t, False, lastu, r=kV + [wk_], w=[ok], sgc=True)
        if nk == 128:
            self.mm(a_, self.cbc('negtrile', nk, 0, nk), spb, False, False, r=['cb', sbk_], w=[ack], sgc=True)
        else:
            self.mm(acc[:, c0:N], self.cbc('negones', nk, 0, 128), spb, False, False, r=['cb', sbk_], w=[ack], sgc=True)
            self.mm(a_, self.cbc('postri', nk, 0, nk), spb, False, False, r=['cb', sbk_], w=[ack], sgc=True)
        self.mm(a_, self.cbc('negident', nk, 0, nk), lsb, False, False, r=['cb', lk], w=[ack], sgc=True)
        if diag:
            self.mm(acc[0:nk, c0:c0 + nk], self.identb[0:nk, 0:nk], self.cbc('posmask_lt', nk, 0, nk), False, False,
                    r=['identb', 'cb'], w=[ack], sgc=True)

    def swamix(self, i, q, N, t, last):
        I, O = self.I, self.O
        self.psrot = list(range(6))
        self.prenorm(i, 1, q, N)
        Hv = self.xv(self.Hb, N)
        Gv = self.xv(self.G, N, FC)
        Yv = self.xv(self.Yf, N)
        win, wout = self.wb_swa_qkv, self.wb_swa_out
        W = 128 + N
        base = 8 * T
        KTd = self.G[:, base:base + 4 * W].rearrange("p (g w) -> p g w", g=4)
        kK = [('G', 8 + x) for x in range(5)]
        vb0 = base + 4 * (128 + T)
        Vt = self.G[:, vb0:vb0 + 5 * 256].rearrange("p (b f) -> p b f", b=5)
        kV = [('G', 13 + x) for x in range(3)]
        nb = (N + 127) // 128
        for c in range(8):
            wv, wk = self.ring_load(win[:, c * 128:(c + 1) * 128].rearrange("(k p) w -> p k w", p=128), KC, 128, r=['wb'])
            bank, pk = self.ps()
            for k in range(KC):
                self.mm(bank[:, 0:N], wv[:, k, :], Hv[:, k, :], k == 0, k == KC - 1, r=[wk, ('Hb', k)], w=[pk])
            self.cp('act', Gv[:, c, :], bank[:, 0:N], r=[pk], w=[('G', c)])
        wv, wk = self.ring_load(win[:, 1024:1280].rearrange("(k p) w -> p k w", p=128), KC, 256, r=['wb'])
        for g in range(4):
            bank, pk = self.ps()
            for half in range(2):
                for k in range(KC):
                    self.mm(bank[half * 64:(half + 1) * 64, 0:N], wv[:, k, g * 64:(g + 1) * 64], Hv[:, k, :], k == 0, k == KC - 1,
                            r=[wk, ('Hb', k)], w=[pk])
            self.cp('act', KTd[:, g, 128:128 + N], bank[:, 0:N], r=[pk], w=kK)
        if last:
            for c2 in range(2):
                bank, pk = self.ps()
                for k in range(KC):
                    self.mm(bank[:, 0:N], wv[:, k, c2 * 128:(c2 + 1) * 128], Hv[:, k, :], k == 0, k == KC - 1,
                            r=[wk, ('Hb', k)], w=[pk])
                self.cp('dve', Yv[:, c2, :], bank[:, 0:N], r=[pk], w=[('Yf', c2)])
        wv, wk = self.ring_load(win[:, 1280:1536].rearrange("(k p) w -> p k w", p=128), KC, 256, r=['wb'])
        for c2 in range(2):
            bank, pk = self.ps()
            for k in range(KC):
                self.mm(bank[:, 0:N], wv[:, k, c2 * 128:(c2 + 1) * 128], Hv[:, k, :], k == 0, k == KC - 1, r=[wk, ('Hb', k)], w=[pk])
            self.cp('dve', Yv[:, 2 + c2, :], bank[:, 0:N], r=[pk], w=[('Yf', 2 + c2)])
        if q == 0 and t > 0:
            self.cp('pool', KTd[:, :, 0:128], self.KTprev[:].rearrange("p (g w) -> p g w", g=4), r=['KTprev'], w=kK)
            self.cp('pool', Vt[:, 0, :], self.Vprev[:], r=['Vprev'], w=kV)
        elif q == 1:
            st, sk = self.stg[self.stgn % 2], ('stg', self.stgn % 2)
            self.stgn += 1
            for dup in range(2):
                self.dma('pool', st[:, 0:512].rearrange("p (g u d) -> p g u d", g=4, u=2)[:, :, dup, :],
                         I['cache_swa_k'].rearrange("p (g d) -> p g d", g=4), r=[], w=[sk], semkey=sk, new_batch=(dup == 0))
            bank, pk = self.ps()
            for g in range(4):
                self.tr(bank[:, g * 128:(g + 1) * 128], st[:, g * 128:(g + 1) * 128], self.identf, r=[sk, 'cf'], w=[pk])
            self.cp('act', KTd[:, :, 0:128], bank[:, 0:512].rearrange("p (g w) -> p g w", g=4), r=[pk], w=kK)
            st, sk = self.stg[self.stgn % 2], ('stg', self.stgn % 2)
            self.stgn += 1
            self.dma('pool', st[:, 0:256], I['cache_swa_v'], r=[], w=[sk], semkey=sk)
            self.cp('pool', Vt[:, 0, :], st[:, 0:256], r=[sk], w=kV)
            self.dma('pool', O['swak_s'][0:96, :], I['cache_swa_k'][32:128, :], r=[], w=[], semkey='swaout', new_batch=True)
            self.dma('pool', O['swav_s'][0:96, :], I['cache_swa_v'][32:128, :], r=[], w=[], semkey='swaout', new_batch=False)
        vdst = None
        if last:
            vdst = O['swav_p'] if q == 0 else O['swav_s'][96:128, :]
        self.store_fm(vdst, N, src=Yv[:, 2:4], skey='Yf', nchunk=2, vtm=lambda tb, n: Vt[0:n, 1 + tb, :], vkeys=lambda tb: kV,
                      only_last=False, kofs=2) if not (last and q == 0) else None
        if last and q == 0:
            for tb in range(nb):
                self.store_fm(O['swav_p'] if tb == nb - 1 else None, 128, src=Yv[:, 2:4, tb * 128:(tb + 1) * 128], skey='Yf', nchunk=2,
                              vtm=lambda tb_, n, tb=tb: Vt[0:n, 1 + tb, :], vkeys=lambda tb_: kV, kofs=2)
            self.store_fm(O['swak_p'], 128, src=Yv[:, 0:2, (nb - 1) * 128:nb * 128], skey='Yf', nchunk=2)
        elif last:
            self.store_fm(O['swak_s'][96:128, :], N, src=Yv[:, 0:2], skey='Yf', nchunk=2)
        OTv = Hv
        ob, ok = self.psum[7], ('ps', 7)
        db, dk = self.psum[6], ('ps', 6)
        NM = self.cbc('nmswa')
        zeros = self.cbc('zeros', 128, 0, 64)
        SQv = self.xv(self.SQ, N)
        if q == 0:
            blocks = []
            for kb in range(-1, nb):
                if kb == -1 and t == 0:
                    continue
                q0, q1 = max(0, 128 * kb), min(N, 128 * kb + 256)
                blocks.append(((kb + 1) * 128, 128, kb + 1, q0, q1, q0 - 128 * kb))
        else:
            blocks = [(0, 128, 0, 0, N, None), (128, N, 1, 0, N, None)]
        for c in range(8):
            for hh in range(2):
                h = 2 * c + hh
                g = h // 4
                self.mm(ob[hh * 64:(hh + 1) * 64, 0:N], zeros, Gv[:, c, :], True, False, r=['cb', ('G', c)], w=[ok], sgc=True)
                self.mm(db[hh * 64:(hh + 1) * 64, 0:N], zeros, Gv[:, c, :], True, False, r=['cb', ('G', c)], w=[dk], sgc=True)
                for bi, (kc0, nk, vblk, q0, q1, p0) in enumerate(blocks):
                    nq = q1 - q0
                    lastb = bi == len(blocks) - 1
                    sb_, sk_ = self.ps()
                    self.mm(sb_[0:nk, 0:nq], KTd[hh * 64:(hh + 1) * 64, g, kc0:kc0 + nk], Gv[hh * 64:(hh + 1) * 64, c, q0:q1],
                            True, p0 is None, r=kK + [('G', c)], w=[sk_])
                    if p0 is not None:
                        self.mm(sb_[0:nk, 0:nq], self.identb[0:nk, 0:nk], NM[0:nk, p0:p0 + nq], False, True, r=['identb', 'cb'], w=[sk_])
                    u = self.un
                    self.un += 1
                    P, pk_ = SQv[0:nk, u % 4, 0:nq], ('SQ', u % 4)
                    self.act(P, sb_[0:nk, 0:nq], AF.Exp, r=[sk_], w=[pk_], scale=0.125)
                    self.mm(ob[hh * 64:(hh + 1) * 64, q0:q1], Vt[0:nk, vblk, g * 64:(g + 1) * 64], P, False, lastb, r=kV + [pk_], w=[ok],
                            sgc=True)
                    self.mm(db[hh * 64:(hh + 1) * 64, q0:q1], self.onesb[0:nk, 0:64], P, False, lastb, r=['onesb', pk_], w=[dk], sgc=True)
            den, dnk = self.tmpf[c % 2][:, 0:N], self.tk(c % 2)
            self.ts('dve', den, db[:, 0:N], self.esink[:, c:c + 1], ALU.add, r=[dk, 'esink'], w=[dnk])
            self.P.op('dve', lambda e, o=den, i_=den: e.reciprocal(o, i_), r=[dnk], w=[dnk])
            self.tt('dve', OTv[:, c, :], ob[:, 0:N], den, ALU.mult, r=[ok, dnk], w=[('Hb', c)])
        if q == 0 and not last:
            self.cp('pool', self.KTprev[:].rearrange("p (g w) -> p g w", g=4), KTd[:, :, N:N + 128], r=kK, w=['KTprev'])
            self.cp('pool', self.Vprev[:], Vt[:, nb, :], r=kV, w=['Vprev'])
        for oc in range(KC):
            wv, wk = self.ring_load(wout[:, oc * 128:(oc + 1) * 128].rearrange("(k p) w -> p k w", p=128), KC, 128, r=['wb'])
            bank, pk = self.ps()
            for k in range(KC):
                self.mm(bank[:, 0:N], wv[:, k, :], OTv[:, k, :], k == 0, k == KC - 1, r=[wk, ('Hb', k)], w=[pk])
            self.evac_y(i, 1, q, N, oc, bank, pk)
        self.postnorm_resid(i, 1, q, N)
        self.psrot = list(range(8))


def build_program(cfg):
    b = Builder(cfg)
    b.epsb = b.sb("epsb", [128, 1], F32)
    b.memset('dve', b.epsb[:], EPS, w=['epsb'])
    b.oneb = b.sb("oneb", [128, 1], F32)
    b.memset('dve', b.oneb[:], 1.0, w=['oneb'])
    nc = b.build()
    return b, nc


def make_in_maps(inputs):
    maps = []
    f = np.ascontiguousarray
    shared = {k: f(inputs[k]) for k in ('ada_w', 'ada_b', 'norm_g', 'ffn_w_in', 'ffn_w_out', 'ssd_w_in', 'ssd_conv_b',
                                        'ssd_dt_bias', 'ssd_a_log', 'ssd_d', 'ssd_norm_g', 'ssd_w_out')}
    shared['ssd_conv_w'] = f(inputs['ssd_conv_w'].reshape(2, 4 * 3072))
    shared['sb_w_qkv'] = f(inputs['sb_w_qkv'][0])
    shared['sb_w_out'] = f(inputs['sb_w_out'][0])
    shared['swa_w_qkv'] = f(inputs['swa_w_qkv'][0])
    shared['swa_w_out'] = f(inputs['swa_w_out'][0])
    shared['swa_sinks'] = f(inputs['swa_sinks'][0])
    for c in range(8):
        m = dict(shared)
        m['xp'] = f(inputs['x_prompt'][c % 4])
        m['xs'] = f(inputs['x_sample'][c])
        m['cvec'] = f(np.stack([inputs['c_prompt'][c % 4], inputs['c_sample'][c]]))
        m['state_ssm'] = f(inputs['state_ssm'][:, c].reshape(2, 2048, 128))
        m['state_conv'] = f(inputs['state_conv'][:, c])
        m['cache_sb_k'] = f(inputs['cache_sb_k'][0, c].reshape(PAST, D))
        m['cache_sb_v'] = f(inputs['cache_sb_v'][0, c].reshape(PAST, D))
        m['cache_swa_k'] = f(inputs['cache_swa_k'][0, c].reshape(128, 256))
        m['cache_swa_v'] = f(inputs['cache_swa_v'][0, c].reshape(128, 256))
        maps.append(m)
    return maps


def kernel(**inputs):
    cfg = {}
    b, nc = build_program(cfg)
    maps = make_in_maps(inputs)
    res = run_bass_kernel_spmd(nc, maps, core_ids=list(range(8)))
    r = res.results
    y_prompt = np.stack([r[c]['yp'] for c in range(4)])
    y_sample = np.stack([r[c]['ys'] for c in range(8)])
    ssm_p = np.stack([r[c]['ssm_p'].reshape(2, 32, 64, 128) for c in range(4)], axis=1)
    ssm_s = np.stack([r[c]['ssm_s'].reshape(2, 32, 64, 128) for c in range(8)], axis=1)
    conv_p = np.stack([r[c]['conv_p'] for c in range(4)], axis=1)
    conv_s = np.stack([r[c]['conv_s'] for c in range(8)], axis=1)
    sbk_p = np.stack([r[c]['sbk_p'].reshape(SEQ, 16, 64) for c in range(4)])[None]
    sbk_s = np.stack([r[c]['sbk_s'].reshape(NS, 16, 64) for c in range(8)])[None]
    sbv_p = np.stack([r[c]['sbv_p'].reshape(SEQ, 16, 64) for c in range(4)])[None]
    sbv_s = np.stack([r[c]['sbv_s'].reshape(NS, 16, 64) for c in range(8)])[None]
    sw = {}
    for nm, n in (('swak_p', 4), ('swak_s', 8), ('swav_p', 4), ('swav_s', 8)):
        sw[nm] = np.stack([r[c][nm].reshape(128, 4, 64) for c in range(n)])[None]
    return (y_prompt, y_sample, ssm_p, ssm_s, conv_p, conv_s, sbk_p, sbk_s, sbv_p, sbv_s,
            sw['swak_p'], sw['swak_s'], sw['swav_p'], sw['swav_s'])
```
